# Optimizing a Trainium2 kernel written in Bass

```python
import math, functools
import jax, jax.numpy as jnp
from jax import lax
import numpy as np

D_MODEL = 1024
BATCH = 8
SEQ = 2048
DEPTH = 4
DEC_BATCH = 8
DEC_SEQ = 16
PAST_LEN = 2048

CHUNK = 64
Q_BLOCK = 128
EPS = 1e-6
ROPE_THETA = 10000.0
NEG = -1e30

A_HEADS = 8
A_KV_HEADS = 2
A_HEAD_DIM = 64
IDX_HEADS = 8
IDX_DIM = 64
TOPK_MAX = 256
B_HEADS = 8
B_HEAD_DIM = 64
P_HEADS = 8
N_KEYS = 128
N_EXPERTS = N_KEYS * N_KEYS
P_KEY_DIM = 128
P_HALF = P_KEY_DIM // 2
P_TOPK = 16
PEER_BLOCK = 128

W_QA = A_HEADS * A_HEAD_DIM
W_KA = A_KV_HEADS * A_HEAD_DIM
W_VA = A_KV_HEADS * A_HEAD_DIM
W_QI = IDX_HEADS * IDX_DIM
W_KI = IDX_DIM
W_WI = IDX_HEADS
W_QB = B_HEADS * B_HEAD_DIM
W_KB = B_HEADS * B_HEAD_DIM
W_VB = B_HEADS * B_HEAD_DIM
W_G = D_MODEL
SPLITS = (W_QA, W_KA, W_VA, W_QI, W_KI, W_WI, W_QB, W_KB, W_VB, W_G, W_G)
N_IN = W_QA + W_KA + W_VA + W_QI + W_KI + W_WI + W_QB + W_KB + W_VB + 2 * W_G

kernel_name = 'hybrid_dsa_stickbreak_peer_stream'


def _rms(x, g):
    xf = x.astype(jnp.float32)
    y = xf * lax.rsqrt(jnp.mean(xf * xf, axis=-1, keepdims=True) + EPS)
    return (y * g.astype(jnp.float32)).astype(x.dtype)


def _rope(x, pos):
    half = x.shape[-1] // 2
    inv = ROPE_THETA ** (-jnp.arange(half, dtype=jnp.float32) / half)
    ang = pos.astype(jnp.float32)[:, None] * inv[None, :]
    cos = jnp.cos(ang)[None, :, None, :]
    sin = jnp.sin(ang)[None, :, None, :]
    xf = x.astype(jnp.float32)
    x1, x2 = xf[..., :half], xf[..., half:]
    return jnp.concatenate([x1 * cos - x2 * sin, x2 * cos + x1 * sin], axis=-1).astype(x.dtype)


def _split_cols(p):
    outs = []
    start = 0
    for w in SPLITS:
        outs.append(p[..., start:start + w])
        start += w
    return outs


def _sweep(fn, qs, q_pos):
    T = q_pos.shape[0]
    if T % Q_BLOCK != 0:
        return fn(qs, q_pos)
    nb = T // Q_BLOCK

    def blk(a):
        return jnp.moveaxis(a.reshape((a.shape[0], nb, Q_BLOCK) + a.shape[2:]), 1, 0)

    out = lax.map(lambda args: fn(args[0], args[1]),
                  (tuple(blk(a) for a in qs), q_pos.reshape(nb, Q_BLOCK)))
    out = jnp.moveaxis(out, 0, 1)
    return out.reshape((out.shape[0], T) + out.shape[3:])


def _dsa_block(qs, q_pos, k, v, ki, k_pos, topk):
    q, qi, wi = qs
    B, Q = q.shape[0], q.shape[1]
    f32 = jnp.float32
    isc = jnp.einsum('bqhd,bsd->bqhs', qi.astype(f32), ki.astype(f32)) * (IDX_DIM ** -0.5)
    isc = jnp.einsum('bqhs,bqh->bqs', jax.nn.relu(isc), wi.astype(f32) * (IDX_HEADS ** -0.5))
    adm = (k_pos[None, :] // CHUNK) <= (q_pos[:, None] // CHUNK)
    isc = jnp.where(adm[None], isc, NEG)
    _, idx = lax.top_k(isc, topk)
    valid = (k_pos[idx] // CHUNK) <= (q_pos[None, :, None] // CHUNK)
    k_sel = jax.vmap(lambda kb, ib: kb[ib])(k, idx)
    v_sel = jax.vmap(lambda vb, ib: vb[ib])(v, idx)
    qg = q.reshape(B, Q, A_KV_HEADS, A_HEADS // A_KV_HEADS, A_HEAD_DIM).astype(f32)
    s = jnp.einsum('bqngd,bqknd->bqngk', qg, k_sel.astype(f32)) * (A_HEAD_DIM ** -0.5)
    s = jnp.where(valid[:, :, None, None, :], s, NEG)
    p = jax.nn.softmax(s, axis=-1)
    o = jnp.einsum('bqngk,bqknd->bqngd', p, v_sel.astype(f32))
    return o.reshape(B, Q, A_HEADS * A_HEAD_DIM).astype(q.dtype)


def _sb_block(qs, q_pos, k, v, k_pos):
    q = qs[0]
    B, Q = q.shape[0], q.shape[1]
    f32 = jnp.float32
    z = jnp.einsum('bqhd,bshd->bhqs', q.astype(f32), k.astype(f32)) * (B_HEAD_DIM ** -0.5)
    before = (k_pos[None, :] < q_pos[:, None])[None, None]
    log_rest = jnp.where(before, jax.nn.log_sigmoid(-z), 0.0)
    after = lax.cumsum(log_rest, axis=3, reverse=True) - log_rest
    att = jnp.where(before, jnp.exp(jax.nn.log_sigmoid(z) + after), 0.0)
    o = jnp.einsum('bhqs,bshd->bqhd', att, v.astype(f32))
    return o.reshape(B, Q, B_HEADS * B_HEAD_DIM).astype(q.dtype)


def _peer(h, w_pq, k1, k2, u_tab, v_tab):
    B, T, D = h.shape
    n = B * T
    xf = h.reshape(n, D)
    f32 = jnp.float32

    def one(xb):
        m = xb.shape[0]
        q = (xb @ w_pq).reshape(m, P_HEADS, 2, P_HALF).astype(f32)
        s1 = jnp.einsum('nhd,hkd->nhk', q[:, :, 0], k1.astype(f32))
        s2 = jnp.einsum('nhd,hkd->nhk', q[:, :, 1], k2.astype(f32))
        v1, i1 = lax.top_k(s1, P_TOPK)
        v2, i2 = lax.top_k(s2, P_TOPK)
        cand = (v1[..., :, None] + v2[..., None, :]).reshape(m, P_HEADS, P_TOPK * P_TOPK)
        cid = (i1[..., :, None] * N_KEYS + i2[..., None, :]).reshape(m, P_HEADS, P_TOPK * P_TOPK)
        sc, pick = lax.top_k(cand, P_TOPK)
        eid = jnp.take_along_axis(cid, pick, axis=-1)
        g = jax.nn.softmax(sc, axis=-1)
        u = u_tab[eid]
        a = jax.nn.gelu(jnp.einsum('nhkd,nd->nhk', u, xb).astype(f32), approximate=False)
        coef = (g * a).astype(xb.dtype)
        return jnp.einsum('nhk,nhkd->nd', coef, v_tab[eid])

    if n % PEER_BLOCK != 0:
        out = one(xf)
    else:
        out = lax.map(one, xf.reshape(n // PEER_BLOCK, PEER_BLOCK, D)).reshape(n, D)
    return out.reshape(B, T, D)


def _layer(x, pos, past, lw, topk):
    (n1, w_in, qn, kn, ikn, w_pa, w_pb, w_o, n2, pwq, pk1, pk2, pu, pv) = lw
    B, T, _ = x.shape
    h = _rms(x, n1)
    qa, ka, va, qi, ki, wi, qb, kb, vb, ga, gb = _split_cols(h @ w_in)
    qa = _rope(_rms(qa.reshape(B, T, A_HEADS, A_HEAD_DIM), qn), pos)
    ka = _rope(_rms(ka.reshape(B, T, A_KV_HEADS, A_HEAD_DIM), kn), pos)
    va = va.reshape(B, T, A_KV_HEADS, A_HEAD_DIM)
    qi = _rope(qi.reshape(B, T, IDX_HEADS, IDX_DIM), pos)
    ki = _rope(_rms(ki, ikn)[:, :, None, :], pos)[:, :, 0, :]
    qb = qb.reshape(B, T, B_HEADS, B_HEAD_DIM)
    kb = kb.reshape(B, T, B_HEADS, B_HEAD_DIM)
    vb = vb.reshape(B, T, B_HEADS, B_HEAD_DIM)
    new = (ka, va, ki, kb, vb)
    if past is None:
        ka_all, va_all, ki_all, kb_all, vb_all = new
        k_pos = pos
    else:
        ka_all = jnp.concatenate([past[0], ka], axis=1)
        va_all = jnp.concatenate([past[1], va], axis=1)
        ki_all = jnp.concatenate([past[2], ki], axis=1)
        kb_all = jnp.concatenate([past[3], kb], axis=1)
        vb_all = jnp.concatenate([past[4], vb], axis=1)
        k_pos = jnp.arange(ka_all.shape[1], dtype=jnp.int32)
    oa = _sweep(lambda qs, qp: _dsa_block(qs, qp, ka_all, va_all, ki_all, k_pos, topk),
                (qa, qi, wi), pos)
    ob = _sweep(lambda qs, qp: _sb_block(qs, qp, kb_all, vb_all, k_pos), (qb,), pos)
    m = jax.nn.sigmoid(ga) * (oa @ w_pa) + jax.nn.sigmoid(gb) * (ob @ w_pb)
    x = x + m @ w_o
    x = x + _peer(_rms(x, n2), pwq, pk1, pk2, pu, pv)
    return x, new


def setup_inputs(seed: int = 0) -> dict:
    key = jax.random.key(seed)
    ks = jax.random.split(key, 24)
    f32 = jnp.float32

    def nrm(k, shape, scale):
        return jax.random.normal(k, shape, f32) * scale

    def gain(k, shape):
        return 1.0 + 0.02 * jax.random.normal(k, shape, f32)

    return {
        'x_prompt': nrm(ks[0], (BATCH, SEQ, D_MODEL), 1.0),
        'x_sample': nrm(ks[1], (DEC_BATCH, DEC_SEQ, D_MODEL), 1.0),
        'cache_a_k': nrm(ks[2], (DEPTH, DEC_BATCH, PAST_LEN, A_KV_HEADS, A_HEAD_DIM), 1.0),
        'cache_a_v': nrm(ks[3], (DEPTH, DEC_BATCH, PAST_LEN, A_KV_HEADS, A_HEAD_DIM), 0.5),
        'cache_idx_k': nrm(ks[4], (DEPTH, DEC_BATCH, PAST_LEN, IDX_DIM), 1.0),
        'cache_b_k': nrm(ks[5], (DEPTH, DEC_BATCH, PAST_LEN, B_HEADS, B_HEAD_DIM), 1.0),
        'cache_b_v': nrm(ks[6], (DEPTH, DEC_BATCH, PAST_LEN, B_HEADS, B_HEAD_DIM), 0.5),
        'norm1': gain(ks[7], (DEPTH, D_MODEL)),
        'w_in': nrm(ks[8], (DEPTH, D_MODEL, N_IN), D_MODEL ** -0.5),
        'q_norm_a': gain(ks[9], (DEPTH, A_HEAD_DIM)),
        'k_norm_a': gain(ks[10], (DEPTH, A_HEAD_DIM)),
        'idx_k_norm': gain(ks[11], (DEPTH, IDX_DIM)),
        'w_pa': nrm(ks[12], (DEPTH, W_QA, D_MODEL), W_QA ** -0.5),
        'w_pb': nrm(ks[13], (DEPTH, W_QB, D_MODEL), W_QB ** -0.5),
        'w_o': nrm(ks[14], (DEPTH, D_MODEL, D_MODEL), D_MODEL ** -0.5),
        'norm2': gain(ks[15], (DEPTH, D_MODEL)),
        'peer_wq': nrm(ks[16], (DEPTH, D_MODEL, P_HEADS * P_KEY_DIM), D_MODEL ** -0.5),
        'peer_k1': nrm(ks[17], (DEPTH, P_HEADS, N_KEYS, P_HALF), P_HALF ** -0.5),
        'peer_k2': nrm(ks[18], (DEPTH, P_HEADS, N_KEYS, P_HALF), P_HALF ** -0.5),
        'peer_u': nrm(ks[19], (DEPTH, N_EXPERTS, D_MODEL), D_MODEL ** -0.5),
        'peer_v': nrm(ks[20], (DEPTH, N_EXPERTS, D_MODEL), (P_HEADS * P_TOPK) ** -0.5),
    }


def reference(x_prompt, x_sample, cache_a_k, cache_a_v, cache_idx_k, cache_b_k, cache_b_v,
              norm1, w_in, q_norm_a, k_norm_a, idx_k_norm, w_pa, w_pb, w_o, norm2,
              peer_wq, peer_k1, peer_k2, peer_u, peer_v):
    t_p = x_prompt.shape[1]
    t_s = x_sample.shape[1]
    past_len = cache_a_k.shape[2]
    pos_p = jnp.arange(t_p, dtype=jnp.int32)
    pos_s = past_len + jnp.arange(t_s, dtype=jnp.int32)
    topk_p = min(TOPK_MAX, t_p // 4)
    topk_s = min(TOPK_MAX, (past_len + t_s) // 4)
    xp, xs = x_prompt, x_sample
    st_p, st_s = [], []
    for l in range(DEPTH):
        lw = (norm1[l], w_in[l], q_norm_a[l], k_norm_a[l], idx_k_norm[l], w_pa[l], w_pb[l],
              w_o[l], norm2[l], peer_wq[l], peer_k1[l], peer_k2[l], peer_u[l], peer_v[l])
        xp, sp = _layer(xp, pos_p, None, lw, topk_p)
        past = (cache_a_k[l], cache_a_v[l], cache_idx_k[l], cache_b_k[l], cache_b_v[l])
        xs, ss = _layer(xs, pos_s, past, lw, topk_s)
        st_p.append(sp)
        st_s.append(ss)
    a_k_p = jnp.stack([s[0] for s in st_p])
    a_v_p = jnp.stack([s[1] for s in st_p])
    idx_k_p = jnp.stack([s[2] for s in st_p])
    b_k_p = jnp.stack([s[3] for s in st_p])
    b_v_p = jnp.stack([s[4] for s in st_p])
    a_k_s = jnp.stack([s[0] for s in st_s])
    a_v_s = jnp.stack([s[1] for s in st_s])
    idx_k_s = jnp.stack([s[2] for s in st_s])
    b_k_s = jnp.stack([s[3] for s in st_s])
    b_v_s = jnp.stack([s[4] for s in st_s])
    return (xp, xs, a_k_p, a_v_p, idx_k_p, b_k_p, b_v_p, a_k_s, a_v_s, idx_k_s, b_k_s, b_v_s)
```

```python
import contextlib
import numpy as np
import concourse.bass as bass
import concourse.mybir as mybir
from concourse.bass_utils import run_bass_kernel_spmd

F32 = mybir.dt.float32
BF16 = mybir.dt.bfloat16
I32 = mybir.dt.int32
U32 = mybir.dt.uint32
ALU = mybir.AluOpType
AF = mybir.ActivationFunctionType
AX = mybir.AxisListType

D = 1024
DEPTH = 4
NT = 17
NKS = 17
SEQ = 2048
DEC = 16
NIN = 4936
EPS = 1e-6
NEG = -1e30
TOPK = 256
NBIS = 22


class Res:
    __slots__ = ("t", "w", "r")

    def __init__(self, t=None):
        self.t = t
        self.w = None
        self.r = {}

    def __getitem__(self, k):
        return self.t[k]


class KB:
    def __init__(self, nc, es):
        self.nc = nc
        self.es = es
        self.E = {"pe": nc.tensor, "act": nc.scalar, "dve": nc.vector, "pool": nc.gpsimd, "sp": nc.sync}
        self.sems = {}
        self.cnt = {}
        self.seen = {e: {} for e in self.E}
        self.ninst = 0
        self.dpool = {"sp": ["dsp%d" % i for i in range(24)], "pool": ["dpl%d" % i for i in range(6)]}
        self.dnext = {"sp": 0, "pool": 0}

    def sem(self, key):
        if key not in self.sems:
            self.sems[key] = self.es.enter_context(self.nc.semaphore("s_" + key))
            self.cnt[key] = 0
        return self.sems[key]

    def op(self, eng, fn, r=(), w=(), dma=False):
        deps = {}
        for x in r:
            if x.w is not None:
                k, v = x.w
                if deps.get(k, 0) < v:
                    deps[k] = v
        for x in w:
            if x.w is not None:
                k, v = x.w
                if deps.get(k, 0) < v:
                    deps[k] = v
            for k, v in x.r.items():
                if deps.get(k, 0) < v:
                    deps[k] = v
        E = self.E[eng]
        seen = self.seen[eng]
        for k, v in deps.items():
            if k == "pe" and eng == "pe" and not dma:
                continue
            if seen.get(k, 0) < v:
                E.wait_ge(self.sem(k), v)
                seen[k] = v
        if dma:
            pl = self.dpool[eng]
            key = pl[self.dnext[eng]]
            self.dnext[eng] = (self.dnext[eng] + 1) % len(pl)
            s = self.sem(key)
            prev = self.cnt[key]
            if prev > 0 and seen.get(key, 0) < prev:
                E.wait_ge(s, prev)
                seen[key] = prev
        else:
            key = eng
            s = self.sem(key)
        ins = fn(E)
        inc = 16 if dma else 1
        self.cnt[key] += inc
        c = self.cnt[key]
        ins.then_inc(s, inc)
        for x in r:
            x.r[key] = c
        for x in w:
            x.w = (key, c)
            x.r = {}
        self.ninst += 1
        return ins

    def barrier(self):
        for en, E in self.E.items():
            seen = self.seen[en]
            for k, sm in self.sems.items():
                v = self.cnt[k]
                if v > 0 and seen.get(k, 0) < v:
                    E.wait_ge(sm, v)
                    seen[k] = v

    def finish(self):
        E = self.E["sp"]
        for k, s in self.sems.items():
            if self.cnt[k] > 0:
                E.wait_ge(s, self.cnt[k])


def build_program(depth=DEPTH, tiles=None, do_peer=True):
    if tiles is None:
        tiles = list(range(NT))
    nc = bass.Bass("TRN2", target_bir_lowering=False)
    es = contextlib.ExitStack()
    kb = KB(nc, es)

    def dram(name, shape, dt, kind):
        return nc.dram_tensor(name, shape, dt, kind=kind)

    x_in = dram("x_in", [NT * 128, D], F32, "ExternalInput")
    rope_in = dram("rope", [NT * 128, 64], F32, "ExternalInput")
    cak = dram("cak", [DEPTH, SEQ, 128], F32, "ExternalInput")
    cav = dram("cav", [DEPTH, SEQ, 128], F32, "ExternalInput")
    cik = dram("cik", [DEPTH, SEQ, 64], F32, "ExternalInput")
    cbk = dram("cbk", [DEPTH, SEQ, 512], F32, "ExternalInput")
    cbv = dram("cbv", [DEPTH, SEQ, 512], F32, "ExternalInput")
    n1T = dram("n1T", [DEPTH, 128, 8], F32, "ExternalInput")
    n2T = dram("n2T", [DEPTH, 128, 8], F32, "ExternalInput")
    n2r = dram("n2r", [DEPTH, D], F32, "ExternalInput")
    w_in = dram("w_in", [DEPTH, D, NIN], F32, "ExternalInput")
    qn = dram("qn", [DEPTH, 64], F32, "ExternalInput")
    kn = dram("kn", [DEPTH, 64], F32, "ExternalInput")
    ikn = dram("ikn", [DEPTH, 64], F32, "ExternalInput")
    w_pa = dram("w_pa", [DEPTH, 512, D], F32, "ExternalInput")
    w_pb = dram("w_pb", [DEPTH, 512, D], F32, "ExternalInput")
    w_o = dram("w_o", [DEPTH, D, D], F32, "ExternalInput")
    pwq = dram("pwq", [DEPTH, D, D], F32, "ExternalInput")
    pk1T = dram("pk1T", [DEPTH, 64, 8, 128], F32, "ExternalInput")
    pk2T = dram("pk2T", [DEPTH, 64, 8, 128], F32, "ExternalInput")
    pu = [dram("pu%d" % l, [16384, D], F32, "ExternalInput") for l in range(DEPTH)]
    pv = [dram("pv%d" % l, [16384, D], F32, "ExternalInput") for l in range(DEPTH)]

    y_p = dram("y_p", [SEQ, D], F32, "ExternalOutput")
    y_s = dram("y_s", [DEC, D], F32, "ExternalOutput")
    ak_p = dram("ak_p", [DEPTH, SEQ, 128], F32, "ExternalOutput")
    av_p = dram("av_p", [DEPTH, SEQ, 128], F32, "ExternalOutput")
    ik_p = dram("ik_p", [DEPTH, SEQ, 64], F32, "ExternalOutput")
    bk_p = dram("bk_p", [DEPTH, SEQ, 512], F32, "ExternalOutput")
    bv_p = dram("bv_p", [DEPTH, SEQ, 512], F32, "ExternalOutput")
    ak_s = dram("ak_s", [DEPTH, DEC, 128], F32, "ExternalOutput")
    av_s = dram("av_s", [DEPTH, DEC, 128], F32, "ExternalOutput")
    ik_s = dram("ik_s", [DEPTH, DEC, 64], F32, "ExternalOutput")
    bk_s = dram("bk_s", [DEPTH, DEC, 512], F32, "ExternalOutput")
    bv_s = dram("bv_s", [DEPTH, DEC, 512], F32, "ExternalOutput")
    xs_dram = [dram("xscr%d" % i, [NT * 128, D], F32, "Internal") for i in range(2)]
    xs_res = [[Res() for _ in range(NT)] for _ in range(2)]
    out_res = Res()
    in_res = Res()

    ARENA_F32 = 52000
    arena = es.enter_context(nc.sbuf_tensor("arena", [128, ARENA_F32], F32))
    aoff = [0]
    DTB = {F32: 4, BF16: 2, I32: 4, U32: 4}

    def sb(name, shape, dt):
        nb = DTB[dt]
        n = 1
        for d_ in shape[1:]:
            n *= d_
        nbytes = (n * nb + 31) // 32 * 32
        o = aoff[0]
        assert o % 4 == 0
        aoff[0] = o + nbytes
        assert aoff[0] <= ARENA_F32 * 4, (name, aoff[0])
        v = arena[0:shape[0], o // 4:(o + nbytes) // 4]
        if dt != F32:
            v = v.bitcast(dt)
        v = v[:, 0:n]
        if len(shape) == 3:
            v = v.rearrange("p (a b) -> p a b", a=shape[1])
        elif len(shape) == 4:
            v = v.rearrange("p (a b c) -> p a b c", a=shape[1], b=shape[2])
        return Res(v)

    def pst(name, shape, dt):
        return Res(es.enter_context(nc.psum_tensor(name, shape, dt)))

    ident = sb("ident", [128, 128], BF16)
    ident4 = sb("ident4", [128, 512], BF16)
    negtri = sb("negtri", [128, 128], BF16)
    mlt = sb("mlt", [128, 128], BF16)
    negm4 = sb("negm4", [128, 512], BF16)
    ones1 = sb("ones1", [128, 1], BF16)
    pow2 = sb("pow2", [128, NBIS], F32)
    iota16 = sb("iota16", [128, 16], F32)
    thr16 = sb("thr16", [128, 16], F32)
    NCST = 128 * 3 + 512 * 2 + NBIS + 16
    cst_in = dram("cst", [128, NCST], F32, "ExternalInput")
    gq = sb("gq", [128, 64], F32)
    gk = sb("gk", [128, 64], F32)
    gik = sb("gik", [128, 64], F32)
    n1s = sb("n1s", [128, 8], F32)
    n2s = sb("n2s", [128, 8], F32)
    n2b = sb("n2b", [128, D], F32)
    xt = sb("xt", [128, D], F32)
    hb = sb("hb", [128, D], BF16)
    hT = sb("hT", [128, 8, 128], BF16)
    sq = sb("sq", [128, D], F32)
    st8 = sb("st8", [128, 64], F32)
    STW = 1536
    stg = [sb("stg%d" % i, [128, STW], F32) for i in range(2)]
    mark0 = aoff[0]

    WARENA_A = sb("warenaA", [128, 8 * 2888], BF16)
    kaT = sb("kaT", [64, 2, NKS * 128], BF16)
    kiT = sb("kiT", [64, NKS * 128], BF16)
    kbT = sb("kbT", [64, 8, NKS * 128], BF16)
    vaA = sb("vaA", [128, NKS, 2, 65], BF16)
    vbB = sb("vbB", [128, NKS, 8, 64], BF16)
    kvres = [Res() for _ in range(NKS)]
    cstage = sb("cstage", [128, NCST], F32)
    ropet = sb("ropet", [128, 64], F32)
    pf = sb("pf", [128, 512], F32)
    pf2 = sb("pf2", [128, 512], F32)
    pbf = sb("pbf", [128, 512], BF16)
    r1 = sb("r1", [128, 256], F32)
    r2 = sb("r2", [128, 256], F32)
    qaT = sb("qaT", [64, 8, 128], BF16)
    qiT = sb("qiT", [64, 8, 128], BF16)
    qbT = sb("qbT", [64, 8, 128], BF16)
    wi = sb("wi", [128, 8], F32)
    isc = sb("isc", [128, 2560], F32)
    mbias = sb("mbias", [128, 2560], BF16)
    rl = [sb("rl%d" % i, [128, 512], F32) for i in range(2)]
    PT = [sb("PT%d" % i, [128, 512], BF16) for i in range(2)]
    oa = sb("oa", [128, 512], F32)
    ob = sb("ob", [128, 512], F32)
    ebuf = sb("ebuf", [128, 1024], F32)
    spT = sb("spT", [128, 1024], BF16)
    ET = sb("ET", [128, 1024], BF16)
    dd = sb("dd", [128, 8], F32)
    bis = sb("bis", [128, 8], F32)
    dtab = sb("dtab", [128, NBIS], F32)
    endA = aoff[0]

    aoff[0] = mark0
    WARENA_B = sb("warenaB", [128, 41984], BF16)
    oabb = sb("oabb", [128, D], BF16)
    sga = sb("sga", [128, D], F32)
    sgb = sb("sgb", [128, D], F32)
    oT = sb("oT", [128, 8, 128], BF16)
    mm = sb("mm", [128, D], F32)
    mbf = sb("mbf", [128, D], BF16)
    h2 = sb("h2", [128, D], F32)
    oabf = h2
    qsb = sb("qsb", [128, D], BF16)
    qT = sb("qT", [64, 16, 128], BF16)
    k12 = sb("k12", [64, 2, 8, 128], BF16)
    s12 = sb("s12", [128, 2, 8, 128], F32)
    swk = sb("swk", [128, 256], F32)
    v12 = sb("v12", [128, 2, 8, 16], F32)
    i12 = sb("i12", [128, 2, 8, 16], U32)
    i12f = sb("i12f", [128, 2, 8, 16], F32)
    cand = sb("cand", [128, 8, 256], F32)
    sc = sb("sc", [128, 8, 16], F32)
    pos = sb("pos", [128, 8, 16], U32)
    posf = sb("posf", [128, 8, 16], F32)
    posrf = sb("posrf", [128, 8, 16], F32)
    poscf = sb("poscf", [128, 8, 16], F32)
    oh = sb("oh", [128, 8, 16, 16], F32)
    sel1 = sb("sel1", [128, 8, 16], F32)
    sel2 = sb("sel2", [128, 8, 16], F32)
    eidf = sb("eidf", [128, 128], F32)
    eidi = sb("eidi", [128, 128], I32)
    gsm = sb("gsm", [128, 8, 16], F32)
    adot = sb("adot", [128, 128], F32)
    coef = sb("coef", [128, 128], F32)
    NGB = 2
    ub = [sb("ub%d" % i, [128, D], F32) for i in range(NGB)]
    vbuf = [sb("vbuf%d" % i, [128, D], F32) for i in range(NGB)]
    junk = sq
    acc = mm
    endB = aoff[0]
    print("arena bytes: persistent", mark0, "passA", endA, "passB", endB, "cap", ARENA_F32 * 4)

    oab_dram = dram("oabscr", [NT * 128, D], F32, "ExternalOutput")
    oab_res = [Res() for _ in range(NT)]

    P = [pst("ps%d" % i, [128, 512], F32) for i in range(7)]
    PTR = pst("ptr", [128, 1024], BF16)

    op = kb.op

    op("sp", lambda e: e.dma_start(out=cstage[:], in_=cst_in[:, :]), r=[in_res], w=[cstage], dma=True)
    o = 0
    for dst, wdt in ((ident, 128), (negtri, 128), (mlt, 128), (ident4, 512), (negm4, 512)):
        op("dve", lambda e, dst=dst, o=o, wdt=wdt: e.tensor_copy(out=dst[:], in_=cstage[:, o:o + wdt]), r=[cstage], w=[dst])
        o += wdt
    op("dve", lambda e, o=o: e.tensor_copy(out=pow2[:], in_=cstage[:, o:o + NBIS]), r=[cstage], w=[pow2])
    o += NBIS
    op("dve", lambda e, o=o: e.tensor_copy(out=iota16[:], in_=cstage[:, o:o + 16]), r=[cstage], w=[iota16])
    op("dve", lambda e: e.memset(ones1[:], 1.0), w=[ones1])
    op("dve", lambda e: e.tensor_scalar(out=thr16[:], in0=iota16[:], scalar1=16.0, scalar2=16.0, op0=ALU.mult, op1=ALU.add), r=[iota16], w=[thr16])
    op("dve", lambda e: e.memset(thr16[:, 15:16], 1e9), w=[thr16])
    op("dve", lambda e: e.memset(vaA[:], 1.0), w=[vaA] + kvres)
    kb.barrier()

    def transpose_blocks(src, nblk, dstT, dst_res, ptile=PTR):
        for b in range(nblk):
            op("pe", lambda e, b=b: e.transpose(out=ptile[0:64, b * 128:(b + 1) * 128], in_=src[:, b * 64:(b + 1) * 64], identity=ident[:]),
               r=[src, ident], w=[ptile])
        op("act", lambda e: e.activation(out=dstT, in_=ptile[0:64, 0:nblk * 128].rearrange("p (b n) -> p b n", b=nblk), func=AF.Copy),
           r=[ptile], w=dst_res)

    def rmsnorm_rows(xres, outbf, scratch):
        op("act", lambda e: e.activation(out=scratch[:], in_=xres[:], func=AF.Square, accum_out=st8[:, 0:1]), r=[xres], w=[scratch, st8])
        op("dve", lambda e: e.tensor_scalar(out=st8[:, 1:2], in0=st8[:, 0:1], scalar1=1.0 / D, scalar2=EPS, op0=ALU.mult, op1=ALU.add), r=[st8], w=[st8])
        op("act", lambda e: e.activation(out=st8[:, 2:3], in_=st8[:, 1:2], func=AF.Sqrt), r=[st8], w=[st8])
        op("dve", lambda e: e.reciprocal(out=st8[:, 3:4], in_=st8[:, 2:3]), r=[st8], w=[st8])
        op("dve", lambda e: e.tensor_scalar(out=outbf[:], in0=xres[:], scalar1=st8[:, 3:4], scalar2=None, op0=ALU.mult), r=[xres, st8], w=[outbf])

    def make_hT():
        for c in range(8):
            op("pe", lambda e, c=c: e.transpose(out=PTR[:, c * 128:(c + 1) * 128], in_=hb[:, c * 128:(c + 1) * 128], identity=ident[:]),
               r=[hb, ident], w=[PTR])
        op("act", lambda e: e.activation(out=hT[:], in_=PTR[:, :].rearrange("p (c n) -> p c n", c=8), func=AF.Copy), r=[PTR], w=[hT])

    def headnorm(src_res, src_ap, H, gain, dst):
        W = H * 64
        op("act", lambda e: e.activation(out=sq[:, 0:W], in_=src_ap, func=AF.Square), r=[src_res], w=[sq])
        op("dve", lambda e: e.tensor_reduce(out=st8[:, 8:8 + H], in_=sq[:, 0:W].rearrange("p (h d) -> p h d", h=H), axis=AX.X, op=ALU.add), r=[sq], w=[st8])
        op("dve", lambda e: e.tensor_scalar(out=st8[:, 16:16 + H], in0=st8[:, 8:8 + H], scalar1=1.0 / 64, scalar2=EPS, op0=ALU.mult, op1=ALU.add), r=[st8], w=[st8])
        op("act", lambda e: e.activation(out=st8[:, 24:24 + H], in_=st8[:, 16:16 + H], func=AF.Sqrt), r=[st8], w=[st8])
        op("dve", lambda e: e.reciprocal(out=st8[:, 32:32 + H], in_=st8[:, 24:24 + H]), r=[st8], w=[st8])
        d3 = dst[:, 0:W].rearrange("p (h d) -> p h d", h=H)
        op("dve", lambda e: e.tensor_tensor(out=d3, in0=src_ap.rearrange("p (h d) -> p h d", h=H),
                                            in1=st8[:, 32:32 + H].unsqueeze(2).to_broadcast([128, H, 64]), op=ALU.mult), r=[st8, src_res], w=[dst])
        op("dve", lambda e: e.tensor_tensor(out=d3, in0=d3, in1=gain[:, :].unsqueeze(1).to_broadcast([128, H, 64]), op=ALU.mult), r=[gain, dst], w=[dst])

    def rope(src, H, dst, scale=None):
        W = H * 64
        s3 = src[:, 0:W].rearrange("p (h d) -> p h d", h=H)
        d3 = dst[:, 0:W].rearrange("p (h d) -> p h d", h=H)
        cosb = ropet[:, 0:32].unsqueeze(1).to_broadcast([128, H, 32])
        sinb = ropet[:, 32:64].unsqueeze(1).to_broadcast([128, H, 32])
        a3 = r1[:, 0:H * 32].rearrange("p (h d) -> p h d", h=H)
        b3 = r2[:, 0:H * 32].rearrange("p (h d) -> p h d", h=H)
        op("dve", lambda e: e.tensor_tensor(out=a3, in0=s3[:, :, 0:32], in1=cosb, op=ALU.mult), r=[src, ropet], w=[r1])
        op("dve", lambda e: e.tensor_tensor(out=b3, in0=s3[:, :, 32:64], in1=sinb, op=ALU.mult), r=[src, ropet], w=[r2])
        op("dve", lambda e: e.tensor_tensor(out=d3[:, :, 0:32], in0=a3, in1=b3, op=ALU.subtract), r=[r1, r2], w=[dst])
        op("dve", lambda e: e.tensor_tensor(out=a3, in0=s3[:, :, 32:64], in1=cosb, op=ALU.mult), r=[src, ropet], w=[r1])
        op("dve", lambda e: e.tensor_tensor(out=b3, in0=s3[:, :, 0:32], in1=sinb, op=ALU.mult), r=[src, ropet], w=[r2])
        op("dve", lambda e: e.tensor_tensor(out=d3[:, :, 32:64], in0=a3, in1=b3, op=ALU.add), r=[r1, r2], w=[dst])

    def load_cast_rows(wres, dram_ap_fn, nchunks, width, dst_fn, scale_res=None, scale_col=None):
        i = 0
        for c in range(nchunks):
            for o0 in range(0, width, STW):
                wdt = min(STW, width - o0)
                s = stg[i % 2]
                i += 1
                op("sp", lambda e, c=c, o0=o0, wdt=wdt, s=s: e.dma_start(out=s[:, 0:wdt], in_=dram_ap_fn(c, o0, wdt)), r=[in_res], w=[s], dma=True)
                if scale_res is not None:
                    op("dve", lambda e, c=c, o0=o0, wdt=wdt, s=s: e.tensor_scalar(out=dst_fn(c, o0, wdt), in0=s[:, 0:wdt], scalar1=scale_res[:, scale_col(c):scale_col(c) + 1], scalar2=None, op0=ALU.mult),
                       r=[s, scale_res], w=[wres])
                else:
                    op("pool", lambda e, c=c, o0=o0, wdt=wdt, s=s: e.tensor_copy(out=dst_fn(c, o0, wdt), in_=s[:, 0:wdt]), r=[s], w=[wres])

    WAA = WARENA_A.t
    WA = WARENA_B.t
    NQKV = 2888
    wa_qkv = lambda c, o0, wdt: WAA[:, c * NQKV + o0: c * NQKV + o0 + wdt]
    GOFF = 0
    PAOFF = 8 * 2048
    PBOFF = PAOFF + 4 * 1024
    WOOFF = PBOFF + 4 * 1024
    PQOFF = WOOFF + 8 * 1024

    for l in range(depth):
        xin_d = x_in if l == 0 else xs_dram[(l - 1) % 2]
        xin_r = [in_res] * NT if l == 0 else xs_res[(l - 1) % 2]
        xout_d = xs_dram[l % 2]
        xout_r = xs_res[l % 2]
        last = (l == depth - 1)

        op("sp", lambda e: e.dma_start(out=n1s[:], in_=n1T[l, :, :]), r=[in_res], w=[n1s], dma=True)
        op("sp", lambda e: e.dma_start(out=n2s[:], in_=n2T[l, :, :]), r=[in_res], w=[n2s], dma=True)
        op("sp", lambda e: e.dma_start(out=gq[:], in_=qn[l, :].partition_broadcast(128)), r=[in_res], w=[gq], dma=True)
        op("sp", lambda e: e.dma_start(out=gk[:], in_=kn[l, :].partition_broadcast(128)), r=[in_res], w=[gk], dma=True)
        op("sp", lambda e: e.dma_start(out=gik[:], in_=ikn[l, :].partition_broadcast(128)), r=[in_res], w=[gik], dma=True)
        op("sp", lambda e: e.dma_start(out=n2b[:], in_=n2r[l, :].partition_broadcast(128)), r=[in_res], w=[n2b], dma=True)

        kb.barrier()
        if l > 0:
            op("pool", lambda e: e.memset(vaA[:], 1.0), w=[vaA] + kvres)
        load_cast_rows(WARENA_A, lambda c, o0, wdt: w_in[l, c * 128:(c + 1) * 128, o0:o0 + wdt], 8, NQKV, wa_qkv, n1s, lambda c: c)

        for t in tiles:
            samp = (t == NT - 1)
            nk = t + 1
            if samp:
                for k0 in range(0, 16, 4):
                    for kk in range(k0, k0 + 4):
                        rows = slice(kk * 128, (kk + 1) * 128)
                        s = stg[kk % 2]
                        op("sp", lambda e, s=s, rows=rows: e.dma_start(out=s[:, 0:128], in_=cak[l, rows, :]), r=[in_res], w=[s], dma=True)
                        op("sp", lambda e, s=s, rows=rows: e.dma_start(out=s[:, 128:192], in_=cik[l, rows, :]), r=[in_res], w=[s], dma=True)
                        op("sp", lambda e, s=s, rows=rows: e.dma_start(out=s[:, 192:320], in_=cav[l, rows, :]), r=[in_res], w=[s], dma=True)
                        op("sp", lambda e, s=s, rows=rows: e.dma_start(out=s[:, 512:1024], in_=cbk[l, rows, :]), r=[in_res], w=[s], dma=True)
                        op("sp", lambda e, s=s, rows=rows: e.dma_start(out=s[:, 1024:1536], in_=cbv[l, rows, :]), r=[in_res], w=[s], dma=True)
                        op("dve", lambda e, s=s: e.tensor_copy(out=pbf[:, 0:192], in_=s[:, 0:192]), r=[s], w=[pbf])
                        for b in range(3):
                            op("pe", lambda e, b=b: e.transpose(out=PTR[0:64, b * 128:(b + 1) * 128], in_=pbf[:, b * 64:(b + 1) * 64], identity=ident[:]), r=[pbf, ident], w=[PTR])
                        ks = slice(kk * 128, (kk + 1) * 128)
                        op("act", lambda e, ks=ks: e.activation(out=kaT[:, :, ks], in_=PTR[0:64, 0:256].rearrange("p (b n) -> p b n", b=2), func=AF.Copy), r=[PTR], w=[kvres[kk]])
                        op("act", lambda e, ks=ks: e.activation(out=kiT[:, ks], in_=PTR[0:64, 256:384], func=AF.Copy), r=[PTR], w=[kvres[kk]])
                        op("dve", lambda e, s=s, kk=kk: e.tensor_copy(out=vaA[:, kk, :, 0:64], in_=s[:, 192:320].rearrange("p (g d) -> p g d", g=2)), r=[s], w=[kvres[kk]])
                        op("dve", lambda e, s=s: e.tensor_copy(out=pbf[:, 0:512], in_=s[:, 512:1024]), r=[s], w=[pbf])
                        for b in range(8):
                            op("pe", lambda e, b=b: e.transpose(out=PTR[0:64, b * 128:(b + 1) * 128], in_=pbf[:, b * 64:(b + 1) * 64], identity=ident[:]), r=[pbf, ident], w=[PTR])
                        op("act", lambda e, ks=ks: e.activation(out=kbT[:, :, ks], in_=PTR[0:64, :].rearrange("p (b n) -> p b n", b=8), func=AF.Copy), r=[PTR], w=[kvres[kk]])
                        op("pool", lambda e, s=s, kk=kk: e.tensor_copy(out=vbB[:, kk, :, :], in_=s[:, 1024:1536].rearrange("p (h d) -> p h d", h=8)), r=[s], w=[kvres[kk]])

            rows = slice(t * 128, (t + 1) * 128)
            op("sp", lambda e: e.dma_start(out=xt[:], in_=xin_d[rows, :]), r=[xin_r[t]], w=[xt], dma=True)
            op("sp", lambda e: e.dma_start(out=ropet[:], in_=rope_in[rows, :]), r=[in_res], w=[ropet], dma=True)
            rmsnorm_rows(xt, hb, sq)
            make_hT()

            def proj(pt, c0, wdt):
                for c in range(8):
                    op("pe", lambda e, c=c: e.matmul(pt[:, 0:wdt], lhsT=hT[:, c, :], rhs=WAA[:, c * NQKV + c0: c * NQKV + c0 + wdt], start=(c == 0), stop=(c == 7)),
                       r=[hT, WARENA_A], w=[pt])

            def out_rows(dst_p, dst_s, src, wdt):
                if samp:
                    op("sp", lambda e: e.dma_start(out=dst_s[l, :, :], in_=src[0:DEC, 0:wdt]), r=[src], w=[out_res], dma=True)
                else:
                    op("sp", lambda e: e.dma_start(out=dst_p[l, rows, :], in_=src[:, 0:wdt]), r=[src], w=[out_res], dma=True)

            ks = slice(t * 128, (t + 1) * 128)
            proj(P[0], 0, 512)
            headnorm(P[0], P[0][:, 0:512], 8, gq, pf)
            rope(pf, 8, pf2)
            op("dve", lambda e: e.tensor_scalar(out=pbf[:], in0=pf2[:], scalar1=0.125, scalar2=None, op0=ALU.mult), r=[pf2], w=[pbf])
            transpose_blocks(pbf, 8, qaT[:], [qaT])
            proj(P[1], 512, 256)
            headnorm(P[1], P[1][:, 0:128], 2, gk, pf)
            rope(pf, 2, pf2)
            out_rows(ak_p, ak_s, pf2, 128)
            op("dve", lambda e: e.tensor_copy(out=pbf[:, 0:128], in_=pf2[:, 0:128]), r=[pf2], w=[pbf])
            op("act", lambda e: e.activation(out=pf[:, 0:128], in_=P[1][:, 128:256], func=AF.Copy), r=[P[1]], w=[pf])
            out_rows(av_p, av_s, pf, 128)
            op("dve", lambda e: e.tensor_copy(out=vaA[:, t, :, 0:64], in_=pf[:, 0:128].rearrange("p (g d) -> p g d", g=2)), r=[pf], w=[kvres[t]])
            transpose_blocks(pbf, 2, kaT[:, :, ks], [kvres[t]])
            proj(P[0], 768, 512)
            rope(P[0], 8, pf2)
            op("dve", lambda e: e.tensor_copy(out=pbf[:], in_=pf2[:]), r=[pf2], w=[pbf])
            transpose_blocks(pbf, 8, qiT[:], [qiT])
            proj(P[1], 1280, 72)
            headnorm(P[1], P[1][:, 0:64], 1, gik, pf)
            rope(pf, 1, pf2)
            out_rows(ik_p, ik_s, pf2, 64)
            op("dve", lambda e: e.tensor_copy(out=pbf[:, 0:64], in_=pf2[:, 0:64]), r=[pf2], w=[pbf])
            op("act", lambda e: e.activation(out=wi[:], in_=P[1][:, 64:72], func=AF.Copy), r=[P[1]], w=[wi])
            for b in range(1):
                op("pe", lambda e: e.transpose(out=PTR[0:64, 0:128], in_=pbf[:, 0:64], identity=ident[:]), r=[pbf, ident], w=[PTR])
            op("act", lambda e: e.activation(out=kiT[:, ks], in_=PTR[0:64, 0:128], func=AF.Copy), r=[PTR], w=[kvres[t]])
            proj(P[0], 1352, 512)
            op("act", lambda e: e.activation(out=pbf[:], in_=P[0][:, :], func=AF.Copy, scale=0.125), r=[P[0]], w=[pbf])
            transpose_blocks(pbf, 8, qbT[:], [qbT])
            proj(P[1], 1864, 512)
            op("act", lambda e: e.activation(out=pf[:], in_=P[1][:, :], func=AF.Copy), r=[P[1]], w=[pf])
            out_rows(bk_p, bk_s, pf, 512)
            op("dve", lambda e: e.tensor_copy(out=pbf[:], in_=pf[:]), r=[pf], w=[pbf])
            transpose_blocks(pbf, 8, kbT[:, :, ks], [kvres[t]])
            proj(P[0], 2376, 512)
            op("act", lambda e: e.activation(out=pf2[:], in_=P[0][:, :], func=AF.Copy), r=[P[0]], w=[pf2])
            out_rows(bv_p, bv_s, pf2, 512)
            op("dve", lambda e: e.tensor_copy(out=vbB[:, t, :, :], in_=pf2[:, :].rearrange("p (h d) -> p h d", h=8)), r=[pf2], w=[kvres[t]])

            S = nk * 128
            nblk = (S + 511) // 512
            kvr = [kvres[i] for i in range(nk)]
            for bi in range(nblk):
                c0 = bi * 512
                wdt = min(512, S - c0)
                for h in range(8):
                    pt = P[2 + (h % 2)]
                    rb = rl[h % 2]
                    op("pe", lambda e, h=h, pt=pt: e.matmul(pt[:, 0:wdt], lhsT=qiT[:, h, :], rhs=kiT[:, c0:c0 + wdt], start=True, stop=True), r=[qiT] + kvr, w=[pt])
                    op("act", lambda e, pt=pt, rb=rb: e.activation(out=rb[:, 0:wdt], in_=pt[:, 0:wdt], func=AF.Relu, scale=0.125 * (8 ** -0.5)), r=[pt], w=[rb])
                    if h == 0:
                        op("dve", lambda e, rb=rb: e.tensor_scalar(out=isc[:, c0:c0 + wdt], in0=rb[:, 0:wdt], scalar1=wi[:, 0:1], scalar2=None, op0=ALU.mult), r=[rb, wi], w=[isc])
                    else:
                        op("dve", lambda e, rb=rb, h=h: e.scalar_tensor_tensor(out=isc[:, c0:c0 + wdt], in0=rb[:, 0:wdt], scalar=wi[:, h:h + 1], in1=isc[:, c0:c0 + wdt], op0=ALU.mult, op1=ALU.add), r=[rb, wi, isc], w=[isc])
            need_thr = nk > 2
            if need_thr:
                op("dve", lambda e: e.tensor_reduce(out=bis[:, 5:6], in_=isc[:, 0:S], axis=AX.X, op=ALU.max), r=[isc], w=[bis])
                op("dve", lambda e: e.tensor_reduce(out=bis[:, 6:7], in_=isc[:, 0:S], axis=AX.X, op=ALU.min), r=[isc], w=[bis])
                op("dve", lambda e: e.tensor_scalar(out=bis[:, 6:7], in0=bis[:, 6:7], scalar1=-1.0, scalar2=None, op0=ALU.mult), r=[bis], w=[bis])
                op("dve", lambda e: e.tensor_tensor(out=bis[:, 4:5], in0=bis[:, 5:6], in1=bis[:, 6:7], op=ALU.max), r=[bis], w=[bis])
                op("dve", lambda e: e.tensor_scalar(out=bis[:, 0:1], in0=bis[:, 4:5], scalar1=-1.0, scalar2=None, op0=ALU.mult), r=[bis], w=[bis])
                op("dve", lambda e: e.tensor_scalar(out=dtab[:], in0=pow2[:], scalar1=bis[:, 4:5], scalar2=2.002, op0=ALU.mult, op1=ALU.mult), r=[bis, pow2], w=[dtab])
            if samp:
                op("dve", lambda e: e.memset(isc[:, 16 * 128 + DEC:17 * 128], NEG), r=[], w=[isc])
            else:
                op("dve", lambda e: e.memset(isc[0:64, t * 128 + 64:(t + 1) * 128], NEG), r=[], w=[isc])
            if need_thr:
                for k in range(NBIS):
                    op("dve", lambda e, k=k: e.tensor_tensor(out=bis[:, 1:2], in0=bis[:, 0:1], in1=dtab[:, k:k + 1], op=ALU.add), r=[bis, dtab], w=[bis])
                    op("dve", lambda e: e.tensor_scalar(out=mbias[:, 0:S], in0=isc[:, 0:S], scalar1=bis[:, 1:2], scalar2=None, op0=ALU.is_ge, op1=ALU.add, accum_out=bis[:, 2:3]), r=[isc, bis], w=[mbias, bis])
                    op("dve", lambda e, k=k: e.scalar_tensor_tensor(out=bis[:, 3:4], in0=bis[:, 2:3], scalar=float(TOPK), in1=dtab[:, k:k + 1], op0=ALU.is_ge, op1=ALU.mult), r=[bis, dtab], w=[bis])
                    op("dve", lambda e: e.tensor_tensor(out=bis[:, 0:1], in0=bis[:, 0:1], in1=bis[:, 3:4], op=ALU.add), r=[bis], w=[bis])
            else:
                op("dve", lambda e: e.memset(bis[:, 0:1], -1e29), r=[], w=[bis])
            op("dve", lambda e: e.tensor_scalar(out=mbias[:, 0:S], in0=isc[:, 0:S], scalar1=bis[:, 0:1], scalar2=-30000.0, op0=ALU.is_lt, op1=ALU.mult), r=[isc, bis], w=[mbias])
            OAp = [P[4], P[5]]
            for kt in range(nk):
                ksl = slice(kt * 128, (kt + 1) * 128)
                for g in range(2):
                    pt = P[2 + g]
                    pb_ = PT[g]
                    op("pe", lambda e, g=g, pt=pt, ksl=ksl: e.matmul(pt[:, :], lhsT=kaT[:, g, ksl], rhs=qaT[:, 4 * g:4 * g + 4, :], start=True, stop=False), r=[kvres[kt], qaT], w=[pt])
                    op("pe", lambda e, pt=pt, ksl=ksl: e.matmul(pt[:, :], lhsT=mbias[:, ksl], rhs=ident4[:], start=False, stop=True), r=[mbias, ident4], w=[pt])
                    op("act", lambda e, pt=pt, pb_=pb_: e.activation(out=pb_[:], in_=pt[:, :], func=AF.Exp), r=[pt], w=[pb_])
                    for hh in range(4):
                        op("pe", lambda e, g=g, hh=hh, pb_=pb_, kt=kt: e.matmul(OAp[g][:, hh * 65:(hh + 1) * 65], lhsT=pb_[:, hh * 128:(hh + 1) * 128], rhs=vaA[:, kt, g, :], start=(kt == 0 and hh == 0), stop=(kt == nk - 1)),
                           r=[pb_, kvres[kt]], w=[OAp[g]])
            for g in range(2):
                o3 = OAp[g][:, 0:260].rearrange("p (h d) -> p h d", h=4)
                op("dve", lambda e, g=g, o3=o3: e.reciprocal(out=st8[:, 40 + 4 * g:44 + 4 * g], in_=o3[:, :, 64]), r=[OAp[g]], w=[st8])
                op("dve", lambda e, g=g, o3=o3: e.tensor_tensor(out=oa[:, g * 256:(g + 1) * 256].rearrange("p (h d) -> p h d", h=4), in0=o3[:, :, 0:64],
                                                                 in1=st8[:, 40 + 4 * g:44 + 4 * g].unsqueeze(2).to_broadcast([128, 4, 64]), op=ALU.mult), r=[OAp[g], st8], w=[oa])
            op("sp", lambda e: e.dma_start(out=oab_dram[rows, 0:512], in_=oa[:]), r=[oa], w=[oab_res[t]], dma=True)

            Z = [P[2], P[3]]
            Z2 = [P[4], P[5]]
            PVb = P[6]
            TSb = P[0]
            for kt in range(nk):
                ksl = slice(kt * 128, (kt + 1) * 128)
                diag = (kt == nk - 1)
                for h in range(8):
                    op("pe", lambda e, h=h: e.matmul(Z[h // 4][:, (h % 4) * 128:(h % 4 + 1) * 128], lhsT=kbT[:, h, ksl], rhs=qbT[:, h, :], start=True, stop=True), r=[kvres[kt], qbT], w=[Z[h // 4]])
                for hf in range(2):
                    op("act", lambda e, hf=hf: e.activation(out=ebuf[:, hf * 512:(hf + 1) * 512], in_=Z[hf][:, :], func=AF.Exp), r=[Z[hf]], w=[ebuf])
                    op("act", lambda e, hf=hf: e.activation(out=spT[:, hf * 512:(hf + 1) * 512], in_=ebuf[:, hf * 512:(hf + 1) * 512], func=AF.Ln, bias=1.0), r=[ebuf], w=[spT])
                if diag:
                    s3 = spT[:, :].rearrange("p (h q) -> p h q", h=8)
                    op("dve", lambda e, s3=s3: e.tensor_tensor(out=s3, in0=s3, in1=mlt[:, :].unsqueeze(1).to_broadcast([128, 8, 128]), op=ALU.mult), r=[spT, mlt], w=[spT])
                for hf in range(2):
                    op("pe", lambda e, hf=hf: e.matmul(Z2[hf][:, :], lhsT=negtri[:], rhs=spT[:, hf * 512:(hf + 1) * 512], start=True, stop=False), r=[negtri, spT], w=[Z2[hf]])
                    if diag:
                        op("pe", lambda e, hf=hf: e.matmul(Z2[hf][:, :], lhsT=ident[:], rhs=negm4[:], start=False, stop=False), r=[ident, negm4], w=[Z2[hf]])
                    for hh in range(4):
                        h = hf * 4 + hh
                        op("pe", lambda e, h=h, hh=hh, hf=hf: e.matmul(Z2[hf][:, hh * 128:(hh + 1) * 128], lhsT=kbT[:, h, ksl], rhs=qbT[:, h, :], start=False, stop=(hh == 3)), r=[kvres[kt], qbT], w=[Z2[hf]])
                    op("act", lambda e, hf=hf: e.activation(out=ET[:, hf * 512:(hf + 1) * 512], in_=Z2[hf][:, :], func=AF.Exp), r=[Z2[hf]], w=[ET])
                for h in range(8):
                    op("pe", lambda e, h=h: e.matmul(TSb[:, h:h + 1], lhsT=spT[:, h * 128:(h + 1) * 128], rhs=ones1[:], start=True, stop=True), r=[spT, ones1], w=[TSb])
                for h in range(8):
                    op("pe", lambda e, h=h, kt=kt: e.matmul(PVb[:, h * 64:(h + 1) * 64], lhsT=ET[:, h * 128:(h + 1) * 128], rhs=vbB[:, kt, h, :], start=True, stop=True), r=[ET, kvres[kt]], w=[PVb])
                if kt == 0:
                    op("dve", lambda e: e.tensor_copy(out=ob[:], in_=PVb[:, :]), r=[PVb], w=[ob])
                else:
                    op("act", lambda e: e.activation(out=dd[:], in_=TSb[:, 0:8], func=AF.Exp, scale=-1.0), r=[TSb], w=[dd])
                    o3 = ob[:, :].rearrange("p (h d) -> p h d", h=8)
                    op("dve", lambda e, o3=o3: e.tensor_tensor(out=o3, in0=o3, in1=dd[:, :].unsqueeze(2).to_broadcast([128, 8, 64]), op=ALU.mult), r=[ob, dd], w=[ob])
                    op("dve", lambda e: e.tensor_tensor(out=ob[:], in0=ob[:], in1=PVb[:, :], op=ALU.add), r=[ob, PVb], w=[ob])
            op("sp", lambda e: e.dma_start(out=oab_dram[rows, 512:1024], in_=ob[:]), r=[ob], w=[oab_res[t]], dma=True)

        kb.barrier()
        load_cast_rows(WARENA_B, lambda c, o0, wdt: w_in[l, c * 128:(c + 1) * 128, 2888 + o0:2888 + o0 + wdt], 8, 2048,
                       lambda c, o0, wdt: WA[:, GOFF + c * 2048 + o0: GOFF + c * 2048 + o0 + wdt], n1s, lambda c: c)
        load_cast_rows(WARENA_B, lambda c, o0, wdt: w_pa[l, c * 128:(c + 1) * 128, o0:o0 + wdt], 4, 1024,
                       lambda c, o0, wdt: WA[:, PAOFF + c * 1024 + o0: PAOFF + c * 1024 + o0 + wdt])
        load_cast_rows(WARENA_B, lambda c, o0, wdt: w_pb[l, c * 128:(c + 1) * 128, o0:o0 + wdt], 4, 1024,
                       lambda c, o0, wdt: WA[:, PBOFF + c * 1024 + o0: PBOFF + c * 1024 + o0 + wdt])
        load_cast_rows(WARENA_B, lambda c, o0, wdt: w_o[l, c * 128:(c + 1) * 128, o0:o0 + wdt], 8, 1024,
                       lambda c, o0, wdt: WA[:, WOOFF + c * 1024 + o0: WOOFF + c * 1024 + o0 + wdt])
        load_cast_rows(WARENA_B, lambda c, o0, wdt: pwq[l, c * 128:(c + 1) * 128, o0:o0 + wdt], 8, 1024,
                       lambda c, o0, wdt: WA[:, PQOFF + c * 1024 + o0: PQOFF + c * 1024 + o0 + wdt], n2s, lambda c: c)
        for which, src in ((0, pk1T), (1, pk2T)):
            s = stg[which]
            op("sp", lambda e, s=s, src=src: e.dma_start(out=s[0:64, 0:1024], in_=src[l, :, :, :].rearrange("d h k -> d (h k)")), r=[in_res], w=[s], dma=True)
            op("dve", lambda e, s=s, which=which: e.tensor_copy(out=k12[:, which, :, :], in_=s[0:64, 0:1024].rearrange("d (h k) -> d h k", h=8)), r=[s], w=[k12])

        for t in tiles:
            samp = (t == NT - 1)
            rows = slice(t * 128, (t + 1) * 128)
            op("sp", lambda e: e.dma_start(out=xt[:], in_=xin_d[rows, :]), r=[xin_r[t]], w=[xt], dma=True)
            rmsnorm_rows(xt, hb, sq)
            make_hT()
            for gi, gdst in ((0, sga), (1, sgb)):
                for hf in range(2):
                    pt = P[hf]
                    c0 = GOFF + gi * 1024 + hf * 512
                    for c in range(8):
                        op("pe", lambda e, c=c, pt=pt, c0=c0: e.matmul(pt[:, :], lhsT=hT[:, c, :], rhs=WA[:, c * 2048 + c0: c * 2048 + c0 + 512], start=(c == 0), stop=(c == 7)), r=[hT, WARENA_B], w=[pt])
                    op("act", lambda e, pt=pt, gdst=gdst, hf=hf: e.activation(out=gdst[:, hf * 512:(hf + 1) * 512], in_=pt[:, :], func=AF.Sigmoid), r=[pt], w=[gdst])
            op("sp", lambda e: e.dma_start(out=oabf[:], in_=oab_dram[rows, :]), r=[oab_res[t]], w=[oabf], dma=True)
            op("pool", lambda e: e.tensor_copy(out=oabb[:], in_=oabf[:]), r=[oabf], w=[oabb])
            for bi, (woff, gdst) in enumerate(((PAOFF, sga), (PBOFF, sgb))):
                for c in range(4):
                    op("pe", lambda e, c=c, bi=bi: e.transpose(out=PTR[:, c * 128:(c + 1) * 128], in_=oabb[:, bi * 512 + c * 128: bi * 512 + (c + 1) * 128], identity=ident[:]), r=[oabb, ident], w=[PTR])
                op("act", lambda e: e.activation(out=oT[:, 0:4, :], in_=PTR[:, 0:512].rearrange("p (c n) -> p c n", c=4), func=AF.Copy), r=[PTR], w=[oT])
                for hf in range(2):
                    pt = P[2 + hf]
                    for c in range(4):
                        op("pe", lambda e, c=c, pt=pt, hf=hf, woff=woff: e.matmul(pt[:, :], lhsT=oT[:, c, :], rhs=WA[:, woff + c * 1024 + hf * 512: woff + c * 1024 + (hf + 1) * 512], start=(c == 0), stop=(c == 3)), r=[oT, WARENA_B], w=[pt])
                    if bi == 0:
                        op("dve", lambda e, pt=pt, hf=hf: e.tensor_tensor(out=mm[:, hf * 512:(hf + 1) * 512], in0=sga[:, hf * 512:(hf + 1) * 512], in1=pt[:, :], op=ALU.mult), r=[sga, pt], w=[mm])
                    else:
                        op("dve", lambda e, pt=pt, hf=hf: e.tensor_tensor(out=sgb[:, hf * 512:(hf + 1) * 512], in0=sgb[:, hf * 512:(hf + 1) * 512], in1=pt[:, :], op=ALU.mult), r=[sgb, pt], w=[sgb])
            op("dve", lambda e: e.tensor_tensor(out=mbf[:], in0=mm[:], in1=sgb[:], op=ALU.add), r=[mm, sgb], w=[mbf])
            for c in range(8):
                op("pe", lambda e, c=c: e.transpose(out=PTR[:, c * 128:(c + 1) * 128], in_=mbf[:, c * 128:(c + 1) * 128], identity=ident[:]), r=[mbf, ident], w=[PTR])
            op("act", lambda e: e.activation(out=oT[:], in_=PTR[:, :].rearrange("p (c n) -> p c n", c=8), func=AF.Copy), r=[PTR], w=[oT])
            for hf in range(2):
                pt = P[4 + hf]
                for c in range(8):
                    op("pe", lambda e, c=c, pt=pt, hf=hf: e.matmul(pt[:, :], lhsT=oT[:, c, :], rhs=WA[:, WOOFF + c * 1024 + hf * 512: WOOFF + c * 1024 + (hf + 1) * 512], start=(c == 0), stop=(c == 7)), r=[oT, WARENA_B], w=[pt])
                op("dve", lambda e, pt=pt, hf=hf: e.tensor_tensor(out=xt[:, hf * 512:(hf + 1) * 512], in0=xt[:, hf * 512:(hf + 1) * 512], in1=pt[:, :], op=ALU.add), r=[xt, pt], w=[xt])

            if do_peer:
                rmsnorm_rows(xt, hb, sq)
                op("dve", lambda e: e.scalar_tensor_tensor(out=h2[:], in0=xt[:], scalar=st8[:, 3:4], in1=n2b[:], op0=ALU.mult, op1=ALU.mult), r=[xt, st8, n2b], w=[h2])
                make_hT()
                for hf in range(2):
                    pt = P[hf]
                    for c in range(8):
                        op("pe", lambda e, c=c, pt=pt, hf=hf: e.matmul(pt[:, :], lhsT=hT[:, c, :], rhs=WA[:, PQOFF + c * 1024 + hf * 512: PQOFF + c * 1024 + (hf + 1) * 512], start=(c == 0), stop=(c == 7)), r=[hT, WARENA_B], w=[pt])
                    op("act", lambda e, pt=pt, hf=hf: e.activation(out=qsb[:, hf * 512:(hf + 1) * 512], in_=pt[:, :], func=AF.Copy), r=[pt], w=[qsb])
                for hf in range(2):
                    for b in range(8):
                        bb = hf * 8 + b
                        op("pe", lambda e, b=b, bb=bb: e.transpose(out=PTR[0:64, b * 128:(b + 1) * 128], in_=qsb[:, bb * 64:(bb + 1) * 64], identity=ident[:]), r=[qsb, ident], w=[PTR])
                    op("act", lambda e, hf=hf: e.activation(out=qT[:, hf * 8:(hf + 1) * 8, :], in_=PTR[0:64, :].rearrange("p (b n) -> p b n", b=8), func=AF.Copy), r=[PTR], w=[qT])
                for which in range(2):
                    for h in range(8):
                        pt = P[2 + which * 2 + h // 4]
                        op("pe", lambda e, h=h, pt=pt, which=which: e.matmul(pt[:, (h % 4) * 128:(h % 4 + 1) * 128], lhsT=qT[:, h * 2 + which, :], rhs=k12[:, which, h, :], start=True, stop=True), r=[qT, k12], w=[pt])
                    for hf in range(2):
                        pt = P[2 + which * 2 + hf]
                        op("act", lambda e, pt=pt, which=which, hf=hf: e.activation(out=s12[:, which, hf * 4:(hf + 1) * 4, :], in_=pt[:, :].rearrange("p (h k) -> p h k", h=4), func=AF.Copy), r=[pt], w=[s12])
                for which in range(2):
                    for h in range(8):
                        sv = s12[:, which, h, :]
                        op("dve", lambda e, sv=sv, which=which, h=h: e.max(out=v12[:, which, h, 0:8], in_=sv), r=[s12], w=[v12])
                        op("dve", lambda e, sv=sv, which=which, h=h: e.max_index(out=i12[:, which, h, 0:8], in_max=v12[:, which, h, 0:8], in_values=sv), r=[s12, v12], w=[i12])
                        op("dve", lambda e, sv=sv, which=which, h=h: e.match_replace(out=swk[:, 0:128], in_to_replace=v12[:, which, h, 0:8], in_values=sv, imm_value=NEG), r=[s12, v12], w=[swk])
                        op("dve", lambda e, which=which, h=h: e.max(out=v12[:, which, h, 8:16], in_=swk[:, 0:128]), r=[swk], w=[v12])
                        op("dve", lambda e, which=which, h=h: e.max_index(out=i12[:, which, h, 8:16], in_max=v12[:, which, h, 8:16], in_values=swk[:, 0:128]), r=[swk, v12], w=[i12])
                op("dve", lambda e: e.tensor_copy(out=i12f[:], in_=i12[:]), r=[i12], w=[i12f])
                c4 = cand[:, :, :].rearrange("p h (r c) -> p h r c", r=16)
                op("dve", lambda e: e.tensor_tensor(out=c4, in0=v12[:, 0, :, :].unsqueeze(3).to_broadcast([128, 8, 16, 16]), in1=v12[:, 1, :, :].unsqueeze(2).to_broadcast([128, 8, 16, 16]), op=ALU.add), r=[v12], w=[cand])
                for h in range(8):
                    op("dve", lambda e, h=h: e.max(out=sc[:, h, 0:8], in_=cand[:, h, :]), r=[cand], w=[sc])
                    op("dve", lambda e, h=h: e.max_index(out=pos[:, h, 0:8], in_max=sc[:, h, 0:8], in_values=cand[:, h, :]), r=[cand, sc], w=[pos])
                    op("dve", lambda e, h=h: e.match_replace(out=swk[:, 0:256], in_to_replace=sc[:, h, 0:8], in_values=cand[:, h, :], imm_value=NEG), r=[cand, sc], w=[swk])
                    op("dve", lambda e, h=h: e.max(out=sc[:, h, 8:16], in_=swk[:, 0:256]), r=[swk], w=[sc])
                    op("dve", lambda e, h=h: e.max_index(out=pos[:, h, 8:16], in_max=sc[:, h, 8:16], in_values=swk[:, 0:256]), r=[swk, sc], w=[pos])
                op("dve", lambda e: e.tensor_copy(out=posf[:], in_=pos[:]), r=[pos], w=[posf])
                op("dve", lambda e: e.tensor_tensor(out=oh[:], in0=posf[:, :, :].unsqueeze(3).to_broadcast([128, 8, 16, 16]),
                                                    in1=thr16[:, :].unsqueeze(1).unsqueeze(1).to_broadcast([128, 8, 16, 16]), op=ALU.is_ge), r=[posf, thr16], w=[oh])
                op("dve", lambda e: e.tensor_reduce(out=posrf[:], in_=oh[:], axis=AX.X, op=ALU.add), r=[oh], w=[posrf])
                op("dve", lambda e: e.scalar_tensor_tensor(out=poscf[:], in0=posrf[:], scalar=-16.0, in1=posf[:], op0=ALU.mult, op1=ALU.add), r=[posrf, posf], w=[poscf])
                io4 = iota16[:, :].unsqueeze(1).unsqueeze(1).to_broadcast([128, 8, 16, 16])
                for pf_, which, dsel in ((posrf, 0, sel1), (poscf, 1, sel2)):
                    op("dve", lambda e, pf_=pf_: e.tensor_tensor(out=oh[:], in0=io4, in1=pf_[:, :, :].unsqueeze(3).to_broadcast([128, 8, 16, 16]), op=ALU.is_equal), r=[iota16, pf_], w=[oh])
                    op("dve", lambda e, which=which: e.tensor_tensor(out=oh[:], in0=oh[:], in1=i12f[:, which, :, :].unsqueeze(2).to_broadcast([128, 8, 16, 16]), op=ALU.mult), r=[oh, i12f], w=[oh])
                    op("dve", lambda e, dsel=dsel: e.tensor_reduce(out=dsel[:], in_=oh[:], axis=AX.X, op=ALU.add), r=[oh], w=[dsel])
                e3 = eidf[:, :].rearrange("p (h k) -> p h k", h=8)
                op("dve", lambda e: e.scalar_tensor_tensor(out=e3, in0=sel1[:], scalar=128.0, in1=sel2[:], op0=ALU.mult, op1=ALU.add), r=[sel1, sel2], w=[eidf])
                op("dve", lambda e: e.tensor_copy(out=eidi[:], in_=eidf[:]), r=[eidf], w=[eidi])
                op("dve", lambda e: e.tensor_tensor(out=gsm[:], in0=sc[:], in1=sc[:, :, 0:1].to_broadcast([128, 8, 16]), op=ALU.subtract), r=[sc], w=[gsm])
                op("act", lambda e: e.activation(out=gsm[:], in_=gsm[:], func=AF.Exp), r=[gsm], w=[gsm])
                op("dve", lambda e: e.tensor_reduce(out=st8[:, 48:56], in_=gsm[:], axis=AX.X, op=ALU.add), r=[gsm], w=[st8])
                op("dve", lambda e: e.reciprocal(out=st8[:, 56:64], in_=st8[:, 48:56]), r=[st8], w=[st8])
                op("dve", lambda e: e.tensor_tensor(out=gsm[:], in0=gsm[:], in1=st8[:, 56:64].unsqueeze(2).to_broadcast([128, 8, 16]), op=ALU.mult), r=[gsm, st8], w=[gsm])
                for j in range(128):
                    b = ub[j % NGB]
                    op("pool", lambda e, b=b, j=j: e.indirect_dma_start(out=b[:, :], out_offset=None, in_=pu[l][:, :], in_offset=bass.IndirectOffsetOnAxis(ap=eidi[:, j:j + 1], axis=0)), r=[eidi, in_res], w=[b], dma=True)
                    op("dve", lambda e, b=b, j=j: e.scalar_tensor_tensor(out=junk[:], in0=b[:], scalar=1.0, in1=h2[:], op0=ALU.mult, op1=ALU.mult, accum_out=adot[:, j:j + 1]), r=[b, h2], w=[junk, adot])
                op("act", lambda e: e.activation(out=coef[:], in_=adot[:], func=AF.Gelu), r=[adot], w=[coef])
                op("dve", lambda e: e.tensor_tensor(out=coef[:], in0=coef[:], in1=gsm[:, :, :].rearrange("p h k -> p (h k)"), op=ALU.mult), r=[coef, gsm], w=[coef])
                for j in range(128):
                    b = vbuf[j % NGB]
                    op("pool", lambda e, b=b, j=j: e.indirect_dma_start(out=b[:, :], out_offset=None, in_=pv[l][:, :], in_offset=bass.IndirectOffsetOnAxis(ap=eidi[:, j:j + 1], axis=0)), r=[eidi, in_res], w=[b], dma=True)
                    if j == 0:
                        op("dve", lambda e, b=b: e.tensor_scalar(out=acc[:], in0=b[:], scalar1=coef[:, 0:1], scalar2=None, op0=ALU.mult), r=[b, coef], w=[acc])
                    else:
                        op("dve", lambda e, b=b, j=j: e.scalar_tensor_tensor(out=acc[:], in0=b[:], scalar=coef[:, j:j + 1], in1=acc[:], op0=ALU.mult, op1=ALU.add), r=[b, coef, acc], w=[acc])
                op("dve", lambda e: e.tensor_tensor(out=xt[:], in0=xt[:], in1=acc[:], op=ALU.add), r=[xt, acc], w=[xt])

            if last:
                if samp:
                    op("sp", lambda e: e.dma_start(out=y_s[:, :], in_=xt[0:DEC, :]), r=[xt], w=[out_res], dma=True)
                else:
                    op("sp", lambda e: e.dma_start(out=y_p[rows, :], in_=xt[:]), r=[xt], w=[out_res], dma=True)
            else:
                op("sp", lambda e: e.dma_start(out=xout_d[rows, :], in_=xt[:]), r=[xt], w=[xout_r[t]], dma=True)

    kb.finish()
    es.close()
    return nc, kb.ninst


def _consts():
    j = np.arange(128)
    ident = np.eye(128, dtype=np.float32)
    negtri = -(j[:, None] >= j[None, :]).astype(np.float32)
    mlt = (j[:, None] < j[None, :]).astype(np.float32)
    ident4 = np.tile(ident, (1, 4))
    negm = np.where(j[:, None] >= j[None, :], -30000.0, 0.0).astype(np.float32)
    negm4 = np.tile(negm, (1, 4))
    pow2 = np.tile((0.5 ** np.arange(1, NBIS + 1)).astype(np.float32)[None, :], (128, 1))
    iota = np.tile(np.arange(16, dtype=np.float32)[None, :], (128, 1))
    return np.concatenate([ident, negtri, mlt, ident4, negm4, pow2, iota], axis=1).astype(np.float32)


def _rope_table():
    pos = np.concatenate([np.arange(SEQ), SEQ + np.arange(128)]).astype(np.float32)
    half = 32
    inv = (10000.0 ** (-np.arange(half, dtype=np.float32) / half)).astype(np.float32)
    ang = pos[:, None] * inv[None, :]
    return np.concatenate([np.cos(ang), np.sin(ang)], axis=1).astype(np.float32)


_PROG = {}


def _prep(x_prompt, x_sample, cache_a_k, cache_a_v, cache_idx_k, cache_b_k, cache_b_v,
          norm1, w_in, q_norm_a, k_norm_a, idx_k_norm, w_pa, w_pb, w_o, norm2,
          peer_wq, peer_k1, peer_k2, peer_u, peer_v, cores=range(8)):
    f = lambda a: np.ascontiguousarray(np.asarray(a, dtype=np.float32))
    x_prompt = f(x_prompt); x_sample = f(x_sample)
    shared = {
        "rope": _rope_table(),
        "cst": _consts(),
        "n1T": f(np.asarray(norm1).reshape(DEPTH, 8, 128).transpose(0, 2, 1)),
        "n2T": f(np.asarray(norm2).reshape(DEPTH, 8, 128).transpose(0, 2, 1)),
        "n2r": f(norm2),
        "w_in": f(w_in), "qn": f(q_norm_a), "kn": f(k_norm_a), "ikn": f(idx_k_norm),
        "w_pa": f(w_pa), "w_pb": f(w_pb), "w_o": f(w_o), "pwq": f(peer_wq),
        "pk1T": f(np.asarray(peer_k1).transpose(0, 3, 1, 2)),
        "pk2T": f(np.asarray(peer_k2).transpose(0, 3, 1, 2)),
    }
    pu = np.asarray(peer_u); pvv = np.asarray(peer_v)
    for l in range(DEPTH):
        shared["pu%d" % l] = f(pu[l])
        shared["pv%d" % l] = f(pvv[l])
    cak = np.asarray(cache_a_k); cav = np.asarray(cache_a_v); cik = np.asarray(cache_idx_k)
    cbk = np.asarray(cache_b_k); cbv = np.asarray(cache_b_v)
    in_maps = []
    for b in cores:
        xa = np.zeros((NT * 128, D), np.float32)
        xa[:SEQ] = x_prompt[b]
        xa[SEQ:SEQ + DEC] = x_sample[b]
        m = dict(shared)
        m["x_in"] = xa
        m["cak"] = f(cak[:, b].reshape(DEPTH, SEQ, 128))
        m["cav"] = f(cav[:, b].reshape(DEPTH, SEQ, 128))
        m["cik"] = f(cik[:, b].reshape(DEPTH, SEQ, 64))
        m["cbk"] = f(cbk[:, b].reshape(DEPTH, SEQ, 512))
        m["cbv"] = f(cbv[:, b].reshape(DEPTH, SEQ, 512))
        in_maps.append(m)
    return in_maps


def kernel(**inputs):
    if "full" not in _PROG:
        _PROG["full"] = build_program()[0]
    nc = _PROG["full"]
    in_maps = _prep(**inputs)
    res = run_bass_kernel_spmd(nc, in_maps, core_ids=list(range(8)))
    R = res.results
    st = lambda k, shp: np.stack([np.asarray(R[b][k], dtype=np.float32) for b in range(8)], axis=1).reshape(shp)
    y_p = np.stack([np.asarray(R[b]["y_p"], dtype=np.float32) for b in range(8)], axis=0)
    y_s = np.stack([np.asarray(R[b]["y_s"], dtype=np.float32) for b in range(8)], axis=0)
    return (y_p, y_s,
            st("ak_p", (DEPTH, 8, SEQ, 2, 64)), st("av_p", (DEPTH, 8, SEQ, 2, 64)), st("ik_p", (DEPTH, 8, SEQ, 64)),
            st("bk_p", (DEPTH, 8, SEQ, 8, 64)), st("bv_p", (DEPTH, 8, SEQ, 8, 64)),
            st("ak_s", (DEPTH, 8, DEC, 2, 64)), st("av_s", (DEPTH, 8, DEC, 2, 64)), st("ik_s", (DEPTH, 8, DEC, 64)),
            st("bk_s", (DEPTH, 8, DEC, 8, 64)), st("bv_s", (DEPTH, 8, DEC, 8, 64)))
```

```python
import contextlib
import numpy as np
import concourse.bass as bass
import concourse.mybir as mybir
from concourse.bass_utils import run_bass_kernel_spmd

F32 = mybir.dt.float32
BF16 = mybir.dt.bfloat16
I32 = mybir.dt.int32
U32 = mybir.dt.uint32
ALU = mybir.AluOpType
AF = mybir.ActivationFunctionType
AX = mybir.AxisListType

D = 1024
DEPTH = 4
NT = 17
NKS = 17
SEQ = 2048
DEC = 16
NIN = 4936
EPS = 1e-6
NEG = -1e30
TOPK = 256
NBIS = 22


class Res:
    __slots__ = ("t", "w", "r")

    def __init__(self, t=None):
        self.t = t
        self.w = None
        self.r = {}

    def __getitem__(self, k):
        return self.t[k]


class KB:
    def __init__(self, nc, es):
        self.nc = nc
        self.es = es
        self.E = {"pe": nc.tensor, "act": nc.scalar, "dve": nc.vector, "pool": nc.gpsimd, "sp": nc.sync}
        self.sems = {}
        self.cnt = {}
        self.seen = {e: {} for e in self.E}
        self.ninst = 0
        self.dpool = {"sp": ["dsp%d" % i for i in range(24)], "pool": ["dpl%d" % i for i in range(6)]}
        self.dnext = {"sp": 0, "pool": 0}

    def sem(self, key):
        if key not in self.sems:
            self.sems[key] = self.es.enter_context(self.nc.semaphore("s_" + key))
            self.cnt[key] = 0
        return self.sems[key]

    def op(self, eng, fn, r=(), w=(), dma=False):
        deps = {}
        for x in r:
            if x.w is not None:
                k, v = x.w
                if deps.get(k, 0) < v:
                    deps[k] = v
        for x in w:
            if x.w is not None:
                k, v = x.w
                if deps.get(k, 0) < v:
                    deps[k] = v
            for k, v in x.r.items():
                if deps.get(k, 0) < v:
                    deps[k] = v
        E = self.E[eng]
        seen = self.seen[eng]
        for k, v in deps.items():
            if k == "pe" and eng == "pe" and not dma:
                continue
            if seen.get(k, 0) < v:
                E.wait_ge(self.sem(k), v)
                seen[k] = v
        if dma:
            pl = self.dpool[eng]
            key = pl[self.dnext[eng]]
            self.dnext[eng] = (self.dnext[eng] + 1) % len(pl)
            s = self.sem(key)
            prev = self.cnt[key]
            if prev > 0 and seen.get(key, 0) < prev:
                E.wait_ge(s, prev)
                seen[key] = prev
        else:
            key = eng
            s = self.sem(key)
        ins = fn(E)
        inc = 16 if dma else 1
        self.cnt[key] += inc
        c = self.cnt[key]
        ins.then_inc(s, inc)
        for x in r:
            x.r[key] = c
        for x in w:
            x.w = (key, c)
            x.r = {}
        self.ninst += 1
        return ins

    def barrier(self):
        for en, E in self.E.items():
            seen = self.seen[en]
            for k, sm in self.sems.items():
                v = self.cnt[k]
                if v > 0 and seen.get(k, 0) < v:
                    E.wait_ge(sm, v)
                    seen[k] = v

    def finish(self):
        E = self.E["sp"]
        for k, s in self.sems.items():
            if self.cnt[k] > 0:
                E.wait_ge(s, self.cnt[k])


def build_program(depth=DEPTH, tiles=None, do_peer=True):
    if tiles is None:
        tiles = list(range(NT))
    nc = bass.Bass("TRN2", target_bir_lowering=False)
    es = contextlib.ExitStack()
    kb = KB(nc, es)

    def dram(name, shape, dt, kind):
        return nc.dram_tensor(name, shape, dt, kind=kind)

    x_in = dram("x_in", [NT * 128, D], F32, "ExternalInput")
    rope_in = dram("rope", [NT * 128, 64], F32, "ExternalInput")
    cak = dram("cak", [DEPTH, SEQ, 128], F32, "ExternalInput")
    cav = dram("cav", [DEPTH, SEQ, 128], F32, "ExternalInput")
    cik = dram("cik", [DEPTH, SEQ, 64], F32, "ExternalInput")
    cbk = dram("cbk", [DEPTH, SEQ, 512], F32, "ExternalInput")
    cbv = dram("cbv", [DEPTH, SEQ, 512], F32, "ExternalInput")
    n1T = dram("n1T", [DEPTH, 128, 8], F32, "ExternalInput")
    n2T = dram("n2T", [DEPTH, 128, 8], F32, "ExternalInput")
    n2r = dram("n2r", [DEPTH, D], F32, "ExternalInput")
    w_in = dram("w_in", [DEPTH, D, NIN], F32, "ExternalInput")
    qn = dram("qn", [DEPTH, 64], F32, "ExternalInput")
    kn = dram("kn", [DEPTH, 64], F32, "ExternalInput")
    ikn = dram("ikn", [DEPTH, 64], F32, "ExternalInput")
    w_pa = dram("w_pa", [DEPTH, 512, D], F32, "ExternalInput")
    w_pb = dram("w_pb", [DEPTH, 512, D], F32, "ExternalInput")
    w_o = dram("w_o", [DEPTH, D, D], F32, "ExternalInput")
    pwq = dram("pwq", [DEPTH, D, D], F32, "ExternalInput")
    pk1T = dram("pk1T", [DEPTH, 64, 8, 128], F32, "ExternalInput")
    pk2T = dram("pk2T", [DEPTH, 64, 8, 128], F32, "ExternalInput")
    puT = [dram("puT%d" % l, [D, 16384], F32, "ExternalInput") for l in range(DEPTH)]
    pv = [dram("pv%d" % l, [16384, D], F32, "ExternalInput") for l in range(DEPTH)]

    y_p = dram("y_p", [SEQ, D], F32, "ExternalOutput")
    y_s = dram("y_s", [DEC, D], F32, "ExternalOutput")
    ak_p = dram("ak_p", [DEPTH, SEQ, 128], F32, "ExternalOutput")
    av_p = dram("av_p", [DEPTH, SEQ, 128], F32, "ExternalOutput")
    ik_p = dram("ik_p", [DEPTH, SEQ, 64], F32, "ExternalOutput")
    bk_p = dram("bk_p", [DEPTH, SEQ, 512], F32, "ExternalOutput")
    bv_p = dram("bv_p", [DEPTH, SEQ, 512], F32, "ExternalOutput")
    ak_s = dram("ak_s", [DEPTH, DEC, 128], F32, "ExternalOutput")
    av_s = dram("av_s", [DEPTH, DEC, 128], F32, "ExternalOutput")
    ik_s = dram("ik_s", [DEPTH, DEC, 64], F32, "ExternalOutput")
    bk_s = dram("bk_s", [DEPTH, DEC, 512], F32, "ExternalOutput")
    bv_s = dram("bv_s", [DEPTH, DEC, 512], F32, "ExternalOutput")
    xs_dram = [dram("xscr%d" % i, [NT * 128, D], F32, "Internal") for i in range(2)]
    xs_res = [[Res() for _ in range(NT)] for _ in range(2)]
    out_res = Res()
    in_res = Res()

    ARENA_F32 = 52000
    arena = es.enter_context(nc.sbuf_tensor("arena", [128, ARENA_F32], F32))
    aoff = [0]
    DTB = {F32: 4, BF16: 2, I32: 4, U32: 4}

    def sb(name, shape, dt):
        nb = DTB[dt]
        n = 1
        for d_ in shape[1:]:
            n *= d_
        nbytes = (n * nb + 31) // 32 * 32
        o = aoff[0]
        assert o % 4 == 0
        aoff[0] = o + nbytes
        assert aoff[0] <= ARENA_F32 * 4, (name, aoff[0])
        v = arena[0:shape[0], o // 4:(o + nbytes) // 4]
        if dt != F32:
            v = v.bitcast(dt)
        v = v[:, 0:n]
        if len(shape) == 3:
            v = v.rearrange("p (a b) -> p a b", a=shape[1])
        elif len(shape) == 4:
            v = v.rearrange("p (a b c) -> p a b c", a=shape[1], b=shape[2])
        return Res(v)

    def pst(name, shape, dt):
        return Res(es.enter_context(nc.psum_tensor(name, shape, dt)))

    ident = sb("ident", [128, 128], BF16)
    ident4 = sb("ident4", [128, 512], BF16)
    negtri = sb("negtri", [128, 128], BF16)
    mlt = sb("mlt", [128, 128], BF16)
    negm4 = sb("negm4", [128, 512], BF16)
    ones1 = sb("ones1", [128, 1], BF16)
    pow2 = sb("pow2", [128, NBIS], F32)
    iota16 = sb("iota16", [128, 16], F32)
    thr16 = sb("thr16", [128, 16], F32)
    identf = sb("identf", [128, 128], F32)
    iota128 = sb("iota128", [128, 128], F32)
    NCST = 128 * 3 + 512 * 2 + NBIS + 16 + 128
    cst_in = dram("cst", [128, NCST], F32, "ExternalInput")
    gq = sb("gq", [128, 64], F32)
    gk = sb("gk", [128, 64], F32)
    gik = sb("gik", [128, 64], F32)
    n1s = sb("n1s", [128, 8], F32)
    n2s = sb("n2s", [128, 8], F32)
    n2b = sb("n2b", [128, D], F32)
    xt = sb("xt", [128, D], F32)
    hb = sb("hb", [128, D], BF16)
    hT = sb("hT", [128, 8, 128], BF16)
    sq = sb("sq", [128, D], F32)
    st8 = sb("st8", [128, 64], F32)
    STW = 1536
    stg = [sb("stg%d" % i, [128, STW], F32) for i in range(2)]
    mark0 = aoff[0]

    WARENA_A = sb("warenaA", [128, 8 * 2888], BF16)
    kaT = sb("kaT", [64, 2, NKS * 128], BF16)
    kiT = sb("kiT", [64, NKS * 128], BF16)
    kbT = sb("kbT", [64, 8, NKS * 128], BF16)
    vaA = sb("vaA", [128, NKS, 2, 65], BF16)
    vbB = sb("vbB", [128, NKS, 8, 64], BF16)
    kvres = [Res() for _ in range(NKS)]
    cstage = sb("cstage", [128, NCST], F32)
    ropet = sb("ropet", [128, 64], F32)
    pf = sb("pf", [128, 512], F32)
    pf2 = sb("pf2", [128, 512], F32)
    pbf = sb("pbf", [128, 512], BF16)
    r1 = sb("r1", [128, 256], F32)
    r2 = sb("r2", [128, 256], F32)
    qaT = sb("qaT", [64, 8, 128], BF16)
    qiT = sb("qiT", [64, 8, 128], BF16)
    qbT = sb("qbT", [64, 8, 128], BF16)
    wi = sb("wi", [128, 8], F32)
    isc = sb("isc", [128, 2560], F32)
    mbias = sb("mbias", [128, 2560], BF16)
    rl = [sb("rl%d" % i, [128, 512], F32) for i in range(2)]
    PT = [sb("PT%d" % i, [128, 512], BF16) for i in range(2)]
    oa = sb("oa", [128, 512], F32)
    ob = sb("ob", [128, 512], F32)
    ebuf = sb("ebuf", [128, 1024], F32)
    spT = sb("spT", [128, 1024], BF16)
    ET = sb("ET", [128, 1024], BF16)
    dd = sb("dd", [128, 8], F32)
    bis = sb("bis", [128, 8], F32)
    dtab = sb("dtab", [128, NBIS], F32)
    endA = aoff[0]

    aoff[0] = mark0
    WARENA_B = sb("warenaB", [128, 32768], BF16)
    oabb = sb("oabb", [128, D], BF16)
    sga = sb("sga", [128, D], F32)
    sgb = sb("sgb", [128, D], F32)
    oT = sb("oT", [128, 8, 128], BF16)
    mm = sb("mm", [128, D], F32)
    mbf = sb("mbf", [128, D], BF16)
    oabf = sb("oabf", [128, D], F32)
    endB1 = aoff[0]

    aoff[0] = mark0
    h2T_all = sb("h2T_all", [128, 8, NT * 128], BF16)
    WQ = sb("WQ", [128, 8 * 1024], BF16)
    qsb = sb("qsb", [128, D], BF16)
    qT = sb("qT", [64, 16, 128], BF16)
    k12 = sb("k12", [64, 2, 8, 128], BF16)
    s12 = sb("s12", [128, 2, 8, 128], F32)
    swk = sb("swk", [128, 256], F32)
    v12 = sb("v12", [128, 2, 8, 16], F32)
    i12 = sb("i12", [128, 2, 8, 16], U32)
    i12f = sb("i12f", [128, 2, 8, 16], F32)
    cand = sb("cand", [128, 8, 256], F32)
    sc = sb("sc", [128, 8, 16], F32)
    pos = sb("pos", [128, 8, 16], U32)
    posf = sb("posf", [128, 8, 16], F32)
    posrf = sb("posrf", [128, 8, 16], F32)
    poscf = sb("poscf", [128, 8, 16], F32)
    oh = sb("oh", [128, 8, 16, 16], F32)
    sel1 = sb("sel1", [128, 8, 16], F32)
    sel2 = sb("sel2", [128, 8, 16], F32)
    gsm = sb("gsm", [128, 8, 16], F32)
    TT = sb("TT", [128, 3, 128], F32)
    P1h = sb("P1h", [128, 64, 128], BF16)
    P2h = sb("P2h", [128, 64, 128], BF16)
    Gs = sb("Gs", [128, 128, 128], BF16)
    endB2 = aoff[0]

    aoff[0] = mark0
    h2T_all_c = sb("h2T_all_c", [128, 8, NT * 128], BF16)
    accs = sb("accs", [128, NT, D], F32)
    accres = [Res() for _ in range(NT)]
    NBLK = 4
    u16 = [sb("u16_%d" % i, [128, 8, NBLK * 128], BF16) for i in range(2)]
    v16 = [sb("v16_%d" % i, [128, NBLK, D], BF16) for i in range(2)]
    Gc = [sb("Gc%d" % i, [128, 2, NBLK, 128], BF16) for i in range(2)]
    gel = [sb("gel%d" % i, [128, 256], BF16) for i in range(2)]
    cfT = [sb("cfT%d" % i, [128, 256], BF16) for i in range(2)]
    endC = aoff[0]
    print("arena bytes: persistent", mark0, "A", endA, "B1", endB1, "B2", endB2, "C", endC, "cap", ARENA_F32 * 4)

    Gd = dram("Gd", [NT, 128, 16384], BF16, "Internal")
    gdres = [Res() for _ in range(NT)]
    xmid = dram("xmid", [NT * 128, D], F32, "Internal")
    xmres = [Res() for _ in range(NT)]
    oab_dram = dram("oabscr", [NT * 128, D], F32, "Internal")
    oab_res = [Res() for _ in range(NT)]

    P = [pst("ps%d" % i, [128, 512], F32) for i in range(7)]
    PTR = pst("ptr", [128, 1024], BF16)

    op = kb.op

    op("sp", lambda e: e.dma_start(out=cstage[:], in_=cst_in[:, :]), r=[in_res], w=[cstage], dma=True)
    o = 0
    for dst, wdt in ((ident, 128), (negtri, 128), (mlt, 128), (ident4, 512), (negm4, 512)):
        op("dve", lambda e, dst=dst, o=o, wdt=wdt: e.tensor_copy(out=dst[:], in_=cstage[:, o:o + wdt]), r=[cstage], w=[dst])
        o += wdt
    op("dve", lambda e, o=o: e.tensor_copy(out=pow2[:], in_=cstage[:, o:o + NBIS]), r=[cstage], w=[pow2])
    o += NBIS
    op("dve", lambda e, o=o: e.tensor_copy(out=iota16[:], in_=cstage[:, o:o + 16]), r=[cstage], w=[iota16])
    o += 16
    op("dve", lambda e, o=o: e.tensor_copy(out=iota128[:], in_=cstage[:, o:o + 128]), r=[cstage], w=[iota128])
    op("dve", lambda e: e.tensor_copy(out=identf[:], in_=cstage[:, 0:128]), r=[cstage], w=[identf])
    op("dve", lambda e: e.memset(ones1[:], 1.0), w=[ones1])
    op("dve", lambda e: e.tensor_scalar(out=thr16[:], in0=iota16[:], scalar1=16.0, scalar2=16.0, op0=ALU.mult, op1=ALU.add), r=[iota16], w=[thr16])
    op("dve", lambda e: e.memset(thr16[:, 15:16], 1e9), w=[thr16])
    op("dve", lambda e: e.memset(vaA[:], 1.0), w=[vaA] + kvres)
    kb.barrier()

    def transpose_blocks(src, nblk, dstT, dst_res, ptile=PTR):
        for b in range(nblk):
            op("pe", lambda e, b=b: e.transpose(out=ptile[0:64, b * 128:(b + 1) * 128], in_=src[:, b * 64:(b + 1) * 64], identity=ident[:]),
               r=[src, ident], w=[ptile])
        op("act", lambda e: e.activation(out=dstT, in_=ptile[0:64, 0:nblk * 128].rearrange("p (b n) -> p b n", b=nblk), func=AF.Copy),
           r=[ptile], w=dst_res)

    def rmsnorm_rows(xres, outbf, scratch):
        op("act", lambda e: e.activation(out=scratch[:], in_=xres[:], func=AF.Square, accum_out=st8[:, 0:1]), r=[xres], w=[scratch, st8])
        op("dve", lambda e: e.tensor_scalar(out=st8[:, 1:2], in0=st8[:, 0:1], scalar1=1.0 / D, scalar2=EPS, op0=ALU.mult, op1=ALU.add), r=[st8], w=[st8])
        op("act", lambda e: e.activation(out=st8[:, 2:3], in_=st8[:, 1:2], func=AF.Sqrt), r=[st8], w=[st8])
        op("dve", lambda e: e.reciprocal(out=st8[:, 3:4], in_=st8[:, 2:3]), r=[st8], w=[st8])
        op("dve", lambda e: e.tensor_scalar(out=outbf[:], in0=xres[:], scalar1=st8[:, 3:4], scalar2=None, op0=ALU.mult), r=[xres, st8], w=[outbf])

    def make_hT():
        for c in range(8):
            op("pe", lambda e, c=c: e.transpose(out=PTR[:, c * 128:(c + 1) * 128], in_=hb[:, c * 128:(c + 1) * 128], identity=ident[:]),
               r=[hb, ident], w=[PTR])
        op("act", lambda e: e.activation(out=hT[:], in_=PTR[:, :].rearrange("p (c n) -> p c n", c=8), func=AF.Copy), r=[PTR], w=[hT])

    def headnorm(src_res, src_ap, H, gain, dst):
        W = H * 64
        op("act", lambda e: e.activation(out=sq[:, 0:W], in_=src_ap, func=AF.Square), r=[src_res], w=[sq])
        op("dve", lambda e: e.tensor_reduce(out=st8[:, 8:8 + H], in_=sq[:, 0:W].rearrange("p (h d) -> p h d", h=H), axis=AX.X, op=ALU.add), r=[sq], w=[st8])
        op("dve", lambda e: e.tensor_scalar(out=st8[:, 16:16 + H], in0=st8[:, 8:8 + H], scalar1=1.0 / 64, scalar2=EPS, op0=ALU.mult, op1=ALU.add), r=[st8], w=[st8])
        op("act", lambda e: e.activation(out=st8[:, 24:24 + H], in_=st8[:, 16:16 + H], func=AF.Sqrt), r=[st8], w=[st8])
        op("dve", lambda e: e.reciprocal(out=st8[:, 32:32 + H], in_=st8[:, 24:24 + H]), r=[st8], w=[st8])
        d3 = dst[:, 0:W].rearrange("p (h d) -> p h d", h=H)
        op("dve", lambda e: e.tensor_tensor(out=d3, in0=src_ap.rearrange("p (h d) -> p h d", h=H),
                                            in1=st8[:, 32:32 + H].unsqueeze(2).to_broadcast([128, H, 64]), op=ALU.mult), r=[st8, src_res], w=[dst])
        op("dve", lambda e: e.tensor_tensor(out=d3, in0=d3, in1=gain[:, :].unsqueeze(1).to_broadcast([128, H, 64]), op=ALU.mult), r=[gain, dst], w=[dst])

    def rope(src, H, dst, scale=None):
        W = H * 64
        s3 = src[:, 0:W].rearrange("p (h d) -> p h d", h=H)
        d3 = dst[:, 0:W].rearrange("p (h d) -> p h d", h=H)
        cosb = ropet[:, 0:32].unsqueeze(1).to_broadcast([128, H, 32])
        sinb = ropet[:, 32:64].unsqueeze(1).to_broadcast([128, H, 32])
        a3 = r1[:, 0:H * 32].rearrange("p (h d) -> p h d", h=H)
        b3 = r2[:, 0:H * 32].rearrange("p (h d) -> p h d", h=H)
        op("dve", lambda e: e.tensor_tensor(out=a3, in0=s3[:, :, 0:32], in1=cosb, op=ALU.mult), r=[src, ropet], w=[r1])
        op("dve", lambda e: e.tensor_tensor(out=b3, in0=s3[:, :, 32:64], in1=sinb, op=ALU.mult), r=[src, ropet], w=[r2])
        op("dve", lambda e: e.tensor_tensor(out=d3[:, :, 0:32], in0=a3, in1=b3, op=ALU.subtract), r=[r1, r2], w=[dst])
        op("dve", lambda e: e.tensor_tensor(out=a3, in0=s3[:, :, 32:64], in1=cosb, op=ALU.mult), r=[src, ropet], w=[r1])
        op("dve", lambda e: e.tensor_tensor(out=b3, in0=s3[:, :, 0:32], in1=sinb, op=ALU.mult), r=[src, ropet], w=[r2])
        op("dve", lambda e: e.tensor_tensor(out=d3[:, :, 32:64], in0=a3, in1=b3, op=ALU.add), r=[r1, r2], w=[dst])

    def load_cast_rows(wres, dram_ap_fn, nchunks, width, dst_fn, scale_res=None, scale_col=None):
        i = 0
        for c in range(nchunks):
            for o0 in range(0, width, STW):
                wdt = min(STW, width - o0)
                s = stg[i % 2]
                i += 1
                op("sp", lambda e, c=c, o0=o0, wdt=wdt, s=s: e.dma_start(out=s[:, 0:wdt], in_=dram_ap_fn(c, o0, wdt)), r=[in_res], w=[s], dma=True)
                if scale_res is not None:
                    op("dve", lambda e, c=c, o0=o0, wdt=wdt, s=s: e.tensor_scalar(out=dst_fn(c, o0, wdt), in0=s[:, 0:wdt], scalar1=scale_res[:, scale_col(c):scale_col(c) + 1], scalar2=None, op0=ALU.mult),
                       r=[s, scale_res], w=[wres])
                else:
                    op("pool", lambda e, c=c, o0=o0, wdt=wdt, s=s: e.tensor_copy(out=dst_fn(c, o0, wdt), in_=s[:, 0:wdt]), r=[s], w=[wres])

    WAA = WARENA_A.t
    WA = WARENA_B.t
    NQKV = 2888
    wa_qkv = lambda c, o0, wdt: WAA[:, c * NQKV + o0: c * NQKV + o0 + wdt]
    GOFF = 0
    PAOFF = 8 * 2048
    PBOFF = PAOFF + 4 * 1024
    WOOFF = PBOFF + 4 * 1024
    PQOFF = WOOFF + 8 * 1024

    for l in range(depth):
        xin_d = x_in if l == 0 else xs_dram[(l - 1) % 2]
        xin_r = [in_res] * NT if l == 0 else xs_res[(l - 1) % 2]
        xout_d = xs_dram[l % 2]
        xout_r = xs_res[l % 2]
        last = (l == depth - 1)

        op("sp", lambda e: e.dma_start(out=n1s[:], in_=n1T[l, :, :]), r=[in_res], w=[n1s], dma=True)
        op("sp", lambda e: e.dma_start(out=n2s[:], in_=n2T[l, :, :]), r=[in_res], w=[n2s], dma=True)
        op("sp", lambda e: e.dma_start(out=gq[:], in_=qn[l, :].partition_broadcast(128)), r=[in_res], w=[gq], dma=True)
        op("sp", lambda e: e.dma_start(out=gk[:], in_=kn[l, :].partition_broadcast(128)), r=[in_res], w=[gk], dma=True)
        op("sp", lambda e: e.dma_start(out=gik[:], in_=ikn[l, :].partition_broadcast(128)), r=[in_res], w=[gik], dma=True)
        op("sp", lambda e: e.dma_start(out=n2b[:], in_=n2r[l, :].partition_broadcast(128)), r=[in_res], w=[n2b], dma=True)

        kb.barrier()
        if l > 0:
            op("pool", lambda e: e.memset(vaA[:], 1.0), w=[vaA] + kvres)
        load_cast_rows(WARENA_A, lambda c, o0, wdt: w_in[l, c * 128:(c + 1) * 128, o0:o0 + wdt], 8, NQKV, wa_qkv, n1s, lambda c: c)

        for t in tiles:
            samp = (t == NT - 1)
            nk = t + 1
            if samp:
                for k0 in range(0, 16, 4):
                    for kk in range(k0, k0 + 4):
                        rows = slice(kk * 128, (kk + 1) * 128)
                        s = stg[kk % 2]
                        op("sp", lambda e, s=s, rows=rows: e.dma_start(out=s[:, 0:128], in_=cak[l, rows, :]), r=[in_res], w=[s], dma=True)
                        op("sp", lambda e, s=s, rows=rows: e.dma_start(out=s[:, 128:192], in_=cik[l, rows, :]), r=[in_res], w=[s], dma=True)
                        op("sp", lambda e, s=s, rows=rows: e.dma_start(out=s[:, 192:320], in_=cav[l, rows, :]), r=[in_res], w=[s], dma=True)
                        op("sp", lambda e, s=s, rows=rows: e.dma_start(out=s[:, 512:1024], in_=cbk[l, rows, :]), r=[in_res], w=[s], dma=True)
                        op("sp", lambda e, s=s, rows=rows: e.dma_start(out=s[:, 1024:1536], in_=cbv[l, rows, :]), r=[in_res], w=[s], dma=True)
                        op("dve", lambda e, s=s: e.tensor_copy(out=pbf[:, 0:192], in_=s[:, 0:192]), r=[s], w=[pbf])
                        for b in range(3):
                            op("pe", lambda e, b=b: e.transpose(out=PTR[0:64, b * 128:(b + 1) * 128], in_=pbf[:, b * 64:(b + 1) * 64], identity=ident[:]), r=[pbf, ident], w=[PTR])
                        ks = slice(kk * 128, (kk + 1) * 128)
                        op("act", lambda e, ks=ks: e.activation(out=kaT[:, :, ks], in_=PTR[0:64, 0:256].rearrange("p (b n) -> p b n", b=2), func=AF.Copy), r=[PTR], w=[kvres[kk]])
                        op("act", lambda e, ks=ks: e.activation(out=kiT[:, ks], in_=PTR[0:64, 256:384], func=AF.Copy), r=[PTR], w=[kvres[kk]])
                        op("dve", lambda e, s=s, kk=kk: e.tensor_copy(out=vaA[:, kk, :, 0:64], in_=s[:, 192:320].rearrange("p (g d) -> p g d", g=2)), r=[s], w=[kvres[kk]])
                        op("dve", lambda e, s=s: e.tensor_copy(out=pbf[:, 0:512], in_=s[:, 512:1024]), r=[s], w=[pbf])
                        for b in range(8):
                            op("pe", lambda e, b=b: e.transpose(out=PTR[0:64, b * 128:(b + 1) * 128], in_=pbf[:, b * 64:(b + 1) * 64], identity=ident[:]), r=[pbf, ident], w=[PTR])
                        op("act", lambda e, ks=ks: e.activation(out=kbT[:, :, ks], in_=PTR[0:64, :].rearrange("p (b n) -> p b n", b=8), func=AF.Copy), r=[PTR], w=[kvres[kk]])
                        op("pool", lambda e, s=s, kk=kk: e.tensor_copy(out=vbB[:, kk, :, :], in_=s[:, 1024:1536].rearrange("p (h d) -> p h d", h=8)), r=[s], w=[kvres[kk]])

            rows = slice(t * 128, (t + 1) * 128)
            op("sp", lambda e: e.dma_start(out=xt[:], in_=xin_d[rows, :]), r=[xin_r[t]], w=[xt], dma=True)
            op("sp", lambda e: e.dma_start(out=ropet[:], in_=rope_in[rows, :]), r=[in_res], w=[ropet], dma=True)
            rmsnorm_rows(xt, hb, sq)
            make_hT()

            def proj(pt, c0, wdt):
                for c in range(8):
                    op("pe", lambda e, c=c: e.matmul(pt[:, 0:wdt], lhsT=hT[:, c, :], rhs=WAA[:, c * NQKV + c0: c * NQKV + c0 + wdt], start=(c == 0), stop=(c == 7)),
                       r=[hT, WARENA_A], w=[pt])

            def out_rows(dst_p, dst_s, src, wdt):
                if samp:
                    op("sp", lambda e: e.dma_start(out=dst_s[l, :, :], in_=src[0:DEC, 0:wdt]), r=[src], w=[out_res], dma=True)
                else:
                    op("sp", lambda e: e.dma_start(out=dst_p[l, rows, :], in_=src[:, 0:wdt]), r=[src], w=[out_res], dma=True)

            ks = slice(t * 128, (t + 1) * 128)
            proj(P[0], 0, 512)
            headnorm(P[0], P[0][:, 0:512], 8, gq, pf)
            rope(pf, 8, pf2)
            op("dve", lambda e: e.tensor_scalar(out=pbf[:], in0=pf2[:], scalar1=0.125, scalar2=None, op0=ALU.mult), r=[pf2], w=[pbf])
            transpose_blocks(pbf, 8, qaT[:], [qaT])
            proj(P[1], 512, 256)
            headnorm(P[1], P[1][:, 0:128], 2, gk, pf)
            rope(pf, 2, pf2)
            out_rows(ak_p, ak_s, pf2, 128)
            op("dve", lambda e: e.tensor_copy(out=pbf[:, 0:128], in_=pf2[:, 0:128]), r=[pf2], w=[pbf])
            op("act", lambda e: e.activation(out=pf[:, 0:128], in_=P[1][:, 128:256], func=AF.Copy), r=[P[1]], w=[pf])
            out_rows(av_p, av_s, pf, 128)
            op("dve", lambda e: e.tensor_copy(out=vaA[:, t, :, 0:64], in_=pf[:, 0:128].rearrange("p (g d) -> p g d", g=2)), r=[pf], w=[kvres[t]])
            transpose_blocks(pbf, 2, kaT[:, :, ks], [kvres[t]])
            proj(P[0], 768, 512)
            rope(P[0], 8, pf2)
            op("dve", lambda e: e.tensor_copy(out=pbf[:], in_=pf2[:]), r=[pf2], w=[pbf])
            transpose_blocks(pbf, 8, qiT[:], [qiT])
            proj(P[1], 1280, 72)
            headnorm(P[1], P[1][:, 0:64], 1, gik, pf)
            rope(pf, 1, pf2)
            out_rows(ik_p, ik_s, pf2, 64)
            op("dve", lambda e: e.tensor_copy(out=pbf[:, 0:64], in_=pf2[:, 0:64]), r=[pf2], w=[pbf])
            op("act", lambda e: e.activation(out=wi[:], in_=P[1][:, 64:72], func=AF.Copy), r=[P[1]], w=[wi])
            for b in range(1):
                op("pe", lambda e: e.transpose(out=PTR[0:64, 0:128], in_=pbf[:, 0:64], identity=ident[:]), r=[pbf, ident], w=[PTR])
            op("act", lambda e: e.activation(out=kiT[:, ks], in_=PTR[0:64, 0:128], func=AF.Copy), r=[PTR], w=[kvres[t]])
            proj(P[0], 1352, 512)
            op("act", lambda e: e.activation(out=pbf[:], in_=P[0][:, :], func=AF.Copy, scale=0.125), r=[P[0]], w=[pbf])
            transpose_blocks(pbf, 8, qbT[:], [qbT])
            proj(P[1], 1864, 512)
            op("act", lambda e: e.activation(out=pf[:], in_=P[1][:, :], func=AF.Copy), r=[P[1]], w=[pf])
            out_rows(bk_p, bk_s, pf, 512)
            op("dve", lambda e: e.tensor_copy(out=pbf[:], in_=pf[:]), r=[pf], w=[pbf])
            transpose_blocks(pbf, 8, kbT[:, :, ks], [kvres[t]])
            proj(P[0], 2376, 512)
            op("act", lambda e: e.activation(out=pf2[:], in_=P[0][:, :], func=AF.Copy), r=[P[0]], w=[pf2])
            out_rows(bv_p, bv_s, pf2, 512)
            op("dve", lambda e: e.tensor_copy(out=vbB[:, t, :, :], in_=pf2[:, :].rearrange("p (h d) -> p h d", h=8)), r=[pf2], w=[kvres[t]])

            S = nk * 128
            nblk = (S + 511) // 512
            kvr = [kvres[i] for i in range(nk)]
            for bi in range(nblk):
                c0 = bi * 512
                wdt = min(512, S - c0)
                for h in range(8):
                    pt = P[2 + (h % 2)]
                    rb = rl[h % 2]
                    op("pe", lambda e, h=h, pt=pt: e.matmul(pt[:, 0:wdt], lhsT=qiT[:, h, :], rhs=kiT[:, c0:c0 + wdt], start=True, stop=True), r=[qiT] + kvr, w=[pt])
                    op("act", lambda e, pt=pt, rb=rb: e.activation(out=rb[:, 0:wdt], in_=pt[:, 0:wdt], func=AF.Relu, scale=0.125 * (8 ** -0.5)), r=[pt], w=[rb])
                    if h == 0:
                        op("dve", lambda e, rb=rb: e.tensor_scalar(out=isc[:, c0:c0 + wdt], in0=rb[:, 0:wdt], scalar1=wi[:, 0:1], scalar2=None, op0=ALU.mult), r=[rb, wi], w=[isc])
                    else:
                        op("dve", lambda e, rb=rb, h=h: e.scalar_tensor_tensor(out=isc[:, c0:c0 + wdt], in0=rb[:, 0:wdt], scalar=wi[:, h:h + 1], in1=isc[:, c0:c0 + wdt], op0=ALU.mult, op1=ALU.add), r=[rb, wi, isc], w=[isc])
            need_thr = nk > 2
            if need_thr:
                op("dve", lambda e: e.tensor_reduce(out=bis[:, 5:6], in_=isc[:, 0:S], axis=AX.X, op=ALU.max), r=[isc], w=[bis])
                op("dve", lambda e: e.tensor_reduce(out=bis[:, 6:7], in_=isc[:, 0:S], axis=AX.X, op=ALU.min), r=[isc], w=[bis])
                op("dve", lambda e: e.tensor_scalar(out=bis[:, 6:7], in0=bis[:, 6:7], scalar1=-1.0, scalar2=None, op0=ALU.mult), r=[bis], w=[bis])
                op("dve", lambda e: e.tensor_tensor(out=bis[:, 4:5], in0=bis[:, 5:6], in1=bis[:, 6:7], op=ALU.max), r=[bis], w=[bis])
                op("dve", lambda e: e.tensor_scalar(out=bis[:, 0:1], in0=bis[:, 4:5], scalar1=-1.0, scalar2=None, op0=ALU.mult), r=[bis], w=[bis])
                op("dve", lambda e: e.tensor_scalar(out=dtab[:], in0=pow2[:], scalar1=bis[:, 4:5], scalar2=2.002, op0=ALU.mult, op1=ALU.mult), r=[bis, pow2], w=[dtab])
            if samp:
                op("dve", lambda e: e.memset(isc[:, 16 * 128 + DEC:17 * 128], NEG), r=[], w=[isc])
            else:
                op("dve", lambda e: e.memset(isc[0:64, t * 128 + 64:(t + 1) * 128], NEG), r=[], w=[isc])
            if need_thr:
                for k in range(NBIS):
                    op("dve", lambda e, k=k: e.tensor_tensor(out=bis[:, 1:2], in0=bis[:, 0:1], in1=dtab[:, k:k + 1], op=ALU.add), r=[bis, dtab], w=[bis])
                    op("dve", lambda e: e.tensor_scalar(out=mbias[:, 0:S], in0=isc[:, 0:S], scalar1=bis[:, 1:2], scalar2=None, op0=ALU.is_ge, op1=ALU.add, accum_out=bis[:, 2:3]), r=[isc, bis], w=[mbias, bis])
                    op("dve", lambda e, k=k: e.scalar_tensor_tensor(out=bis[:, 3:4], in0=bis[:, 2:3], scalar=float(TOPK), in1=dtab[:, k:k + 1], op0=ALU.is_ge, op1=ALU.mult), r=[bis, dtab], w=[bis])
                    op("dve", lambda e: e.tensor_tensor(out=bis[:, 0:1], in0=bis[:, 0:1], in1=bis[:, 3:4], op=ALU.add), r=[bis], w=[bis])
            else:
                op("dve", lambda e: e.memset(bis[:, 0:1], -1e29), r=[], w=[bis])
            op("dve", lambda e: e.tensor_scalar(out=mbias[:, 0:S], in0=isc[:, 0:S], scalar1=bis[:, 0:1], scalar2=-30000.0, op0=ALU.is_lt, op1=ALU.mult), r=[isc, bis], w=[mbias])
            OAp = [P[4], P[5]]
            for kt in range(nk):
                ksl = slice(kt * 128, (kt + 1) * 128)
                for g in range(2):
                    pt = P[2 + g]
                    pb_ = PT[g]
                    op("pe", lambda e, g=g, pt=pt, ksl=ksl: e.matmul(pt[:, :], lhsT=kaT[:, g, ksl], rhs=qaT[:, 4 * g:4 * g + 4, :], start=True, stop=False), r=[kvres[kt], qaT], w=[pt])
                    op("pe", lambda e, pt=pt, ksl=ksl: e.matmul(pt[:, :], lhsT=mbias[:, ksl], rhs=ident4[:], start=False, stop=True), r=[mbias, ident4], w=[pt])
                    op("act", lambda e, pt=pt, pb_=pb_: e.activation(out=pb_[:], in_=pt[:, :], func=AF.Exp), r=[pt], w=[pb_])
                    for hh in range(4):
                        op("pe", lambda e, g=g, hh=hh, pb_=pb_, kt=kt: e.matmul(OAp[g][:, hh * 65:(hh + 1) * 65], lhsT=pb_[:, hh * 128:(hh + 1) * 128], rhs=vaA[:, kt, g, :], start=(kt == 0 and hh == 0), stop=(kt == nk - 1)),
                           r=[pb_, kvres[kt]], w=[OAp[g]])
            for g in range(2):
                o3 = OAp[g][:, 0:260].rearrange("p (h d) -> p h d", h=4)
                op("dve", lambda e, g=g, o3=o3: e.reciprocal(out=st8[:, 40 + 4 * g:44 + 4 * g], in_=o3[:, :, 64]), r=[OAp[g]], w=[st8])
                op("dve", lambda e, g=g, o3=o3: e.tensor_tensor(out=oa[:, g * 256:(g + 1) * 256].rearrange("p (h d) -> p h d", h=4), in0=o3[:, :, 0:64],
                                                                 in1=st8[:, 40 + 4 * g:44 + 4 * g].unsqueeze(2).to_broadcast([128, 4, 64]), op=ALU.mult), r=[OAp[g], st8], w=[oa])
            op("sp", lambda e: e.dma_start(out=oab_dram[rows, 0:512], in_=oa[:]), r=[oa], w=[oab_res[t]], dma=True)

            Z = [P[2], P[3]]
            Z2 = [P[4], P[5]]
            PVb = P[6]
            TSb = P[0]
            for kt in range(nk):
                ksl = slice(kt * 128, (kt + 1) * 128)
                diag = (kt == nk - 1)
                for h in range(8):
                    op("pe", lambda e, h=h: e.matmul(Z[h // 4][:, (h % 4) * 128:(h % 4 + 1) * 128], lhsT=kbT[:, h, ksl], rhs=qbT[:, h, :], start=True, stop=True), r=[kvres[kt], qbT], w=[Z[h // 4]])
                for hf in range(2):
                    op("act", lambda e, hf=hf: e.activation(out=ebuf[:, hf * 512:(hf + 1) * 512], in_=Z[hf][:, :], func=AF.Exp), r=[Z[hf]], w=[ebuf])
                    op("act", lambda e, hf=hf: e.activation(out=spT[:, hf * 512:(hf + 1) * 512], in_=ebuf[:, hf * 512:(hf + 1) * 512], func=AF.Ln, bias=1.0), r=[ebuf], w=[spT])
                if diag:
                    s3 = spT[:, :].rearrange("p (h q) -> p h q", h=8)
                    op("dve", lambda e, s3=s3: e.tensor_tensor(out=s3, in0=s3, in1=mlt[:, :].unsqueeze(1).to_broadcast([128, 8, 128]), op=ALU.mult), r=[spT, mlt], w=[spT])
                for hf in range(2):
                    op("pe", lambda e, hf=hf: e.matmul(Z2[hf][:, :], lhsT=negtri[:], rhs=spT[:, hf * 512:(hf + 1) * 512], start=True, stop=False), r=[negtri, spT], w=[Z2[hf]])
                    if diag:
                        op("pe", lambda e, hf=hf: e.matmul(Z2[hf][:, :], lhsT=ident[:], rhs=negm4[:], start=False, stop=False), r=[ident, negm4], w=[Z2[hf]])
                    for hh in range(4):
                        h = hf * 4 + hh
                        op("pe", lambda e, h=h, hh=hh, hf=hf: e.matmul(Z2[hf][:, hh * 128:(hh + 1) * 128], lhsT=kbT[:, h, ksl], rhs=qbT[:, h, :], start=False, stop=(hh == 3)), r=[kvres[kt], qbT], w=[Z2[hf]])
                    op("act", lambda e, hf=hf: e.activation(out=ET[:, hf * 512:(hf + 1) * 512], in_=Z2[hf][:, :], func=AF.Exp), r=[Z2[hf]], w=[ET])
                for h in range(8):
                    op("pe", lambda e, h=h: e.matmul(TSb[:, h:h + 1], lhsT=spT[:, h * 128:(h + 1) * 128], rhs=ones1[:], start=True, stop=True), r=[spT, ones1], w=[TSb])
                for h in range(8):
                    op("pe", lambda e, h=h, kt=kt: e.matmul(PVb[:, h * 64:(h + 1) * 64], lhsT=ET[:, h * 128:(h + 1) * 128], rhs=vbB[:, kt, h, :], start=True, stop=True), r=[ET, kvres[kt]], w=[PVb])
                if kt == 0:
                    op("dve", lambda e: e.tensor_copy(out=ob[:], in_=PVb[:, :]), r=[PVb], w=[ob])
                else:
                    op("act", lambda e: e.activation(out=dd[:], in_=TSb[:, 0:8], func=AF.Exp, scale=-1.0), r=[TSb], w=[dd])
                    o3 = ob[:, :].rearrange("p (h d) -> p h d", h=8)
                    op("dve", lambda e, o3=o3: e.tensor_tensor(out=o3, in0=o3, in1=dd[:, :].unsqueeze(2).to_broadcast([128, 8, 64]), op=ALU.mult), r=[ob, dd], w=[ob])
                    op("dve", lambda e: e.tensor_tensor(out=ob[:], in0=ob[:], in1=PVb[:, :], op=ALU.add), r=[ob, PVb], w=[ob])
            op("sp", lambda e: e.dma_start(out=oab_dram[rows, 512:1024], in_=ob[:]), r=[ob], w=[oab_res[t]], dma=True)

        kb.barrier()
        load_cast_rows(WARENA_B, lambda c, o0, wdt: w_in[l, c * 128:(c + 1) * 128, 2888 + o0:2888 + o0 + wdt], 8, 2048,
                       lambda c, o0, wdt: WA[:, GOFF + c * 2048 + o0: GOFF + c * 2048 + o0 + wdt], n1s, lambda c: c)
        load_cast_rows(WARENA_B, lambda c, o0, wdt: w_pa[l, c * 128:(c + 1) * 128, o0:o0 + wdt], 4, 1024,
                       lambda c, o0, wdt: WA[:, PAOFF + c * 1024 + o0: PAOFF + c * 1024 + o0 + wdt])
        load_cast_rows(WARENA_B, lambda c, o0, wdt: w_pb[l, c * 128:(c + 1) * 128, o0:o0 + wdt], 4, 1024,
                       lambda c, o0, wdt: WA[:, PBOFF + c * 1024 + o0: PBOFF + c * 1024 + o0 + wdt])
        load_cast_rows(WARENA_B, lambda c, o0, wdt: w_o[l, c * 128:(c + 1) * 128, o0:o0 + wdt], 8, 1024,
                       lambda c, o0, wdt: WA[:, WOOFF + c * 1024 + o0: WOOFF + c * 1024 + o0 + wdt])
        for t in tiles:
            samp = (t == NT - 1)
            rows = slice(t * 128, (t + 1) * 128)
            op("sp", lambda e: e.dma_start(out=xt[:], in_=xin_d[rows, :]), r=[xin_r[t]], w=[xt], dma=True)
            rmsnorm_rows(xt, hb, sq)
            make_hT()
            for gi, gdst in ((0, sga), (1, sgb)):
                for hf in range(2):
                    pt = P[hf]
                    c0 = GOFF + gi * 1024 + hf * 512
                    for c in range(8):
                        op("pe", lambda e, c=c, pt=pt, c0=c0: e.matmul(pt[:, :], lhsT=hT[:, c, :], rhs=WA[:, c * 2048 + c0: c * 2048 + c0 + 512], start=(c == 0), stop=(c == 7)), r=[hT, WARENA_B], w=[pt])
                    op("act", lambda e, pt=pt, gdst=gdst, hf=hf: e.activation(out=gdst[:, hf * 512:(hf + 1) * 512], in_=pt[:, :], func=AF.Sigmoid), r=[pt], w=[gdst])
            op("sp", lambda e: e.dma_start(out=oabf[:], in_=oab_dram[rows, :]), r=[oab_res[t]], w=[oabf], dma=True)
            op("pool", lambda e: e.tensor_copy(out=oabb[:], in_=oabf[:]), r=[oabf], w=[oabb])
            for bi, (woff, gdst) in enumerate(((PAOFF, sga), (PBOFF, sgb))):
                for c in range(4):
                    op("pe", lambda e, c=c, bi=bi: e.transpose(out=PTR[:, c * 128:(c + 1) * 128], in_=oabb[:, bi * 512 + c * 128: bi * 512 + (c + 1) * 128], identity=ident[:]), r=[oabb, ident], w=[PTR])
                op("act", lambda e: e.activation(out=oT[:, 0:4, :], in_=PTR[:, 0:512].rearrange("p (c n) -> p c n", c=4), func=AF.Copy), r=[PTR], w=[oT])
                for hf in range(2):
                    pt = P[2 + hf]
                    for c in range(4):
                        op("pe", lambda e, c=c, pt=pt, hf=hf, woff=woff: e.matmul(pt[:, :], lhsT=oT[:, c, :], rhs=WA[:, woff + c * 1024 + hf * 512: woff + c * 1024 + (hf + 1) * 512], start=(c == 0), stop=(c == 3)), r=[oT, WARENA_B], w=[pt])
                    if bi == 0:
                        op("dve", lambda e, pt=pt, hf=hf: e.tensor_tensor(out=mm[:, hf * 512:(hf + 1) * 512], in0=sga[:, hf * 512:(hf + 1) * 512], in1=pt[:, :], op=ALU.mult), r=[sga, pt], w=[mm])
                    else:
                        op("dve", lambda e, pt=pt, hf=hf: e.tensor_tensor(out=sgb[:, hf * 512:(hf + 1) * 512], in0=sgb[:, hf * 512:(hf + 1) * 512], in1=pt[:, :], op=ALU.mult), r=[sgb, pt], w=[sgb])
            op("dve", lambda e: e.tensor_tensor(out=mbf[:], in0=mm[:], in1=sgb[:], op=ALU.add), r=[mm, sgb], w=[mbf])
            for c in range(8):
                op("pe", lambda e, c=c: e.transpose(out=PTR[:, c * 128:(c + 1) * 128], in_=mbf[:, c * 128:(c + 1) * 128], identity=ident[:]), r=[mbf, ident], w=[PTR])
            op("act", lambda e: e.activation(out=oT[:], in_=PTR[:, :].rearrange("p (c n) -> p c n", c=8), func=AF.Copy), r=[PTR], w=[oT])
            for hf in range(2):
                pt = P[4 + hf]
                for c in range(8):
                    op("pe", lambda e, c=c, pt=pt, hf=hf: e.matmul(pt[:, :], lhsT=oT[:, c, :], rhs=WA[:, WOOFF + c * 1024 + hf * 512: WOOFF + c * 1024 + (hf + 1) * 512], start=(c == 0), stop=(c == 7)), r=[oT, WARENA_B], w=[pt])
                op("dve", lambda e, pt=pt, hf=hf: e.tensor_tensor(out=xt[:, hf * 512:(hf + 1) * 512], in0=xt[:, hf * 512:(hf + 1) * 512], in1=pt[:, :], op=ALU.add), r=[xt, pt], w=[xt])

            op("sp", lambda e: e.dma_start(out=xmid[rows, :], in_=xt[:]), r=[xt], w=[xmres[t]], dma=True)

        if do_peer:
            kb.barrier()
            load_cast_rows(WQ, lambda c, o0, wdt: pwq[l, c * 128:(c + 1) * 128, o0:o0 + wdt], 8, 1024,
                           lambda c, o0, wdt: WQ[:, c * 1024 + o0: c * 1024 + o0 + wdt], n2s, lambda c: c)
            for which, src in ((0, pk1T), (1, pk2T)):
                s_ = stg[which]
                op("sp", lambda e, s_=s_, src=src: e.dma_start(out=s_[0:64, 0:1024], in_=src[l, :, :, :].rearrange("d h k -> d (h k)")), r=[in_res], w=[s_], dma=True)
                op("dve", lambda e, s_=s_, which=which: e.tensor_copy(out=k12[:, which, :, :], in_=s_[0:64, 0:1024].rearrange("d (h k) -> d h k", h=8)), r=[s_], w=[k12])
            for t in tiles:
                rows = slice(t * 128, (t + 1) * 128)
                op("sp", lambda e: e.dma_start(out=xt[:], in_=xmid[rows, :]), r=[xmres[t]], w=[xt], dma=True)
                rmsnorm_rows(xt, hb, sq)
                make_hT()
                op("pool", lambda e: e.tensor_copy(out=h2T_all[:, :, t * 128:(t + 1) * 128], in_=hT[:]), r=[hT], w=[h2T_all])
                for hf in range(2):
                    pt = P[hf]
                    for c in range(8):
                        op("pe", lambda e, c=c, pt=pt, hf=hf: e.matmul(pt[:, :], lhsT=hT[:, c, :], rhs=WQ[:, c * 1024 + hf * 512: c * 1024 + (hf + 1) * 512], start=(c == 0), stop=(c == 7)), r=[hT, WQ], w=[pt])
                    op("act", lambda e, pt=pt, hf=hf: e.activation(out=qsb[:, hf * 512:(hf + 1) * 512], in_=pt[:, :], func=AF.Copy), r=[pt], w=[qsb])
                for hf in range(2):
                    for b in range(8):
                        bb = hf * 8 + b
                        op("pe", lambda e, b=b, bb=bb: e.transpose(out=PTR[0:64, b * 128:(b + 1) * 128], in_=qsb[:, bb * 64:(bb + 1) * 64], identity=ident[:]), r=[qsb, ident], w=[PTR])
                    op("act", lambda e, hf=hf: e.activation(out=qT[:, hf * 8:(hf + 1) * 8, :], in_=PTR[0:64, :].rearrange("p (b n) -> p b n", b=8), func=AF.Copy), r=[PTR], w=[qT])
                for which in range(2):
                    for h in range(8):
                        pt = P[2 + which * 2 + h // 4]
                        op("pe", lambda e, h=h, pt=pt, which=which: e.matmul(pt[:, (h % 4) * 128:(h % 4 + 1) * 128], lhsT=qT[:, h * 2 + which, :], rhs=k12[:, which, h, :], start=True, stop=True), r=[qT, k12], w=[pt])
                    for hf in range(2):
                        pt = P[2 + which * 2 + hf]
                        op("act", lambda e, pt=pt, which=which, hf=hf: e.activation(out=s12[:, which, hf * 4:(hf + 1) * 4, :], in_=pt[:, :].rearrange("p (h k) -> p h k", h=4), func=AF.Copy), r=[pt], w=[s12])
                for which in range(2):
                    for h in range(8):
                        sv = s12[:, which, h, :]
                        op("dve", lambda e, sv=sv, which=which, h=h: e.max(out=v12[:, which, h, 0:8], in_=sv), r=[s12], w=[v12])
                        op("dve", lambda e, sv=sv, which=which, h=h: e.max_index(out=i12[:, which, h, 0:8], in_max=v12[:, which, h, 0:8], in_values=sv), r=[s12, v12], w=[i12])
                        op("dve", lambda e, sv=sv, which=which, h=h: e.match_replace(out=swk[:, 0:128], in_to_replace=v12[:, which, h, 0:8], in_values=sv, imm_value=NEG), r=[s12, v12], w=[swk])
                        op("dve", lambda e, which=which, h=h: e.max(out=v12[:, which, h, 8:16], in_=swk[:, 0:128]), r=[swk], w=[v12])
                        op("dve", lambda e, which=which, h=h: e.max_index(out=i12[:, which, h, 8:16], in_max=v12[:, which, h, 8:16], in_values=swk[:, 0:128]), r=[swk, v12], w=[i12])
                op("dve", lambda e: e.tensor_copy(out=i12f[:], in_=i12[:]), r=[i12], w=[i12f])
                c4 = cand[:, :, :].rearrange("p h (r c) -> p h r c", r=16)
                op("dve", lambda e: e.tensor_tensor(out=c4, in0=v12[:, 0, :, :].unsqueeze(3).to_broadcast([128, 8, 16, 16]), in1=v12[:, 1, :, :].unsqueeze(2).to_broadcast([128, 8, 16, 16]), op=ALU.add), r=[v12], w=[cand])
                for h in range(8):
                    op("dve", lambda e, h=h: e.max(out=sc[:, h, 0:8], in_=cand[:, h, :]), r=[cand], w=[sc])
                    op("dve", lambda e, h=h: e.max_index(out=pos[:, h, 0:8], in_max=sc[:, h, 0:8], in_values=cand[:, h, :]), r=[cand, sc], w=[pos])
                    op("dve", lambda e, h=h: e.match_replace(out=swk[:, 0:256], in_to_replace=sc[:, h, 0:8], in_values=cand[:, h, :], imm_value=NEG), r=[cand, sc], w=[swk])
                    op("dve", lambda e, h=h: e.max(out=sc[:, h, 8:16], in_=swk[:, 0:256]), r=[swk], w=[sc])
                    op("dve", lambda e, h=h: e.max_index(out=pos[:, h, 8:16], in_max=sc[:, h, 8:16], in_values=swk[:, 0:256]), r=[swk, sc], w=[pos])
                op("dve", lambda e: e.tensor_copy(out=posf[:], in_=pos[:]), r=[pos], w=[posf])
                op("dve", lambda e: e.tensor_tensor(out=oh[:], in0=posf[:, :, :].unsqueeze(3).to_broadcast([128, 8, 16, 16]),
                                                    in1=thr16[:, :].unsqueeze(1).unsqueeze(1).to_broadcast([128, 8, 16, 16]), op=ALU.is_ge), r=[posf, thr16], w=[oh])
                op("dve", lambda e: e.tensor_reduce(out=posrf[:], in_=oh[:], axis=AX.X, op=ALU.add), r=[oh], w=[posrf])
                op("dve", lambda e: e.scalar_tensor_tensor(out=poscf[:], in0=posrf[:], scalar=-16.0, in1=posf[:], op0=ALU.mult, op1=ALU.add), r=[posrf, posf], w=[poscf])
                io4 = iota16[:, :].unsqueeze(1).unsqueeze(1).to_broadcast([128, 8, 16, 16])
                for pf_, which, dsel in ((posrf, 0, sel1), (poscf, 1, sel2)):
                    op("dve", lambda e, pf_=pf_: e.tensor_tensor(out=oh[:], in0=io4, in1=pf_[:, :, :].unsqueeze(3).to_broadcast([128, 8, 16, 16]), op=ALU.is_equal), r=[iota16, pf_], w=[oh])
                    op("dve", lambda e, which=which: e.tensor_tensor(out=oh[:], in0=oh[:], in1=i12f[:, which, :, :].unsqueeze(2).to_broadcast([128, 8, 16, 16]), op=ALU.mult), r=[oh, i12f], w=[oh])
                    op("dve", lambda e, dsel=dsel: e.tensor_reduce(out=dsel[:], in_=oh[:], axis=AX.X, op=ALU.add), r=[oh], w=[dsel])
                op("dve", lambda e: e.tensor_tensor(out=gsm[:], in0=sc[:], in1=sc[:, :, 0:1].to_broadcast([128, 8, 16]), op=ALU.subtract), r=[sc], w=[gsm])
                op("act", lambda e: e.activation(out=gsm[:], in_=gsm[:], func=AF.Exp), r=[gsm], w=[gsm])
                op("dve", lambda e: e.tensor_reduce(out=st8[:, 48:56], in_=gsm[:], axis=AX.X, op=ALU.add), r=[gsm], w=[st8])
                op("dve", lambda e: e.reciprocal(out=st8[:, 56:64], in_=st8[:, 48:56]), r=[st8], w=[st8])
                op("dve", lambda e: e.tensor_tensor(out=gsm[:], in0=gsm[:], in1=st8[:, 56:64].unsqueeze(2).to_broadcast([128, 8, 16]), op=ALU.mult), r=[gsm, st8], w=[gsm])
                for i_, src in enumerate((sel1, sel2, gsm)):
                    op("pe", lambda e, i_=i_, src=src: e.transpose(out=P[0][:, i_ * 128:(i_ + 1) * 128], in_=src[:, :, :].rearrange("p h k -> p (h k)"), identity=identf[:]), r=[src, identf], w=[P[0]])
                op("act", lambda e: e.activation(out=TT[:], in_=P[0][:, 0:384].rearrange("p (a n) -> p a n", a=3), func=AF.Copy), r=[P[0]], w=[TT])
                iob = iota128[:, :].unsqueeze(1).to_broadcast([128, 64, 128])
                for half in range(2):
                    n0 = half * 64
                    op("dve", lambda e, n0=n0: e.tensor_tensor(out=P2h[:], in0=iob, in1=TT[:, 1, n0:n0 + 64].unsqueeze(2).to_broadcast([128, 64, 128]), op=ALU.is_equal), r=[iota128, TT], w=[P2h])
                    op("dve", lambda e, n0=n0: e.tensor_tensor(out=P1h[:], in0=iob, in1=TT[:, 0, n0:n0 + 64].unsqueeze(2).to_broadcast([128, 64, 128]), op=ALU.is_equal), r=[iota128, TT], w=[P1h])
                    op("pool", lambda e, n0=n0: e.tensor_tensor(out=P1h[:], in0=P1h[:], in1=TT[:, 2, n0:n0 + 64].unsqueeze(2).to_broadcast([128, 64, 128]), op=ALU.mult), r=[P1h, TT], w=[P1h])
                    for q4 in range(16):
                        bank = P[1 + (q4 % 2)]
                        for k in range(4):
                            n = q4 * 4 + k
                            op("pe", lambda e, bank=bank, k=k, n=n: e.matmul(bank[:, k * 128:(k + 1) * 128], lhsT=P1h[:, n, :], rhs=P2h[:, n, :], start=True, stop=True), r=[P1h, P2h], w=[bank])
                        nn = n0 + q4 * 4
                        op("act", lambda e, bank=bank, nn=nn: e.activation(out=Gs[:, :, nn:nn + 4].rearrange("p i n -> p n i"), in_=bank[:, :].rearrange("p (n i) -> p n i", n=4), func=AF.Copy), r=[bank], w=[Gs])
                op("sp", lambda e: e.dma_start(out=Gd[t, :, :], in_=Gs[:, :, :].rearrange("p i n -> p (i n)")), r=[Gs], w=[gdres[t]], dma=True)

            kb.barrier()
            pvv = pv[l][:, :].rearrange("(i1 i2) d -> i1 i2 d", i2=128)
            tgroups = [tiles[i:i + 2] for i in range(0, len(tiles), 2)]
            sti = 0
            for sbk in range(128 // NBLK):
                ub_ = u16[sbk % 2]
                vb_ = v16[sbk % 2]
                for c in range(8):
                    s_ = stg[sti % 2]
                    sti += 1
                    op("sp", lambda e, s_=s_, c=c: e.dma_start(out=s_[:, 0:NBLK * 128], in_=puT[l][c * 128:(c + 1) * 128, sbk * NBLK * 128:(sbk + 1) * NBLK * 128]), r=[in_res], w=[s_], dma=True)
                    op("pool", lambda e, s_=s_, c=c: e.tensor_scalar(out=ub_[:, c, :], in0=s_[:, 0:NBLK * 128], scalar1=n2s[:, c:c + 1], scalar2=None, op0=ALU.mult), r=[s_, n2s], w=[ub_])
                for blk in range(NBLK):
                    s_ = stg[sti % 2]
                    sti += 1
                    op("sp", lambda e, s_=s_, blk=blk: e.dma_start(out=s_[:, 0:1024], in_=pvv[:, sbk * NBLK + blk, :]), r=[in_res], w=[s_], dma=True)
                    op("pool", lambda e, s_=s_, blk=blk: e.tensor_copy(out=vb_[:, blk, :], in_=s_[:, 0:1024]), r=[s_], w=[vb_])
                for gi_, tl in enumerate(tgroups):
                    gc = Gc[gi_ % 2]
                    nt_ = len(tl)
                    ntk = 128 * nt_
                    contiguous = (nt_ == 2 and tl[1] == tl[0] + 1) or nt_ == 1
                    for ti, t in enumerate(tl):
                        op("sp", lambda e, ti=ti, t=t: e.dma_start(out=gc[:, ti, :, :], in_=Gd[t, :, sbk * NBLK * 128:(sbk + 1) * NBLK * 128].rearrange("p (i n) -> p i n", i=NBLK)), r=[gdres[t]], w=[gc], dma=True)
                    for blk in range(NBLK):
                        A = P[blk % 2]
                        ge = gel[blk % 2]
                        cf = cfT[blk % 2]
                        if contiguous:
                            for c in range(8):
                                op("pe", lambda e, c=c, blk=blk, A=A: e.matmul(A[:, 0:ntk], lhsT=ub_[:, c, blk * 128:(blk + 1) * 128], rhs=h2T_all_c[:, c, tl[0] * 128: tl[0] * 128 + ntk], start=(c == 0), stop=(c == 7)), r=[ub_, h2T_all_c], w=[A])
                        else:
                            for ti, t in enumerate(tl):
                                for c in range(8):
                                    op("pe", lambda e, c=c, blk=blk, A=A, ti=ti, t=t: e.matmul(A[:, ti * 128:(ti + 1) * 128], lhsT=ub_[:, c, blk * 128:(blk + 1) * 128], rhs=h2T_all_c[:, c, t * 128:(t + 1) * 128], start=(c == 0 and ti == 0), stop=(c == 7)), r=[ub_, h2T_all_c], w=[A])
                        op("act", lambda e, A=A, ge=ge: e.activation(out=ge[:, 0:ntk], in_=A[:, 0:ntk], func=AF.Gelu), r=[A], w=[ge])
                        op("dve", lambda e, ge=ge, cf=cf, blk=blk: e.tensor_tensor(out=cf[:, 0:ntk].rearrange("p (t n) -> p t n", t=nt_), in0=ge[:, 0:ntk].rearrange("p (t n) -> p t n", t=nt_), in1=gc[:, 0:nt_, blk, :], op=ALU.mult), r=[ge, gc], w=[cf])
                        for ti, t in enumerate(tl):
                            for hf in range(2):
                                ab = P[2 + ti * 2 + hf]
                                op("pe", lambda e, ab=ab, cf=cf, ti=ti, hf=hf, blk=blk: e.matmul(ab[:, :], lhsT=cf[:, ti * 128:(ti + 1) * 128], rhs=vb_[:, blk, hf * 512:(hf + 1) * 512], start=(blk == 0), stop=(blk == NBLK - 1)), r=[cf, vb_], w=[ab])
                    for ti, t in enumerate(tl):
                        for hf in range(2):
                            ab = P[2 + ti * 2 + hf]
                            if sbk == 0:
                                op("dve", lambda e, ab=ab, t=t, hf=hf: e.tensor_copy(out=accs[:, t, hf * 512:(hf + 1) * 512], in_=ab[:, :]), r=[ab], w=[accres[t]])
                            else:
                                op("dve", lambda e, ab=ab, t=t, hf=hf: e.tensor_tensor(out=accs[:, t, hf * 512:(hf + 1) * 512], in0=accs[:, t, hf * 512:(hf + 1) * 512], in1=ab[:, :], op=ALU.add), r=[ab, accres[t]], w=[accres[t]])

        for t in tiles:
            samp = (t == NT - 1)
            rows = slice(t * 128, (t + 1) * 128)
            op("sp", lambda e: e.dma_start(out=xt[:], in_=xmid[rows, :]), r=[xmres[t]], w=[xt], dma=True)
            if do_peer:
                op("dve", lambda e: e.tensor_tensor(out=xt[:], in0=xt[:], in1=accs[:, t, :], op=ALU.add), r=[xt, accres[t]], w=[xt])
            if last:
                if samp:
                    op("sp", lambda e: e.dma_start(out=y_s[:, :], in_=xt[0:DEC, :]), r=[xt], w=[out_res], dma=True)
                else:
                    op("sp", lambda e: e.dma_start(out=y_p[rows, :], in_=xt[:]), r=[xt], w=[out_res], dma=True)
            else:
                op("sp", lambda e: e.dma_start(out=xout_d[rows, :], in_=xt[:]), r=[xt], w=[xout_r[t]], dma=True)

    kb.finish()
    es.close()
    return nc, kb.ninst


def _consts():
    j = np.arange(128)
    ident = np.eye(128, dtype=np.float32)
    negtri = -(j[:, None] >= j[None, :]).astype(np.float32)
    mlt = (j[:, None] < j[None, :]).astype(np.float32)
    ident4 = np.tile(ident, (1, 4))
    negm = np.where(j[:, None] >= j[None, :], -30000.0, 0.0).astype(np.float32)
    negm4 = np.tile(negm, (1, 4))
    pow2 = np.tile((0.5 ** np.arange(1, NBIS + 1)).astype(np.float32)[None, :], (128, 1))
    iota = np.tile(np.arange(16, dtype=np.float32)[None, :], (128, 1))
    iota128 = np.tile(np.arange(128, dtype=np.float32)[None, :], (128, 1))
    return np.concatenate([ident, negtri, mlt, ident4, negm4, pow2, iota, iota128], axis=1).astype(np.float32)


def _rope_table():
    pos = np.concatenate([np.arange(SEQ), SEQ + np.arange(128)]).astype(np.float32)
    half = 32
    inv = (10000.0 ** (-np.arange(half, dtype=np.float32) / half)).astype(np.float32)
    ang = pos[:, None] * inv[None, :]
    return np.concatenate([np.cos(ang), np.sin(ang)], axis=1).astype(np.float32)


_PROG = {}


def _prep(x_prompt, x_sample, cache_a_k, cache_a_v, cache_idx_k, cache_b_k, cache_b_v,
          norm1, w_in, q_norm_a, k_norm_a, idx_k_norm, w_pa, w_pb, w_o, norm2,
          peer_wq, peer_k1, peer_k2, peer_u, peer_v, cores=range(8)):
    f = lambda a: np.ascontiguousarray(np.asarray(a, dtype=np.float32))
    x_prompt = f(x_prompt); x_sample = f(x_sample)
    shared = {
        "rope": _rope_table(),
        "cst": _consts(),
        "n1T": f(np.asarray(norm1).reshape(DEPTH, 8, 128).transpose(0, 2, 1)),
        "n2T": f(np.asarray(norm2).reshape(DEPTH, 8, 128).transpose(0, 2, 1)),
        "n2r": f(norm2),
        "w_in": f(w_in), "qn": f(q_norm_a), "kn": f(k_norm_a), "ikn": f(idx_k_norm),
        "w_pa": f(w_pa), "w_pb": f(w_pb), "w_o": f(w_o), "pwq": f(peer_wq),
        "pk1T": f(np.asarray(peer_k1).transpose(0, 3, 1, 2)),
        "pk2T": f(np.asarray(peer_k2).transpose(0, 3, 1, 2)),
    }
    pu = np.asarray(peer_u); pvv = np.asarray(peer_v)
    for l in range(DEPTH):
        shared["puT%d" % l] = f(pu[l].reshape(128, 128, D).transpose(2, 1, 0).reshape(D, 16384))
        shared["pv%d" % l] = f(pvv[l])
    cak = np.asarray(cache_a_k); cav = np.asarray(cache_a_v); cik = np.asarray(cache_idx_k)
    cbk = np.asarray(cache_b_k); cbv = np.asarray(cache_b_v)
    in_maps = []
    for b in cores:
        xa = np.zeros((NT * 128, D), np.float32)
        xa[:SEQ] = x_prompt[b]
        xa[SEQ:SEQ + DEC] = x_sample[b]
        m = dict(shared)
        m["x_in"] = xa
        m["cak"] = f(cak[:, b].reshape(DEPTH, SEQ, 128))
        m["cav"] = f(cav[:, b].reshape(DEPTH, SEQ, 128))
        m["cik"] = f(cik[:, b].reshape(DEPTH, SEQ, 64))
        m["cbk"] = f(cbk[:, b].reshape(DEPTH, SEQ, 512))
        m["cbv"] = f(cbv[:, b].reshape(DEPTH, SEQ, 512))
        in_maps.append(m)
    return in_maps


def kernel(**inputs):
    if "full" not in _PROG:
        _PROG["full"] = build_program()[0]
    nc = _PROG["full"]
    in_maps = _prep(**inputs)
    res = run_bass_kernel_spmd(nc, in_maps, core_ids=list(range(8)))
    R = res.results
    st = lambda k, shp: np.stack([np.asarray(R[b][k], dtype=np.float32) for b in range(8)], axis=1).reshape(shp)
    y_p = np.stack([np.asarray(R[b]["y_p"], dtype=np.float32) for b in range(8)], axis=0)
    y_s = np.stack([np.asarray(R[b]["y_s"], dtype=np.float32) for b in range(8)], axis=0)
    return (y_p, y_s,
            st("ak_p", (DEPTH, 8, SEQ, 2, 64)), st("av_p", (DEPTH, 8, SEQ, 2, 64)), st("ik_p", (DEPTH, 8, SEQ, 64)),
            st("bk_p", (DEPTH, 8, SEQ, 8, 64)), st("bv_p", (DEPTH, 8, SEQ, 8, 64)),
            st("ak_s", (DEPTH, 8, DEC, 2, 64)), st("av_s", (DEPTH, 8, DEC, 2, 64)), st("ik_s", (DEPTH, 8, DEC, 64)),
            st("bk_s", (DEPTH, 8, DEC, 8, 64)), st("bv_s", (DEPTH, 8, DEC, 8, 64)))
```

```python
import contextlib
import numpy as np
import concourse.bass as bass
import concourse.mybir as mybir
from concourse.bass_utils import run_bass_kernel_spmd

F32 = mybir.dt.float32
BF16 = mybir.dt.bfloat16
I32 = mybir.dt.int32
U32 = mybir.dt.uint32
ALU = mybir.AluOpType
AF = mybir.ActivationFunctionType
AX = mybir.AxisListType

D = 1024
DEPTH = 4
NT = 17
NKS = 17
SEQ = 2048
DEC = 16
NIN = 4936
EPS = 1e-6
NEG = -1e30
TOPK = 256
NBIS = 22


class Res:
    __slots__ = ("t", "w", "r")

    def __init__(self, t=None):
        self.t = t
        self.w = None
        self.r = {}

    def __getitem__(self, k):
        return self.t[k]


class KB:
    def __init__(self, nc, es):
        self.nc = nc
        self.es = es
        self.E = {"pe": nc.tensor, "act": nc.scalar, "dve": nc.vector, "pool": nc.gpsimd, "sp": nc.sync}
        self.sems = {}
        self.cnt = {}
        self.seen = {e: {} for e in self.E}
        self.ninst = 0
        self.dpool = {"sp": ["dsp%d" % i for i in range(24)], "pool": ["dpl%d" % i for i in range(6)]}
        self.dnext = {"sp": 0, "pool": 0}

    def sem(self, key):
        if key not in self.sems:
            self.sems[key] = self.es.enter_context(self.nc.semaphore("s_" + key))
            self.cnt[key] = 0
        return self.sems[key]

    def op(self, eng, fn, r=(), w=(), dma=False):
        deps = {}
        for x in r:
            if x.w is not None:
                k, v = x.w
                if deps.get(k, 0) < v:
                    deps[k] = v
        for x in w:
            if x.w is not None:
                k, v = x.w
                if deps.get(k, 0) < v:
                    deps[k] = v
            for k, v in x.r.items():
                if deps.get(k, 0) < v:
                    deps[k] = v
        E = self.E[eng]
        seen = self.seen[eng]
        for k, v in deps.items():
            if k == "pe" and eng == "pe" and not dma:
                continue
            if seen.get(k, 0) < v:
                E.wait_ge(self.sem(k), v)
                seen[k] = v
        if dma:
            pl = self.dpool[eng]
            key = pl[self.dnext[eng]]
            self.dnext[eng] = (self.dnext[eng] + 1) % len(pl)
            s = self.sem(key)
            prev = self.cnt[key]
            if prev > 0 and seen.get(key, 0) < prev:
                E.wait_ge(s, prev)
                seen[key] = prev
        else:
            key = eng
            s = self.sem(key)
        ins = fn(E)
        inc = 16 if dma else 1
        self.cnt[key] += inc
        c = self.cnt[key]
        ins.then_inc(s, inc)
        for x in r:
            x.r[key] = c
        for x in w:
            x.w = (key, c)
            x.r = {}
        self.ninst += 1
        return ins

    def barrier(self):
        for en, E in self.E.items():
            seen = self.seen[en]
            for k, sm in self.sems.items():
                v = self.cnt[k]
                if v > 0 and seen.get(k, 0) < v:
                    E.wait_ge(sm, v)
                    seen[k] = v

    def finish(self):
        E = self.E["sp"]
        for k, s in self.sems.items():
            if self.cnt[k] > 0:
                E.wait_ge(s, self.cnt[k])


def build_program(depth=DEPTH, tiles=None, do_peer=True):
    if tiles is None:
        tiles = list(range(NT))
    nc = bass.Bass("TRN2", target_bir_lowering=False)
    es = contextlib.ExitStack()
    kb = KB(nc, es)

    def dram(name, shape, dt, kind):
        return nc.dram_tensor(name, shape, dt, kind=kind)

    x_in = dram("x_in", [NT * 128, D], F32, "ExternalInput")
    rope_in = dram("rope", [NT * 128, 64], F32, "ExternalInput")
    cak = dram("cak", [DEPTH, SEQ, 128], F32, "ExternalInput")
    cav = dram("cav", [DEPTH, SEQ, 128], F32, "ExternalInput")
    cik = dram("cik", [DEPTH, SEQ, 64], F32, "ExternalInput")
    cbk = dram("cbk", [DEPTH, SEQ, 512], F32, "ExternalInput")
    cbv = dram("cbv", [DEPTH, SEQ, 512], F32, "ExternalInput")
    n1T = dram("n1T", [DEPTH, 128, 8], F32, "ExternalInput")
    n2T = dram("n2T", [DEPTH, 128, 8], F32, "ExternalInput")
    n2r = dram("n2r", [DEPTH, D], F32, "ExternalInput")
    w_in = dram("w_in", [DEPTH, D, NIN], F32, "ExternalInput")
    qn = dram("qn", [DEPTH, 64], F32, "ExternalInput")
    kn = dram("kn", [DEPTH, 64], F32, "ExternalInput")
    ikn = dram("ikn", [DEPTH, 64], F32, "ExternalInput")
    w_pa = dram("w_pa", [DEPTH, 512, D], F32, "ExternalInput")
    w_pb = dram("w_pb", [DEPTH, 512, D], F32, "ExternalInput")
    w_o = dram("w_o", [DEPTH, D, D], F32, "ExternalInput")
    pwq = dram("pwq", [DEPTH, D, D], F32, "ExternalInput")
    pk1T = dram("pk1T", [DEPTH, 64, 8, 128], F32, "ExternalInput")
    pk2T = dram("pk2T", [DEPTH, 64, 8, 128], F32, "ExternalInput")
    puT = [dram("puT%d" % l, [D, 16384], F32, "ExternalInput") for l in range(DEPTH)]
    pv = [dram("pv%d" % l, [16384, D], F32, "ExternalInput") for l in range(DEPTH)]

    y_p = dram("y_p", [SEQ, D], F32, "ExternalOutput")
    y_s = dram("y_s", [DEC, D], F32, "ExternalOutput")
    ak_p = dram("ak_p", [DEPTH, SEQ, 128], F32, "ExternalOutput")
    av_p = dram("av_p", [DEPTH, SEQ, 128], F32, "ExternalOutput")
    ik_p = dram("ik_p", [DEPTH, SEQ, 64], F32, "ExternalOutput")
    bk_p = dram("bk_p", [DEPTH, SEQ, 512], F32, "ExternalOutput")
    bv_p = dram("bv_p", [DEPTH, SEQ, 512], F32, "ExternalOutput")
    ak_s = dram("ak_s", [DEPTH, DEC, 128], F32, "ExternalOutput")
    av_s = dram("av_s", [DEPTH, DEC, 128], F32, "ExternalOutput")
    ik_s = dram("ik_s", [DEPTH, DEC, 64], F32, "ExternalOutput")
    bk_s = dram("bk_s", [DEPTH, DEC, 512], F32, "ExternalOutput")
    bv_s = dram("bv_s", [DEPTH, DEC, 512], F32, "ExternalOutput")
    xs_dram = [dram("xscr%d" % i, [NT * 128, D], F32, "Internal") for i in range(2)]
    xs_res = [[Res() for _ in range(NT)] for _ in range(2)]
    out_res = Res()
    in_res = Res()

    ARENA_F32 = 52000
    arena = es.enter_context(nc.sbuf_tensor("arena", [128, ARENA_F32], F32))
    aoff = [0]
    DTB = {F32: 4, BF16: 2, I32: 4, U32: 4}

    def sb(name, shape, dt):
        nb = DTB[dt]
        n = 1
        for d_ in shape[1:]:
            n *= d_
        nbytes = (n * nb + 31) // 32 * 32
        o = aoff[0]
        assert o % 4 == 0
        aoff[0] = o + nbytes
        assert aoff[0] <= ARENA_F32 * 4, (name, aoff[0])
        v = arena[0:shape[0], o // 4:(o + nbytes) // 4]
        if dt != F32:
            v = v.bitcast(dt)
        v = v[:, 0:n]
        if len(shape) == 3:
            v = v.rearrange("p (a b) -> p a b", a=shape[1])
        elif len(shape) == 4:
            v = v.rearrange("p (a b c) -> p a b c", a=shape[1], b=shape[2])
        return Res(v)

    def pst(name, shape, dt):
        return Res(es.enter_context(nc.psum_tensor(name, shape, dt)))

    ident = sb("ident", [128, 128], BF16)
    ident4 = sb("ident4", [128, 512], BF16)
    negtri = sb("negtri", [128, 128], BF16)
    mlt = sb("mlt", [128, 128], BF16)
    negm4 = sb("negm4", [128, 512], BF16)
    ones1 = sb("ones1", [128, 1], BF16)
    pow2 = sb("pow2", [128, NBIS], F32)
    iota16 = sb("iota16", [128, 16], F32)
    thr16 = sb("thr16", [128, 16], F32)
    identf = sb("identf", [128, 128], F32)
    iota128 = sb("iota128", [128, 128], F32)
    NCST = 128 * 3 + 512 * 2 + NBIS + 16 + 128
    cst_in = dram("cst", [128, NCST], F32, "ExternalInput")
    gq = sb("gq", [128, 64], F32)
    gk = sb("gk", [128, 64], F32)
    gik = sb("gik", [128, 64], F32)
    n1s = sb("n1s", [128, 8], F32)
    n2s = sb("n2s", [128, 8], F32)
    n2b = sb("n2b", [128, D], F32)
    xt = sb("xt", [128, D], F32)
    hb = sb("hb", [128, D], BF16)
    hT = sb("hT", [128, 8, 128], BF16)
    sq = sb("sq", [128, D], F32)
    st8 = sb("st8", [128, 64], F32)
    STW = 1536
    stg = [sb("stg%d" % i, [128, STW], F32) for i in range(2)]
    mark0 = aoff[0]

    WARENA_A = sb("warenaA", [128, 8 * 2888], BF16)
    kaT = sb("kaT", [64, 2, NKS * 128], BF16)
    kiT = sb("kiT", [64, NKS * 128], BF16)
    kbT = sb("kbT", [64, 8, NKS * 128], BF16)
    vaA = sb("vaA", [128, NKS, 2, 65], BF16)
    vbB = sb("vbB", [128, NKS, 8, 64], BF16)
    kvres = [Res() for _ in range(NKS)]
    cstage = sb("cstage", [128, NCST], F32)
    ropet = sb("ropet", [128, 64], F32)
    pf = sb("pf", [128, 512], F32)
    pf2 = sb("pf2", [128, 512], F32)
    pbf = sb("pbf", [128, 512], BF16)
    r1 = sb("r1", [128, 256], F32)
    r2 = sb("r2", [128, 256], F32)
    qaT = sb("qaT", [64, 8, 128], BF16)
    qiT = sb("qiT", [64, 8, 128], BF16)
    qbT = sb("qbT", [64, 8, 128], BF16)
    wi = sb("wi", [128, 8], F32)
    isc = sb("isc", [128, 2560], F32)
    mbias = sb("mbias", [128, 2560], BF16)
    rl = [sb("rl%d" % i, [128, 512], F32) for i in range(2)]
    PT = [sb("PT%d" % i, [128, 512], BF16) for i in range(2)]
    oa = sb("oa", [128, 512], F32)
    ob = sb("ob", [128, 512], F32)
    ebuf = sb("ebuf", [128, 1024], F32)
    spT = sb("spT", [128, 1024], BF16)
    ET = sb("ET", [128, 1024], BF16)
    dd = sb("dd", [128, 8], F32)
    bis = sb("bis", [128, 8], F32)
    dtab = sb("dtab", [128, NBIS], F32)
    endA = aoff[0]

    aoff[0] = mark0
    WARENA_B = sb("warenaB", [128, 32768], BF16)
    oabb = sb("oabb", [128, D], BF16)
    sga = sb("sga", [128, D], F32)
    sgb = sb("sgb", [128, D], F32)
    oT = sb("oT", [128, 8, 128], BF16)
    mm = sb("mm", [128, D], F32)
    mbf = sb("mbf", [128, D], BF16)
    oabf = sb("oabf", [128, D], F32)
    endB1 = aoff[0]

    aoff[0] = mark0
    h2T_all = sb("h2T_all", [128, 8, NT * 128], BF16)
    WQ = sb("WQ", [128, 8 * 1024], BF16)
    qsb = sb("qsb", [128, D], BF16)
    qT = sb("qT", [64, 16, 128], BF16)
    k12 = sb("k12", [64, 2, 8, 128], BF16)
    s12 = sb("s12", [128, 2, 8, 128], F32)
    swk = sb("swk", [128, 256], F32)
    v12 = sb("v12", [128, 2, 8, 16], F32)
    i12 = sb("i12", [128, 2, 8, 16], U32)
    i12f = sb("i12f", [128, 2, 8, 16], F32)
    cand = sb("cand", [128, 8, 256], F32)
    sc = sb("sc", [128, 8, 16], F32)
    pos = sb("pos", [128, 8, 16], U32)
    posf = sb("posf", [128, 8, 16], F32)
    posrf = sb("posrf", [128, 8, 16], F32)
    poscf = sb("poscf", [128, 8, 16], F32)
    oh = sb("oh", [128, 8, 16, 16], F32)
    sel1 = sb("sel1", [128, 8, 16], F32)
    sel2 = sb("sel2", [128, 8, 16], F32)
    gsm = sb("gsm", [128, 8, 16], F32)
    TT = sb("TT", [128, 3, 128], F32)
    P1q = [sb("P1q%d" % i, [128, 32, 128], BF16) for i in range(2)]
    P2q = [sb("P2q%d" % i, [128, 32, 128], BF16) for i in range(2)]
    Gs = sb("Gs", [128, 128, 128], BF16)
    endB2 = aoff[0]

    aoff[0] = mark0
    h2T_all_c = sb("h2T_all_c", [128, 8, NT * 128], BF16)
    accs = sb("accs", [128, NT, D], F32)
    accres = [Res() for _ in range(NT)]
    NBLK = 4
    u16 = [sb("u16_%d" % i, [128, 8, NBLK * 128], BF16) for i in range(2)]
    v16 = [sb("v16_%d" % i, [128, NBLK, D], BF16) for i in range(2)]
    Gc = [sb("Gc%d" % i, [128, 2, NBLK, 128], BF16) for i in range(2)]
    gel = [sb("gel%d" % i, [128, 256], BF16) for i in range(2)]
    cfT = [sb("cfT%d" % i, [128, 256], BF16) for i in range(2)]
    cstg = [sb("cstg%d" % i, [128, 512], F32) for i in range(12)]
    cstg += [Res(stg[i][:, k * 512:(k + 1) * 512]) for i in range(2) for k in range(2)]
    NCS = len(cstg)
    assert NCS == 16
    endC = aoff[0]
    print("arena bytes: persistent", mark0, "A", endA, "B1", endB1, "B2", endB2, "C", endC, "cap", ARENA_F32 * 4)

    Gd = dram("Gd", [NT, 128, 16384], BF16, "Internal")
    gdres = [Res() for _ in range(NT)]
    xmid = dram("xmid", [NT * 128, D], F32, "Internal")
    xmres = [Res() for _ in range(NT)]
    oab_dram = dram("oabscr", [NT * 128, D], F32, "Internal")
    oab_res = [Res() for _ in range(NT)]

    P = [pst("ps%d" % i, [128, 512], F32) for i in range(7)]
    PTR = pst("ptr", [128, 1024], BF16)

    op = kb.op

    op("sp", lambda e: e.dma_start(out=cstage[:], in_=cst_in[:, :]), r=[in_res], w=[cstage], dma=True)
    o = 0
    for dst, wdt in ((ident, 128), (negtri, 128), (mlt, 128), (ident4, 512), (negm4, 512)):
        op("dve", lambda e, dst=dst, o=o, wdt=wdt: e.tensor_copy(out=dst[:], in_=cstage[:, o:o + wdt]), r=[cstage], w=[dst])
        o += wdt
    op("dve", lambda e, o=o: e.tensor_copy(out=pow2[:], in_=cstage[:, o:o + NBIS]), r=[cstage], w=[pow2])
    o += NBIS
    op("dve", lambda e, o=o: e.tensor_copy(out=iota16[:], in_=cstage[:, o:o + 16]), r=[cstage], w=[iota16])
    o += 16
    op("dve", lambda e, o=o: e.tensor_copy(out=iota128[:], in_=cstage[:, o:o + 128]), r=[cstage], w=[iota128])
    op("dve", lambda e: e.tensor_copy(out=identf[:], in_=cstage[:, 0:128]), r=[cstage], w=[identf])
    op("dve", lambda e: e.memset(ones1[:], 1.0), w=[ones1])
    op("dve", lambda e: e.tensor_scalar(out=thr16[:], in0=iota16[:], scalar1=16.0, scalar2=16.0, op0=ALU.mult, op1=ALU.add), r=[iota16], w=[thr16])
    op("dve", lambda e: e.memset(thr16[:, 15:16], 1e9), w=[thr16])
    op("dve", lambda e: e.memset(vaA[:], 1.0), w=[vaA] + kvres)
    kb.barrier()

    def transpose_blocks(src, nblk, dstT, dst_res, ptile=PTR):
        for b in range(nblk):
            op("pe", lambda e, b=b: e.transpose(out=ptile[0:64, b * 128:(b + 1) * 128], in_=src[:, b * 64:(b + 1) * 64], identity=ident[:]),
               r=[src, ident], w=[ptile])
        op("act", lambda e: e.activation(out=dstT, in_=ptile[0:64, 0:nblk * 128].rearrange("p (b n) -> p b n", b=nblk), func=AF.Copy),
           r=[ptile], w=dst_res)

    def rmsnorm_rows(xres, outbf, scratch):
        op("act", lambda e: e.activation(out=scratch[:], in_=xres[:], func=AF.Square, accum_out=st8[:, 0:1]), r=[xres], w=[scratch, st8])
        op("dve", lambda e: e.tensor_scalar(out=st8[:, 1:2], in0=st8[:, 0:1], scalar1=1.0 / D, scalar2=EPS, op0=ALU.mult, op1=ALU.add), r=[st8], w=[st8])
        op("act", lambda e: e.activation(out=st8[:, 2:3], in_=st8[:, 1:2], func=AF.Sqrt), r=[st8], w=[st8])
        op("dve", lambda e: e.reciprocal(out=st8[:, 3:4], in_=st8[:, 2:3]), r=[st8], w=[st8])
        op("dve", lambda e: e.tensor_scalar(out=outbf[:], in0=xres[:], scalar1=st8[:, 3:4], scalar2=None, op0=ALU.mult), r=[xres, st8], w=[outbf])

    def make_hT():
        for c in range(8):
            op("pe", lambda e, c=c: e.transpose(out=PTR[:, c * 128:(c + 1) * 128], in_=hb[:, c * 128:(c + 1) * 128], identity=ident[:]),
               r=[hb, ident], w=[PTR])
        op("act", lambda e: e.activation(out=hT[:], in_=PTR[:, :].rearrange("p (c n) -> p c n", c=8), func=AF.Copy), r=[PTR], w=[hT])

    def headnorm(src_res, src_ap, H, gain, dst):
        W = H * 64
        op("act", lambda e: e.activation(out=sq[:, 0:W], in_=src_ap, func=AF.Square), r=[src_res], w=[sq])
        op("dve", lambda e: e.tensor_reduce(out=st8[:, 8:8 + H], in_=sq[:, 0:W].rearrange("p (h d) -> p h d", h=H), axis=AX.X, op=ALU.add), r=[sq], w=[st8])
        op("dve", lambda e: e.tensor_scalar(out=st8[:, 16:16 + H], in0=st8[:, 8:8 + H], scalar1=1.0 / 64, scalar2=EPS, op0=ALU.mult, op1=ALU.add), r=[st8], w=[st8])
        op("act", lambda e: e.activation(out=st8[:, 24:24 + H], in_=st8[:, 16:16 + H], func=AF.Sqrt), r=[st8], w=[st8])
        op("dve", lambda e: e.reciprocal(out=st8[:, 32:32 + H], in_=st8[:, 24:24 + H]), r=[st8], w=[st8])
        d3 = dst[:, 0:W].rearrange("p (h d) -> p h d", h=H)
        op("dve", lambda e: e.tensor_tensor(out=d3, in0=src_ap.rearrange("p (h d) -> p h d", h=H),
                                            in1=st8[:, 32:32 + H].unsqueeze(2).to_broadcast([128, H, 64]), op=ALU.mult), r=[st8, src_res], w=[dst])
        op("dve", lambda e: e.tensor_tensor(out=d3, in0=d3, in1=gain[:, :].unsqueeze(1).to_broadcast([128, H, 64]), op=ALU.mult), r=[gain, dst], w=[dst])

    def rope(src, H, dst, scale=None):
        W = H * 64
        s3 = src[:, 0:W].rearrange("p (h d) -> p h d", h=H)
        d3 = dst[:, 0:W].rearrange("p (h d) -> p h d", h=H)
        cosb = ropet[:, 0:32].unsqueeze(1).to_broadcast([128, H, 32])
        sinb = ropet[:, 32:64].unsqueeze(1).to_broadcast([128, H, 32])
        a3 = r1[:, 0:H * 32].rearrange("p (h d) -> p h d", h=H)
        b3 = r2[:, 0:H * 32].rearrange("p (h d) -> p h d", h=H)
        op("dve", lambda e: e.tensor_tensor(out=a3, in0=s3[:, :, 0:32], in1=cosb, op=ALU.mult), r=[src, ropet], w=[r1])
        op("dve", lambda e: e.tensor_tensor(out=b3, in0=s3[:, :, 32:64], in1=sinb, op=ALU.mult), r=[src, ropet], w=[r2])
        op("dve", lambda e: e.tensor_tensor(out=d3[:, :, 0:32], in0=a3, in1=b3, op=ALU.subtract), r=[r1, r2], w=[dst])
        op("dve", lambda e: e.tensor_tensor(out=a3, in0=s3[:, :, 32:64], in1=cosb, op=ALU.mult), r=[src, ropet], w=[r1])
        op("dve", lambda e: e.tensor_tensor(out=b3, in0=s3[:, :, 0:32], in1=sinb, op=ALU.mult), r=[src, ropet], w=[r2])
        op("dve", lambda e: e.tensor_tensor(out=d3[:, :, 32:64], in0=a3, in1=b3, op=ALU.add), r=[r1, r2], w=[dst])

    def load_cast_rows(wres, dram_ap_fn, nchunks, width, dst_fn, scale_res=None, scale_col=None):
        i = 0
        for c in range(nchunks):
            for o0 in range(0, width, STW):
                wdt = min(STW, width - o0)
                s = stg[i % 2]
                i += 1
                op("sp", lambda e, c=c, o0=o0, wdt=wdt, s=s: e.dma_start(out=s[:, 0:wdt], in_=dram_ap_fn(c, o0, wdt)), r=[in_res], w=[s], dma=True)
                if scale_res is not None:
                    op("dve", lambda e, c=c, o0=o0, wdt=wdt, s=s: e.tensor_scalar(out=dst_fn(c, o0, wdt), in0=s[:, 0:wdt], scalar1=scale_res[:, scale_col(c):scale_col(c) + 1], scalar2=None, op0=ALU.mult),
                       r=[s, scale_res], w=[wres])
                else:
                    op("pool", lambda e, c=c, o0=o0, wdt=wdt, s=s: e.tensor_copy(out=dst_fn(c, o0, wdt), in_=s[:, 0:wdt]), r=[s], w=[wres])

    WAA = WARENA_A.t
    WA = WARENA_B.t
    NQKV = 2888
    wa_qkv = lambda c, o0, wdt: WAA[:, c * NQKV + o0: c * NQKV + o0 + wdt]
    GOFF = 0
    PAOFF = 8 * 2048
    PBOFF = PAOFF + 4 * 1024
    WOOFF = PBOFF + 4 * 1024
    PQOFF = WOOFF + 8 * 1024

    for l in range(depth):
        xin_d = x_in if l == 0 else xs_dram[(l - 1) % 2]
        xin_r = [in_res] * NT if l == 0 else xs_res[(l - 1) % 2]
        xout_d = xs_dram[l % 2]
        xout_r = xs_res[l % 2]
        last = (l == depth - 1)

        op("sp", lambda e: e.dma_start(out=n1s[:], in_=n1T[l, :, :]), r=[in_res], w=[n1s], dma=True)
        op("sp", lambda e: e.dma_start(out=n2s[:], in_=n2T[l, :, :]), r=[in_res], w=[n2s], dma=True)
        op("sp", lambda e: e.dma_start(out=gq[:], in_=qn[l, :].partition_broadcast(128)), r=[in_res], w=[gq], dma=True)
        op("sp", lambda e: e.dma_start(out=gk[:], in_=kn[l, :].partition_broadcast(128)), r=[in_res], w=[gk], dma=True)
        op("sp", lambda e: e.dma_start(out=gik[:], in_=ikn[l, :].partition_broadcast(128)), r=[in_res], w=[gik], dma=True)
        op("sp", lambda e: e.dma_start(out=n2b[:], in_=n2r[l, :].partition_broadcast(128)), r=[in_res], w=[n2b], dma=True)

        kb.barrier()
        if l > 0:
            op("pool", lambda e: e.memset(vaA[:], 1.0), w=[vaA] + kvres)
        load_cast_rows(WARENA_A, lambda c, o0, wdt: w_in[l, c * 128:(c + 1) * 128, o0:o0 + wdt], 8, NQKV, wa_qkv, n1s, lambda c: c)

        for t in tiles:
            samp = (t == NT - 1)
            nk = t + 1
            if samp:
                for k0 in range(0, 16, 4):
                    for kk in range(k0, k0 + 4):
                        rows = slice(kk * 128, (kk + 1) * 128)
                        s = stg[kk % 2]
                        op("sp", lambda e, s=s, rows=rows: e.dma_start(out=s[:, 0:128], in_=cak[l, rows, :]), r=[in_res], w=[s], dma=True)
                        op("sp", lambda e, s=s, rows=rows: e.dma_start(out=s[:, 128:192], in_=cik[l, rows, :]), r=[in_res], w=[s], dma=True)
                        op("sp", lambda e, s=s, rows=rows: e.dma_start(out=s[:, 192:320], in_=cav[l, rows, :]), r=[in_res], w=[s], dma=True)
                        op("sp", lambda e, s=s, rows=rows: e.dma_start(out=s[:, 512:1024], in_=cbk[l, rows, :]), r=[in_res], w=[s], dma=True)
                        op("sp", lambda e, s=s, rows=rows: e.dma_start(out=s[:, 1024:1536], in_=cbv[l, rows, :]), r=[in_res], w=[s], dma=True)
                        op("dve", lambda e, s=s: e.tensor_copy(out=pbf[:, 0:192], in_=s[:, 0:192]), r=[s], w=[pbf])
                        for b in range(3):
                            op("pe", lambda e, b=b: e.transpose(out=PTR[0:64, b * 128:(b + 1) * 128], in_=pbf[:, b * 64:(b + 1) * 64], identity=ident[:]), r=[pbf, ident], w=[PTR])
                        ks = slice(kk * 128, (kk + 1) * 128)
                        op("act", lambda e, ks=ks: e.activation(out=kaT[:, :, ks], in_=PTR[0:64, 0:256].rearrange("p (b n) -> p b n", b=2), func=AF.Copy), r=[PTR], w=[kvres[kk]])
                        op("act", lambda e, ks=ks: e.activation(out=kiT[:, ks], in_=PTR[0:64, 256:384], func=AF.Copy), r=[PTR], w=[kvres[kk]])
                        op("dve", lambda e, s=s, kk=kk: e.tensor_copy(out=vaA[:, kk, :, 0:64], in_=s[:, 192:320].rearrange("p (g d) -> p g d", g=2)), r=[s], w=[kvres[kk]])
                        op("dve", lambda e, s=s: e.tensor_copy(out=pbf[:, 0:512], in_=s[:, 512:1024]), r=[s], w=[pbf])
                        for b in range(8):
                            op("pe", lambda e, b=b: e.transpose(out=PTR[0:64, b * 128:(b + 1) * 128], in_=pbf[:, b * 64:(b + 1) * 64], identity=ident[:]), r=[pbf, ident], w=[PTR])
                        op("act", lambda e, ks=ks: e.activation(out=kbT[:, :, ks], in_=PTR[0:64, :].rearrange("p (b n) -> p b n", b=8), func=AF.Copy), r=[PTR], w=[kvres[kk]])
                        op("pool", lambda e, s=s, kk=kk: e.tensor_copy(out=vbB[:, kk, :, :], in_=s[:, 1024:1536].rearrange("p (h d) -> p h d", h=8)), r=[s], w=[kvres[kk]])

            rows = slice(t * 128, (t + 1) * 128)
            op("sp", lambda e: e.dma_start(out=xt[:], in_=xin_d[rows, :]), r=[xin_r[t]], w=[xt], dma=True)
            op("sp", lambda e: e.dma_start(out=ropet[:], in_=rope_in[rows, :]), r=[in_res], w=[ropet], dma=True)
            rmsnorm_rows(xt, hb, sq)
            make_hT()

            def proj(pt, c0, wdt):
                for c in range(8):
                    op("pe", lambda e, c=c: e.matmul(pt[:, 0:wdt], lhsT=hT[:, c, :], rhs=WAA[:, c * NQKV + c0: c * NQKV + c0 + wdt], start=(c == 0), stop=(c == 7)),
                       r=[hT, WARENA_A], w=[pt])

            def out_rows(dst_p, dst_s, src, wdt):
                if samp:
                    op("sp", lambda e: e.dma_start(out=dst_s[l, :, :], in_=src[0:DEC, 0:wdt]), r=[src], w=[out_res], dma=True)
                else:
                    op("sp", lambda e: e.dma_start(out=dst_p[l, rows, :], in_=src[:, 0:wdt]), r=[src], w=[out_res], dma=True)

            ks = slice(t * 128, (t + 1) * 128)
            proj(P[0], 0, 512)
            headnorm(P[0], P[0][:, 0:512], 8, gq, pf)
            rope(pf, 8, pf2)
            op("dve", lambda e: e.tensor_scalar(out=pbf[:], in0=pf2[:], scalar1=0.125, scalar2=None, op0=ALU.mult), r=[pf2], w=[pbf])
            transpose_blocks(pbf, 8, qaT[:], [qaT])
            proj(P[1], 512, 256)
            headnorm(P[1], P[1][:, 0:128], 2, gk, pf)
            rope(pf, 2, pf2)
            out_rows(ak_p, ak_s, pf2, 128)
            op("dve", lambda e: e.tensor_copy(out=pbf[:, 0:128], in_=pf2[:, 0:128]), r=[pf2], w=[pbf])
            op("act", lambda e: e.activation(out=pf[:, 0:128], in_=P[1][:, 128:256], func=AF.Copy), r=[P[1]], w=[pf])
            out_rows(av_p, av_s, pf, 128)
            op("dve", lambda e: e.tensor_copy(out=vaA[:, t, :, 0:64], in_=pf[:, 0:128].rearrange("p (g d) -> p g d", g=2)), r=[pf], w=[kvres[t]])
            transpose_blocks(pbf, 2, kaT[:, :, ks], [kvres[t]])
            proj(P[0], 768, 512)
            rope(P[0], 8, pf2)
            op("dve", lambda e: e.tensor_copy(out=pbf[:], in_=pf2[:]), r=[pf2], w=[pbf])
            transpose_blocks(pbf, 8, qiT[:], [qiT])
            proj(P[1], 1280, 72)
            headnorm(P[1], P[1][:, 0:64], 1, gik, pf)
            rope(pf, 1, pf2)
            out_rows(ik_p, ik_s, pf2, 64)
            op("dve", lambda e: e.tensor_copy(out=pbf[:, 0:64], in_=pf2[:, 0:64]), r=[pf2], w=[pbf])
            op("act", lambda e: e.activation(out=wi[:], in_=P[1][:, 64:72], func=AF.Copy), r=[P[1]], w=[wi])
            for b in range(1):
                op("pe", lambda e: e.transpose(out=PTR[0:64, 0:128], in_=pbf[:, 0:64], identity=ident[:]), r=[pbf, ident], w=[PTR])
            op("act", lambda e: e.activation(out=kiT[:, ks], in_=PTR[0:64, 0:128], func=AF.Copy), r=[PTR], w=[kvres[t]])
            proj(P[0], 1352, 512)
            op("act", lambda e: e.activation(out=pbf[:], in_=P[0][:, :], func=AF.Copy, scale=0.125), r=[P[0]], w=[pbf])
            transpose_blocks(pbf, 8, qbT[:], [qbT])
            proj(P[1], 1864, 512)
            op("act", lambda e: e.activation(out=pf[:], in_=P[1][:, :], func=AF.Copy), r=[P[1]], w=[pf])
            out_rows(bk_p, bk_s, pf, 512)
            op("dve", lambda e: e.tensor_copy(out=pbf[:], in_=pf[:]), r=[pf], w=[pbf])
            transpose_blocks(pbf, 8, kbT[:, :, ks], [kvres[t]])
            proj(P[0], 2376, 512)
            op("act", lambda e: e.activation(out=pf2[:], in_=P[0][:, :], func=AF.Copy), r=[P[0]], w=[pf2])
            out_rows(bv_p, bv_s, pf2, 512)
            op("dve", lambda e: e.tensor_copy(out=vbB[:, t, :, :], in_=pf2[:, :].rearrange("p (h d) -> p h d", h=8)), r=[pf2], w=[kvres[t]])

            S = nk * 128
            nblk = (S + 511) // 512
            kvr = [kvres[i] for i in range(nk)]
            for bi in range(nblk):
                c0 = bi * 512
                wdt = min(512, S - c0)
                for h in range(8):
                    pt = P[2 + (h % 2)]
                    rb = rl[h % 2]
                    op("pe", lambda e, h=h, pt=pt: e.matmul(pt[:, 0:wdt], lhsT=qiT[:, h, :], rhs=kiT[:, c0:c0 + wdt], start=True, stop=True), r=[qiT] + kvr, w=[pt])
                    op("act", lambda e, pt=pt, rb=rb: e.activation(out=rb[:, 0:wdt], in_=pt[:, 0:wdt], func=AF.Relu, scale=0.125 * (8 ** -0.5)), r=[pt], w=[rb])
                    if h == 0:
                        op("dve", lambda e, rb=rb: e.tensor_scalar(out=isc[:, c0:c0 + wdt], in0=rb[:, 0:wdt], scalar1=wi[:, 0:1], scalar2=None, op0=ALU.mult), r=[rb, wi], w=[isc])
                    else:
                        op("dve", lambda e, rb=rb, h=h: e.scalar_tensor_tensor(out=isc[:, c0:c0 + wdt], in0=rb[:, 0:wdt], scalar=wi[:, h:h + 1], in1=isc[:, c0:c0 + wdt], op0=ALU.mult, op1=ALU.add), r=[rb, wi, isc], w=[isc])
            need_thr = nk > 2
            if need_thr:
                op("dve", lambda e: e.tensor_reduce(out=bis[:, 5:6], in_=isc[:, 0:S], axis=AX.X, op=ALU.max), r=[isc], w=[bis])
                op("dve", lambda e: e.tensor_reduce(out=bis[:, 6:7], in_=isc[:, 0:S], axis=AX.X, op=ALU.min), r=[isc], w=[bis])
                op("dve", lambda e: e.tensor_scalar(out=bis[:, 6:7], in0=bis[:, 6:7], scalar1=-1.0, scalar2=None, op0=ALU.mult), r=[bis], w=[bis])
                op("dve", lambda e: e.tensor_tensor(out=bis[:, 4:5], in0=bis[:, 5:6], in1=bis[:, 6:7], op=ALU.max), r=[bis], w=[bis])
                op("dve", lambda e: e.tensor_scalar(out=bis[:, 0:1], in0=bis[:, 4:5], scalar1=-1.0, scalar2=None, op0=ALU.mult), r=[bis], w=[bis])
                op("dve", lambda e: e.tensor_scalar(out=dtab[:], in0=pow2[:], scalar1=bis[:, 4:5], scalar2=2.002, op0=ALU.mult, op1=ALU.mult), r=[bis, pow2], w=[dtab])
            if samp:
                op("dve", lambda e: e.memset(isc[:, 16 * 128 + DEC:17 * 128], NEG), r=[], w=[isc])
            else:
                op("dve", lambda e: e.memset(isc[0:64, t * 128 + 64:(t + 1) * 128], NEG), r=[], w=[isc])
            if need_thr:
                for k in range(NBIS):
                    op("dve", lambda e, k=k: e.tensor_tensor(out=bis[:, 1:2], in0=bis[:, 0:1], in1=dtab[:, k:k + 1], op=ALU.add), r=[bis, dtab], w=[bis])
                    op("dve", lambda e: e.tensor_scalar(out=mbias[:, 0:S], in0=isc[:, 0:S], scalar1=bis[:, 1:2], scalar2=None, op0=ALU.is_ge, op1=ALU.add, accum_out=bis[:, 2:3]), r=[isc, bis], w=[mbias, bis])
                    op("dve", lambda e, k=k: e.scalar_tensor_tensor(out=bis[:, 3:4], in0=bis[:, 2:3], scalar=float(TOPK), in1=dtab[:, k:k + 1], op0=ALU.is_ge, op1=ALU.mult), r=[bis, dtab], w=[bis])
                    op("dve", lambda e: e.tensor_tensor(out=bis[:, 0:1], in0=bis[:, 0:1], in1=bis[:, 3:4], op=ALU.add), r=[bis], w=[bis])
            else:
                op("dve", lambda e: e.memset(bis[:, 0:1], -1e29), r=[], w=[bis])
            op("dve", lambda e: e.tensor_scalar(out=mbias[:, 0:S], in0=isc[:, 0:S], scalar1=bis[:, 0:1], scalar2=-30000.0, op0=ALU.is_lt, op1=ALU.mult), r=[isc, bis], w=[mbias])
            OAp = [P[4], P[5]]
            for kt in range(nk):
                ksl = slice(kt * 128, (kt + 1) * 128)
                for g in range(2):
                    pt = P[2 + g]
                    pb_ = PT[g]
                    op("pe", lambda e, g=g, pt=pt, ksl=ksl: e.matmul(pt[:, :], lhsT=kaT[:, g, ksl], rhs=qaT[:, 4 * g:4 * g + 4, :], start=True, stop=False), r=[kvres[kt], qaT], w=[pt])
                    op("pe", lambda e, pt=pt, ksl=ksl: e.matmul(pt[:, :], lhsT=mbias[:, ksl], rhs=ident4[:], start=False, stop=True), r=[mbias, ident4], w=[pt])
                    op("act", lambda e, pt=pt, pb_=pb_: e.activation(out=pb_[:], in_=pt[:, :], func=AF.Exp), r=[pt], w=[pb_])
                    for hh in range(4):
                        op("pe", lambda e, g=g, hh=hh, pb_=pb_, kt=kt: e.matmul(OAp[g][:, hh * 65:(hh + 1) * 65], lhsT=pb_[:, hh * 128:(hh + 1) * 128], rhs=vaA[:, kt, g, :], start=(kt == 0 and hh == 0), stop=(kt == nk - 1)),
                           r=[pb_, kvres[kt]], w=[OAp[g]])
            for g in range(2):
                o3 = OAp[g][:, 0:260].rearrange("p (h d) -> p h d", h=4)
                op("dve", lambda e, g=g, o3=o3: e.reciprocal(out=st8[:, 40 + 4 * g:44 + 4 * g], in_=o3[:, :, 64]), r=[OAp[g]], w=[st8])
                op("dve", lambda e, g=g, o3=o3: e.tensor_tensor(out=oa[:, g * 256:(g + 1) * 256].rearrange("p (h d) -> p h d", h=4), in0=o3[:, :, 0:64],
                                                                 in1=st8[:, 40 + 4 * g:44 + 4 * g].unsqueeze(2).to_broadcast([128, 4, 64]), op=ALU.mult), r=[OAp[g], st8], w=[oa])
            op("sp", lambda e: e.dma_start(out=oab_dram[rows, 0:512], in_=oa[:]), r=[oa], w=[oab_res[t]], dma=True)

            Z = [P[2], P[3]]
            Z2 = [P[4], P[5]]
            PVb = P[6]
            TSb = P[0]
            for kt in range(nk):
                ksl = slice(kt * 128, (kt + 1) * 128)
                diag = (kt == nk - 1)
                for h in range(8):
                    op("pe", lambda e, h=h: e.matmul(Z[h // 4][:, (h % 4) * 128:(h % 4 + 1) * 128], lhsT=kbT[:, h, ksl], rhs=qbT[:, h, :], start=True, stop=True), r=[kvres[kt], qbT], w=[Z[h // 4]])
                for hf in range(2):
                    op("act", lambda e, hf=hf: e.activation(out=ebuf[:, hf * 512:(hf + 1) * 512], in_=Z[hf][:, :], func=AF.Exp), r=[Z[hf]], w=[ebuf])
                    op("act", lambda e, hf=hf: e.activation(out=spT[:, hf * 512:(hf + 1) * 512], in_=ebuf[:, hf * 512:(hf + 1) * 512], func=AF.Ln, bias=1.0), r=[ebuf], w=[spT])
                if diag:
                    s3 = spT[:, :].rearrange("p (h q) -> p h q", h=8)
                    op("dve", lambda e, s3=s3: e.tensor_tensor(out=s3, in0=s3, in1=mlt[:, :].unsqueeze(1).to_broadcast([128, 8, 128]), op=ALU.mult), r=[spT, mlt], w=[spT])
                for hf in range(2):
                    op("pe", lambda e, hf=hf: e.matmul(Z2[hf][:, :], lhsT=negtri[:], rhs=spT[:, hf * 512:(hf + 1) * 512], start=True, stop=False), r=[negtri, spT], w=[Z2[hf]])
                    if diag:
                        op("pe", lambda e, hf=hf: e.matmul(Z2[hf][:, :], lhsT=ident[:], rhs=negm4[:], start=False, stop=False), r=[ident, negm4], w=[Z2[hf]])
                    for hh in range(4):
                        h = hf * 4 + hh
                        op("pe", lambda e, h=h, hh=hh, hf=hf: e.matmul(Z2[hf][:, hh * 128:(hh + 1) * 128], lhsT=kbT[:, h, ksl], rhs=qbT[:, h, :], start=False, stop=(hh == 3)), r=[kvres[kt], qbT], w=[Z2[hf]])
                    op("act", lambda e, hf=hf: e.activation(out=ET[:, hf * 512:(hf + 1) * 512], in_=Z2[hf][:, :], func=AF.Exp), r=[Z2[hf]], w=[ET])
                for h in range(8):
                    op("pe", lambda e, h=h: e.matmul(TSb[:, h:h + 1], lhsT=spT[:, h * 128:(h + 1) * 128], rhs=ones1[:], start=True, stop=True), r=[spT, ones1], w=[TSb])
                for h in range(8):
                    op("pe", lambda e, h=h, kt=kt: e.matmul(PVb[:, h * 64:(h + 1) * 64], lhsT=ET[:, h * 128:(h + 1) * 128], rhs=vbB[:, kt, h, :], start=True, stop=True), r=[ET, kvres[kt]], w=[PVb])
                if kt == 0:
                    op("dve", lambda e: e.tensor_copy(out=ob[:], in_=PVb[:, :]), r=[PVb], w=[ob])
                else:
                    op("act", lambda e: e.activation(out=dd[:], in_=TSb[:, 0:8], func=AF.Exp, scale=-1.0), r=[TSb], w=[dd])
                    o3 = ob[:, :].rearrange("p (h d) -> p h d", h=8)
                    op("dve", lambda e, o3=o3: e.tensor_tensor(out=o3, in0=o3, in1=dd[:, :].unsqueeze(2).to_broadcast([128, 8, 64]), op=ALU.mult), r=[ob, dd], w=[ob])
                    op("dve", lambda e: e.tensor_tensor(out=ob[:], in0=ob[:], in1=PVb[:, :], op=ALU.add), r=[ob, PVb], w=[ob])
            op("sp", lambda e: e.dma_start(out=oab_dram[rows, 512:1024], in_=ob[:]), r=[ob], w=[oab_res[t]], dma=True)

        kb.barrier()
        load_cast_rows(WARENA_B, lambda c, o0, wdt: w_in[l, c * 128:(c + 1) * 128, 2888 + o0:2888 + o0 + wdt], 8, 2048,
                       lambda c, o0, wdt: WA[:, GOFF + c * 2048 + o0: GOFF + c * 2048 + o0 + wdt], n1s, lambda c: c)
        load_cast_rows(WARENA_B, lambda c, o0, wdt: w_pa[l, c * 128:(c + 1) * 128, o0:o0 + wdt], 4, 1024,
                       lambda c, o0, wdt: WA[:, PAOFF + c * 1024 + o0: PAOFF + c * 1024 + o0 + wdt])
        load_cast_rows(WARENA_B, lambda c, o0, wdt: w_pb[l, c * 128:(c + 1) * 128, o0:o0 + wdt], 4, 1024,
                       lambda c, o0, wdt: WA[:, PBOFF + c * 1024 + o0: PBOFF + c * 1024 + o0 + wdt])
        load_cast_rows(WARENA_B, lambda c, o0, wdt: w_o[l, c * 128:(c + 1) * 128, o0:o0 + wdt], 8, 1024,
                       lambda c, o0, wdt: WA[:, WOOFF + c * 1024 + o0: WOOFF + c * 1024 + o0 + wdt])
        for t in tiles:
            samp = (t == NT - 1)
            rows = slice(t * 128, (t + 1) * 128)
            op("sp", lambda e: e.dma_start(out=xt[:], in_=xin_d[rows, :]), r=[xin_r[t]], w=[xt], dma=True)
            rmsnorm_rows(xt, hb, sq)
            make_hT()
            for gi, gdst in ((0, sga), (1, sgb)):
                for hf in range(2):
                    pt = P[hf]
                    c0 = GOFF + gi * 1024 + hf * 512
                    for c in range(8):
                        op("pe", lambda e, c=c, pt=pt, c0=c0: e.matmul(pt[:, :], lhsT=hT[:, c, :], rhs=WA[:, c * 2048 + c0: c * 2048 + c0 + 512], start=(c == 0), stop=(c == 7)), r=[hT, WARENA_B], w=[pt])
                    op("act", lambda e, pt=pt, gdst=gdst, hf=hf: e.activation(out=gdst[:, hf * 512:(hf + 1) * 512], in_=pt[:, :], func=AF.Sigmoid), r=[pt], w=[gdst])
            op("sp", lambda e: e.dma_start(out=oabf[:], in_=oab_dram[rows, :]), r=[oab_res[t]], w=[oabf], dma=True)
            op("pool", lambda e: e.tensor_copy(out=oabb[:], in_=oabf[:]), r=[oabf], w=[oabb])
            for bi, (woff, gdst) in enumerate(((PAOFF, sga), (PBOFF, sgb))):
                for c in range(4):
                    op("pe", lambda e, c=c, bi=bi: e.transpose(out=PTR[:, c * 128:(c + 1) * 128], in_=oabb[:, bi * 512 + c * 128: bi * 512 + (c + 1) * 128], identity=ident[:]), r=[oabb, ident], w=[PTR])
                op("act", lambda e: e.activation(out=oT[:, 0:4, :], in_=PTR[:, 0:512].rearrange("p (c n) -> p c n", c=4), func=AF.Copy), r=[PTR], w=[oT])
                for hf in range(2):
                    pt = P[2 + hf]
                    for c in range(4):
                        op("pe", lambda e, c=c, pt=pt, hf=hf, woff=woff: e.matmul(pt[:, :], lhsT=oT[:, c, :], rhs=WA[:, woff + c * 1024 + hf * 512: woff + c * 1024 + (hf + 1) * 512], start=(c == 0), stop=(c == 3)), r=[oT, WARENA_B], w=[pt])
                    if bi == 0:
                        op("dve", lambda e, pt=pt, hf=hf: e.tensor_tensor(out=mm[:, hf * 512:(hf + 1) * 512], in0=sga[:, hf * 512:(hf + 1) * 512], in1=pt[:, :], op=ALU.mult), r=[sga, pt], w=[mm])
                    else:
                        op("dve", lambda e, pt=pt, hf=hf: e.tensor_tensor(out=sgb[:, hf * 512:(hf + 1) * 512], in0=sgb[:, hf * 512:(hf + 1) * 512], in1=pt[:, :], op=ALU.mult), r=[sgb, pt], w=[sgb])
            op("dve", lambda e: e.tensor_tensor(out=mbf[:], in0=mm[:], in1=sgb[:], op=ALU.add), r=[mm, sgb], w=[mbf])
            for c in range(8):
                op("pe", lambda e, c=c: e.transpose(out=PTR[:, c * 128:(c + 1) * 128], in_=mbf[:, c * 128:(c + 1) * 128], identity=ident[:]), r=[mbf, ident], w=[PTR])
            op("act", lambda e: e.activation(out=oT[:], in_=PTR[:, :].rearrange("p (c n) -> p c n", c=8), func=AF.Copy), r=[PTR], w=[oT])
            for hf in range(2):
                pt = P[4 + hf]
                for c in range(8):
                    op("pe", lambda e, c=c, pt=pt, hf=hf: e.matmul(pt[:, :], lhsT=oT[:, c, :], rhs=WA[:, WOOFF + c * 1024 + hf * 512: WOOFF + c * 1024 + (hf + 1) * 512], start=(c == 0), stop=(c == 7)), r=[oT, WARENA_B], w=[pt])
                op("dve", lambda e, pt=pt, hf=hf: e.tensor_tensor(out=xt[:, hf * 512:(hf + 1) * 512], in0=xt[:, hf * 512:(hf + 1) * 512], in1=pt[:, :], op=ALU.add), r=[xt, pt], w=[xt])

            op("sp", lambda e: e.dma_start(out=xmid[rows, :], in_=xt[:]), r=[xt], w=[xmres[t]], dma=True)

        if do_peer:
            kb.barrier()
            load_cast_rows(WQ, lambda c, o0, wdt: pwq[l, c * 128:(c + 1) * 128, o0:o0 + wdt], 8, 1024,
                           lambda c, o0, wdt: WQ[:, c * 1024 + o0: c * 1024 + o0 + wdt], n2s, lambda c: c)
            for which, src in ((0, pk1T), (1, pk2T)):
                s_ = stg[which]
                op("sp", lambda e, s_=s_, src=src: e.dma_start(out=s_[0:64, 0:1024], in_=src[l, :, :, :].rearrange("d h k -> d (h k)")), r=[in_res], w=[s_], dma=True)
                op("dve", lambda e, s_=s_, which=which: e.tensor_copy(out=k12[:, which, :, :], in_=s_[0:64, 0:1024].rearrange("d (h k) -> d h k", h=8)), r=[s_], w=[k12])
            for t in tiles:
                rows = slice(t * 128, (t + 1) * 128)
                op("sp", lambda e: e.dma_start(out=xt[:], in_=xmid[rows, :]), r=[xmres[t]], w=[xt], dma=True)
                rmsnorm_rows(xt, hb, sq)
                make_hT()
                op("pool", lambda e: e.tensor_copy(out=h2T_all[:, :, t * 128:(t + 1) * 128], in_=hT[:]), r=[hT], w=[h2T_all])
                for hf in range(2):
                    pt = P[hf]
                    for c in range(8):
                        op("pe", lambda e, c=c, pt=pt, hf=hf: e.matmul(pt[:, :], lhsT=hT[:, c, :], rhs=WQ[:, c * 1024 + hf * 512: c * 1024 + (hf + 1) * 512], start=(c == 0), stop=(c == 7)), r=[hT, WQ], w=[pt])
                    op("act", lambda e, pt=pt, hf=hf: e.activation(out=qsb[:, hf * 512:(hf + 1) * 512], in_=pt[:, :], func=AF.Copy), r=[pt], w=[qsb])
                for hf in range(2):
                    for b in range(8):
                        bb = hf * 8 + b
                        op("pe", lambda e, b=b, bb=bb: e.transpose(out=PTR[0:64, b * 128:(b + 1) * 128], in_=qsb[:, bb * 64:(bb + 1) * 64], identity=ident[:]), r=[qsb, ident], w=[PTR])
                    op("act", lambda e, hf=hf: e.activation(out=qT[:, hf * 8:(hf + 1) * 8, :], in_=PTR[0:64, :].rearrange("p (b n) -> p b n", b=8), func=AF.Copy), r=[PTR], w=[qT])
                for which in range(2):
                    for h in range(8):
                        pt = P[2 + which * 2 + h // 4]
                        op("pe", lambda e, h=h, pt=pt, which=which: e.matmul(pt[:, (h % 4) * 128:(h % 4 + 1) * 128], lhsT=qT[:, h * 2 + which, :], rhs=k12[:, which, h, :], start=True, stop=True), r=[qT, k12], w=[pt])
                    for hf in range(2):
                        pt = P[2 + which * 2 + hf]
                        op("act", lambda e, pt=pt, which=which, hf=hf: e.activation(out=s12[:, which, hf * 4:(hf + 1) * 4, :], in_=pt[:, :].rearrange("p (h k) -> p h k", h=4), func=AF.Copy), r=[pt], w=[s12])
                for which in range(2):
                    for h in range(8):
                        sv = s12[:, which, h, :]
                        op("dve", lambda e, sv=sv, which=which, h=h: e.max(out=v12[:, which, h, 0:8], in_=sv), r=[s12], w=[v12])
                        op("dve", lambda e, sv=sv, which=which, h=h: e.max_index(out=i12[:, which, h, 0:8], in_max=v12[:, which, h, 0:8], in_values=sv), r=[s12, v12], w=[i12])
                        op("dve", lambda e, sv=sv, which=which, h=h: e.match_replace(out=swk[:, 0:128], in_to_replace=v12[:, which, h, 0:8], in_values=sv, imm_value=NEG), r=[s12, v12], w=[swk])
                        op("dve", lambda e, which=which, h=h: e.max(out=v12[:, which, h, 8:16], in_=swk[:, 0:128]), r=[swk], w=[v12])
                        op("dve", lambda e, which=which, h=h: e.max_index(out=i12[:, which, h, 8:16], in_max=v12[:, which, h, 8:16], in_values=swk[:, 0:128]), r=[swk, v12], w=[i12])
                op("dve", lambda e: e.tensor_copy(out=i12f[:], in_=i12[:]), r=[i12], w=[i12f])
                c4 = cand[:, :, :].rearrange("p h (r c) -> p h r c", r=16)
                op("dve", lambda e: e.tensor_tensor(out=c4, in0=v12[:, 0, :, :].unsqueeze(3).to_broadcast([128, 8, 16, 16]), in1=v12[:, 1, :, :].unsqueeze(2).to_broadcast([128, 8, 16, 16]), op=ALU.add), r=[v12], w=[cand])
                for h in range(8):
                    op("dve", lambda e, h=h: e.max(out=sc[:, h, 0:8], in_=cand[:, h, :]), r=[cand], w=[sc])
                    op("dve", lambda e, h=h: e.max_index(out=pos[:, h, 0:8], in_max=sc[:, h, 0:8], in_values=cand[:, h, :]), r=[cand, sc], w=[pos])
                    op("dve", lambda e, h=h: e.match_replace(out=swk[:, 0:256], in_to_replace=sc[:, h, 0:8], in_values=cand[:, h, :], imm_value=NEG), r=[cand, sc], w=[swk])
                    op("dve", lambda e, h=h: e.max(out=sc[:, h, 8:16], in_=swk[:, 0:256]), r=[swk], w=[sc])
                    op("dve", lambda e, h=h: e.max_index(out=pos[:, h, 8:16], in_max=sc[:, h, 8:16], in_values=swk[:, 0:256]), r=[swk, sc], w=[pos])
                op("dve", lambda e: e.tensor_copy(out=posf[:], in_=pos[:]), r=[pos], w=[posf])
                op("dve", lambda e: e.tensor_tensor(out=oh[:], in0=posf[:, :, :].unsqueeze(3).to_broadcast([128, 8, 16, 16]),
                                                    in1=thr16[:, :].unsqueeze(1).unsqueeze(1).to_broadcast([128, 8, 16, 16]), op=ALU.is_ge), r=[posf, thr16], w=[oh])
                op("dve", lambda e: e.tensor_reduce(out=posrf[:], in_=oh[:], axis=AX.X, op=ALU.add), r=[oh], w=[posrf])
                op("dve", lambda e: e.scalar_tensor_tensor(out=poscf[:], in0=posrf[:], scalar=-16.0, in1=posf[:], op0=ALU.mult, op1=ALU.add), r=[posrf, posf], w=[poscf])
                io4 = iota16[:, :].unsqueeze(1).unsqueeze(1).to_broadcast([128, 8, 16, 16])
                for pf_, which, dsel in ((posrf, 0, sel1), (poscf, 1, sel2)):
                    op("dve", lambda e, pf_=pf_: e.tensor_tensor(out=oh[:], in0=io4, in1=pf_[:, :, :].unsqueeze(3).to_broadcast([128, 8, 16, 16]), op=ALU.is_equal), r=[iota16, pf_], w=[oh])
                    op("dve", lambda e, which=which: e.tensor_tensor(out=oh[:], in0=oh[:], in1=i12f[:, which, :, :].unsqueeze(2).to_broadcast([128, 8, 16, 16]), op=ALU.mult), r=[oh, i12f], w=[oh])
                    op("dve", lambda e, dsel=dsel: e.tensor_reduce(out=dsel[:], in_=oh[:], axis=AX.X, op=ALU.add), r=[oh], w=[dsel])
                op("dve", lambda e: e.tensor_tensor(out=gsm[:], in0=sc[:], in1=sc[:, :, 0:1].to_broadcast([128, 8, 16]), op=ALU.subtract), r=[sc], w=[gsm])
                op("act", lambda e: e.activation(out=gsm[:], in_=gsm[:], func=AF.Exp), r=[gsm], w=[gsm])
                op("dve", lambda e: e.tensor_reduce(out=st8[:, 48:56], in_=gsm[:], axis=AX.X, op=ALU.add), r=[gsm], w=[st8])
                op("dve", lambda e: e.reciprocal(out=st8[:, 56:64], in_=st8[:, 48:56]), r=[st8], w=[st8])
                op("dve", lambda e: e.tensor_tensor(out=gsm[:], in0=gsm[:], in1=st8[:, 56:64].unsqueeze(2).to_broadcast([128, 8, 16]), op=ALU.mult), r=[gsm, st8], w=[gsm])
                for i_, src in enumerate((sel1, sel2, gsm)):
                    op("pe", lambda e, i_=i_, src=src: e.transpose(out=P[0][:, i_ * 128:(i_ + 1) * 128], in_=src[:, :, :].rearrange("p h k -> p (h k)"), identity=identf[:]), r=[src, identf], w=[P[0]])
                op("act", lambda e: e.activation(out=TT[:], in_=P[0][:, 0:384].rearrange("p (a n) -> p a n", a=3), func=AF.Copy), r=[P[0]], w=[TT])
                iob = iota128[:, :].unsqueeze(1).to_broadcast([128, 32, 128])
                for qq in range(4):
                    n0 = qq * 32
                    p1 = P1q[qq % 2]
                    p2 = P2q[qq % 2]
                    op("dve", lambda e, n0=n0, p2=p2: e.tensor_tensor(out=p2[:], in0=iob, in1=TT[:, 1, n0:n0 + 32].unsqueeze(2).to_broadcast([128, 32, 128]), op=ALU.is_equal), r=[iota128, TT], w=[p2])
                    op("dve", lambda e, n0=n0, p1=p1: e.tensor_tensor(out=p1[:], in0=iob, in1=TT[:, 0, n0:n0 + 32].unsqueeze(2).to_broadcast([128, 32, 128]), op=ALU.is_equal), r=[iota128, TT], w=[p1])
                    op("pool", lambda e, n0=n0, p1=p1: e.tensor_tensor(out=p1[:], in0=p1[:], in1=TT[:, 2, n0:n0 + 32].unsqueeze(2).to_broadcast([128, 32, 128]), op=ALU.mult), r=[p1, TT], w=[p1])
                    for q4 in range(8):
                        bank = P[1 + (q4 % 4)]
                        for k in range(4):
                            n = q4 * 4 + k
                            op("pe", lambda e, bank=bank, k=k, n=n, p1=p1, p2=p2: e.matmul(bank[:, k * 128:(k + 1) * 128], lhsT=p1[:, n, :], rhs=p2[:, n, :], start=True, stop=True), r=[p1, p2], w=[bank])
                        nn = n0 + q4 * 4
                        op("act", lambda e, bank=bank, nn=nn: e.activation(out=Gs[:, :, nn:nn + 4].rearrange("p i n -> p n i"), in_=bank[:, :].rearrange("p (n i) -> p n i", n=4), func=AF.Copy), r=[bank], w=[Gs])
                op("sp", lambda e: e.dma_start(out=Gd[t, :, :], in_=Gs[:, :, :].rearrange("p i n -> p (i n)")), r=[Gs], w=[gdres[t]], dma=True)

            kb.barrier()
            pvv = pv[l][:, :].rearrange("(i1 i2) d -> i1 i2 d", i2=128)
            tgroups = [tiles[i:i + 2] for i in range(0, len(tiles), 2)]
            NSB = 128 // NBLK
            cring = [0]
            pend_casts = {}

            def emit_loads(sbk):
                ub_ = u16[sbk % 2]
                vb_ = v16[sbk % 2]
                casts = []
                for c in range(8):
                    s_ = cstg[cring[0] % NCS]
                    cring[0] += 1
                    op("pool", lambda e, s_=s_, c=c: e.dma_start(out=s_[:, 0:512], in_=puT[l][c * 128:(c + 1) * 128, sbk * NBLK * 128:(sbk + 1) * NBLK * 128]), r=[in_res], w=[s_], dma=True)
                    casts.append(lambda s_=s_, c=c, ub_=ub_: op("act", lambda e: e.activation(out=ub_[:, c, :], in_=s_[:, 0:512], func=AF.Copy, scale=n2s[:, c:c + 1]), r=[s_, n2s], w=[ub_]))
                for blk in range(NBLK):
                    for hf in range(2):
                        s_ = cstg[cring[0] % NCS]
                        cring[0] += 1
                        op("pool", lambda e, s_=s_, blk=blk, hf=hf: e.dma_start(out=s_[:, 0:512], in_=pvv[:, sbk * NBLK + blk, hf * 512:(hf + 1) * 512]), r=[in_res], w=[s_], dma=True)
                        casts.append(lambda s_=s_, blk=blk, hf=hf, vb_=vb_: op("act", lambda e: e.activation(out=vb_[:, blk, hf * 512:(hf + 1) * 512], in_=s_[:, 0:512], func=AF.Copy), r=[s_], w=[vb_]))
                return casts

            items = [(sbk, gi_, blk) for sbk in range(NSB) for gi_ in range(len(tgroups)) for blk in range(NBLK)]

            def emit_a(it_):
                sbk, gi_, blk = it_
                ub_ = u16[sbk % 2]
                tl = tgroups[gi_]
                nt_ = len(tl)
                ntk = 128 * nt_
                A = P[blk % 2]
                if blk == 0:
                    gc = Gc[(sbk * len(tgroups) + gi_) % 2]
                    for ti, t in enumerate(tl):
                        op("sp", lambda e, ti=ti, t=t: e.dma_start(out=gc[:, ti, :, :], in_=Gd[t, :, sbk * NBLK * 128:(sbk + 1) * NBLK * 128].rearrange("p (i n) -> p i n", i=NBLK)), r=[gdres[t]], w=[gc], dma=True)
                contiguous = (nt_ == 2 and tl[1] == tl[0] + 1) or nt_ == 1
                if contiguous:
                    for c in range(8):
                        op("pe", lambda e, c=c: e.matmul(A[:, 0:ntk], lhsT=ub_[:, c, blk * 128:(blk + 1) * 128], rhs=h2T_all_c[:, c, tl[0] * 128: tl[0] * 128 + ntk], start=(c == 0), stop=(c == 7)), r=[ub_, h2T_all_c], w=[A])
                else:
                    for ti, t in enumerate(tl):
                        for c in range(8):
                            op("pe", lambda e, c=c, ti=ti, t=t: e.matmul(A[:, ti * 128:(ti + 1) * 128], lhsT=ub_[:, c, blk * 128:(blk + 1) * 128], rhs=h2T_all_c[:, c, t * 128:(t + 1) * 128], start=(c == 0 and ti == 0), stop=(c == 7)), r=[ub_, h2T_all_c], w=[A])

            def emit_rest(it_):
                sbk, gi_, blk = it_
                vb_ = v16[sbk % 2]
                tl = tgroups[gi_]
                nt_ = len(tl)
                ntk = 128 * nt_
                A = P[blk % 2]
                ge = gel[blk % 2]
                cf = cfT[blk % 2]
                gc = Gc[(sbk * len(tgroups) + gi_) % 2]
                op("act", lambda e: e.activation(out=ge[:, 0:ntk], in_=A[:, 0:ntk], func=AF.Gelu), r=[A], w=[ge])
                op("dve", lambda e: e.tensor_tensor(out=cf[:, 0:ntk].rearrange("p (t n) -> p t n", t=nt_), in0=ge[:, 0:ntk].rearrange("p (t n) -> p t n", t=nt_), in1=gc[:, 0:nt_, blk, :], op=ALU.mult), r=[ge, gc], w=[cf])
                for ti, t in enumerate(tl):
                    for hf in range(2):
                        ab = P[2 + ti * 2 + hf]
                        op("pe", lambda e, ab=ab, ti=ti, hf=hf: e.matmul(ab[:, :], lhsT=cf[:, ti * 128:(ti + 1) * 128], rhs=vb_[:, blk, hf * 512:(hf + 1) * 512], start=(blk == 0), stop=(blk == NBLK - 1)), r=[cf, vb_], w=[ab])
                if blk == NBLK - 1:
                    for ti, t in enumerate(tl):
                        for hf in range(2):
                            ab = P[2 + ti * 2 + hf]
                            if sbk == 0:
                                op("dve", lambda e, ab=ab, t=t, hf=hf: e.tensor_copy(out=accs[:, t, hf * 512:(hf + 1) * 512], in_=ab[:, :]), r=[ab], w=[accres[t]])
                            else:
                                op("dve", lambda e, ab=ab, t=t, hf=hf: e.tensor_tensor(out=accs[:, t, hf * 512:(hf + 1) * 512], in0=accs[:, t, hf * 512:(hf + 1) * 512], in1=ab[:, :], op=ALU.add), r=[ab, accres[t]], w=[accres[t]])

            for cst_ in emit_loads(0):
                cst_()
            ng = len(tgroups)
            cast_at = {max(0, ng // 3): (0, 8), max(1, (2 * ng) // 3): (8, 16)} if ng >= 3 else {0: (0, 16)}
            emit_a(items[0])
            for i_, it_ in enumerate(items):
                sbk, gi_, blk = it_
                if gi_ == 0 and blk == 0 and sbk + 1 < NSB:
                    pend_casts[sbk + 1] = emit_loads(sbk + 1)
                if blk == 0 and gi_ in cast_at and (sbk + 1) in pend_casts:
                    a_, b_ = cast_at[gi_]
                    for cst_ in pend_casts[sbk + 1][a_:b_]:
                        cst_()
                if i_ + 1 < len(items):
                    emit_a(items[i_ + 1])
                emit_rest(it_)

        for t in tiles:
            samp = (t == NT - 1)
            rows = slice(t * 128, (t + 1) * 128)
            op("sp", lambda e: e.dma_start(out=xt[:], in_=xmid[rows, :]), r=[xmres[t]], w=[xt], dma=True)
            if do_peer:
                op("dve", lambda e: e.tensor_tensor(out=xt[:], in0=xt[:], in1=accs[:, t, :], op=ALU.add), r=[xt, accres[t]], w=[xt])
            if last:
                if samp:
                    op("sp", lambda e: e.dma_start(out=y_s[:, :], in_=xt[0:DEC, :]), r=[xt], w=[out_res], dma=True)
                else:
                    op("sp", lambda e: e.dma_start(out=y_p[rows, :], in_=xt[:]), r=[xt], w=[out_res], dma=True)
            else:
                op("sp", lambda e: e.dma_start(out=xout_d[rows, :], in_=xt[:]), r=[xt], w=[xout_r[t]], dma=True)

    kb.finish()
    es.close()
    return nc, kb.ninst


def _consts():
    j = np.arange(128)
    ident = np.eye(128, dtype=np.float32)
    negtri = -(j[:, None] >= j[None, :]).astype(np.float32)
    mlt = (j[:, None] < j[None, :]).astype(np.float32)
    ident4 = np.tile(ident, (1, 4))
    negm = np.where(j[:, None] >= j[None, :], -30000.0, 0.0).astype(np.float32)
    negm4 = np.tile(negm, (1, 4))
    pow2 = np.tile((0.5 ** np.arange(1, NBIS + 1)).astype(np.float32)[None, :], (128, 1))
    iota = np.tile(np.arange(16, dtype=np.float32)[None, :], (128, 1))
    iota128 = np.tile(np.arange(128, dtype=np.float32)[None, :], (128, 1))
    return np.concatenate([ident, negtri, mlt, ident4, negm4, pow2, iota, iota128], axis=1).astype(np.float32)


def _rope_table():
    pos = np.concatenate([np.arange(SEQ), SEQ + np.arange(128)]).astype(np.float32)
    half = 32
    inv = (10000.0 ** (-np.arange(half, dtype=np.float32) / half)).astype(np.float32)
    ang = pos[:, None] * inv[None, :]
    return np.concatenate([np.cos(ang), np.sin(ang)], axis=1).astype(np.float32)


_PROG = {}


def _prep(x_prompt, x_sample, cache_a_k, cache_a_v, cache_idx_k, cache_b_k, cache_b_v,
          norm1, w_in, q_norm_a, k_norm_a, idx_k_norm, w_pa, w_pb, w_o, norm2,
          peer_wq, peer_k1, peer_k2, peer_u, peer_v, cores=range(8)):
    f = lambda a: np.ascontiguousarray(np.asarray(a, dtype=np.float32))
    x_prompt = f(x_prompt); x_sample = f(x_sample)
    shared = {
        "rope": _rope_table(),
        "cst": _consts(),
        "n1T": f(np.asarray(norm1).reshape(DEPTH, 8, 128).transpose(0, 2, 1)),
        "n2T": f(np.asarray(norm2).reshape(DEPTH, 8, 128).transpose(0, 2, 1)),
        "n2r": f(norm2),
        "w_in": f(w_in), "qn": f(q_norm_a), "kn": f(k_norm_a), "ikn": f(idx_k_norm),
        "w_pa": f(w_pa), "w_pb": f(w_pb), "w_o": f(w_o), "pwq": f(peer_wq),
        "pk1T": f(np.asarray(peer_k1).transpose(0, 3, 1, 2)),
        "pk2T": f(np.asarray(peer_k2).transpose(0, 3, 1, 2)),
    }
    pu = np.asarray(peer_u); pvv = np.asarray(peer_v)
    for l in range(DEPTH):
        shared["puT%d" % l] = f(pu[l].reshape(128, 128, D).transpose(2, 1, 0).reshape(D, 16384))
        shared["pv%d" % l] = f(pvv[l])
    cak = np.asarray(cache_a_k); cav = np.asarray(cache_a_v); cik = np.asarray(cache_idx_k)
    cbk = np.asarray(cache_b_k); cbv = np.asarray(cache_b_v)
    in_maps = []
    for b in cores:
        xa = np.zeros((NT * 128, D), np.float32)
        xa[:SEQ] = x_prompt[b]
        xa[SEQ:SEQ + DEC] = x_sample[b]
        m = dict(shared)
        m["x_in"] = xa
        m["cak"] = f(cak[:, b].reshape(DEPTH, SEQ, 128))
        m["cav"] = f(cav[:, b].reshape(DEPTH, SEQ, 128))
        m["cik"] = f(cik[:, b].reshape(DEPTH, SEQ, 64))
        m["cbk"] = f(cbk[:, b].reshape(DEPTH, SEQ, 512))
        m["cbv"] = f(cbv[:, b].reshape(DEPTH, SEQ, 512))
        in_maps.append(m)
    return in_maps


def kernel(**inputs):
    if "full" not in _PROG:
        _PROG["full"] = build_program()[0]
    nc = _PROG["full"]
    in_maps = _prep(**inputs)
    res = run_bass_kernel_spmd(nc, in_maps, core_ids=list(range(8)))
    R = res.results
    st = lambda k, shp: np.stack([np.asarray(R[b][k], dtype=np.float32) for b in range(8)], axis=1).reshape(shp)
    y_p = np.stack([np.asarray(R[b]["y_p"], dtype=np.float32) for b in range(8)], axis=0)
    y_s = np.stack([np.asarray(R[b]["y_s"], dtype=np.float32) for b in range(8)], axis=0)
    return (y_p, y_s,
            st("ak_p", (DEPTH, 8, SEQ, 2, 64)), st("av_p", (DEPTH, 8, SEQ, 2, 64)), st("ik_p", (DEPTH, 8, SEQ, 64)),
            st("bk_p", (DEPTH, 8, SEQ, 8, 64)), st("bv_p", (DEPTH, 8, SEQ, 8, 64)),
            st("ak_s", (DEPTH, 8, DEC, 2, 64)), st("av_s", (DEPTH, 8, DEC, 2, 64)), st("ik_s", (DEPTH, 8, DEC, 64)),
            st("bk_s", (DEPTH, 8, DEC, 8, 64)), st("bv_s", (DEPTH, 8, DEC, 8, 64)))
```

```python
import contextlib
import numpy as np
import concourse.bass as bass
import concourse.mybir as mybir
from concourse.bass_utils import run_bass_kernel_spmd

F32 = mybir.dt.float32
BF16 = mybir.dt.bfloat16
I32 = mybir.dt.int32
U32 = mybir.dt.uint32
ALU = mybir.AluOpType
AF = mybir.ActivationFunctionType
AX = mybir.AxisListType

D = 1024
DEPTH = 4
NT = 17
NKS = 17
SEQ = 2048
DEC = 16
NIN = 4936
EPS = 1e-6
NEG = -1e30
TOPK = 256
NBIS = 22


class Res:
    __slots__ = ("t", "w", "r")

    def __init__(self, t=None):
        self.t = t
        self.w = None
        self.r = {}

    def __getitem__(self, k):
        return self.t[k]


class KB:
    def __init__(self, nc, es):
        self.nc = nc
        self.es = es
        self.E = {"pe": nc.tensor, "act": nc.scalar, "dve": nc.vector, "pool": nc.gpsimd, "sp": nc.sync}
        self.sems = {}
        self.cnt = {}
        self.seen = {e: {} for e in self.E}
        self.ninst = 0
        self.dpool = {"sp": ["dsp%d" % i for i in range(24)], "pool": ["dpl%d" % i for i in range(6)]}
        self.dnext = {"sp": 0, "pool": 0}

    def sem(self, key):
        if key not in self.sems:
            self.sems[key] = self.es.enter_context(self.nc.semaphore("s_" + key))
            self.cnt[key] = 0
        return self.sems[key]

    def op(self, eng, fn, r=(), w=(), dma=False):
        deps = {}
        for x in r:
            if x.w is not None:
                k, v = x.w
                if deps.get(k, 0) < v:
                    deps[k] = v
        for x in w:
            if x.w is not None:
                k, v = x.w
                if deps.get(k, 0) < v:
                    deps[k] = v
            for k, v in x.r.items():
                if deps.get(k, 0) < v:
                    deps[k] = v
        E = self.E[eng]
        seen = self.seen[eng]
        for k, v in deps.items():
            if k == "pe" and eng == "pe" and not dma:
                continue
            if seen.get(k, 0) < v:
                E.wait_ge(self.sem(k), v)
                seen[k] = v
        if dma:
            pl = self.dpool[eng]
            key = pl[self.dnext[eng]]
            self.dnext[eng] = (self.dnext[eng] + 1) % len(pl)
            s = self.sem(key)
            prev = self.cnt[key]
            if prev > 0 and seen.get(key, 0) < prev:
                E.wait_ge(s, prev)
                seen[key] = prev
        else:
            key = eng
            s = self.sem(key)
        ins = fn(E)
        inc = 16 if dma else 1
        self.cnt[key] += inc
        c = self.cnt[key]
        ins.then_inc(s, inc)
        for x in r:
            x.r[key] = c
        for x in w:
            x.w = (key, c)
            x.r = {}
        self.ninst += 1
        return ins

    def barrier(self):
        for en, E in self.E.items():
            seen = self.seen[en]
            for k, sm in self.sems.items():
                v = self.cnt[k]
                if v > 0 and seen.get(k, 0) < v:
                    E.wait_ge(sm, v)
                    seen[k] = v

    def finish(self):
        E = self.E["sp"]
        for k, s in self.sems.items():
            if self.cnt[k] > 0:
                E.wait_ge(s, self.cnt[k])


def build_program(depth=DEPTH, tiles=None, do_peer=True):
    if tiles is None:
        tiles = list(range(NT))
    nc = bass.Bass("TRN2", target_bir_lowering=False)
    es = contextlib.ExitStack()
    kb = KB(nc, es)

    def dram(name, shape, dt, kind):
        return nc.dram_tensor(name, shape, dt, kind=kind)

    x_in = dram("x_in", [NT * 128, D], F32, "ExternalInput")
    rope_in = dram("rope", [NT * 128, 64], F32, "ExternalInput")
    cak = dram("cak", [DEPTH, SEQ, 128], F32, "ExternalInput")
    cav = dram("cav", [DEPTH, SEQ, 128], F32, "ExternalInput")
    cik = dram("cik", [DEPTH, SEQ, 64], F32, "ExternalInput")
    cbk = dram("cbk", [DEPTH, SEQ, 512], F32, "ExternalInput")
    cbv = dram("cbv", [DEPTH, SEQ, 512], F32, "ExternalInput")
    n1T = dram("n1T", [DEPTH, 128, 8], F32, "ExternalInput")
    n2T = dram("n2T", [DEPTH, 128, 8], F32, "ExternalInput")
    n2r = dram("n2r", [DEPTH, D], F32, "ExternalInput")
    w_in = dram("w_in", [DEPTH, D, NIN], F32, "ExternalInput")
    qn = dram("qn", [DEPTH, 64], F32, "ExternalInput")
    kn = dram("kn", [DEPTH, 64], F32, "ExternalInput")
    ikn = dram("ikn", [DEPTH, 64], F32, "ExternalInput")
    w_pa = dram("w_pa", [DEPTH, 512, D], F32, "ExternalInput")
    w_pb = dram("w_pb", [DEPTH, 512, D], F32, "ExternalInput")
    w_o = dram("w_o", [DEPTH, D, D], F32, "ExternalInput")
    pwq = dram("pwq", [DEPTH, D, D], F32, "ExternalInput")
    pk1T = dram("pk1T", [DEPTH, 64, 8, 128], F32, "ExternalInput")
    pk2T = dram("pk2T", [DEPTH, 64, 8, 128], F32, "ExternalInput")
    puT = [dram("puT%d" % l, [D, 16384], F32, "ExternalInput") for l in range(DEPTH)]
    pv = [dram("pv%d" % l, [16384, D], F32, "ExternalInput") for l in range(DEPTH)]

    y_p = dram("y_p", [SEQ, D], F32, "ExternalOutput")
    y_s = dram("y_s", [DEC, D], F32, "ExternalOutput")
    ak_p = dram("ak_p", [DEPTH, SEQ, 128], F32, "ExternalOutput")
    av_p = dram("av_p", [DEPTH, SEQ, 128], F32, "ExternalOutput")
    ik_p = dram("ik_p", [DEPTH, SEQ, 64], F32, "ExternalOutput")
    bk_p = dram("bk_p", [DEPTH, SEQ, 512], F32, "ExternalOutput")
    bv_p = dram("bv_p", [DEPTH, SEQ, 512], F32, "ExternalOutput")
    ak_s = dram("ak_s", [DEPTH, DEC, 128], F32, "ExternalOutput")
    av_s = dram("av_s", [DEPTH, DEC, 128], F32, "ExternalOutput")
    ik_s = dram("ik_s", [DEPTH, DEC, 64], F32, "ExternalOutput")
    bk_s = dram("bk_s", [DEPTH, DEC, 512], F32, "ExternalOutput")
    bv_s = dram("bv_s", [DEPTH, DEC, 512], F32, "ExternalOutput")
    xs_dram = [dram("xscr%d" % i, [NT * 128, D], F32, "Internal") for i in range(2)]
    xs_res = [[Res() for _ in range(NT)] for _ in range(2)]
    out_res = Res()
    in_res = Res()

    ARENA_F32 = 52000
    arena = es.enter_context(nc.sbuf_tensor("arena", [128, ARENA_F32], F32))
    aoff = [0]
    DTB = {F32: 4, BF16: 2, I32: 4, U32: 4}

    def sb(name, shape, dt):
        nb = DTB[dt]
        n = 1
        for d_ in shape[1:]:
            n *= d_
        nbytes = (n * nb + 31) // 32 * 32
        o = aoff[0]
        assert o % 4 == 0
        aoff[0] = o + nbytes
        assert aoff[0] <= ARENA_F32 * 4, (name, aoff[0])
        v = arena[0:shape[0], o // 4:(o + nbytes) // 4]
        if dt != F32:
            v = v.bitcast(dt)
        v = v[:, 0:n]
        if len(shape) == 3:
            v = v.rearrange("p (a b) -> p a b", a=shape[1])
        elif len(shape) == 4:
            v = v.rearrange("p (a b c) -> p a b c", a=shape[1], b=shape[2])
        return Res(v)

    def pst(name, shape, dt):
        return Res(es.enter_context(nc.psum_tensor(name, shape, dt)))

    ident = sb("ident", [128, 128], BF16)
    ident4 = sb("ident4", [128, 512], BF16)
    negtri = sb("negtri", [128, 128], BF16)
    mlt = sb("mlt", [128, 128], BF16)
    negm4 = sb("negm4", [128, 512], BF16)
    ones1 = sb("ones1", [128, 1], BF16)
    pow2 = sb("pow2", [128, NBIS], F32)
    iota16 = sb("iota16", [128, 16], F32)
    thr16 = sb("thr16", [128, 16], F32)
    identf = sb("identf", [128, 128], F32)
    iota128 = sb("iota128", [128, 128], F32)
    NCST = 128 * 3 + 512 * 2 + NBIS + 16 + 128
    cst_in = dram("cst", [128, NCST], F32, "ExternalInput")
    gq = sb("gq", [128, 64], F32)
    gk = sb("gk", [128, 64], F32)
    gik = sb("gik", [128, 64], F32)
    n1s = sb("n1s", [128, 8], F32)
    n2s = sb("n2s", [128, 8], F32)
    n2b = sb("n2b", [128, D], F32)
    xt = sb("xt", [128, D], F32)
    hb = sb("hb", [128, D], BF16)
    hT = sb("hT", [128, 8, 128], BF16)
    sq = sb("sq", [128, D], F32)
    st8 = sb("st8", [128, 64], F32)
    STW = 1536
    stg = [sb("stg%d" % i, [128, STW], F32) for i in range(2)]
    mark0 = aoff[0]

    WARENA_A = sb("warenaA", [128, 8 * 2888], BF16)
    kaT = sb("kaT", [64, 2, NKS * 128], BF16)
    kiT = sb("kiT", [64, NKS * 128], BF16)
    kbT = sb("kbT", [64, 8, NKS * 128], BF16)
    vaA = sb("vaA", [128, NKS, 2, 65], BF16)
    vbB = sb("vbB", [128, NKS, 8, 64], BF16)
    kvres = [Res() for _ in range(NKS)]
    cstage = sb("cstage", [128, NCST], F32)
    ropet = sb("ropet", [128, 64], F32)
    pf = sb("pf", [128, 512], F32)
    pf2 = sb("pf2", [128, 512], F32)
    pbf = sb("pbf", [128, 512], BF16)
    r1 = sb("r1", [128, 256], F32)
    r2 = sb("r2", [128, 256], F32)
    qaT = sb("qaT", [64, 8, 128], BF16)
    qiT = sb("qiT", [64, 8, 128], BF16)
    qbT = sb("qbT", [64, 8, 128], BF16)
    wi = sb("wi", [128, 8], F32)
    isc = sb("isc", [128, 2560], F32)
    mbias = sb("mbias", [128, 2560], BF16)
    rl = [sb("rl%d" % i, [128, 512], F32) for i in range(2)]
    PT = [sb("PT%d" % i, [128, 512], BF16) for i in range(2)]
    oa = sb("oa", [128, 512], F32)
    ob = sb("ob", [128, 512], F32)
    ebuf = sb("ebuf", [128, 1024], F32)
    spT = sb("spT", [128, 1024], BF16)
    ET = sb("ET", [128, 1024], BF16)
    dd = sb("dd", [128, 8], F32)
    pvs = sb("pvs", [128, 512], F32)
    bis = sb("bis", [128, 8], F32)
    dtab = sb("dtab", [128, NBIS], F32)
    endA = aoff[0]

    aoff[0] = mark0
    WARENA_B = sb("warenaB", [128, 32768], BF16)
    oabb = sb("oabb", [128, D], BF16)
    sga = sb("sga", [128, D], F32)
    sgb = sb("sgb", [128, D], F32)
    oT = sb("oT", [128, 8, 128], BF16)
    mm = sb("mm", [128, D], F32)
    mbf = sb("mbf", [128, D], BF16)
    oabf = sb("oabf", [128, D], F32)
    endB1 = aoff[0]

    aoff[0] = mark0
    h2T_all = sb("h2T_all", [128, 8, NT * 128], BF16)
    WQ = sb("WQ", [128, 8 * 1024], BF16)
    qsb = sb("qsb", [128, D], BF16)
    qT = sb("qT", [64, 16, 128], BF16)
    k12 = sb("k12", [64, 2, 8, 128], BF16)
    s12 = sb("s12", [128, 2, 8, 128], F32)
    swk = sb("swk", [128, 256], F32)
    v12 = sb("v12", [128, 2, 8, 16], F32)
    i12 = sb("i12", [128, 2, 8, 16], U32)
    i12f = sb("i12f", [128, 2, 8, 16], F32)
    cand = sb("cand", [128, 8, 256], F32)
    sc = sb("sc", [128, 8, 16], F32)
    pos = sb("pos", [128, 8, 16], U32)
    posf = sb("posf", [128, 8, 16], F32)
    posrf = sb("posrf", [128, 8, 16], F32)
    poscf = sb("poscf", [128, 8, 16], F32)
    oh = sb("oh", [128, 8, 16, 16], F32)
    sel1 = sb("sel1", [128, 8, 16], F32)
    sel2 = sb("sel2", [128, 8, 16], F32)
    gsm = sb("gsm", [128, 8, 16], F32)
    TT = sb("TT", [128, 3, 128], F32)
    TT2 = sb("TT2", [128, 3, 128], F32)
    P1q = [sb("P1q%d" % i, [128, 32, 128], BF16) for i in range(2)]
    P2q = [sb("P2q%d" % i, [128, 32, 128], BF16) for i in range(2)]
    Gs = sb("Gs", [128, 128, 128], BF16)
    endB2 = aoff[0]

    aoff[0] = mark0
    h2T_all_c = sb("h2T_all_c", [128, 8, NT * 128], BF16)
    accs = sb("accs", [128, NT, D], F32)
    accres = [Res() for _ in range(NT)]
    NBLK = 4
    u16 = [sb("u16_%d" % i, [128, 8, NBLK * 128], BF16) for i in range(2)]
    v16 = [sb("v16_%d" % i, [128, NBLK, D], BF16) for i in range(2)]
    Gc = [sb("Gc%d" % i, [128, 2, NBLK, 128], BF16) for i in range(2)]
    gel = [sb("gel%d" % i, [128, 256], BF16) for i in range(2)]
    cfT = [sb("cfT%d" % i, [128, 256], BF16) for i in range(2)]
    cstg = [sb("cstg%d" % i, [128, 512], F32) for i in range(12)]
    cstg += [Res(stg[i][:, k * 512:(k + 1) * 512]) for i in range(2) for k in range(2)]
    NCS = len(cstg)
    assert NCS == 16
    endC = aoff[0]
    print("arena bytes: persistent", mark0, "A", endA, "B1", endB1, "B2", endB2, "C", endC, "cap", ARENA_F32 * 4)

    Gd = dram("Gd", [NT, 128, 16384], BF16, "Internal")
    gdres = [Res() for _ in range(NT)]
    xmid = dram("xmid", [NT * 128, D], F32, "Internal")
    xmres = [Res() for _ in range(NT)]
    oab_dram = dram("oabscr", [NT * 128, D], F32, "Internal")
    oab_res = [Res() for _ in range(NT)]

    P = [pst("ps%d" % i, [128, 512], F32) for i in range(7)]
    PTR = pst("ptr", [128, 1024], BF16)

    op = kb.op

    op("sp", lambda e: e.dma_start(out=cstage[:], in_=cst_in[:, :]), r=[in_res], w=[cstage], dma=True)
    o = 0
    for dst, wdt in ((ident, 128), (negtri, 128), (mlt, 128), (ident4, 512), (negm4, 512)):
        op("dve", lambda e, dst=dst, o=o, wdt=wdt: e.tensor_copy(out=dst[:], in_=cstage[:, o:o + wdt]), r=[cstage], w=[dst])
        o += wdt
    op("dve", lambda e, o=o: e.tensor_copy(out=pow2[:], in_=cstage[:, o:o + NBIS]), r=[cstage], w=[pow2])
    o += NBIS
    op("dve", lambda e, o=o: e.tensor_copy(out=iota16[:], in_=cstage[:, o:o + 16]), r=[cstage], w=[iota16])
    o += 16
    op("dve", lambda e, o=o: e.tensor_copy(out=iota128[:], in_=cstage[:, o:o + 128]), r=[cstage], w=[iota128])
    op("dve", lambda e: e.tensor_copy(out=identf[:], in_=cstage[:, 0:128]), r=[cstage], w=[identf])
    op("dve", lambda e: e.memset(ones1[:], 1.0), w=[ones1])
    op("dve", lambda e: e.tensor_scalar(out=thr16[:], in0=iota16[:], scalar1=16.0, scalar2=16.0, op0=ALU.mult, op1=ALU.add), r=[iota16], w=[thr16])
    op("dve", lambda e: e.memset(thr16[:, 15:16], 1e9), w=[thr16])
    op("dve", lambda e: e.memset(vaA[:], 1.0), w=[vaA] + kvres)
    kb.barrier()

    def transpose_blocks(src, nblk, dstT, dst_res, ptile=PTR):
        for b in range(nblk):
            op("pe", lambda e, b=b: e.transpose(out=ptile[0:64, b * 128:(b + 1) * 128], in_=src[:, b * 64:(b + 1) * 64], identity=ident[:]),
               r=[src, ident], w=[ptile])
        op("act", lambda e: e.activation(out=dstT, in_=ptile[0:64, 0:nblk * 128].rearrange("p (b n) -> p b n", b=nblk), func=AF.Copy),
           r=[ptile], w=dst_res)

    def rmsnorm_rows(xres, outbf, scratch):
        op("act", lambda e: e.activation(out=scratch[:], in_=xres[:], func=AF.Square, accum_out=st8[:, 0:1]), r=[xres], w=[scratch, st8])
        op("dve", lambda e: e.tensor_scalar(out=st8[:, 1:2], in0=st8[:, 0:1], scalar1=1.0 / D, scalar2=EPS, op0=ALU.mult, op1=ALU.add), r=[st8], w=[st8])
        op("act", lambda e: e.activation(out=st8[:, 2:3], in_=st8[:, 1:2], func=AF.Sqrt), r=[st8], w=[st8])
        op("dve", lambda e: e.reciprocal(out=st8[:, 3:4], in_=st8[:, 2:3]), r=[st8], w=[st8])
        op("dve", lambda e: e.tensor_scalar(out=outbf[:], in0=xres[:], scalar1=st8[:, 3:4], scalar2=None, op0=ALU.mult), r=[xres, st8], w=[outbf])

    def make_hT():
        for c in range(8):
            op("pe", lambda e, c=c: e.transpose(out=PTR[:, c * 128:(c + 1) * 128], in_=hb[:, c * 128:(c + 1) * 128], identity=ident[:]),
               r=[hb, ident], w=[PTR])
        op("act", lambda e: e.activation(out=hT[:], in_=PTR[:, :].rearrange("p (c n) -> p c n", c=8), func=AF.Copy), r=[PTR], w=[hT])

    def headnorm(src_res, src_ap, H, gain, dst):
        W = H * 64
        op("act", lambda e: e.activation(out=sq[:, 0:W], in_=src_ap, func=AF.Square), r=[src_res], w=[sq])
        op("dve", lambda e: e.tensor_reduce(out=st8[:, 8:8 + H], in_=sq[:, 0:W].rearrange("p (h d) -> p h d", h=H), axis=AX.X, op=ALU.add), r=[sq], w=[st8])
        op("dve", lambda e: e.tensor_scalar(out=st8[:, 16:16 + H], in0=st8[:, 8:8 + H], scalar1=1.0 / 64, scalar2=EPS, op0=ALU.mult, op1=ALU.add), r=[st8], w=[st8])
        op("act", lambda e: e.activation(out=st8[:, 24:24 + H], in_=st8[:, 16:16 + H], func=AF.Sqrt), r=[st8], w=[st8])
        op("dve", lambda e: e.reciprocal(out=st8[:, 32:32 + H], in_=st8[:, 24:24 + H]), r=[st8], w=[st8])
        d3 = dst[:, 0:W].rearrange("p (h d) -> p h d", h=H)
        op("dve", lambda e: e.tensor_tensor(out=d3, in0=src_ap.rearrange("p (h d) -> p h d", h=H),
                                            in1=st8[:, 32:32 + H].unsqueeze(2).to_broadcast([128, H, 64]), op=ALU.mult), r=[st8, src_res], w=[dst])
        op("dve", lambda e: e.tensor_tensor(out=d3, in0=d3, in1=gain[:, :].unsqueeze(1).to_broadcast([128, H, 64]), op=ALU.mult), r=[gain, dst], w=[dst])

    def rope(src, H, dst, scale=None):
        W = H * 64
        s3 = src[:, 0:W].rearrange("p (h d) -> p h d", h=H)
        d3 = dst[:, 0:W].rearrange("p (h d) -> p h d", h=H)
        cosb = ropet[:, 0:32].unsqueeze(1).to_broadcast([128, H, 32])
        sinb = ropet[:, 32:64].unsqueeze(1).to_broadcast([128, H, 32])
        a3 = r1[:, 0:H * 32].rearrange("p (h d) -> p h d", h=H)
        b3 = r2[:, 0:H * 32].rearrange("p (h d) -> p h d", h=H)
        op("dve", lambda e: e.tensor_tensor(out=a3, in0=s3[:, :, 0:32], in1=cosb, op=ALU.mult), r=[src, ropet], w=[r1])
        op("dve", lambda e: e.tensor_tensor(out=b3, in0=s3[:, :, 32:64], in1=sinb, op=ALU.mult), r=[src, ropet], w=[r2])
        op("dve", lambda e: e.tensor_tensor(out=d3[:, :, 0:32], in0=a3, in1=b3, op=ALU.subtract), r=[r1, r2], w=[dst])
        op("dve", lambda e: e.tensor_tensor(out=a3, in0=s3[:, :, 32:64], in1=cosb, op=ALU.mult), r=[src, ropet], w=[r1])
        op("dve", lambda e: e.tensor_tensor(out=b3, in0=s3[:, :, 0:32], in1=sinb, op=ALU.mult), r=[src, ropet], w=[r2])
        op("dve", lambda e: e.tensor_tensor(out=d3[:, :, 32:64], in0=a3, in1=b3, op=ALU.add), r=[r1, r2], w=[dst])

    def load_cast_rows(wres, dram_ap_fn, nchunks, width, dst_fn, scale_res=None, scale_col=None):
        i = 0
        for c in range(nchunks):
            for o0 in range(0, width, STW):
                wdt = min(STW, width - o0)
                s = stg[i % 2]
                i += 1
                op("sp", lambda e, c=c, o0=o0, wdt=wdt, s=s: e.dma_start(out=s[:, 0:wdt], in_=dram_ap_fn(c, o0, wdt)), r=[in_res], w=[s], dma=True)
                if scale_res is not None:
                    op("dve", lambda e, c=c, o0=o0, wdt=wdt, s=s: e.tensor_scalar(out=dst_fn(c, o0, wdt), in0=s[:, 0:wdt], scalar1=scale_res[:, scale_col(c):scale_col(c) + 1], scalar2=None, op0=ALU.mult),
                       r=[s, scale_res], w=[wres])
                else:
                    op("pool", lambda e, c=c, o0=o0, wdt=wdt, s=s: e.tensor_copy(out=dst_fn(c, o0, wdt), in_=s[:, 0:wdt]), r=[s], w=[wres])

    WAA = WARENA_A.t
    WA = WARENA_B.t
    NQKV = 2888
    wa_qkv = lambda c, o0, wdt: WAA[:, c * NQKV + o0: c * NQKV + o0 + wdt]
    GOFF = 0
    PAOFF = 8 * 2048
    PBOFF = PAOFF + 4 * 1024
    WOOFF = PBOFF + 4 * 1024
    PQOFF = WOOFF + 8 * 1024

    for l in range(depth):
        xin_d = x_in if l == 0 else xs_dram[(l - 1) % 2]
        xin_r = [in_res] * NT if l == 0 else xs_res[(l - 1) % 2]
        xout_d = xs_dram[l % 2]
        xout_r = xs_res[l % 2]
        last = (l == depth - 1)

        op("sp", lambda e: e.dma_start(out=n1s[:], in_=n1T[l, :, :]), r=[in_res], w=[n1s], dma=True)
        op("sp", lambda e: e.dma_start(out=n2s[:], in_=n2T[l, :, :]), r=[in_res], w=[n2s], dma=True)
        op("sp", lambda e: e.dma_start(out=gq[:], in_=qn[l, :].partition_broadcast(128)), r=[in_res], w=[gq], dma=True)
        op("sp", lambda e: e.dma_start(out=gk[:], in_=kn[l, :].partition_broadcast(128)), r=[in_res], w=[gk], dma=True)
        op("sp", lambda e: e.dma_start(out=gik[:], in_=ikn[l, :].partition_broadcast(128)), r=[in_res], w=[gik], dma=True)
        op("sp", lambda e: e.dma_start(out=n2b[:], in_=n2r[l, :].partition_broadcast(128)), r=[in_res], w=[n2b], dma=True)

        kb.barrier()
        if l > 0:
            op("pool", lambda e: e.memset(vaA[:], 1.0), w=[vaA] + kvres)
        load_cast_rows(WARENA_A, lambda c, o0, wdt: w_in[l, c * 128:(c + 1) * 128, o0:o0 + wdt], 8, NQKV, wa_qkv, n1s, lambda c: c)

        for t in tiles:
            samp = (t == NT - 1)
            nk = t + 1
            if samp:
                for k0 in range(0, 16, 4):
                    for kk in range(k0, k0 + 4):
                        rows = slice(kk * 128, (kk + 1) * 128)
                        s = stg[kk % 2]
                        op("sp", lambda e, s=s, rows=rows: e.dma_start(out=s[:, 0:128], in_=cak[l, rows, :]), r=[in_res], w=[s], dma=True)
                        op("sp", lambda e, s=s, rows=rows: e.dma_start(out=s[:, 128:192], in_=cik[l, rows, :]), r=[in_res], w=[s], dma=True)
                        op("sp", lambda e, s=s, rows=rows: e.dma_start(out=s[:, 192:320], in_=cav[l, rows, :]), r=[in_res], w=[s], dma=True)
                        op("sp", lambda e, s=s, rows=rows: e.dma_start(out=s[:, 512:1024], in_=cbk[l, rows, :]), r=[in_res], w=[s], dma=True)
                        op("sp", lambda e, s=s, rows=rows: e.dma_start(out=s[:, 1024:1536], in_=cbv[l, rows, :]), r=[in_res], w=[s], dma=True)
                        op("dve", lambda e, s=s: e.tensor_copy(out=pbf[:, 0:192], in_=s[:, 0:192]), r=[s], w=[pbf])
                        for b in range(3):
                            op("pe", lambda e, b=b: e.transpose(out=PTR[0:64, b * 128:(b + 1) * 128], in_=pbf[:, b * 64:(b + 1) * 64], identity=ident[:]), r=[pbf, ident], w=[PTR])
                        ks = slice(kk * 128, (kk + 1) * 128)
                        op("act", lambda e, ks=ks: e.activation(out=kaT[:, :, ks], in_=PTR[0:64, 0:256].rearrange("p (b n) -> p b n", b=2), func=AF.Copy), r=[PTR], w=[kvres[kk]])
                        op("act", lambda e, ks=ks: e.activation(out=kiT[:, ks], in_=PTR[0:64, 256:384], func=AF.Copy), r=[PTR], w=[kvres[kk]])
                        op("dve", lambda e, s=s, kk=kk: e.tensor_copy(out=vaA[:, kk, :, 0:64], in_=s[:, 192:320].rearrange("p (g d) -> p g d", g=2)), r=[s], w=[kvres[kk]])
                        op("dve", lambda e, s=s: e.tensor_copy(out=pbf[:, 0:512], in_=s[:, 512:1024]), r=[s], w=[pbf])
                        for b in range(8):
                            op("pe", lambda e, b=b: e.transpose(out=PTR[0:64, b * 128:(b + 1) * 128], in_=pbf[:, b * 64:(b + 1) * 64], identity=ident[:]), r=[pbf, ident], w=[PTR])
                        op("act", lambda e, ks=ks: e.activation(out=kbT[:, :, ks], in_=PTR[0:64, :].rearrange("p (b n) -> p b n", b=8), func=AF.Copy), r=[PTR], w=[kvres[kk]])
                        op("pool", lambda e, s=s, kk=kk: e.tensor_copy(out=vbB[:, kk, :, :], in_=s[:, 1024:1536].rearrange("p (h d) -> p h d", h=8)), r=[s], w=[kvres[kk]])

            rows = slice(t * 128, (t + 1) * 128)
            op("sp", lambda e: e.dma_start(out=xt[:], in_=xin_d[rows, :]), r=[xin_r[t]], w=[xt], dma=True)
            op("sp", lambda e: e.dma_start(out=ropet[:], in_=rope_in[rows, :]), r=[in_res], w=[ropet], dma=True)
            rmsnorm_rows(xt, hb, sq)
            make_hT()

            def proj(pt, c0, wdt):
                for c in range(8):
                    op("pe", lambda e, c=c: e.matmul(pt[:, 0:wdt], lhsT=hT[:, c, :], rhs=WAA[:, c * NQKV + c0: c * NQKV + c0 + wdt], start=(c == 0), stop=(c == 7)),
                       r=[hT, WARENA_A], w=[pt])

            def out_rows(dst_p, dst_s, src, wdt):
                if samp:
                    op("sp", lambda e: e.dma_start(out=dst_s[l, :, :], in_=src[0:DEC, 0:wdt]), r=[src], w=[out_res], dma=True)
                else:
                    op("sp", lambda e: e.dma_start(out=dst_p[l, rows, :], in_=src[:, 0:wdt]), r=[src], w=[out_res], dma=True)

            ks = slice(t * 128, (t + 1) * 128)
            proj(P[0], 0, 512)
            headnorm(P[0], P[0][:, 0:512], 8, gq, pf)
            rope(pf, 8, pf2)
            op("dve", lambda e: e.tensor_scalar(out=pbf[:], in0=pf2[:], scalar1=0.125, scalar2=None, op0=ALU.mult), r=[pf2], w=[pbf])
            transpose_blocks(pbf, 8, qaT[:], [qaT])
            proj(P[1], 512, 256)
            headnorm(P[1], P[1][:, 0:128], 2, gk, pf)
            rope(pf, 2, pf2)
            out_rows(ak_p, ak_s, pf2, 128)
            op("dve", lambda e: e.tensor_copy(out=pbf[:, 0:128], in_=pf2[:, 0:128]), r=[pf2], w=[pbf])
            op("act", lambda e: e.activation(out=pf[:, 0:128], in_=P[1][:, 128:256], func=AF.Copy), r=[P[1]], w=[pf])
            out_rows(av_p, av_s, pf, 128)
            op("dve", lambda e: e.tensor_copy(out=vaA[:, t, :, 0:64], in_=pf[:, 0:128].rearrange("p (g d) -> p g d", g=2)), r=[pf], w=[kvres[t]])
            transpose_blocks(pbf, 2, kaT[:, :, ks], [kvres[t]])
            proj(P[0], 768, 512)
            rope(P[0], 8, pf2)
            op("dve", lambda e: e.tensor_copy(out=pbf[:], in_=pf2[:]), r=[pf2], w=[pbf])
            transpose_blocks(pbf, 8, qiT[:], [qiT])
            proj(P[1], 1280, 72)
            headnorm(P[1], P[1][:, 0:64], 1, gik, pf)
            rope(pf, 1, pf2)
            out_rows(ik_p, ik_s, pf2, 64)
            op("dve", lambda e: e.tensor_copy(out=pbf[:, 0:64], in_=pf2[:, 0:64]), r=[pf2], w=[pbf])
            op("act", lambda e: e.activation(out=wi[:], in_=P[1][:, 64:72], func=AF.Copy), r=[P[1]], w=[wi])
            for b in range(1):
                op("pe", lambda e: e.transpose(out=PTR[0:64, 0:128], in_=pbf[:, 0:64], identity=ident[:]), r=[pbf, ident], w=[PTR])
            op("act", lambda e: e.activation(out=kiT[:, ks], in_=PTR[0:64, 0:128], func=AF.Copy), r=[PTR], w=[kvres[t]])
            proj(P[0], 1352, 512)
            op("act", lambda e: e.activation(out=pbf[:], in_=P[0][:, :], func=AF.Copy, scale=0.125), r=[P[0]], w=[pbf])
            transpose_blocks(pbf, 8, qbT[:], [qbT])
            proj(P[1], 1864, 512)
            op("act", lambda e: e.activation(out=pf[:], in_=P[1][:, :], func=AF.Copy), r=[P[1]], w=[pf])
            out_rows(bk_p, bk_s, pf, 512)
            op("dve", lambda e: e.tensor_copy(out=pbf[:], in_=pf[:]), r=[pf], w=[pbf])
            transpose_blocks(pbf, 8, kbT[:, :, ks], [kvres[t]])
            proj(P[0], 2376, 512)
            op("act", lambda e: e.activation(out=pf2[:], in_=P[0][:, :], func=AF.Copy), r=[P[0]], w=[pf2])
            out_rows(bv_p, bv_s, pf2, 512)
            op("dve", lambda e: e.tensor_copy(out=vbB[:, t, :, :], in_=pf2[:, :].rearrange("p (h d) -> p h d", h=8)), r=[pf2], w=[kvres[t]])

            S = nk * 128
            nblk = (S + 511) // 512
            kvr = [kvres[i] for i in range(nk)]
            for bi in range(nblk):
                c0 = bi * 512
                wdt = min(512, S - c0)
                for h in range(8):
                    pt = P[2 + (h % 2)]
                    rb = rl[h % 2]
                    op("pe", lambda e, h=h, pt=pt: e.matmul(pt[:, 0:wdt], lhsT=qiT[:, h, :], rhs=kiT[:, c0:c0 + wdt], start=True, stop=True), r=[qiT] + kvr, w=[pt])
                    op("act", lambda e, pt=pt, rb=rb: e.activation(out=rb[:, 0:wdt], in_=pt[:, 0:wdt], func=AF.Relu, scale=0.125 * (8 ** -0.5)), r=[pt], w=[rb])
                    if h == 0:
                        op("dve", lambda e, rb=rb: e.tensor_scalar(out=isc[:, c0:c0 + wdt], in0=rb[:, 0:wdt], scalar1=wi[:, 0:1], scalar2=None, op0=ALU.mult), r=[rb, wi], w=[isc])
                    else:
                        op("dve", lambda e, rb=rb, h=h: e.scalar_tensor_tensor(out=isc[:, c0:c0 + wdt], in0=rb[:, 0:wdt], scalar=wi[:, h:h + 1], in1=isc[:, c0:c0 + wdt], op0=ALU.mult, op1=ALU.add), r=[rb, wi, isc], w=[isc])
            Z = [P[2], P[3]]
            Z2 = [P[4], P[5]]
            PVb = P[6]
            TSb = P[0]
            for kt in range(nk):
                ksl = slice(kt * 128, (kt + 1) * 128)
                diag = (kt == nk - 1)
                for h in range(8):
                    op("pe", lambda e, h=h: e.matmul(Z[h // 4][:, (h % 4) * 128:(h % 4 + 1) * 128], lhsT=kbT[:, h, ksl], rhs=qbT[:, h, :], start=True, stop=True), r=[kvres[kt], qbT], w=[Z[h // 4]])
                for hf in range(2):
                    op("act", lambda e, hf=hf: e.activation(out=ebuf[:, hf * 512:(hf + 1) * 512], in_=Z[hf][:, :], func=AF.Exp), r=[Z[hf]], w=[ebuf])
                    op("act", lambda e, hf=hf: e.activation(out=spT[:, hf * 512:(hf + 1) * 512], in_=ebuf[:, hf * 512:(hf + 1) * 512], func=AF.Ln, bias=1.0), r=[ebuf], w=[spT])
                if diag:
                    s3 = spT[:, :].rearrange("p (h q) -> p h q", h=8)
                    op("pool", lambda e, s3=s3: e.tensor_tensor(out=s3, in0=s3, in1=mlt[:, :].unsqueeze(1).to_broadcast([128, 8, 128]), op=ALU.mult), r=[spT, mlt], w=[spT])
                for hf in range(2):
                    op("pe", lambda e, hf=hf: e.matmul(Z2[hf][:, :], lhsT=negtri[:], rhs=spT[:, hf * 512:(hf + 1) * 512], start=True, stop=False), r=[negtri, spT], w=[Z2[hf]])
                    if diag:
                        op("pe", lambda e, hf=hf: e.matmul(Z2[hf][:, :], lhsT=ident[:], rhs=negm4[:], start=False, stop=False), r=[ident, negm4], w=[Z2[hf]])
                    for hh in range(4):
                        h = hf * 4 + hh
                        op("pe", lambda e, h=h, hh=hh, hf=hf: e.matmul(Z2[hf][:, hh * 128:(hh + 1) * 128], lhsT=kbT[:, h, ksl], rhs=qbT[:, h, :], start=False, stop=(hh == 3)), r=[kvres[kt], qbT], w=[Z2[hf]])
                    op("act", lambda e, hf=hf: e.activation(out=ET[:, hf * 512:(hf + 1) * 512], in_=Z2[hf][:, :], func=AF.Exp), r=[Z2[hf]], w=[ET])
                for h in range(8):
                    op("pe", lambda e, h=h: e.matmul(TSb[:, h:h + 1], lhsT=spT[:, h * 128:(h + 1) * 128], rhs=ones1[:], start=True, stop=True), r=[spT, ones1], w=[TSb])
                for h in range(8):
                    op("pe", lambda e, h=h, kt=kt: e.matmul(PVb[:, h * 64:(h + 1) * 64], lhsT=ET[:, h * 128:(h + 1) * 128], rhs=vbB[:, kt, h, :], start=True, stop=True), r=[ET, kvres[kt]], w=[PVb])
                if kt == 0:
                    op("act", lambda e: e.activation(out=ob[:], in_=PVb[:, :], func=AF.Copy), r=[PVb], w=[ob])
                else:
                    op("act", lambda e: e.activation(out=dd[:], in_=TSb[:, 0:8], func=AF.Exp, scale=-1.0), r=[TSb], w=[dd])
                    o3 = ob[:, :].rearrange("p (h d) -> p h d", h=8)
                    op("act", lambda e: e.activation(out=pvs[:], in_=PVb[:, :], func=AF.Copy), r=[PVb], w=[pvs])
                    op("pool", lambda e, o3=o3: e.tensor_tensor(out=o3, in0=o3, in1=dd[:, :].unsqueeze(2).to_broadcast([128, 8, 64]), op=ALU.mult), r=[ob, dd], w=[ob])
                    op("pool", lambda e: e.tensor_tensor(out=ob[:], in0=ob[:], in1=pvs[:], op=ALU.add), r=[ob, pvs], w=[ob])
            op("sp", lambda e: e.dma_start(out=oab_dram[rows, 512:1024], in_=ob[:]), r=[ob], w=[oab_res[t]], dma=True)

            need_thr = nk > 2
            if need_thr:
                op("dve", lambda e: e.tensor_reduce(out=bis[:, 5:6], in_=isc[:, 0:S], axis=AX.X, op=ALU.max), r=[isc], w=[bis])
                op("dve", lambda e: e.tensor_reduce(out=bis[:, 6:7], in_=isc[:, 0:S], axis=AX.X, op=ALU.min), r=[isc], w=[bis])
                op("dve", lambda e: e.tensor_scalar(out=bis[:, 6:7], in0=bis[:, 6:7], scalar1=-1.0, scalar2=None, op0=ALU.mult), r=[bis], w=[bis])
                op("dve", lambda e: e.tensor_tensor(out=bis[:, 4:5], in0=bis[:, 5:6], in1=bis[:, 6:7], op=ALU.max), r=[bis], w=[bis])
                op("dve", lambda e: e.tensor_scalar(out=bis[:, 0:1], in0=bis[:, 4:5], scalar1=-1.0, scalar2=None, op0=ALU.mult), r=[bis], w=[bis])
                op("dve", lambda e: e.tensor_scalar(out=dtab[:], in0=pow2[:], scalar1=bis[:, 4:5], scalar2=2.002, op0=ALU.mult, op1=ALU.mult), r=[bis, pow2], w=[dtab])
            if samp:
                op("dve", lambda e: e.memset(isc[:, 16 * 128 + DEC:17 * 128], NEG), r=[], w=[isc])
            else:
                op("dve", lambda e: e.memset(isc[0:64, t * 128 + 64:(t + 1) * 128], NEG), r=[], w=[isc])
            if need_thr:
                for k in range(NBIS):
                    op("dve", lambda e, k=k: e.tensor_tensor(out=bis[:, 1:2], in0=bis[:, 0:1], in1=dtab[:, k:k + 1], op=ALU.add), r=[bis, dtab], w=[bis])
                    op("dve", lambda e: e.tensor_scalar(out=mbias[:, 0:S], in0=isc[:, 0:S], scalar1=bis[:, 1:2], scalar2=None, op0=ALU.is_ge, op1=ALU.add, accum_out=bis[:, 2:3]), r=[isc, bis], w=[mbias, bis])
                    op("dve", lambda e, k=k: e.scalar_tensor_tensor(out=bis[:, 3:4], in0=bis[:, 2:3], scalar=float(TOPK), in1=dtab[:, k:k + 1], op0=ALU.is_ge, op1=ALU.mult), r=[bis, dtab], w=[bis])
                    op("dve", lambda e: e.tensor_tensor(out=bis[:, 0:1], in0=bis[:, 0:1], in1=bis[:, 3:4], op=ALU.add), r=[bis], w=[bis])
            else:
                op("dve", lambda e: e.memset(bis[:, 0:1], -1e29), r=[], w=[bis])
            op("dve", lambda e: e.tensor_scalar(out=mbias[:, 0:S], in0=isc[:, 0:S], scalar1=bis[:, 0:1], scalar2=-30000.0, op0=ALU.is_lt, op1=ALU.mult), r=[isc, bis], w=[mbias])
            OAp = [P[4], P[5]]
            for kt in range(nk):
                ksl = slice(kt * 128, (kt + 1) * 128)
                for g in range(2):
                    pt = P[2 + g]
                    pb_ = PT[g]
                    op("pe", lambda e, g=g, pt=pt, ksl=ksl: e.matmul(pt[:, :], lhsT=kaT[:, g, ksl], rhs=qaT[:, 4 * g:4 * g + 4, :], start=True, stop=False), r=[kvres[kt], qaT], w=[pt])
                    op("pe", lambda e, pt=pt, ksl=ksl: e.matmul(pt[:, :], lhsT=mbias[:, ksl], rhs=ident4[:], start=False, stop=True), r=[mbias, ident4], w=[pt])
                    op("act", lambda e, pt=pt, pb_=pb_: e.activation(out=pb_[:], in_=pt[:, :], func=AF.Exp), r=[pt], w=[pb_])
                    for hh in range(4):
                        op("pe", lambda e, g=g, hh=hh, pb_=pb_, kt=kt: e.matmul(OAp[g][:, hh * 65:(hh + 1) * 65], lhsT=pb_[:, hh * 128:(hh + 1) * 128], rhs=vaA[:, kt, g, :], start=(kt == 0 and hh == 0), stop=(kt == nk - 1)),
                           r=[pb_, kvres[kt]], w=[OAp[g]])
            for g in range(2):
                o3 = OAp[g][:, 0:260].rearrange("p (h d) -> p h d", h=4)
                op("dve", lambda e, g=g, o3=o3: e.reciprocal(out=st8[:, 40 + 4 * g:44 + 4 * g], in_=o3[:, :, 64]), r=[OAp[g]], w=[st8])
                op("dve", lambda e, g=g, o3=o3: e.tensor_tensor(out=oa[:, g * 256:(g + 1) * 256].rearrange("p (h d) -> p h d", h=4), in0=o3[:, :, 0:64],
                                                                 in1=st8[:, 40 + 4 * g:44 + 4 * g].unsqueeze(2).to_broadcast([128, 4, 64]), op=ALU.mult), r=[OAp[g], st8], w=[oa])
            op("sp", lambda e: e.dma_start(out=oab_dram[rows, 0:512], in_=oa[:]), r=[oa], w=[oab_res[t]], dma=True)

        kb.barrier()
        load_cast_rows(WARENA_B, lambda c, o0, wdt: w_in[l, c * 128:(c + 1) * 128, 2888 + o0:2888 + o0 + wdt], 8, 2048,
                       lambda c, o0, wdt: WA[:, GOFF + c * 2048 + o0: GOFF + c * 2048 + o0 + wdt], n1s, lambda c: c)
        load_cast_rows(WARENA_B, lambda c, o0, wdt: w_pa[l, c * 128:(c + 1) * 128, o0:o0 + wdt], 4, 1024,
                       lambda c, o0, wdt: WA[:, PAOFF + c * 1024 + o0: PAOFF + c * 1024 + o0 + wdt])
        load_cast_rows(WARENA_B, lambda c, o0, wdt: w_pb[l, c * 128:(c + 1) * 128, o0:o0 + wdt], 4, 1024,
                       lambda c, o0, wdt: WA[:, PBOFF + c * 1024 + o0: PBOFF + c * 1024 + o0 + wdt])
        load_cast_rows(WARENA_B, lambda c, o0, wdt: w_o[l, c * 128:(c + 1) * 128, o0:o0 + wdt], 8, 1024,
                       lambda c, o0, wdt: WA[:, WOOFF + c * 1024 + o0: WOOFF + c * 1024 + o0 + wdt])
        for t in tiles:
            samp = (t == NT - 1)
            rows = slice(t * 128, (t + 1) * 128)
            op("sp", lambda e: e.dma_start(out=xt[:], in_=xin_d[rows, :]), r=[xin_r[t]], w=[xt], dma=True)
            rmsnorm_rows(xt, hb, sq)
            make_hT()
            for gi, gdst in ((0, sga), (1, sgb)):
                for hf in range(2):
                    pt = P[hf]
                    c0 = GOFF + gi * 1024 + hf * 512
                    for c in range(8):
                        op("pe", lambda e, c=c, pt=pt, c0=c0: e.matmul(pt[:, :], lhsT=hT[:, c, :], rhs=WA[:, c * 2048 + c0: c * 2048 + c0 + 512], start=(c == 0), stop=(c == 7)), r=[hT, WARENA_B], w=[pt])
                    op("act", lambda e, pt=pt, gdst=gdst, hf=hf: e.activation(out=gdst[:, hf * 512:(hf + 1) * 512], in_=pt[:, :], func=AF.Sigmoid), r=[pt], w=[gdst])
            op("sp", lambda e: e.dma_start(out=oabf[:], in_=oab_dram[rows, :]), r=[oab_res[t]], w=[oabf], dma=True)
            op("pool", lambda e: e.tensor_copy(out=oabb[:], in_=oabf[:]), r=[oabf], w=[oabb])
            for bi, (woff, gdst) in enumerate(((PAOFF, sga), (PBOFF, sgb))):
                for c in range(4):
                    op("pe", lambda e, c=c, bi=bi: e.transpose(out=PTR[:, c * 128:(c + 1) * 128], in_=oabb[:, bi * 512 + c * 128: bi * 512 + (c + 1) * 128], identity=ident[:]), r=[oabb, ident], w=[PTR])
                op("act", lambda e: e.activation(out=oT[:, 0:4, :], in_=PTR[:, 0:512].rearrange("p (c n) -> p c n", c=4), func=AF.Copy), r=[PTR], w=[oT])
                for hf in range(2):
                    pt = P[2 + hf]
                    for c in range(4):
                        op("pe", lambda e, c=c, pt=pt, hf=hf, woff=woff: e.matmul(pt[:, :], lhsT=oT[:, c, :], rhs=WA[:, woff + c * 1024 + hf * 512: woff + c * 1024 + (hf + 1) * 512], start=(c == 0), stop=(c == 3)), r=[oT, WARENA_B], w=[pt])
                    if bi == 0:
                        op("dve", lambda e, pt=pt, hf=hf: e.tensor_tensor(out=mm[:, hf * 512:(hf + 1) * 512], in0=sga[:, hf * 512:(hf + 1) * 512], in1=pt[:, :], op=ALU.mult), r=[sga, pt], w=[mm])
                    else:
                        op("dve", lambda e, pt=pt, hf=hf: e.tensor_tensor(out=sgb[:, hf * 512:(hf + 1) * 512], in0=sgb[:, hf * 512:(hf + 1) * 512], in1=pt[:, :], op=ALU.mult), r=[sgb, pt], w=[sgb])
            op("dve", lambda e: e.tensor_tensor(out=mbf[:], in0=mm[:], in1=sgb[:], op=ALU.add), r=[mm, sgb], w=[mbf])
            for c in range(8):
                op("pe", lambda e, c=c: e.transpose(out=PTR[:, c * 128:(c + 1) * 128], in_=mbf[:, c * 128:(c + 1) * 128], identity=ident[:]), r=[mbf, ident], w=[PTR])
            op("act", lambda e: e.activation(out=oT[:], in_=PTR[:, :].rearrange("p (c n) -> p c n", c=8), func=AF.Copy), r=[PTR], w=[oT])
            for hf in range(2):
                pt = P[4 + hf]
                for c in range(8):
                    op("pe", lambda e, c=c, pt=pt, hf=hf: e.matmul(pt[:, :], lhsT=oT[:, c, :], rhs=WA[:, WOOFF + c * 1024 + hf * 512: WOOFF + c * 1024 + (hf + 1) * 512], start=(c == 0), stop=(c == 7)), r=[oT, WARENA_B], w=[pt])
                op("dve", lambda e, pt=pt, hf=hf: e.tensor_tensor(out=xt[:, hf * 512:(hf + 1) * 512], in0=xt[:, hf * 512:(hf + 1) * 512], in1=pt[:, :], op=ALU.add), r=[xt, pt], w=[xt])

            op("sp", lambda e: e.dma_start(out=xmid[rows, :], in_=xt[:]), r=[xt], w=[xmres[t]], dma=True)

        if do_peer:
            kb.barrier()
            load_cast_rows(WQ, lambda c, o0, wdt: pwq[l, c * 128:(c + 1) * 128, o0:o0 + wdt], 8, 1024,
                           lambda c, o0, wdt: WQ[:, c * 1024 + o0: c * 1024 + o0 + wdt], n2s, lambda c: c)
            for which, src in ((0, pk1T), (1, pk2T)):
                s_ = stg[which]
                op("sp", lambda e, s_=s_, src=src: e.dma_start(out=s_[0:64, 0:1024], in_=src[l, :, :, :].rearrange("d h k -> d (h k)")), r=[in_res], w=[s_], dma=True)
                op("dve", lambda e, s_=s_, which=which: e.tensor_copy(out=k12[:, which, :, :], in_=s_[0:64, 0:1024].rearrange("d (h k) -> d h k", h=8)), r=[s_], w=[k12])
            def b2_front(t, TT):
                rows = slice(t * 128, (t + 1) * 128)
                op("sp", lambda e: e.dma_start(out=xt[:], in_=xmid[rows, :]), r=[xmres[t]], w=[xt], dma=True)
                rmsnorm_rows(xt, hb, sq)
                make_hT()
                op("pool", lambda e: e.tensor_copy(out=h2T_all[:, :, t * 128:(t + 1) * 128], in_=hT[:]), r=[hT], w=[h2T_all])
                for hf in range(2):
                    pt = P[hf]
                    for c in range(8):
                        op("pe", lambda e, c=c, pt=pt, hf=hf: e.matmul(pt[:, :], lhsT=hT[:, c, :], rhs=WQ[:, c * 1024 + hf * 512: c * 1024 + (hf + 1) * 512], start=(c == 0), stop=(c == 7)), r=[hT, WQ], w=[pt])
                    op("act", lambda e, pt=pt, hf=hf: e.activation(out=qsb[:, hf * 512:(hf + 1) * 512], in_=pt[:, :], func=AF.Copy), r=[pt], w=[qsb])
                for hf in range(2):
                    for b in range(8):
                        bb = hf * 8 + b
                        op("pe", lambda e, b=b, bb=bb: e.transpose(out=PTR[0:64, b * 128:(b + 1) * 128], in_=qsb[:, bb * 64:(bb + 1) * 64], identity=ident[:]), r=[qsb, ident], w=[PTR])
                    op("act", lambda e, hf=hf: e.activation(out=qT[:, hf * 8:(hf + 1) * 8, :], in_=PTR[0:64, :].rearrange("p (b n) -> p b n", b=8), func=AF.Copy), r=[PTR], w=[qT])
                yield
                for which in range(2):
                    for h in range(8):
                        pt = P[2 + h // 4]
                        op("pe", lambda e, h=h, pt=pt, which=which: e.matmul(pt[:, (h % 4) * 128:(h % 4 + 1) * 128], lhsT=qT[:, h * 2 + which, :], rhs=k12[:, which, h, :], start=True, stop=True), r=[qT, k12], w=[pt])
                    for hf in range(2):
                        pt = P[2 + hf]
                        op("act", lambda e, pt=pt, which=which, hf=hf: e.activation(out=s12[:, which, hf * 4:(hf + 1) * 4, :], in_=pt[:, :].rearrange("p (h k) -> p h k", h=4), func=AF.Copy), r=[pt], w=[s12])
                yield
                for which in range(2):
                    for h in range(8):
                        sv = s12[:, which, h, :]
                        op("dve", lambda e, sv=sv, which=which, h=h: e.max(out=v12[:, which, h, 0:8], in_=sv), r=[s12], w=[v12])
                        op("dve", lambda e, sv=sv, which=which, h=h: e.max_index(out=i12[:, which, h, 0:8], in_max=v12[:, which, h, 0:8], in_values=sv), r=[s12, v12], w=[i12])
                        op("dve", lambda e, sv=sv, which=which, h=h: e.match_replace(out=swk[:, 0:128], in_to_replace=v12[:, which, h, 0:8], in_values=sv, imm_value=NEG), r=[s12, v12], w=[swk])
                        op("dve", lambda e, which=which, h=h: e.max(out=v12[:, which, h, 8:16], in_=swk[:, 0:128]), r=[swk], w=[v12])
                        op("dve", lambda e, which=which, h=h: e.max_index(out=i12[:, which, h, 8:16], in_max=v12[:, which, h, 8:16], in_values=swk[:, 0:128]), r=[swk, v12], w=[i12])
                yield
                op("dve", lambda e: e.tensor_copy(out=i12f[:], in_=i12[:]), r=[i12], w=[i12f])
                c4 = cand[:, :, :].rearrange("p h (r c) -> p h r c", r=16)
                op("dve", lambda e: e.tensor_tensor(out=c4, in0=v12[:, 0, :, :].unsqueeze(3).to_broadcast([128, 8, 16, 16]), in1=v12[:, 1, :, :].unsqueeze(2).to_broadcast([128, 8, 16, 16]), op=ALU.add), r=[v12], w=[cand])
                for h in range(8):
                    op("dve", lambda e, h=h: e.max(out=sc[:, h, 0:8], in_=cand[:, h, :]), r=[cand], w=[sc])
                    op("dve", lambda e, h=h: e.max_index(out=pos[:, h, 0:8], in_max=sc[:, h, 0:8], in_values=cand[:, h, :]), r=[cand, sc], w=[pos])
                    op("dve", lambda e, h=h: e.match_replace(out=swk[:, 0:256], in_to_replace=sc[:, h, 0:8], in_values=cand[:, h, :], imm_value=NEG), r=[cand, sc], w=[swk])
                    op("dve", lambda e, h=h: e.max(out=sc[:, h, 8:16], in_=swk[:, 0:256]), r=[swk], w=[sc])
                    op("dve", lambda e, h=h: e.max_index(out=pos[:, h, 8:16], in_max=sc[:, h, 8:16], in_values=swk[:, 0:256]), r=[swk, sc], w=[pos])
                op("dve", lambda e: e.tensor_copy(out=posf[:], in_=pos[:]), r=[pos], w=[posf])
                op("dve", lambda e: e.tensor_tensor(out=oh[:], in0=posf[:, :, :].unsqueeze(3).to_broadcast([128, 8, 16, 16]),
                                                    in1=thr16[:, :].unsqueeze(1).unsqueeze(1).to_broadcast([128, 8, 16, 16]), op=ALU.is_ge), r=[posf, thr16], w=[oh])
                op("dve", lambda e: e.tensor_reduce(out=posrf[:], in_=oh[:], axis=AX.X, op=ALU.add), r=[oh], w=[posrf])
                op("dve", lambda e: e.scalar_tensor_tensor(out=poscf[:], in0=posrf[:], scalar=-16.0, in1=posf[:], op0=ALU.mult, op1=ALU.add), r=[posrf, posf], w=[poscf])
                io4 = iota16[:, :].unsqueeze(1).unsqueeze(1).to_broadcast([128, 8, 16, 16])
                for pf_, which, dsel in ((posrf, 0, sel1), (poscf, 1, sel2)):
                    op("dve", lambda e, pf_=pf_: e.tensor_tensor(out=oh[:], in0=io4, in1=pf_[:, :, :].unsqueeze(3).to_broadcast([128, 8, 16, 16]), op=ALU.is_equal), r=[iota16, pf_], w=[oh])
                    op("dve", lambda e, which=which: e.tensor_tensor(out=oh[:], in0=oh[:], in1=i12f[:, which, :, :].unsqueeze(2).to_broadcast([128, 8, 16, 16]), op=ALU.mult), r=[oh, i12f], w=[oh])
                    op("dve", lambda e, dsel=dsel: e.tensor_reduce(out=dsel[:], in_=oh[:], axis=AX.X, op=ALU.add), r=[oh], w=[dsel])
                op("dve", lambda e: e.tensor_tensor(out=gsm[:], in0=sc[:], in1=sc[:, :, 0:1].to_broadcast([128, 8, 16]), op=ALU.subtract), r=[sc], w=[gsm])
                op("act", lambda e: e.activation(out=gsm[:], in_=gsm[:], func=AF.Exp), r=[gsm], w=[gsm])
                op("dve", lambda e: e.tensor_reduce(out=st8[:, 48:56], in_=gsm[:], axis=AX.X, op=ALU.add), r=[gsm], w=[st8])
                op("dve", lambda e: e.reciprocal(out=st8[:, 56:64], in_=st8[:, 48:56]), r=[st8], w=[st8])
                op("dve", lambda e: e.tensor_tensor(out=gsm[:], in0=gsm[:], in1=st8[:, 56:64].unsqueeze(2).to_broadcast([128, 8, 16]), op=ALU.mult), r=[gsm, st8], w=[gsm])
                for i_, src in enumerate((sel1, sel2, gsm)):
                    op("pe", lambda e, i_=i_, src=src: e.transpose(out=P[0][:, i_ * 128:(i_ + 1) * 128], in_=src[:, :, :].rearrange("p h k -> p (h k)"), identity=identf[:]), r=[src, identf], w=[P[0]])
                op("act", lambda e: e.activation(out=TT[:], in_=P[0][:, 0:384].rearrange("p (a n) -> p a n", a=3), func=AF.Copy), r=[P[0]], w=[TT])

            def b2_back(t, TT):
                iob = iota128[:, :].unsqueeze(1).to_broadcast([128, 32, 128])
                for qq in range(4):
                    n0 = qq * 32
                    p1 = P1q[qq % 2]
                    p2 = P2q[qq % 2]
                    op("dve", lambda e, n0=n0, p2=p2: e.tensor_tensor(out=p2[:], in0=iob, in1=TT[:, 1, n0:n0 + 32].unsqueeze(2).to_broadcast([128, 32, 128]), op=ALU.is_equal), r=[iota128, TT], w=[p2])
                    op("dve", lambda e, n0=n0, p1=p1: e.tensor_tensor(out=p1[:], in0=iob, in1=TT[:, 0, n0:n0 + 32].unsqueeze(2).to_broadcast([128, 32, 128]), op=ALU.is_equal), r=[iota128, TT], w=[p1])
                    op("pool", lambda e, n0=n0, p1=p1: e.tensor_tensor(out=p1[:], in0=p1[:], in1=TT[:, 2, n0:n0 + 32].unsqueeze(2).to_broadcast([128, 32, 128]), op=ALU.mult), r=[p1, TT], w=[p1])
                    for q4 in range(8):
                        bank = P[4 + (q4 % 3)]
                        for k in range(4):
                            n = q4 * 4 + k
                            op("pe", lambda e, bank=bank, k=k, n=n, p1=p1, p2=p2: e.matmul(bank[:, k * 128:(k + 1) * 128], lhsT=p1[:, n, :], rhs=p2[:, n, :], start=True, stop=True), r=[p1, p2], w=[bank])
                        nn = n0 + q4 * 4
                        op("act", lambda e, bank=bank, nn=nn: e.activation(out=Gs[:, :, nn:nn + 4].rearrange("p i n -> p n i"), in_=bank[:, :].rearrange("p (n i) -> p n i", n=4), func=AF.Copy), r=[bank], w=[Gs])
                    yield

            TTs = [TT, TT2]
            if tiles:
                for _ in b2_front(tiles[0], TTs[0]):
                    pass
            for i_, t in enumerate(tiles):
                fg = b2_front(tiles[i_ + 1], TTs[(i_ + 1) % 2]) if i_ + 1 < len(tiles) else iter(())
                bg = b2_back(t, TTs[i_ % 2])
                for _q in range(4):
                    next(bg, None)
                    next(fg, None)
                for _ in bg:
                    pass
                for _ in fg:
                    pass
                op("sp", lambda e: e.dma_start(out=Gd[t, :, :], in_=Gs[:, :, :].rearrange("p i n -> p (i n)")), r=[Gs], w=[gdres[t]], dma=True)

            kb.barrier()
            pvv = pv[l][:, :].rearrange("(i1 i2) d -> i1 i2 d", i2=128)
            tgroups = [tiles[i:i + 2] for i in range(0, len(tiles), 2)]
            NSB = 128 // NBLK
            cring = [0]
            pend_casts = {}

            def emit_loads(sbk):
                ub_ = u16[sbk % 2]
                vb_ = v16[sbk % 2]
                casts = []
                for c in range(8):
                    s_ = cstg[cring[0] % NCS]
                    cring[0] += 1
                    op("pool", lambda e, s_=s_, c=c: e.dma_start(out=s_[:, 0:512], in_=puT[l][c * 128:(c + 1) * 128, sbk * NBLK * 128:(sbk + 1) * NBLK * 128]), r=[in_res], w=[s_], dma=True)
                    casts.append(lambda s_=s_, c=c, ub_=ub_: op("act", lambda e: e.activation(out=ub_[:, c, :], in_=s_[:, 0:512], func=AF.Copy, scale=n2s[:, c:c + 1]), r=[s_, n2s], w=[ub_]))
                for blk in range(NBLK):
                    for hf in range(2):
                        s_ = cstg[cring[0] % NCS]
                        cring[0] += 1
                        op("pool", lambda e, s_=s_, blk=blk, hf=hf: e.dma_start(out=s_[:, 0:512], in_=pvv[:, sbk * NBLK + blk, hf * 512:(hf + 1) * 512]), r=[in_res], w=[s_], dma=True)
                        casts.append(lambda s_=s_, blk=blk, hf=hf, vb_=vb_: op("act", lambda e: e.activation(out=vb_[:, blk, hf * 512:(hf + 1) * 512], in_=s_[:, 0:512], func=AF.Copy), r=[s_], w=[vb_]))
                return casts

            items = [(sbk, gi_, blk) for sbk in range(NSB) for gi_ in range(len(tgroups)) for blk in range(NBLK)]

            def emit_a(it_):
                sbk, gi_, blk = it_
                ub_ = u16[sbk % 2]
                tl = tgroups[gi_]
                nt_ = len(tl)
                ntk = 128 * nt_
                A = P[blk % 2]
                if blk == 0:
                    gc = Gc[(sbk * len(tgroups) + gi_) % 2]
                    for ti, t in enumerate(tl):
                        op("sp", lambda e, ti=ti, t=t: e.dma_start(out=gc[:, ti, :, :], in_=Gd[t, :, sbk * NBLK * 128:(sbk + 1) * NBLK * 128].rearrange("p (i n) -> p i n", i=NBLK)), r=[gdres[t]], w=[gc], dma=True)
                contiguous = (nt_ == 2 and tl[1] == tl[0] + 1) or nt_ == 1
                if contiguous:
                    for c in range(8):
                        op("pe", lambda e, c=c: e.matmul(A[:, 0:ntk], lhsT=ub_[:, c, blk * 128:(blk + 1) * 128], rhs=h2T_all_c[:, c, tl[0] * 128: tl[0] * 128 + ntk], start=(c == 0), stop=(c == 7)), r=[ub_, h2T_all_c], w=[A])
                else:
                    for ti, t in enumerate(tl):
                        for c in range(8):
                            op("pe", lambda e, c=c, ti=ti, t=t: e.matmul(A[:, ti * 128:(ti + 1) * 128], lhsT=ub_[:, c, blk * 128:(blk + 1) * 128], rhs=h2T_all_c[:, c, t * 128:(t + 1) * 128], start=(c == 0 and ti == 0), stop=(c == 7)), r=[ub_, h2T_all_c], w=[A])

            def emit_rest(it_):
                sbk, gi_, blk = it_
                vb_ = v16[sbk % 2]
                tl = tgroups[gi_]
                nt_ = len(tl)
                ntk = 128 * nt_
                A = P[blk % 2]
                ge = gel[blk % 2]
                cf = cfT[blk % 2]
                gc = Gc[(sbk * len(tgroups) + gi_) % 2]
                op("act", lambda e: e.activation(out=ge[:, 0:ntk], in_=A[:, 0:ntk], func=AF.Gelu), r=[A], w=[ge])
                op("dve", lambda e: e.tensor_tensor(out=cf[:, 0:ntk].rearrange("p (t n) -> p t n", t=nt_), in0=ge[:, 0:ntk].rearrange("p (t n) -> p t n", t=nt_), in1=gc[:, 0:nt_, blk, :], op=ALU.mult), r=[ge, gc], w=[cf])
                for ti, t in enumerate(tl):
                    for hf in range(2):
                        ab = P[2 + ti * 2 + hf]
                        op("pe", lambda e, ab=ab, ti=ti, hf=hf: e.matmul(ab[:, :], lhsT=cf[:, ti * 128:(ti + 1) * 128], rhs=vb_[:, blk, hf * 512:(hf + 1) * 512], start=(blk == 0), stop=(blk == NBLK - 1)), r=[cf, vb_], w=[ab])
                if blk == NBLK - 1:
                    for ti, t in enumerate(tl):
                        for hf in range(2):
                            ab = P[2 + ti * 2 + hf]
                            if sbk == 0:
                                op("dve", lambda e, ab=ab, t=t, hf=hf: e.tensor_copy(out=accs[:, t, hf * 512:(hf + 1) * 512], in_=ab[:, :]), r=[ab], w=[accres[t]])
                            else:
                                op("dve", lambda e, ab=ab, t=t, hf=hf: e.tensor_tensor(out=accs[:, t, hf * 512:(hf + 1) * 512], in0=accs[:, t, hf * 512:(hf + 1) * 512], in1=ab[:, :], op=ALU.add), r=[ab, accres[t]], w=[accres[t]])

            for cst_ in emit_loads(0):
                cst_()
            ng = len(tgroups)
            cast_at = {max(0, ng // 3): (0, 8), max(1, (2 * ng) // 3): (8, 16)} if ng >= 3 else {0: (0, 16)}
            emit_a(items[0])
            for i_, it_ in enumerate(items):
                sbk, gi_, blk = it_
                if gi_ == 0 and blk == 0 and sbk + 1 < NSB:
                    pend_casts[sbk + 1] = emit_loads(sbk + 1)
                if blk == 0 and gi_ in cast_at and (sbk + 1) in pend_casts:
                    a_, b_ = cast_at[gi_]
                    for cst_ in pend_casts[sbk + 1][a_:b_]:
                        cst_()
                if i_ + 1 < len(items):
                    emit_a(items[i_ + 1])
                emit_rest(it_)

        for t in tiles:
            samp = (t == NT - 1)
            rows = slice(t * 128, (t + 1) * 128)
            op("sp", lambda e: e.dma_start(out=xt[:], in_=xmid[rows, :]), r=[xmres[t]], w=[xt], dma=True)
            if do_peer:
                op("dve", lambda e: e.tensor_tensor(out=xt[:], in0=xt[:], in1=accs[:, t, :], op=ALU.add), r=[xt, accres[t]], w=[xt])
            if last:
                if samp:
                    op("sp", lambda e: e.dma_start(out=y_s[:, :], in_=xt[0:DEC, :]), r=[xt], w=[out_res], dma=True)
                else:
                    op("sp", lambda e: e.dma_start(out=y_p[rows, :], in_=xt[:]), r=[xt], w=[out_res], dma=True)
            else:
                op("sp", lambda e: e.dma_start(out=xout_d[rows, :], in_=xt[:]), r=[xt], w=[xout_r[t]], dma=True)

    kb.finish()
    es.close()
    return nc, kb.ninst


def _consts():
    j = np.arange(128)
    ident = np.eye(128, dtype=np.float32)
    negtri = -(j[:, None] >= j[None, :]).astype(np.float32)
    mlt = (j[:, None] < j[None, :]).astype(np.float32)
    ident4 = np.tile(ident, (1, 4))
    negm = np.where(j[:, None] >= j[None, :], -30000.0, 0.0).astype(np.float32)
    negm4 = np.tile(negm, (1, 4))
    pow2 = np.tile((0.5 ** np.arange(1, NBIS + 1)).astype(np.float32)[None, :], (128, 1))
    iota = np.tile(np.arange(16, dtype=np.float32)[None, :], (128, 1))
    iota128 = np.tile(np.arange(128, dtype=np.float32)[None, :], (128, 1))
    return np.concatenate([ident, negtri, mlt, ident4, negm4, pow2, iota, iota128], axis=1).astype(np.float32)


def _rope_table():
    pos = np.concatenate([np.arange(SEQ), SEQ + np.arange(128)]).astype(np.float32)
    half = 32
    inv = (10000.0 ** (-np.arange(half, dtype=np.float32) / half)).astype(np.float32)
    ang = pos[:, None] * inv[None, :]
    return np.concatenate([np.cos(ang), np.sin(ang)], axis=1).astype(np.float32)


_PROG = {}


def _prep(x_prompt, x_sample, cache_a_k, cache_a_v, cache_idx_k, cache_b_k, cache_b_v,
          norm1, w_in, q_norm_a, k_norm_a, idx_k_norm, w_pa, w_pb, w_o, norm2,
          peer_wq, peer_k1, peer_k2, peer_u, peer_v, cores=range(8)):
    f = lambda a: np.ascontiguousarray(np.asarray(a, dtype=np.float32))
    x_prompt = f(x_prompt); x_sample = f(x_sample)
    shared = {
        "rope": _rope_table(),
        "cst": _consts(),
        "n1T": f(np.asarray(norm1).reshape(DEPTH, 8, 128).transpose(0, 2, 1)),
        "n2T": f(np.asarray(norm2).reshape(DEPTH, 8, 128).transpose(0, 2, 1)),
        "n2r": f(norm2),
        "w_in": f(w_in), "qn": f(q_norm_a), "kn": f(k_norm_a), "ikn": f(idx_k_norm),
        "w_pa": f(w_pa), "w_pb": f(w_pb), "w_o": f(w_o), "pwq": f(peer_wq),
        "pk1T": f(np.asarray(peer_k1).transpose(0, 3, 1, 2)),
        "pk2T": f(np.asarray(peer_k2).transpose(0, 3, 1, 2)),
    }
    pu = np.asarray(peer_u); pvv = np.asarray(peer_v)
    for l in range(DEPTH):
        shared["puT%d" % l] = f(pu[l].reshape(128, 128, D).transpose(2, 1, 0).reshape(D, 16384))
        shared["pv%d" % l] = f(pvv[l])
    cak = np.asarray(cache_a_k); cav = np.asarray(cache_a_v); cik = np.asarray(cache_idx_k)
    cbk = np.asarray(cache_b_k); cbv = np.asarray(cache_b_v)
    in_maps = []
    for b in cores:
        xa = np.zeros((NT * 128, D), np.float32)
        xa[:SEQ] = x_prompt[b]
        xa[SEQ:SEQ + DEC] = x_sample[b]
        m = dict(shared)
        m["x_in"] = xa
        m["cak"] = f(cak[:, b].reshape(DEPTH, SEQ, 128))
        m["cav"] = f(cav[:, b].reshape(DEPTH, SEQ, 128))
        m["cik"] = f(cik[:, b].reshape(DEPTH, SEQ, 64))
        m["cbk"] = f(cbk[:, b].reshape(DEPTH, SEQ, 512))
        m["cbv"] = f(cbv[:, b].reshape(DEPTH, SEQ, 512))
        in_maps.append(m)
    return in_maps


def kernel(**inputs):
    if "full" not in _PROG:
        _PROG["full"] = build_program()[0]
    nc = _PROG["full"]
    in_maps = _prep(**inputs)
    res = run_bass_kernel_spmd(nc, in_maps, core_ids=list(range(8)))
    R = res.results
    st = lambda k, shp: np.stack([np.asarray(R[b][k], dtype=np.float32) for b in range(8)], axis=1).reshape(shp)
    y_p = np.stack([np.asarray(R[b]["y_p"], dtype=np.float32) for b in range(8)], axis=0)
    y_s = np.stack([np.asarray(R[b]["y_s"], dtype=np.float32) for b in range(8)], axis=0)
    return (y_p, y_s,
            st("ak_p", (DEPTH, 8, SEQ, 2, 64)), st("av_p", (DEPTH, 8, SEQ, 2, 64)), st("ik_p", (DEPTH, 8, SEQ, 64)),
            st("bk_p", (DEPTH, 8, SEQ, 8, 64)), st("bv_p", (DEPTH, 8, SEQ, 8, 64)),
            st("ak_s", (DEPTH, 8, DEC, 2, 64)), st("av_s", (DEPTH, 8, DEC, 2, 64)), st("ik_s", (DEPTH, 8, DEC, 64)),
            st("bk_s", (DEPTH, 8, DEC, 8, 64)), st("bv_s", (DEPTH, 8, DEC, 8, 64)))
```

```python
import contextlib
import numpy as np
import concourse.bass as bass
import concourse.mybir as mybir
from concourse.bass_utils import run_bass_kernel_spmd

F32 = mybir.dt.float32
BF16 = mybir.dt.bfloat16
I32 = mybir.dt.int32
U32 = mybir.dt.uint32
ALU = mybir.AluOpType
AF = mybir.ActivationFunctionType
AX = mybir.AxisListType

D = 1024
DEPTH = 4
NT = 17
NKS = 17
SEQ = 2048
DEC = 16
NIN = 4936
EPS = 1e-6
NEG = -1e30
TOPK = 256
NBIS = 22


class Res:
    __slots__ = ("t", "w", "r")

    def __init__(self, t=None):
        self.t = t
        self.w = None
        self.r = {}

    def __getitem__(self, k):
        return self.t[k]


class KB:
    def __init__(self, nc, es):
        self.nc = nc
        self.es = es
        self.E = {"pe": nc.tensor, "act": nc.scalar, "dve": nc.vector, "pool": nc.gpsimd, "sp": nc.sync}
        self.sems = {}
        self.cnt = {}
        self.seen = {e: {} for e in self.E}
        self.ninst = 0
        self.dpool = {"sp": ["dsp%d" % i for i in range(24)], "pool": ["dpl%d" % i for i in range(6)]}
        self.dnext = {"sp": 0, "pool": 0}

    def sem(self, key):
        if key not in self.sems:
            self.sems[key] = self.es.enter_context(self.nc.semaphore("s_" + key))
            self.cnt[key] = 0
        return self.sems[key]

    def op(self, eng, fn, r=(), w=(), dma=False):
        deps = {}
        for x in r:
            if x.w is not None:
                k, v = x.w
                if deps.get(k, 0) < v:
                    deps[k] = v
        for x in w:
            if x.w is not None:
                k, v = x.w
                if deps.get(k, 0) < v:
                    deps[k] = v
            for k, v in x.r.items():
                if deps.get(k, 0) < v:
                    deps[k] = v
        E = self.E[eng]
        seen = self.seen[eng]
        for k, v in deps.items():
            if k == "pe" and eng == "pe" and not dma:
                continue
            if seen.get(k, 0) < v:
                E.wait_ge(self.sem(k), v)
                seen[k] = v
        if dma:
            pl = self.dpool[eng]
            key = pl[self.dnext[eng]]
            self.dnext[eng] = (self.dnext[eng] + 1) % len(pl)
            s = self.sem(key)
            prev = self.cnt[key]
            if prev > 0 and seen.get(key, 0) < prev:
                E.wait_ge(s, prev)
                seen[key] = prev
        else:
            key = eng
            s = self.sem(key)
        ins = fn(E)
        inc = 16 if dma else 1
        self.cnt[key] += inc
        c = self.cnt[key]
        ins.then_inc(s, inc)
        for x in r:
            x.r[key] = c
        for x in w:
            x.w = (key, c)
            x.r = {}
        self.ninst += 1
        return ins

    def barrier(self):
        for en, E in self.E.items():
            seen = self.seen[en]
            for k, sm in self.sems.items():
                v = self.cnt[k]
                if v > 0 and seen.get(k, 0) < v:
                    E.wait_ge(sm, v)
                    seen[k] = v

    def finish(self):
        E = self.E["sp"]
        for k, s in self.sems.items():
            if self.cnt[k] > 0:
                E.wait_ge(s, self.cnt[k])


def build_program(depth=DEPTH, tiles=None, do_peer=True):
    if tiles is None:
        tiles = list(range(NT))
    nc = bass.Bass("TRN2", target_bir_lowering=False)
    es = contextlib.ExitStack()
    kb = KB(nc, es)

    def dram(name, shape, dt, kind):
        return nc.dram_tensor(name, shape, dt, kind=kind)

    x_in = dram("x_in", [NT * 128, D], F32, "ExternalInput")
    rope_in = dram("rope", [NT * 128, 64], F32, "ExternalInput")
    cak = dram("cak", [DEPTH, SEQ, 128], F32, "ExternalInput")
    cav = dram("cav", [DEPTH, SEQ, 128], F32, "ExternalInput")
    cik = dram("cik", [DEPTH, SEQ, 64], F32, "ExternalInput")
    cbk = dram("cbk", [DEPTH, SEQ, 512], F32, "ExternalInput")
    cbv = dram("cbv", [DEPTH, SEQ, 512], F32, "ExternalInput")
    n1T = dram("n1T", [DEPTH, 128, 8], F32, "ExternalInput")
    n2T = dram("n2T", [DEPTH, 128, 8], F32, "ExternalInput")
    n2r = dram("n2r", [DEPTH, D], F32, "ExternalInput")
    w_in = dram("w_in", [DEPTH, D, NIN], F32, "ExternalInput")
    qn = dram("qn", [DEPTH, 64], F32, "ExternalInput")
    kn = dram("kn", [DEPTH, 64], F32, "ExternalInput")
    ikn = dram("ikn", [DEPTH, 64], F32, "ExternalInput")
    w_pa = dram("w_pa", [DEPTH, 512, D], F32, "ExternalInput")
    w_pb = dram("w_pb", [DEPTH, 512, D], F32, "ExternalInput")
    w_o = dram("w_o", [DEPTH, D, D], F32, "ExternalInput")
    pwq = dram("pwq", [DEPTH, D, D], F32, "ExternalInput")
    pk1T = dram("pk1T", [DEPTH, 64, 8, 128], F32, "ExternalInput")
    pk2T = dram("pk2T", [DEPTH, 64, 8, 128], F32, "ExternalInput")
    puT = [dram("puT%d" % l, [D, 16384], F32, "ExternalInput") for l in range(DEPTH)]
    pv = [dram("pv%d" % l, [16384, D], F32, "ExternalInput") for l in range(DEPTH)]

    y_p = dram("y_p", [SEQ, D], F32, "ExternalOutput")
    y_s = dram("y_s", [DEC, D], F32, "ExternalOutput")
    ak_p = dram("ak_p", [DEPTH, SEQ, 128], F32, "ExternalOutput")
    av_p = dram("av_p", [DEPTH, SEQ, 128], F32, "ExternalOutput")
    ik_p = dram("ik_p", [DEPTH, SEQ, 64], F32, "ExternalOutput")
    bk_p = dram("bk_p", [DEPTH, SEQ, 512], F32, "ExternalOutput")
    bv_p = dram("bv_p", [DEPTH, SEQ, 512], F32, "ExternalOutput")
    ak_s = dram("ak_s", [DEPTH, DEC, 128], F32, "ExternalOutput")
    av_s = dram("av_s", [DEPTH, DEC, 128], F32, "ExternalOutput")
    ik_s = dram("ik_s", [DEPTH, DEC, 64], F32, "ExternalOutput")
    bk_s = dram("bk_s", [DEPTH, DEC, 512], F32, "ExternalOutput")
    bv_s = dram("bv_s", [DEPTH, DEC, 512], F32, "ExternalOutput")
    xs_dram = [dram("xscr%d" % i, [NT * 128, D], F32, "Internal") for i in range(2)]
    xs_res = [[Res() for _ in range(NT)] for _ in range(2)]
    out_res = Res()
    in_res = Res()

    ARENA_F32 = 52000
    arena = es.enter_context(nc.sbuf_tensor("arena", [128, ARENA_F32], F32))
    aoff = [0]
    DTB = {F32: 4, BF16: 2, I32: 4, U32: 4}

    def sb(name, shape, dt):
        nb = DTB[dt]
        n = 1
        for d_ in shape[1:]:
            n *= d_
        nbytes = (n * nb + 31) // 32 * 32
        o = aoff[0]
        assert o % 4 == 0
        aoff[0] = o + nbytes
        assert aoff[0] <= ARENA_F32 * 4, (name, aoff[0])
        v = arena[0:shape[0], o // 4:(o + nbytes) // 4]
        if dt != F32:
            v = v.bitcast(dt)
        v = v[:, 0:n]
        if len(shape) == 3:
            v = v.rearrange("p (a b) -> p a b", a=shape[1])
        elif len(shape) == 4:
            v = v.rearrange("p (a b c) -> p a b c", a=shape[1], b=shape[2])
        return Res(v)

    def pst(name, shape, dt):
        return Res(es.enter_context(nc.psum_tensor(name, shape, dt)))

    ident = sb("ident", [128, 128], BF16)
    ident4 = sb("ident4", [128, 512], BF16)
    negtri = sb("negtri", [128, 128], BF16)
    mlt = sb("mlt", [128, 128], BF16)
    negm4 = sb("negm4", [128, 512], BF16)
    ones1 = sb("ones1", [128, 1], BF16)
    pow2 = sb("pow2", [128, NBIS], F32)
    iota16 = sb("iota16", [128, 16], F32)
    thr16 = sb("thr16", [128, 16], F32)
    identf = sb("identf", [128, 128], F32)
    iota128 = sb("iota128", [128, 128], F32)
    NCST = 128 * 3 + 512 * 2 + NBIS + 16 + 128
    cst_in = dram("cst", [128, NCST], F32, "ExternalInput")
    gq = sb("gq", [128, 64], F32)
    gk = sb("gk", [128, 64], F32)
    gik = sb("gik", [128, 64], F32)
    n1s = sb("n1s", [128, 8], F32)
    n2s = sb("n2s", [128, 8], F32)
    n2b = sb("n2b", [128, D], F32)
    xt = sb("xt", [128, D], F32)
    hb = sb("hb", [128, D], BF16)
    hT = sb("hT", [128, 8, 128], BF16)
    sq = sb("sq", [128, D], F32)
    st8 = sb("st8", [128, 64], F32)
    STW = 1536
    stg = [sb("stg%d" % i, [128, STW], F32) for i in range(2)]
    mark0 = aoff[0]

    WARENA_A = sb("warenaA", [128, 8 * 2888], BF16)
    kaT = sb("kaT", [64, 2, NKS * 128], BF16)
    kiT = sb("kiT", [64, NKS * 128], BF16)
    kbT = sb("kbT", [64, 8, NKS * 128], BF16)
    vaA = sb("vaA", [128, NKS, 2, 65], BF16)
    vbB = sb("vbB", [128, NKS, 8, 64], BF16)
    kvres = [Res() for _ in range(NKS)]
    cstage = sb("cstage", [128, NCST], F32)
    ropet = sb("ropet", [128, 64], F32)
    pf = sb("pf", [128, 512], F32)
    pf2 = sb("pf2", [128, 512], F32)
    pbf = sb("pbf", [128, 512], BF16)
    r1 = sb("r1", [128, 256], F32)
    r2 = sb("r2", [128, 256], F32)
    qaT = sb("qaT", [64, 8, 128], BF16)
    qiT = sb("qiT", [64, 8, 128], BF16)
    qbT = sb("qbT", [64, 8, 128], BF16)
    wi = sb("wi", [128, 8], F32)
    isc = sb("isc", [128, 2560], F32)
    mbias = sb("mbias", [128, 2560], BF16)
    rl = [sb("rl%d" % i, [128, 512], F32) for i in range(2)]
    PT = [sb("PT%d" % i, [128, 512], BF16) for i in range(2)]
    oa = sb("oa", [128, 512], F32)
    ob = sb("ob", [128, 512], F32)
    ebuf = sb("ebuf", [128, 1024], F32)
    spT = sb("spT", [128, 1024], BF16)
    ET = sb("ET", [128, 1024], BF16)
    dd = sb("dd", [128, 8], F32)
    pvs = sb("pvs", [128, 512], F32)
    bis = sb("bis", [128, 8], F32)
    dtab = sb("dtab", [128, NBIS], F32)
    endA = aoff[0]

    aoff[0] = mark0
    WARENA_B = sb("warenaB", [128, 32768], BF16)
    oabb = sb("oabb", [128, D], BF16)
    sga = sb("sga", [128, D], F32)
    sgb = sb("sgb", [128, D], F32)
    oT = sb("oT", [128, 8, 128], BF16)
    mm = sb("mm", [128, D], F32)
    mbf = sb("mbf", [128, D], BF16)
    oabf = sb("oabf", [128, D], F32)
    endB1 = aoff[0]

    aoff[0] = mark0
    h2T_all = sb("h2T_all", [128, 8, NT * 128], BF16)
    WQ = sb("WQ", [128, 8 * 1024], BF16)
    qsb = sb("qsb", [128, D], BF16)
    qT = sb("qT", [64, 16, 128], BF16)
    k12 = sb("k12", [64, 2, 8, 128], BF16)
    s12 = sb("s12", [128, 2, 8, 128], F32)
    swk = sb("swk", [128, 256], F32)
    v12 = sb("v12", [128, 2, 8, 16], F32)
    i12 = sb("i12", [128, 2, 8, 16], U32)
    i12f = sb("i12f", [128, 2, 8, 16], F32)
    cand = sb("cand", [128, 8, 256], F32)
    sc = sb("sc", [128, 8, 16], F32)
    pos = sb("pos", [128, 8, 16], U32)
    posf = sb("posf", [128, 8, 16], F32)
    posrf = sb("posrf", [128, 8, 16], F32)
    poscf = sb("poscf", [128, 8, 16], F32)
    oh = sb("oh", [128, 8, 16, 16], F32)
    sel1 = sb("sel1", [128, 8, 16], F32)
    sel2 = sb("sel2", [128, 8, 16], F32)
    gsm = sb("gsm", [128, 8, 16], F32)
    TT = sb("TT", [128, 3, 128], F32)
    TT2 = sb("TT2", [128, 3, 128], F32)
    P1q = [sb("P1q%d" % i, [128, 32, 128], BF16) for i in range(2)]
    P2q = [sb("P2q%d" % i, [128, 32, 128], BF16) for i in range(2)]
    Gs = sb("Gs", [128, 128, 128], BF16)
    endB2 = aoff[0]

    aoff[0] = mark0
    h2T_all_c = sb("h2T_all_c", [128, 8, NT * 128], BF16)
    accs = sb("accs", [128, NT, D], F32)
    accres = [Res() for _ in range(NT)]
    NBLK = 4
    u16 = [sb("u16_%d" % i, [128, 8, NBLK * 128], BF16) for i in range(2)]
    v16 = [sb("v16_%d" % i, [128, NBLK, D], BF16) for i in range(2)]
    Gc = [sb("Gc%d" % i, [128, 2, NBLK, 128], BF16) for i in range(2)]
    gel = [sb("gel%d" % i, [128, 256], BF16) for i in range(2)]
    cfT = [sb("cfT%d" % i, [128, 256], BF16) for i in range(2)]
    cstg = [sb("cstg%d" % i, [128, 512], F32) for i in range(12)]
    cstg += [Res(stg[i][:, k * 512:(k + 1) * 512]) for i in range(2) for k in range(2)]
    NCS = len(cstg)
    assert NCS == 16
    endC = aoff[0]
    print("arena bytes: persistent", mark0, "A", endA, "B1", endB1, "B2", endB2, "C", endC, "cap", ARENA_F32 * 4)

    Gd = dram("Gd", [NT, 128, 16384], BF16, "Internal")
    gdres = [Res() for _ in range(NT)]
    xmid = dram("xmid", [NT * 128, D], F32, "Internal")
    xmres = [Res() for _ in range(NT)]
    oab_dram = dram("oabscr", [NT * 128, D], F32, "Internal")
    oab_res = [Res() for _ in range(NT)]

    P = [pst("ps%d" % i, [128, 512], F32) for i in range(7)]
    PTR = pst("ptr", [128, 1024], BF16)

    op = kb.op

    op("sp", lambda e: e.dma_start(out=cstage[:], in_=cst_in[:, :]), r=[in_res], w=[cstage], dma=True)
    o = 0
    for dst, wdt in ((ident, 128), (negtri, 128), (mlt, 128), (ident4, 512), (negm4, 512)):
        op("dve", lambda e, dst=dst, o=o, wdt=wdt: e.tensor_copy(out=dst[:], in_=cstage[:, o:o + wdt]), r=[cstage], w=[dst])
        o += wdt
    op("dve", lambda e, o=o: e.tensor_copy(out=pow2[:], in_=cstage[:, o:o + NBIS]), r=[cstage], w=[pow2])
    o += NBIS
    op("dve", lambda e, o=o: e.tensor_copy(out=iota16[:], in_=cstage[:, o:o + 16]), r=[cstage], w=[iota16])
    o += 16
    op("dve", lambda e, o=o: e.tensor_copy(out=iota128[:], in_=cstage[:, o:o + 128]), r=[cstage], w=[iota128])
    op("dve", lambda e: e.tensor_copy(out=identf[:], in_=cstage[:, 0:128]), r=[cstage], w=[identf])
    op("dve", lambda e: e.memset(ones1[:], 1.0), w=[ones1])
    op("dve", lambda e: e.tensor_scalar(out=thr16[:], in0=iota16[:], scalar1=16.0, scalar2=16.0, op0=ALU.mult, op1=ALU.add), r=[iota16], w=[thr16])
    op("dve", lambda e: e.memset(thr16[:, 15:16], 1e9), w=[thr16])
    op("dve", lambda e: e.memset(vaA[:], 1.0), w=[vaA] + kvres)
    kb.barrier()

    def transpose_blocks(src, nblk, dstT, dst_res, ptile=PTR):
        for b in range(nblk):
            op("pe", lambda e, b=b: e.transpose(out=ptile[0:64, b * 128:(b + 1) * 128], in_=src[:, b * 64:(b + 1) * 64], identity=ident[:]),
               r=[src, ident], w=[ptile])
        op("act", lambda e: e.activation(out=dstT, in_=ptile[0:64, 0:nblk * 128].rearrange("p (b n) -> p b n", b=nblk), func=AF.Copy),
           r=[ptile], w=dst_res)

    def rmsnorm_rows(xres, outbf, scratch):
        op("act", lambda e: e.activation(out=scratch[:], in_=xres[:], func=AF.Square, accum_out=st8[:, 0:1]), r=[xres], w=[scratch, st8])
        op("dve", lambda e: e.tensor_scalar(out=st8[:, 1:2], in0=st8[:, 0:1], scalar1=1.0 / D, scalar2=EPS, op0=ALU.mult, op1=ALU.add), r=[st8], w=[st8])
        op("act", lambda e: e.activation(out=st8[:, 2:3], in_=st8[:, 1:2], func=AF.Sqrt), r=[st8], w=[st8])
        op("dve", lambda e: e.reciprocal(out=st8[:, 3:4], in_=st8[:, 2:3]), r=[st8], w=[st8])
        op("dve", lambda e: e.tensor_scalar(out=outbf[:], in0=xres[:], scalar1=st8[:, 3:4], scalar2=None, op0=ALU.mult), r=[xres, st8], w=[outbf])

    def make_hT():
        for c in range(8):
            op("pe", lambda e, c=c: e.transpose(out=PTR[:, c * 128:(c + 1) * 128], in_=hb[:, c * 128:(c + 1) * 128], identity=ident[:]),
               r=[hb, ident], w=[PTR])
        op("act", lambda e: e.activation(out=hT[:], in_=PTR[:, :].rearrange("p (c n) -> p c n", c=8), func=AF.Copy), r=[PTR], w=[hT])

    def headnorm(src_res, src_ap, H, gain, dst):
        W = H * 64
        op("act", lambda e: e.activation(out=sq[:, 0:W], in_=src_ap, func=AF.Square), r=[src_res], w=[sq])
        op("dve", lambda e: e.tensor_reduce(out=st8[:, 8:8 + H], in_=sq[:, 0:W].rearrange("p (h d) -> p h d", h=H), axis=AX.X, op=ALU.add), r=[sq], w=[st8])
        op("dve", lambda e: e.tensor_scalar(out=st8[:, 16:16 + H], in0=st8[:, 8:8 + H], scalar1=1.0 / 64, scalar2=EPS, op0=ALU.mult, op1=ALU.add), r=[st8], w=[st8])
        op("act", lambda e: e.activation(out=st8[:, 24:24 + H], in_=st8[:, 16:16 + H], func=AF.Sqrt), r=[st8], w=[st8])
        op("dve", lambda e: e.reciprocal(out=st8[:, 32:32 + H], in_=st8[:, 24:24 + H]), r=[st8], w=[st8])
        d3 = dst[:, 0:W].rearrange("p (h d) -> p h d", h=H)
        op("dve", lambda e: e.tensor_tensor(out=d3, in0=src_ap.rearrange("p (h d) -> p h d", h=H),
                                            in1=st8[:, 32:32 + H].unsqueeze(2).to_broadcast([128, H, 64]), op=ALU.mult), r=[st8, src_res], w=[dst])
        op("dve", lambda e: e.tensor_tensor(out=d3, in0=d3, in1=gain[:, :].unsqueeze(1).to_broadcast([128, H, 64]), op=ALU.mult), r=[gain, dst], w=[dst])

    def rope(src, H, dst, scale=None):
        W = H * 64
        s3 = src[:, 0:W].rearrange("p (h d) -> p h d", h=H)
        d3 = dst[:, 0:W].rearrange("p (h d) -> p h d", h=H)
        cosb = ropet[:, 0:32].unsqueeze(1).to_broadcast([128, H, 32])
        sinb = ropet[:, 32:64].unsqueeze(1).to_broadcast([128, H, 32])
        a3 = r1[:, 0:H * 32].rearrange("p (h d) -> p h d", h=H)
        b3 = r2[:, 0:H * 32].rearrange("p (h d) -> p h d", h=H)
        op("dve", lambda e: e.tensor_tensor(out=a3, in0=s3[:, :, 0:32], in1=cosb, op=ALU.mult), r=[src, ropet], w=[r1])
        op("dve", lambda e: e.tensor_tensor(out=b3, in0=s3[:, :, 32:64], in1=sinb, op=ALU.mult), r=[src, ropet], w=[r2])
        op("dve", lambda e: e.tensor_tensor(out=d3[:, :, 0:32], in0=a3, in1=b3, op=ALU.subtract), r=[r1, r2], w=[dst])
        op("dve", lambda e: e.tensor_tensor(out=a3, in0=s3[:, :, 32:64], in1=cosb, op=ALU.mult), r=[src, ropet], w=[r1])
        op("dve", lambda e: e.tensor_tensor(out=b3, in0=s3[:, :, 0:32], in1=sinb, op=ALU.mult), r=[src, ropet], w=[r2])
        op("dve", lambda e: e.tensor_tensor(out=d3[:, :, 32:64], in0=a3, in1=b3, op=ALU.add), r=[r1, r2], w=[dst])

    def load_cast_rows(wres, dram_ap_fn, nchunks, width, dst_fn, scale_res=None, scale_col=None):
        i = 0
        for c in range(nchunks):
            for o0 in range(0, width, STW):
                wdt = min(STW, width - o0)
                s = stg[i % 2]
                i += 1
                op("sp", lambda e, c=c, o0=o0, wdt=wdt, s=s: e.dma_start(out=s[:, 0:wdt], in_=dram_ap_fn(c, o0, wdt)), r=[in_res], w=[s], dma=True)
                if scale_res is not None:
                    op("dve", lambda e, c=c, o0=o0, wdt=wdt, s=s: e.tensor_scalar(out=dst_fn(c, o0, wdt), in0=s[:, 0:wdt], scalar1=scale_res[:, scale_col(c):scale_col(c) + 1], scalar2=None, op0=ALU.mult),
                       r=[s, scale_res], w=[wres])
                else:
                    op("pool", lambda e, c=c, o0=o0, wdt=wdt, s=s: e.tensor_copy(out=dst_fn(c, o0, wdt), in_=s[:, 0:wdt]), r=[s], w=[wres])

    WAA = WARENA_A.t
    WA = WARENA_B.t
    NQKV = 2888
    wa_qkv = lambda c, o0, wdt: WAA[:, c * NQKV + o0: c * NQKV + o0 + wdt]
    GOFF = 0
    PAOFF = 8 * 2048
    PBOFF = PAOFF + 4 * 1024
    WOOFF = PBOFF + 4 * 1024
    PQOFF = WOOFF + 8 * 1024

    for l in range(depth):
        xin_d = x_in if l == 0 else xs_dram[(l - 1) % 2]
        xin_r = [in_res] * NT if l == 0 else xs_res[(l - 1) % 2]
        xout_d = xs_dram[l % 2]
        xout_r = xs_res[l % 2]
        last = (l == depth - 1)

        op("sp", lambda e: e.dma_start(out=n1s[:], in_=n1T[l, :, :]), r=[in_res], w=[n1s], dma=True)
        op("sp", lambda e: e.dma_start(out=n2s[:], in_=n2T[l, :, :]), r=[in_res], w=[n2s], dma=True)
        op("sp", lambda e: e.dma_start(out=gq[:], in_=qn[l, :].partition_broadcast(128)), r=[in_res], w=[gq], dma=True)
        op("sp", lambda e: e.dma_start(out=gk[:], in_=kn[l, :].partition_broadcast(128)), r=[in_res], w=[gk], dma=True)
        op("sp", lambda e: e.dma_start(out=gik[:], in_=ikn[l, :].partition_broadcast(128)), r=[in_res], w=[gik], dma=True)
        op("sp", lambda e: e.dma_start(out=n2b[:], in_=n2r[l, :].partition_broadcast(128)), r=[in_res], w=[n2b], dma=True)

        kb.barrier()
        if l > 0:
            op("pool", lambda e: e.memset(vaA[:], 1.0), w=[vaA] + kvres)
        load_cast_rows(WARENA_A, lambda c, o0, wdt: w_in[l, c * 128:(c + 1) * 128, o0:o0 + wdt], 8, NQKV, wa_qkv, n1s, lambda c: c)

        for t in tiles:
            samp = (t == NT - 1)
            nk = t + 1
            if samp:
                for k0 in range(0, 16, 4):
                    for kk in range(k0, k0 + 4):
                        rows = slice(kk * 128, (kk + 1) * 128)
                        s = stg[kk % 2]
                        op("sp", lambda e, s=s, rows=rows: e.dma_start(out=s[:, 0:128], in_=cak[l, rows, :]), r=[in_res], w=[s], dma=True)
                        op("sp", lambda e, s=s, rows=rows: e.dma_start(out=s[:, 128:192], in_=cik[l, rows, :]), r=[in_res], w=[s], dma=True)
                        op("sp", lambda e, s=s, rows=rows: e.dma_start(out=s[:, 192:320], in_=cav[l, rows, :]), r=[in_res], w=[s], dma=True)
                        op("sp", lambda e, s=s, rows=rows: e.dma_start(out=s[:, 512:1024], in_=cbk[l, rows, :]), r=[in_res], w=[s], dma=True)
                        op("sp", lambda e, s=s, rows=rows: e.dma_start(out=s[:, 1024:1536], in_=cbv[l, rows, :]), r=[in_res], w=[s], dma=True)
                        op("dve", lambda e, s=s: e.tensor_copy(out=pbf[:, 0:192], in_=s[:, 0:192]), r=[s], w=[pbf])
                        for b in range(3):
                            op("pe", lambda e, b=b: e.transpose(out=PTR[0:64, b * 128:(b + 1) * 128], in_=pbf[:, b * 64:(b + 1) * 64], identity=ident[:]), r=[pbf, ident], w=[PTR])
                        ks = slice(kk * 128, (kk + 1) * 128)
                        op("act", lambda e, ks=ks: e.activation(out=kaT[:, :, ks], in_=PTR[0:64, 0:256].rearrange("p (b n) -> p b n", b=2), func=AF.Copy), r=[PTR], w=[kvres[kk]])
                        op("act", lambda e, ks=ks: e.activation(out=kiT[:, ks], in_=PTR[0:64, 256:384], func=AF.Copy), r=[PTR], w=[kvres[kk]])
                        op("dve", lambda e, s=s, kk=kk: e.tensor_copy(out=vaA[:, kk, :, 0:64], in_=s[:, 192:320].rearrange("p (g d) -> p g d", g=2)), r=[s], w=[kvres[kk]])
                        op("dve", lambda e, s=s: e.tensor_copy(out=pbf[:, 0:512], in_=s[:, 512:1024]), r=[s], w=[pbf])
                        for b in range(8):
                            op("pe", lambda e, b=b: e.transpose(out=PTR[0:64, b * 128:(b + 1) * 128], in_=pbf[:, b * 64:(b + 1) * 64], identity=ident[:]), r=[pbf, ident], w=[PTR])
                        op("act", lambda e, ks=ks: e.activation(out=kbT[:, :, ks], in_=PTR[0:64, :].rearrange("p (b n) -> p b n", b=8), func=AF.Copy), r=[PTR], w=[kvres[kk]])
                        op("pool", lambda e, s=s, kk=kk: e.tensor_copy(out=vbB[:, kk, :, :], in_=s[:, 1024:1536].rearrange("p (h d) -> p h d", h=8)), r=[s], w=[kvres[kk]])

            rows = slice(t * 128, (t + 1) * 128)
            op("sp", lambda e: e.dma_start(out=xt[:], in_=xin_d[rows, :]), r=[xin_r[t]], w=[xt], dma=True)
            op("sp", lambda e: e.dma_start(out=ropet[:], in_=rope_in[rows, :]), r=[in_res], w=[ropet], dma=True)
            rmsnorm_rows(xt, hb, sq)
            make_hT()

            def proj(pt, c0, wdt):
                for c in range(8):
                    op("pe", lambda e, c=c: e.matmul(pt[:, 0:wdt], lhsT=hT[:, c, :], rhs=WAA[:, c * NQKV + c0: c * NQKV + c0 + wdt], start=(c == 0), stop=(c == 7)),
                       r=[hT, WARENA_A], w=[pt])

            def out_rows(dst_p, dst_s, src, wdt):
                if samp:
                    op("sp", lambda e: e.dma_start(out=dst_s[l, :, :], in_=src[0:DEC, 0:wdt]), r=[src], w=[out_res], dma=True)
                else:
                    op("sp", lambda e: e.dma_start(out=dst_p[l, rows, :], in_=src[:, 0:wdt]), r=[src], w=[out_res], dma=True)

            ks = slice(t * 128, (t + 1) * 128)
            proj(P[0], 0, 512)
            headnorm(P[0], P[0][:, 0:512], 8, gq, pf)
            rope(pf, 8, pf2)
            op("dve", lambda e: e.tensor_scalar(out=pbf[:], in0=pf2[:], scalar1=0.125, scalar2=None, op0=ALU.mult), r=[pf2], w=[pbf])
            transpose_blocks(pbf, 8, qaT[:], [qaT])
            proj(P[1], 512, 256)
            headnorm(P[1], P[1][:, 0:128], 2, gk, pf)
            rope(pf, 2, pf2)
            out_rows(ak_p, ak_s, pf2, 128)
            op("dve", lambda e: e.tensor_copy(out=pbf[:, 0:128], in_=pf2[:, 0:128]), r=[pf2], w=[pbf])
            op("act", lambda e: e.activation(out=pf[:, 0:128], in_=P[1][:, 128:256], func=AF.Copy), r=[P[1]], w=[pf])
            out_rows(av_p, av_s, pf, 128)
            op("dve", lambda e: e.tensor_copy(out=vaA[:, t, :, 0:64], in_=pf[:, 0:128].rearrange("p (g d) -> p g d", g=2)), r=[pf], w=[kvres[t]])
            transpose_blocks(pbf, 2, kaT[:, :, ks], [kvres[t]])
            proj(P[0], 768, 512)
            rope(P[0], 8, pf2)
            op("dve", lambda e: e.tensor_copy(out=pbf[:], in_=pf2[:]), r=[pf2], w=[pbf])
            transpose_blocks(pbf, 8, qiT[:], [qiT])
            proj(P[1], 1280, 72)
            headnorm(P[1], P[1][:, 0:64], 1, gik, pf)
            rope(pf, 1, pf2)
            out_rows(ik_p, ik_s, pf2, 64)
            op("dve", lambda e: e.tensor_copy(out=pbf[:, 0:64], in_=pf2[:, 0:64]), r=[pf2], w=[pbf])
            op("act", lambda e: e.activation(out=wi[:], in_=P[1][:, 64:72], func=AF.Copy), r=[P[1]], w=[wi])
            for b in range(1):
                op("pe", lambda e: e.transpose(out=PTR[0:64, 0:128], in_=pbf[:, 0:64], identity=ident[:]), r=[pbf, ident], w=[PTR])
            op("act", lambda e: e.activation(out=kiT[:, ks], in_=PTR[0:64, 0:128], func=AF.Copy), r=[PTR], w=[kvres[t]])
            proj(P[0], 1352, 512)
            op("act", lambda e: e.activation(out=pbf[:], in_=P[0][:, :], func=AF.Copy, scale=0.125), r=[P[0]], w=[pbf])
            transpose_blocks(pbf, 8, qbT[:], [qbT])
            proj(P[1], 1864, 512)
            op("act", lambda e: e.activation(out=pf[:], in_=P[1][:, :], func=AF.Copy), r=[P[1]], w=[pf])
            out_rows(bk_p, bk_s, pf, 512)
            op("dve", lambda e: e.tensor_copy(out=pbf[:], in_=pf[:]), r=[pf], w=[pbf])
            transpose_blocks(pbf, 8, kbT[:, :, ks], [kvres[t]])
            proj(P[0], 2376, 512)
            op("act", lambda e: e.activation(out=pf2[:], in_=P[0][:, :], func=AF.Copy), r=[P[0]], w=[pf2])
            out_rows(bv_p, bv_s, pf2, 512)
            op("dve", lambda e: e.tensor_copy(out=vbB[:, t, :, :], in_=pf2[:, :].rearrange("p (h d) -> p h d", h=8)), r=[pf2], w=[kvres[t]])

            S = nk * 128
            nblk = (S + 511) // 512
            kvr = [kvres[i] for i in range(nk)]
            for bi in range(nblk):
                c0 = bi * 512
                wdt = min(512, S - c0)
                for h in range(8):
                    pt = P[2 + (h % 2)]
                    rb = rl[h % 2]
                    op("pe", lambda e, h=h, pt=pt: e.matmul(pt[:, 0:wdt], lhsT=qiT[:, h, :], rhs=kiT[:, c0:c0 + wdt], start=True, stop=True), r=[qiT] + kvr, w=[pt])
                    op("act", lambda e, pt=pt, rb=rb: e.activation(out=rb[:, 0:wdt], in_=pt[:, 0:wdt], func=AF.Relu, scale=0.125 * (8 ** -0.5)), r=[pt], w=[rb])
                    if h == 0:
                        op("dve", lambda e, rb=rb: e.tensor_scalar(out=isc[:, c0:c0 + wdt], in0=rb[:, 0:wdt], scalar1=wi[:, 0:1], scalar2=None, op0=ALU.mult), r=[rb, wi], w=[isc])
                    else:
                        op("dve", lambda e, rb=rb, h=h: e.scalar_tensor_tensor(out=isc[:, c0:c0 + wdt], in0=rb[:, 0:wdt], scalar=wi[:, h:h + 1], in1=isc[:, c0:c0 + wdt], op0=ALU.mult, op1=ALU.add), r=[rb, wi, isc], w=[isc])
            Z = [P[2], P[3]]
            Z2 = [P[4], P[5]]
            PVb = P[6]
            TSb = P[0]
            for kt in range(nk):
                ksl = slice(kt * 128, (kt + 1) * 128)
                diag = (kt == nk - 1)
                for h in range(8):
                    op("pe", lambda e, h=h: e.matmul(Z[h // 4][:, (h % 4) * 128:(h % 4 + 1) * 128], lhsT=kbT[:, h, ksl], rhs=qbT[:, h, :], start=True, stop=True), r=[kvres[kt], qbT], w=[Z[h // 4]])
                for hf in range(2):
                    op("act", lambda e, hf=hf: e.activation(out=ebuf[:, hf * 512:(hf + 1) * 512], in_=Z[hf][:, :], func=AF.Exp), r=[Z[hf]], w=[ebuf])
                    op("act", lambda e, hf=hf: e.activation(out=spT[:, hf * 512:(hf + 1) * 512], in_=ebuf[:, hf * 512:(hf + 1) * 512], func=AF.Ln, bias=1.0), r=[ebuf], w=[spT])
                if diag:
                    s3 = spT[:, :].rearrange("p (h q) -> p h q", h=8)
                    op("pool", lambda e, s3=s3: e.tensor_tensor(out=s3, in0=s3, in1=mlt[:, :].unsqueeze(1).to_broadcast([128, 8, 128]), op=ALU.mult), r=[spT, mlt], w=[spT])
                for hf in range(2):
                    op("pe", lambda e, hf=hf: e.matmul(Z2[hf][:, :], lhsT=negtri[:], rhs=spT[:, hf * 512:(hf + 1) * 512], start=True, stop=False), r=[negtri, spT], w=[Z2[hf]])
                    if diag:
                        op("pe", lambda e, hf=hf: e.matmul(Z2[hf][:, :], lhsT=ident[:], rhs=negm4[:], start=False, stop=False), r=[ident, negm4], w=[Z2[hf]])
                    for hh in range(4):
                        h = hf * 4 + hh
                        op("pe", lambda e, h=h, hh=hh, hf=hf: e.matmul(Z2[hf][:, hh * 128:(hh + 1) * 128], lhsT=kbT[:, h, ksl], rhs=qbT[:, h, :], start=False, stop=(hh == 3)), r=[kvres[kt], qbT], w=[Z2[hf]])
                    op("act", lambda e, hf=hf: e.activation(out=ET[:, hf * 512:(hf + 1) * 512], in_=Z2[hf][:, :], func=AF.Exp), r=[Z2[hf]], w=[ET])
                for h in range(8):
                    op("pe", lambda e, h=h: e.matmul(TSb[:, h:h + 1], lhsT=spT[:, h * 128:(h + 1) * 128], rhs=ones1[:], start=True, stop=True), r=[spT, ones1], w=[TSb])
                for h in range(8):
                    op("pe", lambda e, h=h, kt=kt: e.matmul(PVb[:, h * 64:(h + 1) * 64], lhsT=ET[:, h * 128:(h + 1) * 128], rhs=vbB[:, kt, h, :], start=True, stop=True), r=[ET, kvres[kt]], w=[PVb])
                if kt == 0:
                    op("act", lambda e: e.activation(out=ob[:], in_=PVb[:, :], func=AF.Copy), r=[PVb], w=[ob])
                else:
                    op("act", lambda e: e.activation(out=dd[:], in_=TSb[:, 0:8], func=AF.Exp, scale=-1.0), r=[TSb], w=[dd])
                    o3 = ob[:, :].rearrange("p (h d) -> p h d", h=8)
                    op("act", lambda e: e.activation(out=pvs[:], in_=PVb[:, :], func=AF.Copy), r=[PVb], w=[pvs])
                    op("pool", lambda e, o3=o3: e.tensor_tensor(out=o3, in0=o3, in1=dd[:, :].unsqueeze(2).to_broadcast([128, 8, 64]), op=ALU.mult), r=[ob, dd], w=[ob])
                    op("pool", lambda e: e.tensor_tensor(out=ob[:], in0=ob[:], in1=pvs[:], op=ALU.add), r=[ob, pvs], w=[ob])
            op("sp", lambda e: e.dma_start(out=oab_dram[rows, 512:1024], in_=ob[:]), r=[ob], w=[oab_res[t]], dma=True)

            need_thr = nk > 2
            if need_thr:
                op("dve", lambda e: e.tensor_reduce(out=bis[:, 5:6], in_=isc[:, 0:S], axis=AX.X, op=ALU.max), r=[isc], w=[bis])
                op("dve", lambda e: e.tensor_reduce(out=bis[:, 6:7], in_=isc[:, 0:S], axis=AX.X, op=ALU.min), r=[isc], w=[bis])
                op("dve", lambda e: e.tensor_scalar(out=bis[:, 6:7], in0=bis[:, 6:7], scalar1=-1.0, scalar2=None, op0=ALU.mult), r=[bis], w=[bis])
                op("dve", lambda e: e.tensor_tensor(out=bis[:, 4:5], in0=bis[:, 5:6], in1=bis[:, 6:7], op=ALU.max), r=[bis], w=[bis])
                op("dve", lambda e: e.tensor_scalar(out=bis[:, 0:1], in0=bis[:, 4:5], scalar1=-1.0, scalar2=None, op0=ALU.mult), r=[bis], w=[bis])
                op("dve", lambda e: e.tensor_scalar(out=dtab[:], in0=pow2[:], scalar1=bis[:, 4:5], scalar2=2.002, op0=ALU.mult, op1=ALU.mult), r=[bis, pow2], w=[dtab])
            if samp:
                op("dve", lambda e: e.memset(isc[:, 16 * 128 + DEC:17 * 128], NEG), r=[], w=[isc])
            else:
                op("dve", lambda e: e.memset(isc[0:64, t * 128 + 64:(t + 1) * 128], NEG), r=[], w=[isc])
            if need_thr:
                for k in range(NBIS):
                    op("dve", lambda e, k=k: e.tensor_tensor(out=bis[:, 1:2], in0=bis[:, 0:1], in1=dtab[:, k:k + 1], op=ALU.add), r=[bis, dtab], w=[bis])
                    op("dve", lambda e: e.tensor_scalar(out=mbias[:, 0:S], in0=isc[:, 0:S], scalar1=bis[:, 1:2], scalar2=None, op0=ALU.is_ge, op1=ALU.add, accum_out=bis[:, 2:3]), r=[isc, bis], w=[mbias, bis])
                    op("dve", lambda e, k=k: e.scalar_tensor_tensor(out=bis[:, 3:4], in0=bis[:, 2:3], scalar=float(TOPK), in1=dtab[:, k:k + 1], op0=ALU.is_ge, op1=ALU.mult), r=[bis, dtab], w=[bis])
                    op("dve", lambda e: e.tensor_tensor(out=bis[:, 0:1], in0=bis[:, 0:1], in1=bis[:, 3:4], op=ALU.add), r=[bis], w=[bis])
            else:
                op("dve", lambda e: e.memset(bis[:, 0:1], -1e29), r=[], w=[bis])
            op("dve", lambda e: e.tensor_scalar(out=mbias[:, 0:S], in0=isc[:, 0:S], scalar1=bis[:, 0:1], scalar2=-30000.0, op0=ALU.is_lt, op1=ALU.mult), r=[isc, bis], w=[mbias])
            OAp = [P[4], P[5]]
            for kt in range(nk):
                ksl = slice(kt * 128, (kt + 1) * 128)
                for g in range(2):
                    pt = P[2 + g]
                    pb_ = PT[g]
                    op("pe", lambda e, g=g, pt=pt, ksl=ksl: e.matmul(pt[:, :], lhsT=kaT[:, g, ksl], rhs=qaT[:, 4 * g:4 * g + 4, :], start=True, stop=False), r=[kvres[kt], qaT], w=[pt])
                    op("pe", lambda e, pt=pt, ksl=ksl: e.matmul(pt[:, :], lhsT=mbias[:, ksl], rhs=ident4[:], start=False, stop=True), r=[mbias, ident4], w=[pt])
                    op("act", lambda e, pt=pt, pb_=pb_: e.activation(out=pb_[:], in_=pt[:, :], func=AF.Exp), r=[pt], w=[pb_])
                    for hh in range(4):
                        op("pe", lambda e, g=g, hh=hh, pb_=pb_, kt=kt: e.matmul(OAp[g][:, hh * 65:(hh + 1) * 65], lhsT=pb_[:, hh * 128:(hh + 1) * 128], rhs=vaA[:, kt, g, :], start=(kt == 0 and hh == 0), stop=(kt == nk - 1)),
                           r=[pb_, kvres[kt]], w=[OAp[g]])
            for g in range(2):
                o3 = OAp[g][:, 0:260].rearrange("p (h d) -> p h d", h=4)
                op("dve", lambda e, g=g, o3=o3: e.reciprocal(out=st8[:, 40 + 4 * g:44 + 4 * g], in_=o3[:, :, 64]), r=[OAp[g]], w=[st8])
                op("dve", lambda e, g=g, o3=o3: e.tensor_tensor(out=oa[:, g * 256:(g + 1) * 256].rearrange("p (h d) -> p h d", h=4), in0=o3[:, :, 0:64],
                                                                 in1=st8[:, 40 + 4 * g:44 + 4 * g].unsqueeze(2).to_broadcast([128, 4, 64]), op=ALU.mult), r=[OAp[g], st8], w=[oa])
            op("sp", lambda e: e.dma_start(out=oab_dram[rows, 0:512], in_=oa[:]), r=[oa], w=[oab_res[t]], dma=True)

        kb.barrier()
        load_cast_rows(WARENA_B, lambda c, o0, wdt: w_in[l, c * 128:(c + 1) * 128, 2888 + o0:2888 + o0 + wdt], 8, 2048,
                       lambda c, o0, wdt: WA[:, GOFF + c * 2048 + o0: GOFF + c * 2048 + o0 + wdt], n1s, lambda c: c)
        load_cast_rows(WARENA_B, lambda c, o0, wdt: w_pa[l, c * 128:(c + 1) * 128, o0:o0 + wdt], 4, 1024,
                       lambda c, o0, wdt: WA[:, PAOFF + c * 1024 + o0: PAOFF + c * 1024 + o0 + wdt])
        load_cast_rows(WARENA_B, lambda c, o0, wdt: w_pb[l, c * 128:(c + 1) * 128, o0:o0 + wdt], 4, 1024,
                       lambda c, o0, wdt: WA[:, PBOFF + c * 1024 + o0: PBOFF + c * 1024 + o0 + wdt])
        load_cast_rows(WARENA_B, lambda c, o0, wdt: w_o[l, c * 128:(c + 1) * 128, o0:o0 + wdt], 8, 1024,
                       lambda c, o0, wdt: WA[:, WOOFF + c * 1024 + o0: WOOFF + c * 1024 + o0 + wdt])
        for t in tiles:
            samp = (t == NT - 1)
            rows = slice(t * 128, (t + 1) * 128)
            op("sp", lambda e: e.dma_start(out=xt[:], in_=xin_d[rows, :]), r=[xin_r[t]], w=[xt], dma=True)
            rmsnorm_rows(xt, hb, sq)
            make_hT()
            for gi, gdst in ((0, sga), (1, sgb)):
                for hf in range(2):
                    pt = P[hf]
                    c0 = GOFF + gi * 1024 + hf * 512
                    for c in range(8):
                        op("pe", lambda e, c=c, pt=pt, c0=c0: e.matmul(pt[:, :], lhsT=hT[:, c, :], rhs=WA[:, c * 2048 + c0: c * 2048 + c0 + 512], start=(c == 0), stop=(c == 7)), r=[hT, WARENA_B], w=[pt])
                    op("act", lambda e, pt=pt, gdst=gdst, hf=hf: e.activation(out=gdst[:, hf * 512:(hf + 1) * 512], in_=pt[:, :], func=AF.Sigmoid), r=[pt], w=[gdst])
            op("sp", lambda e: e.dma_start(out=oabf[:], in_=oab_dram[rows, :]), r=[oab_res[t]], w=[oabf], dma=True)
            op("pool", lambda e: e.tensor_copy(out=oabb[:], in_=oabf[:]), r=[oabf], w=[oabb])
            for bi, (woff, gdst) in enumerate(((PAOFF, sga), (PBOFF, sgb))):
                for c in range(4):
                    op("pe", lambda e, c=c, bi=bi: e.transpose(out=PTR[:, c * 128:(c + 1) * 128], in_=oabb[:, bi * 512 + c * 128: bi * 512 + (c + 1) * 128], identity=ident[:]), r=[oabb, ident], w=[PTR])
                op("act", lambda e: e.activation(out=oT[:, 0:4, :], in_=PTR[:, 0:512].rearrange("p (c n) -> p c n", c=4), func=AF.Copy), r=[PTR], w=[oT])
                for hf in range(2):
                    pt = P[2 + hf]
                    for c in range(4):
                        op("pe", lambda e, c=c, pt=pt, hf=hf, woff=woff: e.matmul(pt[:, :], lhsT=oT[:, c, :], rhs=WA[:, woff + c * 1024 + hf * 512: woff + c * 1024 + (hf + 1) * 512], start=(c == 0), stop=(c == 3)), r=[oT, WARENA_B], w=[pt])
                    if bi == 0:
                        op("dve", lambda e, pt=pt, hf=hf: e.tensor_tensor(out=mm[:, hf * 512:(hf + 1) * 512], in0=sga[:, hf * 512:(hf + 1) * 512], in1=pt[:, :], op=ALU.mult), r=[sga, pt], w=[mm])
                    else:
                        op("dve", lambda e, pt=pt, hf=hf: e.tensor_tensor(out=sgb[:, hf * 512:(hf + 1) * 512], in0=sgb[:, hf * 512:(hf + 1) * 512], in1=pt[:, :], op=ALU.mult), r=[sgb, pt], w=[sgb])
            op("dve", lambda e: e.tensor_tensor(out=mbf[:], in0=mm[:], in1=sgb[:], op=ALU.add), r=[mm, sgb], w=[mbf])
            for c in range(8):
                op("pe", lambda e, c=c: e.transpose(out=PTR[:, c * 128:(c + 1) * 128], in_=mbf[:, c * 128:(c + 1) * 128], identity=ident[:]), r=[mbf, ident], w=[PTR])
            op("act", lambda e: e.activation(out=oT[:], in_=PTR[:, :].rearrange("p (c n) -> p c n", c=8), func=AF.Copy), r=[PTR], w=[oT])
            for hf in range(2):
                pt = P[4 + hf]
                for c in range(8):
                    op("pe", lambda e, c=c, pt=pt, hf=hf: e.matmul(pt[:, :], lhsT=oT[:, c, :], rhs=WA[:, WOOFF + c * 1024 + hf * 512: WOOFF + c * 1024 + (hf + 1) * 512], start=(c == 0), stop=(c == 7)), r=[oT, WARENA_B], w=[pt])
                op("dve", lambda e, pt=pt, hf=hf: e.tensor_tensor(out=xt[:, hf * 512:(hf + 1) * 512], in0=xt[:, hf * 512:(hf + 1) * 512], in1=pt[:, :], op=ALU.add), r=[xt, pt], w=[xt])

            op("sp", lambda e: e.dma_start(out=xmid[rows, :], in_=xt[:]), r=[xt], w=[xmres[t]], dma=True)

        if do_peer:
            kb.barrier()
            load_cast_rows(WQ, lambda c, o0, wdt: pwq[l, c * 128:(c + 1) * 128, o0:o0 + wdt], 8, 1024,
                           lambda c, o0, wdt: WQ[:, c * 1024 + o0: c * 1024 + o0 + wdt], n2s, lambda c: c)
            for which, src in ((0, pk1T), (1, pk2T)):
                s_ = stg[which]
                op("sp", lambda e, s_=s_, src=src: e.dma_start(out=s_[0:64, 0:1024], in_=src[l, :, :, :].rearrange("d h k -> d (h k)")), r=[in_res], w=[s_], dma=True)
                op("dve", lambda e, s_=s_, which=which: e.tensor_copy(out=k12[:, which, :, :], in_=s_[0:64, 0:1024].rearrange("d (h k) -> d h k", h=8)), r=[s_], w=[k12])
            def b2_front(t, TT):
                rows = slice(t * 128, (t + 1) * 128)
                op("sp", lambda e: e.dma_start(out=xt[:], in_=xmid[rows, :]), r=[xmres[t]], w=[xt], dma=True)
                rmsnorm_rows(xt, hb, sq)
                make_hT()
                op("pool", lambda e: e.tensor_copy(out=h2T_all[:, :, t * 128:(t + 1) * 128], in_=hT[:]), r=[hT], w=[h2T_all])
                for hf in range(2):
                    pt = P[hf]
                    for c in range(8):
                        op("pe", lambda e, c=c, pt=pt, hf=hf: e.matmul(pt[:, :], lhsT=hT[:, c, :], rhs=WQ[:, c * 1024 + hf * 512: c * 1024 + (hf + 1) * 512], start=(c == 0), stop=(c == 7)), r=[hT, WQ], w=[pt])
                    op("act", lambda e, pt=pt, hf=hf: e.activation(out=qsb[:, hf * 512:(hf + 1) * 512], in_=pt[:, :], func=AF.Copy), r=[pt], w=[qsb])
                for hf in range(2):
                    for b in range(8):
                        bb = hf * 8 + b
                        op("pe", lambda e, b=b, bb=bb: e.transpose(out=PTR[0:64, b * 128:(b + 1) * 128], in_=qsb[:, bb * 64:(bb + 1) * 64], identity=ident[:]), r=[qsb, ident], w=[PTR])
                    op("act", lambda e, hf=hf: e.activation(out=qT[:, hf * 8:(hf + 1) * 8, :], in_=PTR[0:64, :].rearrange("p (b n) -> p b n", b=8), func=AF.Copy), r=[PTR], w=[qT])
                yield
                for which in range(2):
                    for h in range(8):
                        pt = P[2 + h // 4]
                        op("pe", lambda e, h=h, pt=pt, which=which: e.matmul(pt[:, (h % 4) * 128:(h % 4 + 1) * 128], lhsT=qT[:, h * 2 + which, :], rhs=k12[:, which, h, :], start=True, stop=True), r=[qT, k12], w=[pt])
                    for hf in range(2):
                        pt = P[2 + hf]
                        op("act", lambda e, pt=pt, which=which, hf=hf: e.activation(out=s12[:, which, hf * 4:(hf + 1) * 4, :], in_=pt[:, :].rearrange("p (h k) -> p h k", h=4), func=AF.Copy), r=[pt], w=[s12])
                yield
                for which in range(2):
                    for h in range(8):
                        sv = s12[:, which, h, :]
                        op("dve", lambda e, sv=sv, which=which, h=h: e.max(out=v12[:, which, h, 0:8], in_=sv), r=[s12], w=[v12])
                        op("dve", lambda e, sv=sv, which=which, h=h: e.max_index(out=i12[:, which, h, 0:8], in_max=v12[:, which, h, 0:8], in_values=sv), r=[s12, v12], w=[i12])
                        op("dve", lambda e, sv=sv, which=which, h=h: e.match_replace(out=swk[:, 0:128], in_to_replace=v12[:, which, h, 0:8], in_values=sv, imm_value=NEG), r=[s12, v12], w=[swk])
                        op("dve", lambda e, which=which, h=h: e.max(out=v12[:, which, h, 8:16], in_=swk[:, 0:128]), r=[swk], w=[v12])
                        op("dve", lambda e, which=which, h=h: e.max_index(out=i12[:, which, h, 8:16], in_max=v12[:, which, h, 8:16], in_values=swk[:, 0:128]), r=[swk, v12], w=[i12])
                yield
                op("dve", lambda e: e.tensor_copy(out=i12f[:], in_=i12[:]), r=[i12], w=[i12f])
                c4 = cand[:, :, :].rearrange("p h (r c) -> p h r c", r=16)
                op("dve", lambda e: e.tensor_tensor(out=c4, in0=v12[:, 0, :, :].unsqueeze(3).to_broadcast([128, 8, 16, 16]), in1=v12[:, 1, :, :].unsqueeze(2).to_broadcast([128, 8, 16, 16]), op=ALU.add), r=[v12], w=[cand])
                for h in range(8):
                    op("dve", lambda e, h=h: e.max(out=sc[:, h, 0:8], in_=cand[:, h, :]), r=[cand], w=[sc])
                    op("dve", lambda e, h=h: e.max_index(out=pos[:, h, 0:8], in_max=sc[:, h, 0:8], in_values=cand[:, h, :]), r=[cand, sc], w=[pos])
                    op("dve", lambda e, h=h: e.match_replace(out=swk[:, 0:256], in_to_replace=sc[:, h, 0:8], in_values=cand[:, h, :], imm_value=NEG), r=[cand, sc], w=[swk])
                    op("dve", lambda e, h=h: e.max(out=sc[:, h, 8:16], in_=swk[:, 0:256]), r=[swk], w=[sc])
                    op("dve", lambda e, h=h: e.max_index(out=pos[:, h, 8:16], in_max=sc[:, h, 8:16], in_values=swk[:, 0:256]), r=[swk, sc], w=[pos])
                op("dve", lambda e: e.tensor_copy(out=posf[:], in_=pos[:]), r=[pos], w=[posf])
                op("dve", lambda e: e.tensor_tensor(out=oh[:], in0=posf[:, :, :].unsqueeze(3).to_broadcast([128, 8, 16, 16]),
                                                    in1=thr16[:, :].unsqueeze(1).unsqueeze(1).to_broadcast([128, 8, 16, 16]), op=ALU.is_ge), r=[posf, thr16], w=[oh])
                op("dve", lambda e: e.tensor_reduce(out=posrf[:], in_=oh[:], axis=AX.X, op=ALU.add), r=[oh], w=[posrf])
                op("dve", lambda e: e.scalar_tensor_tensor(out=poscf[:], in0=posrf[:], scalar=-16.0, in1=posf[:], op0=ALU.mult, op1=ALU.add), r=[posrf, posf], w=[poscf])
                io4 = iota16[:, :].unsqueeze(1).unsqueeze(1).to_broadcast([128, 8, 16, 16])
                for pf_, which, dsel in ((posrf, 0, sel1), (poscf, 1, sel2)):
                    op("dve", lambda e, pf_=pf_: e.tensor_tensor(out=oh[:], in0=io4, in1=pf_[:, :, :].unsqueeze(3).to_broadcast([128, 8, 16, 16]), op=ALU.is_equal), r=[iota16, pf_], w=[oh])
                    op("dve", lambda e, which=which: e.tensor_tensor(out=oh[:], in0=oh[:], in1=i12f[:, which, :, :].unsqueeze(2).to_broadcast([128, 8, 16, 16]), op=ALU.mult), r=[oh, i12f], w=[oh])
                    op("dve", lambda e, dsel=dsel: e.tensor_reduce(out=dsel[:], in_=oh[:], axis=AX.X, op=ALU.add), r=[oh], w=[dsel])
                op("dve", lambda e: e.tensor_tensor(out=gsm[:], in0=sc[:], in1=sc[:, :, 0:1].to_broadcast([128, 8, 16]), op=ALU.subtract), r=[sc], w=[gsm])
                op("act", lambda e: e.activation(out=gsm[:], in_=gsm[:], func=AF.Exp), r=[gsm], w=[gsm])
                op("dve", lambda e: e.tensor_reduce(out=st8[:, 48:56], in_=gsm[:], axis=AX.X, op=ALU.add), r=[gsm], w=[st8])
                op("dve", lambda e: e.reciprocal(out=st8[:, 56:64], in_=st8[:, 48:56]), r=[st8], w=[st8])
                op("dve", lambda e: e.tensor_tensor(out=gsm[:], in0=gsm[:], in1=st8[:, 56:64].unsqueeze(2).to_broadcast([128, 8, 16]), op=ALU.mult), r=[gsm, st8], w=[gsm])
                for i_, src in enumerate((sel1, sel2, gsm)):
                    op("pe", lambda e, i_=i_, src=src: e.transpose(out=P[0][:, i_ * 128:(i_ + 1) * 128], in_=src[:, :, :].rearrange("p h k -> p (h k)"), identity=identf[:]), r=[src, identf], w=[P[0]])
                op("act", lambda e: e.activation(out=TT[:], in_=P[0][:, 0:384].rearrange("p (a n) -> p a n", a=3), func=AF.Copy), r=[P[0]], w=[TT])

            def b2_back(t, TT):
                iob = iota128[:, :].unsqueeze(1).to_broadcast([128, 32, 128])
                for qq in range(4):
                    n0 = qq * 32
                    p1 = P1q[qq % 2]
                    p2 = P2q[qq % 2]
                    op("dve", lambda e, n0=n0, p2=p2: e.tensor_tensor(out=p2[:], in0=iob, in1=TT[:, 1, n0:n0 + 32].unsqueeze(2).to_broadcast([128, 32, 128]), op=ALU.is_equal), r=[iota128, TT], w=[p2])
                    op("dve", lambda e, n0=n0, p1=p1: e.tensor_tensor(out=p1[:], in0=iob, in1=TT[:, 0, n0:n0 + 32].unsqueeze(2).to_broadcast([128, 32, 128]), op=ALU.is_equal), r=[iota128, TT], w=[p1])
                    op("pool", lambda e, n0=n0, p1=p1: e.tensor_tensor(out=p1[:], in0=p1[:], in1=TT[:, 2, n0:n0 + 32].unsqueeze(2).to_broadcast([128, 32, 128]), op=ALU.mult), r=[p1, TT], w=[p1])
                    for q4 in range(8):
                        bank = P[4 + (q4 % 3)]
                        for k in range(4):
                            n = q4 * 4 + k
                            op("pe", lambda e, bank=bank, k=k, n=n, p1=p1, p2=p2: e.matmul(bank[:, :].rearrange("p (i n) -> p n i", n=4)[:, k, :], lhsT=p1[:, n, :], rhs=p2[:, n, :], start=True, stop=True), r=[p1, p2], w=[bank])
                        nn = n0 + q4 * 4
                        op("act", lambda e, bank=bank, nn=nn: e.activation(out=Gs[:, :, nn:nn + 4], in_=bank[:, :].rearrange("p (i n) -> p i n", n=4), func=AF.Copy), r=[bank], w=[Gs])
                    yield

            TTs = [TT, TT2]
            if tiles:
                for _ in b2_front(tiles[0], TTs[0]):
                    pass
            for i_, t in enumerate(tiles):
                fg = b2_front(tiles[i_ + 1], TTs[(i_ + 1) % 2]) if i_ + 1 < len(tiles) else iter(())
                bg = b2_back(t, TTs[i_ % 2])
                for _q in range(4):
                    next(bg, None)
                    next(fg, None)
                for _ in bg:
                    pass
                for _ in fg:
                    pass
                op("sp", lambda e: e.dma_start(out=Gd[t, :, :], in_=Gs[:, :, :].rearrange("p i n -> p (i n)")), r=[Gs], w=[gdres[t]], dma=True)

            kb.barrier()
            pvv = pv[l][:, :].rearrange("(i1 i2) d -> i1 i2 d", i2=128)
            tgroups = [tiles[i:i + 2] for i in range(0, len(tiles), 2)]
            NSB = 128 // NBLK
            cring = [0]
            pend_casts = {}

            def emit_loads(sbk):
                ub_ = u16[sbk % 2]
                vb_ = v16[sbk % 2]
                casts = []
                for c in range(8):
                    s_ = cstg[cring[0] % NCS]
                    cring[0] += 1
                    op("pool", lambda e, s_=s_, c=c: e.dma_start(out=s_[:, 0:512], in_=puT[l][c * 128:(c + 1) * 128, sbk * NBLK * 128:(sbk + 1) * NBLK * 128]), r=[in_res], w=[s_], dma=True)
                    casts.append(lambda s_=s_, c=c, ub_=ub_: op("act", lambda e: e.activation(out=ub_[:, c, :], in_=s_[:, 0:512], func=AF.Copy, scale=n2s[:, c:c + 1]), r=[s_, n2s], w=[ub_]))
                for blk in range(NBLK):
                    for hf in range(2):
                        s_ = cstg[cring[0] % NCS]
                        cring[0] += 1
                        op("pool", lambda e, s_=s_, blk=blk, hf=hf: e.dma_start(out=s_[:, 0:512], in_=pvv[:, sbk * NBLK + blk, hf * 512:(hf + 1) * 512]), r=[in_res], w=[s_], dma=True)
                        casts.append(lambda s_=s_, blk=blk, hf=hf, vb_=vb_: op("act", lambda e: e.activation(out=vb_[:, blk, hf * 512:(hf + 1) * 512], in_=s_[:, 0:512], func=AF.Copy), r=[s_], w=[vb_]))
                return casts

            items = [(sbk, gi_, blk) for sbk in range(NSB) for gi_ in range(len(tgroups)) for blk in range(NBLK)]

            def emit_a(it_):
                sbk, gi_, blk = it_
                ub_ = u16[sbk % 2]
                tl = tgroups[gi_]
                nt_ = len(tl)
                ntk = 128 * nt_
                A = P[blk % 2]
                if blk == 0:
                    gc = Gc[(sbk * len(tgroups) + gi_) % 2]
                    for ti, t in enumerate(tl):
                        op("sp", lambda e, ti=ti, t=t: e.dma_start(out=gc[:, ti, :, :], in_=Gd[t, :, sbk * NBLK * 128:(sbk + 1) * NBLK * 128].rearrange("p (i n) -> p i n", i=NBLK)), r=[gdres[t]], w=[gc], dma=True)
                contiguous = (nt_ == 2 and tl[1] == tl[0] + 1) or nt_ == 1
                if contiguous:
                    for c in range(8):
                        op("pe", lambda e, c=c: e.matmul(A[:, 0:ntk], lhsT=ub_[:, c, blk * 128:(blk + 1) * 128], rhs=h2T_all_c[:, c, tl[0] * 128: tl[0] * 128 + ntk], start=(c == 0), stop=(c == 7)), r=[ub_, h2T_all_c], w=[A])
                else:
                    for ti, t in enumerate(tl):
                        for c in range(8):
                            op("pe", lambda e, c=c, ti=ti, t=t: e.matmul(A[:, ti * 128:(ti + 1) * 128], lhsT=ub_[:, c, blk * 128:(blk + 1) * 128], rhs=h2T_all_c[:, c, t * 128:(t + 1) * 128], start=(c == 0 and ti == 0), stop=(c == 7)), r=[ub_, h2T_all_c], w=[A])

            def emit_rest(it_):
                sbk, gi_, blk = it_
                vb_ = v16[sbk % 2]
                tl = tgroups[gi_]
                nt_ = len(tl)
                ntk = 128 * nt_
                A = P[blk % 2]
                ge = gel[blk % 2]
                cf = cfT[blk % 2]
                gc = Gc[(sbk * len(tgroups) + gi_) % 2]
                op("act", lambda e: e.activation(out=ge[:, 0:ntk], in_=A[:, 0:ntk], func=AF.Gelu), r=[A], w=[ge])
                op("dve", lambda e: e.tensor_tensor(out=cf[:, 0:ntk].rearrange("p (t n) -> p t n", t=nt_), in0=ge[:, 0:ntk].rearrange("p (t n) -> p t n", t=nt_), in1=gc[:, 0:nt_, blk, :], op=ALU.mult), r=[ge, gc], w=[cf])
                for ti, t in enumerate(tl):
                    for hf in range(2):
                        ab = P[2 + ti * 2 + hf]
                        op("pe", lambda e, ab=ab, ti=ti, hf=hf: e.matmul(ab[:, :], lhsT=cf[:, ti * 128:(ti + 1) * 128], rhs=vb_[:, blk, hf * 512:(hf + 1) * 512], start=(blk == 0), stop=(blk == NBLK - 1)), r=[cf, vb_], w=[ab])
                if blk == NBLK - 1:
                    for ti, t in enumerate(tl):
                        for hf in range(2):
                            ab = P[2 + ti * 2 + hf]
                            if sbk == 0:
                                op("dve", lambda e, ab=ab, t=t, hf=hf: e.tensor_copy(out=accs[:, t, hf * 512:(hf + 1) * 512], in_=ab[:, :]), r=[ab], w=[accres[t]])
                            else:
                                op("dve", lambda e, ab=ab, t=t, hf=hf: e.tensor_tensor(out=accs[:, t, hf * 512:(hf + 1) * 512], in0=accs[:, t, hf * 512:(hf + 1) * 512], in1=ab[:, :], op=ALU.add), r=[ab, accres[t]], w=[accres[t]])

            for cst_ in emit_loads(0):
                cst_()
            ng = len(tgroups)
            cast_at = {max(0, ng // 3): (0, 8), max(1, (2 * ng) // 3): (8, 16)} if ng >= 3 else {0: (0, 16)}
            emit_a(items[0])
            for i_, it_ in enumerate(items):
                sbk, gi_, blk = it_
                if gi_ == 0 and blk == 0 and sbk + 1 < NSB:
                    pend_casts[sbk + 1] = emit_loads(sbk + 1)
                if blk == 0 and gi_ in cast_at and (sbk + 1) in pend_casts:
                    a_, b_ = cast_at[gi_]
                    for cst_ in pend_casts[sbk + 1][a_:b_]:
                        cst_()
                if i_ + 1 < len(items):
                    emit_a(items[i_ + 1])
                emit_rest(it_)

        for t in tiles:
            samp = (t == NT - 1)
            rows = slice(t * 128, (t + 1) * 128)
            op("sp", lambda e: e.dma_start(out=xt[:], in_=xmid[rows, :]), r=[xmres[t]], w=[xt], dma=True)
            if do_peer:
                op("dve", lambda e: e.tensor_tensor(out=xt[:], in0=xt[:], in1=accs[:, t, :], op=ALU.add), r=[xt, accres[t]], w=[xt])
            if last:
                if samp:
                    op("sp", lambda e: e.dma_start(out=y_s[:, :], in_=xt[0:DEC, :]), r=[xt], w=[out_res], dma=True)
                else:
                    op("sp", lambda e: e.dma_start(out=y_p[rows, :], in_=xt[:]), r=[xt], w=[out_res], dma=True)
            else:
                op("sp", lambda e: e.dma_start(out=xout_d[rows, :], in_=xt[:]), r=[xt], w=[xout_r[t]], dma=True)

    kb.finish()
    es.close()
    return nc, kb.ninst


def _consts():
    j = np.arange(128)
    ident = np.eye(128, dtype=np.float32)
    negtri = -(j[:, None] >= j[None, :]).astype(np.float32)
    mlt = (j[:, None] < j[None, :]).astype(np.float32)
    ident4 = np.tile(ident, (1, 4))
    negm = np.where(j[:, None] >= j[None, :], -30000.0, 0.0).astype(np.float32)
    negm4 = np.tile(negm, (1, 4))
    pow2 = np.tile((0.5 ** np.arange(1, NBIS + 1)).astype(np.float32)[None, :], (128, 1))
    iota = np.tile(np.arange(16, dtype=np.float32)[None, :], (128, 1))
    iota128 = np.tile(np.arange(128, dtype=np.float32)[None, :], (128, 1))
    return np.concatenate([ident, negtri, mlt, ident4, negm4, pow2, iota, iota128], axis=1).astype(np.float32)


def _rope_table():
    pos = np.concatenate([np.arange(SEQ), SEQ + np.arange(128)]).astype(np.float32)
    half = 32
    inv = (10000.0 ** (-np.arange(half, dtype=np.float32) / half)).astype(np.float32)
    ang = pos[:, None] * inv[None, :]
    return np.concatenate([np.cos(ang), np.sin(ang)], axis=1).astype(np.float32)


_PROG = {}


def _prep(x_prompt, x_sample, cache_a_k, cache_a_v, cache_idx_k, cache_b_k, cache_b_v,
          norm1, w_in, q_norm_a, k_norm_a, idx_k_norm, w_pa, w_pb, w_o, norm2,
          peer_wq, peer_k1, peer_k2, peer_u, peer_v, cores=range(8)):
    f = lambda a: np.ascontiguousarray(np.asarray(a, dtype=np.float32))
    x_prompt = f(x_prompt); x_sample = f(x_sample)
    shared = {
        "rope": _rope_table(),
        "cst": _consts(),
        "n1T": f(np.asarray(norm1).reshape(DEPTH, 8, 128).transpose(0, 2, 1)),
        "n2T": f(np.asarray(norm2).reshape(DEPTH, 8, 128).transpose(0, 2, 1)),
        "n2r": f(norm2),
        "w_in": f(w_in), "qn": f(q_norm_a), "kn": f(k_norm_a), "ikn": f(idx_k_norm),
        "w_pa": f(w_pa), "w_pb": f(w_pb), "w_o": f(w_o), "pwq": f(peer_wq),
        "pk1T": f(np.asarray(peer_k1).transpose(0, 3, 1, 2)),
        "pk2T": f(np.asarray(peer_k2).transpose(0, 3, 1, 2)),
    }
    pu = np.asarray(peer_u); pvv = np.asarray(peer_v)
    for l in range(DEPTH):
        shared["puT%d" % l] = f(pu[l].reshape(128, 128, D).transpose(2, 1, 0).reshape(D, 16384))
        shared["pv%d" % l] = f(pvv[l])
    cak = np.asarray(cache_a_k); cav = np.asarray(cache_a_v); cik = np.asarray(cache_idx_k)
    cbk = np.asarray(cache_b_k); cbv = np.asarray(cache_b_v)
    in_maps = []
    for b in cores:
        xa = np.zeros((NT * 128, D), np.float32)
        xa[:SEQ] = x_prompt[b]
        xa[SEQ:SEQ + DEC] = x_sample[b]
        m = dict(shared)
        m["x_in"] = xa
        m["cak"] = f(cak[:, b].reshape(DEPTH, SEQ, 128))
        m["cav"] = f(cav[:, b].reshape(DEPTH, SEQ, 128))
        m["cik"] = f(cik[:, b].reshape(DEPTH, SEQ, 64))
        m["cbk"] = f(cbk[:, b].reshape(DEPTH, SEQ, 512))
        m["cbv"] = f(cbv[:, b].reshape(DEPTH, SEQ, 512))
        in_maps.append(m)
    return in_maps


def kernel(**inputs):
    if "full" not in _PROG:
        _PROG["full"] = build_program()[0]
    nc = _PROG["full"]
    in_maps = _prep(**inputs)
    res = run_bass_kernel_spmd(nc, in_maps, core_ids=list(range(8)))
    R = res.results
    st = lambda k, shp: np.stack([np.asarray(R[b][k], dtype=np.float32) for b in range(8)], axis=1).reshape(shp)
    y_p = np.stack([np.asarray(R[b]["y_p"], dtype=np.float32) for b in range(8)], axis=0)
    y_s = np.stack([np.asarray(R[b]["y_s"], dtype=np.float32) for b in range(8)], axis=0)
    return (y_p, y_s,
            st("ak_p", (DEPTH, 8, SEQ, 2, 64)), st("av_p", (DEPTH, 8, SEQ, 2, 64)), st("ik_p", (DEPTH, 8, SEQ, 64)),
            st("bk_p", (DEPTH, 8, SEQ, 8, 64)), st("bv_p", (DEPTH, 8, SEQ, 8, 64)),
            st("ak_s", (DEPTH, 8, DEC, 2, 64)), st("av_s", (DEPTH, 8, DEC, 2, 64)), st("ik_s", (DEPTH, 8, DEC, 64)),
            st("bk_s", (DEPTH, 8, DEC, 8, 64)), st("bv_s", (DEPTH, 8, DEC, 8, 64)))
```

```python
import contextlib
import numpy as np
import concourse.bass as bass
import concourse.mybir as mybir
from concourse.bass_utils import run_bass_kernel_spmd

F32 = mybir.dt.float32
BF16 = mybir.dt.bfloat16
I32 = mybir.dt.int32
U32 = mybir.dt.uint32
ALU = mybir.AluOpType
AF = mybir.ActivationFunctionType
AX = mybir.AxisListType

D = 1024
DEPTH = 4
NT = 17
NKS = 17
SEQ = 2048
DEC = 16
NIN = 4936
EPS = 1e-6
NEG = -1e30
TOPK = 256
NBIS = 22


class Res:
    __slots__ = ("t", "w", "r")

    def __init__(self, t=None):
        self.t = t
        self.w = None
        self.r = {}

    def __getitem__(self, k):
        return self.t[k]


class KB:
    def __init__(self, nc, es):
        self.nc = nc
        self.es = es
        self.E = {"pe": nc.tensor, "act": nc.scalar, "dve": nc.vector, "pool": nc.gpsimd, "sp": nc.sync}
        self.sems = {}
        self.cnt = {}
        self.seen = {e: {} for e in self.E}
        self.ninst = 0
        self.dpool = {"sp": ["dsp%d" % i for i in range(24)], "pool": ["dpl%d" % i for i in range(6)]}
        self.dnext = {"sp": 0, "pool": 0}

    def sem(self, key):
        if key not in self.sems:
            self.sems[key] = self.es.enter_context(self.nc.semaphore("s_" + key))
            self.cnt[key] = 0
        return self.sems[key]

    def op(self, eng, fn, r=(), w=(), dma=False):
        deps = {}
        for x in r:
            if x.w is not None:
                k, v = x.w
                if deps.get(k, 0) < v:
                    deps[k] = v
        inorder = (not dma) and eng in ("act", "dve")
        for x in w:
            if x.w is not None:
                k, v = x.w
                if not (inorder and k == eng) and deps.get(k, 0) < v:
                    deps[k] = v
            for k, v in x.r.items():
                if not (inorder and k == eng) and deps.get(k, 0) < v:
                    deps[k] = v
        E = self.E[eng]
        seen = self.seen[eng]
        for k, v in deps.items():
            if k == "pe" and eng == "pe" and not dma:
                continue
            if seen.get(k, 0) < v:
                E.wait_ge(self.sem(k), v)
                seen[k] = v
        if dma:
            pl = self.dpool[eng]
            key = pl[self.dnext[eng]]
            self.dnext[eng] = (self.dnext[eng] + 1) % len(pl)
            s = self.sem(key)
            prev = self.cnt[key]
            if prev > 0 and seen.get(key, 0) < prev:
                E.wait_ge(s, prev)
                seen[key] = prev
        else:
            key = eng
            s = self.sem(key)
        ins = fn(E)
        inc = 16 if dma else 1
        self.cnt[key] += inc
        c = self.cnt[key]
        ins.then_inc(s, inc)
        for x in r:
            x.r[key] = c
        for x in w:
            x.w = (key, c)
            x.r = {}
        self.ninst += 1
        return ins

    def barrier(self):
        for en, E in self.E.items():
            seen = self.seen[en]
            for k, sm in self.sems.items():
                v = self.cnt[k]
                if v > 0 and seen.get(k, 0) < v:
                    E.wait_ge(sm, v)
                    seen[k] = v

    def finish(self):
        E = self.E["sp"]
        for k, s in self.sems.items():
            if self.cnt[k] > 0:
                E.wait_ge(s, self.cnt[k])


def build_program(depth=DEPTH, tiles=None, do_peer=True):
    if tiles is None:
        tiles = list(range(NT))
    nc = bass.Bass("TRN2", target_bir_lowering=False)
    es = contextlib.ExitStack()
    kb = KB(nc, es)

    def dram(name, shape, dt, kind):
        return nc.dram_tensor(name, shape, dt, kind=kind)

    x_in = dram("x_in", [NT * 128, D], F32, "ExternalInput")
    rope_in = dram("rope", [NT * 128, 64], F32, "ExternalInput")
    cak = dram("cak", [DEPTH, SEQ, 128], F32, "ExternalInput")
    cav = dram("cav", [DEPTH, SEQ, 128], F32, "ExternalInput")
    cik = dram("cik", [DEPTH, SEQ, 64], F32, "ExternalInput")
    cbk = dram("cbk", [DEPTH, SEQ, 512], F32, "ExternalInput")
    cbv = dram("cbv", [DEPTH, SEQ, 512], F32, "ExternalInput")
    n1T = dram("n1T", [DEPTH, 128, 8], F32, "ExternalInput")
    n2T = dram("n2T", [DEPTH, 128, 8], F32, "ExternalInput")
    n2r = dram("n2r", [DEPTH, D], F32, "ExternalInput")
    w_in = dram("w_in", [DEPTH, D, NIN], F32, "ExternalInput")
    qn = dram("qn", [DEPTH, 64], F32, "ExternalInput")
    kn = dram("kn", [DEPTH, 64], F32, "ExternalInput")
    ikn = dram("ikn", [DEPTH, 64], F32, "ExternalInput")
    w_pa = dram("w_pa", [DEPTH, 512, D], F32, "ExternalInput")
    w_pb = dram("w_pb", [DEPTH, 512, D], F32, "ExternalInput")
    w_o = dram("w_o", [DEPTH, D, D], F32, "ExternalInput")
    pwq = dram("pwq", [DEPTH, D, D], F32, "ExternalInput")
    pk1T = dram("pk1T", [DEPTH, 64, 8, 128], F32, "ExternalInput")
    pk2T = dram("pk2T", [DEPTH, 64, 8, 128], F32, "ExternalInput")
    puT = [dram("puT%d" % l, [D, 16384], F32, "ExternalInput") for l in range(DEPTH)]
    pv = [dram("pv%d" % l, [16384, D], F32, "ExternalInput") for l in range(DEPTH)]

    y_p = dram("y_p", [SEQ, D], F32, "ExternalOutput")
    y_s = dram("y_s", [DEC, D], F32, "ExternalOutput")
    ak_p = dram("ak_p", [DEPTH, SEQ, 128], F32, "ExternalOutput")
    av_p = dram("av_p", [DEPTH, SEQ, 128], F32, "ExternalOutput")
    ik_p = dram("ik_p", [DEPTH, SEQ, 64], F32, "ExternalOutput")
    bk_p = dram("bk_p", [DEPTH, SEQ, 512], F32, "ExternalOutput")
    bv_p = dram("bv_p", [DEPTH, SEQ, 512], F32, "ExternalOutput")
    ak_s = dram("ak_s", [DEPTH, DEC, 128], F32, "ExternalOutput")
    av_s = dram("av_s", [DEPTH, DEC, 128], F32, "ExternalOutput")
    ik_s = dram("ik_s", [DEPTH, DEC, 64], F32, "ExternalOutput")
    bk_s = dram("bk_s", [DEPTH, DEC, 512], F32, "ExternalOutput")
    bv_s = dram("bv_s", [DEPTH, DEC, 512], F32, "ExternalOutput")
    xs_dram = [dram("xscr%d" % i, [NT * 128, D], F32, "Internal") for i in range(2)]
    xs_res = [[Res() for _ in range(NT)] for _ in range(2)]
    out_res = Res()
    in_res = Res()

    ARENA_F32 = 52000
    arena = es.enter_context(nc.sbuf_tensor("arena", [128, ARENA_F32], F32))
    aoff = [0]
    DTB = {F32: 4, BF16: 2, I32: 4, U32: 4}

    def sb(name, shape, dt):
        nb = DTB[dt]
        n = 1
        for d_ in shape[1:]:
            n *= d_
        nbytes = (n * nb + 31) // 32 * 32
        o = aoff[0]
        assert o % 4 == 0
        aoff[0] = o + nbytes
        assert aoff[0] <= ARENA_F32 * 4, (name, aoff[0])
        v = arena[0:shape[0], o // 4:(o + nbytes) // 4]
        if dt != F32:
            v = v.bitcast(dt)
        v = v[:, 0:n]
        if len(shape) == 3:
            v = v.rearrange("p (a b) -> p a b", a=shape[1])
        elif len(shape) == 4:
            v = v.rearrange("p (a b c) -> p a b c", a=shape[1], b=shape[2])
        return Res(v)

    def pst(name, shape, dt):
        return Res(es.enter_context(nc.psum_tensor(name, shape, dt)))

    ident = sb("ident", [128, 128], BF16)
    ident4 = sb("ident4", [128, 512], BF16)
    negtri = sb("negtri", [128, 128], BF16)
    mlt = sb("mlt", [128, 128], BF16)
    negm4 = sb("negm4", [128, 512], BF16)
    ones1 = sb("ones1", [128, 1], BF16)
    pow2 = sb("pow2", [128, NBIS], F32)
    iota16 = sb("iota16", [128, 16], F32)
    thr16 = sb("thr16", [128, 16], F32)
    identf = sb("identf", [128, 128], F32)
    iota128 = sb("iota128", [128, 128], F32)
    NCST = 128 * 3 + 512 * 2 + NBIS + 16 + 128
    cst_in = dram("cst", [128, NCST], F32, "ExternalInput")
    gq = sb("gq", [128, 64], F32)
    gk = sb("gk", [128, 64], F32)
    gik = sb("gik", [128, 64], F32)
    n1s = sb("n1s", [128, 8], F32)
    n2s = sb("n2s", [128, 8], F32)
    n2b = sb("n2b", [128, D], F32)
    xt = sb("xt", [128, D], F32)
    hb = sb("hb", [128, D], BF16)
    hT = sb("hT", [128, 8, 128], BF16)
    sq = sb("sq", [128, D], F32)
    st8 = sb("st8", [128, 64], F32)
    STW = 1536
    stg = [sb("stg%d" % i, [128, STW], F32) for i in range(2)]
    mark0 = aoff[0]

    WARENA_A = sb("warenaA", [128, 8 * 2888], BF16)
    kaT = sb("kaT", [64, 2, NKS * 128], BF16)
    kiT = sb("kiT", [64, NKS * 128], BF16)
    kbT = sb("kbT", [64, 8, NKS * 128], BF16)
    vaA = sb("vaA", [128, NKS, 2, 65], BF16)
    vbB = sb("vbB", [128, NKS, 8, 64], BF16)
    kvres = [Res() for _ in range(NKS)]
    cstage = sb("cstage", [128, NCST], F32)
    ropet = sb("ropet", [128, 64], F32)
    pf = sb("pf", [128, 512], F32)
    pf2 = sb("pf2", [128, 512], F32)
    pbf = sb("pbf", [128, 512], BF16)
    r1 = sb("r1", [128, 256], F32)
    r2 = sb("r2", [128, 256], F32)
    qaT = sb("qaT", [64, 8, 128], BF16)
    qiT = sb("qiT", [64, 8, 128], BF16)
    qbT = sb("qbT", [64, 8, 128], BF16)
    wi = sb("wi", [128, 8], F32)
    isc = sb("isc", [128, 2560], F32)
    mbias = sb("mbias", [128, 2560], BF16)
    rl = [sb("rl%d" % i, [128, 512], F32) for i in range(2)]
    PT = [sb("PT%d" % i, [128, 512], BF16) for i in range(2)]
    oa = sb("oa", [128, 512], F32)
    ob = sb("ob", [128, 512], F32)
    ebuf = sb("ebuf", [128, 1024], F32)
    spT = sb("spT", [128, 1024], BF16)
    ET = sb("ET", [128, 1024], BF16)
    dd = sb("dd", [128, 8], F32)
    pvs = sb("pvs", [128, 512], F32)
    bis = sb("bis", [128, 8], F32)
    dtab = sb("dtab", [128, NBIS], F32)
    endA = aoff[0]

    aoff[0] = mark0
    WARENA_B = sb("warenaB", [128, 32768], BF16)
    oabb = sb("oabb", [128, D], BF16)
    sga = sb("sga", [128, D], F32)
    sgb = sb("sgb", [128, D], F32)
    oT = sb("oT", [128, 8, 128], BF16)
    mm = sb("mm", [128, D], F32)
    mbf = sb("mbf", [128, D], BF16)
    oabf = sb("oabf", [128, D], F32)
    endB1 = aoff[0]

    aoff[0] = mark0
    h2T_all = sb("h2T_all", [128, 8, NT * 128], BF16)
    WQ = sb("WQ", [128, 8 * 1024], BF16)
    qsb = sb("qsb", [128, D], BF16)
    qT = sb("qT", [64, 16, 128], BF16)
    k12 = sb("k12", [64, 2, 8, 128], BF16)
    s12 = sb("s12", [128, 2, 8, 128], F32)
    swk = sb("swk", [128, 256], F32)
    v12 = sb("v12", [128, 2, 8, 16], F32)
    i12 = sb("i12", [128, 2, 8, 16], U32)
    i12f = sb("i12f", [128, 2, 8, 16], F32)
    cand = sb("cand", [128, 8, 256], F32)
    sc = sb("sc", [128, 8, 16], F32)
    pos = sb("pos", [128, 8, 16], U32)
    posf = sb("posf", [128, 8, 16], F32)
    posrf = sb("posrf", [128, 8, 16], F32)
    poscf = sb("poscf", [128, 8, 16], F32)
    oh = sb("oh", [128, 8, 16, 16], F32)
    sel1 = sb("sel1", [128, 8, 16], F32)
    sel2 = sb("sel2", [128, 8, 16], F32)
    gsm = sb("gsm", [128, 8, 16], F32)
    TT = sb("TT", [128, 3, 128], F32)
    TT2 = sb("TT2", [128, 3, 128], F32)
    P1q = [sb("P1q%d" % i, [128, 32, 128], BF16) for i in range(2)]
    P2q = [sb("P2q%d" % i, [128, 32, 128], BF16) for i in range(2)]
    Gs = sb("Gs", [128, 128, 128], BF16)
    endB2 = aoff[0]

    aoff[0] = mark0
    h2T_all_c = sb("h2T_all_c", [128, 8, NT * 128], BF16)
    accs = sb("accs", [128, NT, D], F32)
    accres = [Res() for _ in range(NT)]
    NBLK = 4
    u16 = [sb("u16_%d" % i, [128, 8, NBLK * 128], BF16) for i in range(2)]
    v16 = [sb("v16_%d" % i, [128, NBLK, D], BF16) for i in range(2)]
    Gc = [sb("Gc%d" % i, [128, 2, NBLK, 128], BF16) for i in range(2)]
    gel = [sb("gel%d" % i, [128, 256], BF16) for i in range(2)]
    cfT = [sb("cfT%d" % i, [128, 256], BF16) for i in range(2)]
    cstg = [sb("cstg%d" % i, [128, 512], F32) for i in range(12)]
    cstg += [Res(stg[i][:, k * 512:(k + 1) * 512]) for i in range(2) for k in range(2)]
    NCS = len(cstg)
    assert NCS == 16
    endC = aoff[0]
    print("arena bytes: persistent", mark0, "A", endA, "B1", endB1, "B2", endB2, "C", endC, "cap", ARENA_F32 * 4)

    Gd = dram("Gd", [NT, 128, 16384], BF16, "Internal")
    gdres = [Res() for _ in range(NT)]
    xmid = dram("xmid", [NT * 128, D], F32, "Internal")
    xmres = [Res() for _ in range(NT)]
    oab_dram = dram("oabscr", [NT * 128, D], F32, "Internal")
    oab_res = [Res() for _ in range(NT)]

    P = [pst("ps%d" % i, [128, 512], F32) for i in range(7)]
    PTR = pst("ptr", [128, 1024], BF16)

    op = kb.op

    op("sp", lambda e: e.dma_start(out=cstage[:], in_=cst_in[:, :]), r=[in_res], w=[cstage], dma=True)
    o = 0
    for dst, wdt in ((ident, 128), (negtri, 128), (mlt, 128), (ident4, 512), (negm4, 512)):
        op("dve", lambda e, dst=dst, o=o, wdt=wdt: e.tensor_copy(out=dst[:], in_=cstage[:, o:o + wdt]), r=[cstage], w=[dst])
        o += wdt
    op("dve", lambda e, o=o: e.tensor_copy(out=pow2[:], in_=cstage[:, o:o + NBIS]), r=[cstage], w=[pow2])
    o += NBIS
    op("dve", lambda e, o=o: e.tensor_copy(out=iota16[:], in_=cstage[:, o:o + 16]), r=[cstage], w=[iota16])
    o += 16
    op("dve", lambda e, o=o: e.tensor_copy(out=iota128[:], in_=cstage[:, o:o + 128]), r=[cstage], w=[iota128])
    op("dve", lambda e: e.tensor_copy(out=identf[:], in_=cstage[:, 0:128]), r=[cstage], w=[identf])
    op("dve", lambda e: e.memset(ones1[:], 1.0), w=[ones1])
    op("dve", lambda e: e.tensor_scalar(out=thr16[:], in0=iota16[:], scalar1=16.0, scalar2=16.0, op0=ALU.mult, op1=ALU.add), r=[iota16], w=[thr16])
    op("dve", lambda e: e.memset(thr16[:, 15:16], 1e9), w=[thr16])
    op("dve", lambda e: e.memset(vaA[:], 1.0), w=[vaA] + kvres)
    kb.barrier()

    def transpose_blocks(src, nblk, dstT, dst_res, ptile=PTR):
        for b in range(nblk):
            op("pe", lambda e, b=b: e.transpose(out=ptile[0:64, b * 128:(b + 1) * 128], in_=src[:, b * 64:(b + 1) * 64], identity=ident[:]),
               r=[src, ident], w=[ptile])
        op("act", lambda e: e.activation(out=dstT, in_=ptile[0:64, 0:nblk * 128].rearrange("p (b n) -> p b n", b=nblk), func=AF.Copy),
           r=[ptile], w=dst_res)

    def rmsnorm_rows(xres, outbf, scratch):
        op("act", lambda e: e.activation(out=scratch[:], in_=xres[:], func=AF.Square, accum_out=st8[:, 0:1]), r=[xres], w=[scratch, st8])
        op("dve", lambda e: e.tensor_scalar(out=st8[:, 1:2], in0=st8[:, 0:1], scalar1=1.0 / D, scalar2=EPS, op0=ALU.mult, op1=ALU.add), r=[st8], w=[st8])
        op("act", lambda e: e.activation(out=st8[:, 2:3], in_=st8[:, 1:2], func=AF.Sqrt), r=[st8], w=[st8])
        op("dve", lambda e: e.reciprocal(out=st8[:, 3:4], in_=st8[:, 2:3]), r=[st8], w=[st8])
        op("dve", lambda e: e.tensor_scalar(out=outbf[:], in0=xres[:], scalar1=st8[:, 3:4], scalar2=None, op0=ALU.mult), r=[xres, st8], w=[outbf])

    def make_hT():
        for c in range(8):
            op("pe", lambda e, c=c: e.transpose(out=PTR[:, c * 128:(c + 1) * 128], in_=hb[:, c * 128:(c + 1) * 128], identity=ident[:]),
               r=[hb, ident], w=[PTR])
        op("act", lambda e: e.activation(out=hT[:], in_=PTR[:, :].rearrange("p (c n) -> p c n", c=8), func=AF.Copy), r=[PTR], w=[hT])

    def headnorm(src_res, src_ap, H, gain, dst):
        W = H * 64
        op("act", lambda e: e.activation(out=sq[:, 0:W], in_=src_ap, func=AF.Square), r=[src_res], w=[sq])
        op("dve", lambda e: e.tensor_reduce(out=st8[:, 8:8 + H], in_=sq[:, 0:W].rearrange("p (h d) -> p h d", h=H), axis=AX.X, op=ALU.add), r=[sq], w=[st8])
        op("dve", lambda e: e.tensor_scalar(out=st8[:, 16:16 + H], in0=st8[:, 8:8 + H], scalar1=1.0 / 64, scalar2=EPS, op0=ALU.mult, op1=ALU.add), r=[st8], w=[st8])
        op("act", lambda e: e.activation(out=st8[:, 24:24 + H], in_=st8[:, 16:16 + H], func=AF.Sqrt), r=[st8], w=[st8])
        op("dve", lambda e: e.reciprocal(out=st8[:, 32:32 + H], in_=st8[:, 24:24 + H]), r=[st8], w=[st8])
        d3 = dst[:, 0:W].rearrange("p (h d) -> p h d", h=H)
        op("dve", lambda e: e.tensor_tensor(out=d3, in0=src_ap.rearrange("p (h d) -> p h d", h=H),
                                            in1=st8[:, 32:32 + H].unsqueeze(2).to_broadcast([128, H, 64]), op=ALU.mult), r=[st8, src_res], w=[dst])
        op("dve", lambda e: e.tensor_tensor(out=d3, in0=d3, in1=gain[:, :].unsqueeze(1).to_broadcast([128, H, 64]), op=ALU.mult), r=[gain, dst], w=[dst])

    def rope(src, H, dst, scale=None):
        W = H * 64
        s3 = src[:, 0:W].rearrange("p (h d) -> p h d", h=H)
        d3 = dst[:, 0:W].rearrange("p (h d) -> p h d", h=H)
        cosb = ropet[:, 0:32].unsqueeze(1).to_broadcast([128, H, 32])
        sinb = ropet[:, 32:64].unsqueeze(1).to_broadcast([128, H, 32])
        a3 = r1[:, 0:H * 32].rearrange("p (h d) -> p h d", h=H)
        b3 = r2[:, 0:H * 32].rearrange("p (h d) -> p h d", h=H)
        op("dve", lambda e: e.tensor_tensor(out=a3, in0=s3[:, :, 0:32], in1=cosb, op=ALU.mult), r=[src, ropet], w=[r1])
        op("dve", lambda e: e.tensor_tensor(out=b3, in0=s3[:, :, 32:64], in1=sinb, op=ALU.mult), r=[src, ropet], w=[r2])
        op("dve", lambda e: e.tensor_tensor(out=d3[:, :, 0:32], in0=a3, in1=b3, op=ALU.subtract), r=[r1, r2], w=[dst])
        op("dve", lambda e: e.tensor_tensor(out=a3, in0=s3[:, :, 32:64], in1=cosb, op=ALU.mult), r=[src, ropet], w=[r1])
        op("dve", lambda e: e.tensor_tensor(out=b3, in0=s3[:, :, 0:32], in1=sinb, op=ALU.mult), r=[src, ropet], w=[r2])
        op("dve", lambda e: e.tensor_tensor(out=d3[:, :, 32:64], in0=a3, in1=b3, op=ALU.add), r=[r1, r2], w=[dst])

    def load_cast_rows(wres, dram_ap_fn, nchunks, width, dst_fn, scale_res=None, scale_col=None):
        i = 0
        for c in range(nchunks):
            for o0 in range(0, width, STW):
                wdt = min(STW, width - o0)
                s = stg[i % 2]
                i += 1
                op("sp", lambda e, c=c, o0=o0, wdt=wdt, s=s: e.dma_start(out=s[:, 0:wdt], in_=dram_ap_fn(c, o0, wdt)), r=[in_res], w=[s], dma=True)
                if scale_res is not None:
                    op("dve", lambda e, c=c, o0=o0, wdt=wdt, s=s: e.tensor_scalar(out=dst_fn(c, o0, wdt), in0=s[:, 0:wdt], scalar1=scale_res[:, scale_col(c):scale_col(c) + 1], scalar2=None, op0=ALU.mult),
                       r=[s, scale_res], w=[wres])
                else:
                    op("pool", lambda e, c=c, o0=o0, wdt=wdt, s=s: e.tensor_copy(out=dst_fn(c, o0, wdt), in_=s[:, 0:wdt]), r=[s], w=[wres])

    WAA = WARENA_A.t
    WA = WARENA_B.t
    NQKV = 2888
    wa_qkv = lambda c, o0, wdt: WAA[:, c * NQKV + o0: c * NQKV + o0 + wdt]
    GOFF = 0
    PAOFF = 8 * 2048
    PBOFF = PAOFF + 4 * 1024
    WOOFF = PBOFF + 4 * 1024
    PQOFF = WOOFF + 8 * 1024

    for l in range(depth):
        xin_d = x_in if l == 0 else xs_dram[(l - 1) % 2]
        xin_r = [in_res] * NT if l == 0 else xs_res[(l - 1) % 2]
        xout_d = xs_dram[l % 2]
        xout_r = xs_res[l % 2]
        last = (l == depth - 1)

        op("sp", lambda e: e.dma_start(out=n1s[:], in_=n1T[l, :, :]), r=[in_res], w=[n1s], dma=True)
        op("sp", lambda e: e.dma_start(out=n2s[:], in_=n2T[l, :, :]), r=[in_res], w=[n2s], dma=True)
        op("sp", lambda e: e.dma_start(out=gq[:], in_=qn[l, :].partition_broadcast(128)), r=[in_res], w=[gq], dma=True)
        op("sp", lambda e: e.dma_start(out=gk[:], in_=kn[l, :].partition_broadcast(128)), r=[in_res], w=[gk], dma=True)
        op("sp", lambda e: e.dma_start(out=gik[:], in_=ikn[l, :].partition_broadcast(128)), r=[in_res], w=[gik], dma=True)
        op("sp", lambda e: e.dma_start(out=n2b[:], in_=n2r[l, :].partition_broadcast(128)), r=[in_res], w=[n2b], dma=True)

        kb.barrier()
        if l > 0:
            op("pool", lambda e: e.memset(vaA[:], 1.0), w=[vaA] + kvres)
        load_cast_rows(WARENA_A, lambda c, o0, wdt: w_in[l, c * 128:(c + 1) * 128, o0:o0 + wdt], 8, NQKV, wa_qkv, n1s, lambda c: c)

        for t in tiles:
            samp = (t == NT - 1)
            nk = t + 1
            if samp:
                for k0 in range(0, 16, 4):
                    for kk in range(k0, k0 + 4):
                        rows = slice(kk * 128, (kk + 1) * 128)
                        s = stg[kk % 2]
                        op("sp", lambda e, s=s, rows=rows: e.dma_start(out=s[:, 0:128], in_=cak[l, rows, :]), r=[in_res], w=[s], dma=True)
                        op("sp", lambda e, s=s, rows=rows: e.dma_start(out=s[:, 128:192], in_=cik[l, rows, :]), r=[in_res], w=[s], dma=True)
                        op("sp", lambda e, s=s, rows=rows: e.dma_start(out=s[:, 192:320], in_=cav[l, rows, :]), r=[in_res], w=[s], dma=True)
                        op("sp", lambda e, s=s, rows=rows: e.dma_start(out=s[:, 512:1024], in_=cbk[l, rows, :]), r=[in_res], w=[s], dma=True)
                        op("sp", lambda e, s=s, rows=rows: e.dma_start(out=s[:, 1024:1536], in_=cbv[l, rows, :]), r=[in_res], w=[s], dma=True)
                        op("dve", lambda e, s=s: e.tensor_copy(out=pbf[:, 0:192], in_=s[:, 0:192]), r=[s], w=[pbf])
                        for b in range(3):
                            op("pe", lambda e, b=b: e.transpose(out=PTR[0:64, b * 128:(b + 1) * 128], in_=pbf[:, b * 64:(b + 1) * 64], identity=ident[:]), r=[pbf, ident], w=[PTR])
                        ks = slice(kk * 128, (kk + 1) * 128)
                        op("act", lambda e, ks=ks: e.activation(out=kaT[:, :, ks], in_=PTR[0:64, 0:256].rearrange("p (b n) -> p b n", b=2), func=AF.Copy), r=[PTR], w=[kvres[kk]])
                        op("act", lambda e, ks=ks: e.activation(out=kiT[:, ks], in_=PTR[0:64, 256:384], func=AF.Copy), r=[PTR], w=[kvres[kk]])
                        op("dve", lambda e, s=s, kk=kk: e.tensor_copy(out=vaA[:, kk, :, 0:64], in_=s[:, 192:320].rearrange("p (g d) -> p g d", g=2)), r=[s], w=[kvres[kk]])
                        op("dve", lambda e, s=s: e.tensor_copy(out=pbf[:, 0:512], in_=s[:, 512:1024]), r=[s], w=[pbf])
                        for b in range(8):
                            op("pe", lambda e, b=b: e.transpose(out=PTR[0:64, b * 128:(b + 1) * 128], in_=pbf[:, b * 64:(b + 1) * 64], identity=ident[:]), r=[pbf, ident], w=[PTR])
                        op("act", lambda e, ks=ks: e.activation(out=kbT[:, :, ks], in_=PTR[0:64, :].rearrange("p (b n) -> p b n", b=8), func=AF.Copy), r=[PTR], w=[kvres[kk]])
                        op("pool", lambda e, s=s, kk=kk: e.tensor_copy(out=vbB[:, kk, :, :], in_=s[:, 1024:1536].rearrange("p (h d) -> p h d", h=8)), r=[s], w=[kvres[kk]])

            rows = slice(t * 128, (t + 1) * 128)
            op("sp", lambda e: e.dma_start(out=xt[:], in_=xin_d[rows, :]), r=[xin_r[t]], w=[xt], dma=True)
            op("sp", lambda e: e.dma_start(out=ropet[:], in_=rope_in[rows, :]), r=[in_res], w=[ropet], dma=True)
            rmsnorm_rows(xt, hb, sq)
            make_hT()

            def proj(pt, c0, wdt):
                for c in range(8):
                    op("pe", lambda e, c=c: e.matmul(pt[:, 0:wdt], lhsT=hT[:, c, :], rhs=WAA[:, c * NQKV + c0: c * NQKV + c0 + wdt], start=(c == 0), stop=(c == 7)),
                       r=[hT, WARENA_A], w=[pt])

            def out_rows(dst_p, dst_s, src, wdt):
                if samp:
                    op("sp", lambda e: e.dma_start(out=dst_s[l, :, :], in_=src[0:DEC, 0:wdt]), r=[src], w=[out_res], dma=True)
                else:
                    op("sp", lambda e: e.dma_start(out=dst_p[l, rows, :], in_=src[:, 0:wdt]), r=[src], w=[out_res], dma=True)

            ks = slice(t * 128, (t + 1) * 128)
            proj(P[0], 0, 512)
            headnorm(P[0], P[0][:, 0:512], 8, gq, pf)
            rope(pf, 8, pf2)
            op("dve", lambda e: e.tensor_scalar(out=pbf[:], in0=pf2[:], scalar1=0.125, scalar2=None, op0=ALU.mult), r=[pf2], w=[pbf])
            transpose_blocks(pbf, 8, qaT[:], [qaT])
            proj(P[1], 512, 256)
            headnorm(P[1], P[1][:, 0:128], 2, gk, pf)
            rope(pf, 2, pf2)
            out_rows(ak_p, ak_s, pf2, 128)
            op("dve", lambda e: e.tensor_copy(out=pbf[:, 0:128], in_=pf2[:, 0:128]), r=[pf2], w=[pbf])
            op("act", lambda e: e.activation(out=pf[:, 0:128], in_=P[1][:, 128:256], func=AF.Copy), r=[P[1]], w=[pf])
            out_rows(av_p, av_s, pf, 128)
            op("dve", lambda e: e.tensor_copy(out=vaA[:, t, :, 0:64], in_=pf[:, 0:128].rearrange("p (g d) -> p g d", g=2)), r=[pf], w=[kvres[t]])
            transpose_blocks(pbf, 2, kaT[:, :, ks], [kvres[t]])
            proj(P[0], 768, 512)
            rope(P[0], 8, pf2)
            op("dve", lambda e: e.tensor_copy(out=pbf[:], in_=pf2[:]), r=[pf2], w=[pbf])
            transpose_blocks(pbf, 8, qiT[:], [qiT])
            proj(P[1], 1280, 72)
            headnorm(P[1], P[1][:, 0:64], 1, gik, pf)
            rope(pf, 1, pf2)
            out_rows(ik_p, ik_s, pf2, 64)
            op("dve", lambda e: e.tensor_copy(out=pbf[:, 0:64], in_=pf2[:, 0:64]), r=[pf2], w=[pbf])
            op("act", lambda e: e.activation(out=wi[:], in_=P[1][:, 64:72], func=AF.Copy), r=[P[1]], w=[wi])
            for b in range(1):
                op("pe", lambda e: e.transpose(out=PTR[0:64, 0:128], in_=pbf[:, 0:64], identity=ident[:]), r=[pbf, ident], w=[PTR])
            op("act", lambda e: e.activation(out=kiT[:, ks], in_=PTR[0:64, 0:128], func=AF.Copy), r=[PTR], w=[kvres[t]])
            proj(P[0], 1352, 512)
            op("act", lambda e: e.activation(out=pbf[:], in_=P[0][:, :], func=AF.Copy, scale=0.125), r=[P[0]], w=[pbf])
            transpose_blocks(pbf, 8, qbT[:], [qbT])
            proj(P[1], 1864, 512)
            op("act", lambda e: e.activation(out=pf[:], in_=P[1][:, :], func=AF.Copy), r=[P[1]], w=[pf])
            out_rows(bk_p, bk_s, pf, 512)
            op("dve", lambda e: e.tensor_copy(out=pbf[:], in_=pf[:]), r=[pf], w=[pbf])
            transpose_blocks(pbf, 8, kbT[:, :, ks], [kvres[t]])
            proj(P[0], 2376, 512)
            op("act", lambda e: e.activation(out=pf2[:], in_=P[0][:, :], func=AF.Copy), r=[P[0]], w=[pf2])
            out_rows(bv_p, bv_s, pf2, 512)
            op("dve", lambda e: e.tensor_copy(out=vbB[:, t, :, :], in_=pf2[:, :].rearrange("p (h d) -> p h d", h=8)), r=[pf2], w=[kvres[t]])

            S = nk * 128
            nblk = (S + 511) // 512
            kvr = [kvres[i] for i in range(nk)]
            for bi in range(nblk):
                c0 = bi * 512
                wdt = min(512, S - c0)
                for h in range(8):
                    pt = P[2 + (h % 2)]
                    rb = rl[h % 2]
                    op("pe", lambda e, h=h, pt=pt: e.matmul(pt[:, 0:wdt], lhsT=qiT[:, h, :], rhs=kiT[:, c0:c0 + wdt], start=True, stop=True), r=[qiT] + kvr, w=[pt])
                    op("act", lambda e, pt=pt, rb=rb: e.activation(out=rb[:, 0:wdt], in_=pt[:, 0:wdt], func=AF.Relu, scale=0.125 * (8 ** -0.5)), r=[pt], w=[rb])
                    if h == 0:
                        op("dve", lambda e, rb=rb: e.tensor_scalar(out=isc[:, c0:c0 + wdt], in0=rb[:, 0:wdt], scalar1=wi[:, 0:1], scalar2=None, op0=ALU.mult), r=[rb, wi], w=[isc])
                    else:
                        op("dve", lambda e, rb=rb, h=h: e.scalar_tensor_tensor(out=isc[:, c0:c0 + wdt], in0=rb[:, 0:wdt], scalar=wi[:, h:h + 1], in1=isc[:, c0:c0 + wdt], op0=ALU.mult, op1=ALU.add), r=[rb, wi, isc], w=[isc])
            Z = [P[2], P[3]]
            Z2 = [P[4], P[5]]
            PVb = P[6]
            TSb = P[0]
            for kt in range(nk):
                ksl = slice(kt * 128, (kt + 1) * 128)
                diag = (kt == nk - 1)
                for h in range(8):
                    op("pe", lambda e, h=h: e.matmul(Z[h // 4][:, (h % 4) * 128:(h % 4 + 1) * 128], lhsT=kbT[:, h, ksl], rhs=qbT[:, h, :], start=True, stop=True), r=[kvres[kt], qbT], w=[Z[h // 4]])
                for hf in range(2):
                    op("act", lambda e, hf=hf: e.activation(out=ebuf[:, hf * 512:(hf + 1) * 512], in_=Z[hf][:, :], func=AF.Exp), r=[Z[hf]], w=[ebuf])
                    op("act", lambda e, hf=hf: e.activation(out=spT[:, hf * 512:(hf + 1) * 512], in_=ebuf[:, hf * 512:(hf + 1) * 512], func=AF.Ln, bias=1.0), r=[ebuf], w=[spT])
                if diag:
                    s3 = spT[:, :].rearrange("p (h q) -> p h q", h=8)
                    op("pool", lambda e, s3=s3: e.tensor_tensor(out=s3, in0=s3, in1=mlt[:, :].unsqueeze(1).to_broadcast([128, 8, 128]), op=ALU.mult), r=[spT, mlt], w=[spT])
                for hf in range(2):
                    op("pe", lambda e, hf=hf: e.matmul(Z2[hf][:, :], lhsT=negtri[:], rhs=spT[:, hf * 512:(hf + 1) * 512], start=True, stop=False), r=[negtri, spT], w=[Z2[hf]])
                    if diag:
                        op("pe", lambda e, hf=hf: e.matmul(Z2[hf][:, :], lhsT=ident[:], rhs=negm4[:], start=False, stop=False), r=[ident, negm4], w=[Z2[hf]])
                    for hh in range(4):
                        h = hf * 4 + hh
                        op("pe", lambda e, h=h, hh=hh, hf=hf: e.matmul(Z2[hf][:, hh * 128:(hh + 1) * 128], lhsT=kbT[:, h, ksl], rhs=qbT[:, h, :], start=False, stop=(hh == 3)), r=[kvres[kt], qbT], w=[Z2[hf]])
                    op("act", lambda e, hf=hf: e.activation(out=ET[:, hf * 512:(hf + 1) * 512], in_=Z2[hf][:, :], func=AF.Exp), r=[Z2[hf]], w=[ET])
                for h in range(8):
                    op("pe", lambda e, h=h: e.matmul(TSb[:, h:h + 1], lhsT=spT[:, h * 128:(h + 1) * 128], rhs=ones1[:], start=True, stop=True), r=[spT, ones1], w=[TSb])
                for h in range(8):
                    op("pe", lambda e, h=h, kt=kt: e.matmul(PVb[:, h * 64:(h + 1) * 64], lhsT=ET[:, h * 128:(h + 1) * 128], rhs=vbB[:, kt, h, :], start=True, stop=True), r=[ET, kvres[kt]], w=[PVb])
                if kt == 0:
                    op("act", lambda e: e.activation(out=ob[:], in_=PVb[:, :], func=AF.Copy), r=[PVb], w=[ob])
                else:
                    op("act", lambda e: e.activation(out=dd[:], in_=TSb[:, 0:8], func=AF.Exp, scale=-1.0), r=[TSb], w=[dd])
                    o3 = ob[:, :].rearrange("p (h d) -> p h d", h=8)
                    op("act", lambda e: e.activation(out=pvs[:], in_=PVb[:, :], func=AF.Copy), r=[PVb], w=[pvs])
                    op("pool", lambda e, o3=o3: e.tensor_tensor(out=o3, in0=o3, in1=dd[:, :].unsqueeze(2).to_broadcast([128, 8, 64]), op=ALU.mult), r=[ob, dd], w=[ob])
                    op("pool", lambda e: e.tensor_tensor(out=ob[:], in0=ob[:], in1=pvs[:], op=ALU.add), r=[ob, pvs], w=[ob])
            op("sp", lambda e: e.dma_start(out=oab_dram[rows, 512:1024], in_=ob[:]), r=[ob], w=[oab_res[t]], dma=True)

            need_thr = nk > 2
            if need_thr:
                op("dve", lambda e: e.tensor_reduce(out=bis[:, 5:6], in_=isc[:, 0:S], axis=AX.X, op=ALU.max), r=[isc], w=[bis])
                op("dve", lambda e: e.tensor_reduce(out=bis[:, 6:7], in_=isc[:, 0:S], axis=AX.X, op=ALU.min), r=[isc], w=[bis])
                op("dve", lambda e: e.tensor_scalar(out=bis[:, 6:7], in0=bis[:, 6:7], scalar1=-1.0, scalar2=None, op0=ALU.mult), r=[bis], w=[bis])
                op("dve", lambda e: e.tensor_tensor(out=bis[:, 4:5], in0=bis[:, 5:6], in1=bis[:, 6:7], op=ALU.max), r=[bis], w=[bis])
                op("dve", lambda e: e.tensor_scalar(out=bis[:, 0:1], in0=bis[:, 4:5], scalar1=-1.0, scalar2=None, op0=ALU.mult), r=[bis], w=[bis])
                op("dve", lambda e: e.tensor_scalar(out=dtab[:], in0=pow2[:], scalar1=bis[:, 4:5], scalar2=2.002, op0=ALU.mult, op1=ALU.mult), r=[bis, pow2], w=[dtab])
            if samp:
                op("dve", lambda e: e.memset(isc[:, 16 * 128 + DEC:17 * 128], NEG), r=[], w=[isc])
            else:
                op("dve", lambda e: e.memset(isc[0:64, t * 128 + 64:(t + 1) * 128], NEG), r=[], w=[isc])
            if need_thr:
                for k in range(NBIS):
                    op("dve", lambda e, k=k: e.tensor_tensor(out=bis[:, 1:2], in0=bis[:, 0:1], in1=dtab[:, k:k + 1], op=ALU.add), r=[bis, dtab], w=[bis])
                    op("dve", lambda e: e.tensor_scalar(out=mbias[:, 0:S], in0=isc[:, 0:S], scalar1=bis[:, 1:2], scalar2=None, op0=ALU.is_ge, op1=ALU.add, accum_out=bis[:, 2:3]), r=[isc, bis], w=[mbias, bis])
                    op("dve", lambda e, k=k: e.scalar_tensor_tensor(out=bis[:, 3:4], in0=bis[:, 2:3], scalar=float(TOPK), in1=dtab[:, k:k + 1], op0=ALU.is_ge, op1=ALU.mult), r=[bis, dtab], w=[bis])
                    op("dve", lambda e: e.tensor_tensor(out=bis[:, 0:1], in0=bis[:, 0:1], in1=bis[:, 3:4], op=ALU.add), r=[bis], w=[bis])
            else:
                op("dve", lambda e: e.memset(bis[:, 0:1], -1e29), r=[], w=[bis])
            op("dve", lambda e: e.tensor_scalar(out=mbias[:, 0:S], in0=isc[:, 0:S], scalar1=bis[:, 0:1], scalar2=-30000.0, op0=ALU.is_lt, op1=ALU.mult), r=[isc, bis], w=[mbias])
            OAp = [P[4], P[5]]
            for kt in range(nk):
                ksl = slice(kt * 128, (kt + 1) * 128)
                for g in range(2):
                    pt = P[2 + g]
                    pb_ = PT[g]
                    op("pe", lambda e, g=g, pt=pt, ksl=ksl: e.matmul(pt[:, :], lhsT=kaT[:, g, ksl], rhs=qaT[:, 4 * g:4 * g + 4, :], start=True, stop=False), r=[kvres[kt], qaT], w=[pt])
                    op("pe", lambda e, pt=pt, ksl=ksl: e.matmul(pt[:, :], lhsT=mbias[:, ksl], rhs=ident4[:], start=False, stop=True), r=[mbias, ident4], w=[pt])
                    op("act", lambda e, pt=pt, pb_=pb_: e.activation(out=pb_[:], in_=pt[:, :], func=AF.Exp), r=[pt], w=[pb_])
                    for hh in range(4):
                        op("pe", lambda e, g=g, hh=hh, pb_=pb_, kt=kt: e.matmul(OAp[g][:, hh * 65:(hh + 1) * 65], lhsT=pb_[:, hh * 128:(hh + 1) * 128], rhs=vaA[:, kt, g, :], start=(kt == 0 and hh == 0), stop=(kt == nk - 1)),
                           r=[pb_, kvres[kt]], w=[OAp[g]])
            for g in range(2):
                o3 = OAp[g][:, 0:260].rearrange("p (h d) -> p h d", h=4)
                op("dve", lambda e, g=g, o3=o3: e.reciprocal(out=st8[:, 40 + 4 * g:44 + 4 * g], in_=o3[:, :, 64]), r=[OAp[g]], w=[st8])
                op("dve", lambda e, g=g, o3=o3: e.tensor_tensor(out=oa[:, g * 256:(g + 1) * 256].rearrange("p (h d) -> p h d", h=4), in0=o3[:, :, 0:64],
                                                                 in1=st8[:, 40 + 4 * g:44 + 4 * g].unsqueeze(2).to_broadcast([128, 4, 64]), op=ALU.mult), r=[OAp[g], st8], w=[oa])
            op("sp", lambda e: e.dma_start(out=oab_dram[rows, 0:512], in_=oa[:]), r=[oa], w=[oab_res[t]], dma=True)

        kb.barrier()
        load_cast_rows(WARENA_B, lambda c, o0, wdt: w_in[l, c * 128:(c + 1) * 128, 2888 + o0:2888 + o0 + wdt], 8, 2048,
                       lambda c, o0, wdt: WA[:, GOFF + c * 2048 + o0: GOFF + c * 2048 + o0 + wdt], n1s, lambda c: c)
        load_cast_rows(WARENA_B, lambda c, o0, wdt: w_pa[l, c * 128:(c + 1) * 128, o0:o0 + wdt], 4, 1024,
                       lambda c, o0, wdt: WA[:, PAOFF + c * 1024 + o0: PAOFF + c * 1024 + o0 + wdt])
        load_cast_rows(WARENA_B, lambda c, o0, wdt: w_pb[l, c * 128:(c + 1) * 128, o0:o0 + wdt], 4, 1024,
                       lambda c, o0, wdt: WA[:, PBOFF + c * 1024 + o0: PBOFF + c * 1024 + o0 + wdt])
        load_cast_rows(WARENA_B, lambda c, o0, wdt: w_o[l, c * 128:(c + 1) * 128, o0:o0 + wdt], 8, 1024,
                       lambda c, o0, wdt: WA[:, WOOFF + c * 1024 + o0: WOOFF + c * 1024 + o0 + wdt])
        for t in tiles:
            samp = (t == NT - 1)
            rows = slice(t * 128, (t + 1) * 128)
            op("sp", lambda e: e.dma_start(out=xt[:], in_=xin_d[rows, :]), r=[xin_r[t]], w=[xt], dma=True)
            rmsnorm_rows(xt, hb, sq)
            make_hT()
            for gi, gdst in ((0, sga), (1, sgb)):
                for hf in range(2):
                    pt = P[hf]
                    c0 = GOFF + gi * 1024 + hf * 512
                    for c in range(8):
                        op("pe", lambda e, c=c, pt=pt, c0=c0: e.matmul(pt[:, :], lhsT=hT[:, c, :], rhs=WA[:, c * 2048 + c0: c * 2048 + c0 + 512], start=(c == 0), stop=(c == 7)), r=[hT, WARENA_B], w=[pt])
                    op("act", lambda e, pt=pt, gdst=gdst, hf=hf: e.activation(out=gdst[:, hf * 512:(hf + 1) * 512], in_=pt[:, :], func=AF.Sigmoid), r=[pt], w=[gdst])
            op("sp", lambda e: e.dma_start(out=oabf[:], in_=oab_dram[rows, :]), r=[oab_res[t]], w=[oabf], dma=True)
            op("pool", lambda e: e.tensor_copy(out=oabb[:], in_=oabf[:]), r=[oabf], w=[oabb])
            for bi, (woff, gdst) in enumerate(((PAOFF, sga), (PBOFF, sgb))):
                for c in range(4):
                    op("pe", lambda e, c=c, bi=bi: e.transpose(out=PTR[:, c * 128:(c + 1) * 128], in_=oabb[:, bi * 512 + c * 128: bi * 512 + (c + 1) * 128], identity=ident[:]), r=[oabb, ident], w=[PTR])
                op("act", lambda e: e.activation(out=oT[:, 0:4, :], in_=PTR[:, 0:512].rearrange("p (c n) -> p c n", c=4), func=AF.Copy), r=[PTR], w=[oT])
                for hf in range(2):
                    pt = P[2 + hf]
                    for c in range(4):
                        op("pe", lambda e, c=c, pt=pt, hf=hf, woff=woff: e.matmul(pt[:, :], lhsT=oT[:, c, :], rhs=WA[:, woff + c * 1024 + hf * 512: woff + c * 1024 + (hf + 1) * 512], start=(c == 0), stop=(c == 3)), r=[oT, WARENA_B], w=[pt])
                    if bi == 0:
                        op("dve", lambda e, pt=pt, hf=hf: e.tensor_tensor(out=mm[:, hf * 512:(hf + 1) * 512], in0=sga[:, hf * 512:(hf + 1) * 512], in1=pt[:, :], op=ALU.mult), r=[sga, pt], w=[mm])
                    else:
                        op("dve", lambda e, pt=pt, hf=hf: e.tensor_tensor(out=sgb[:, hf * 512:(hf + 1) * 512], in0=sgb[:, hf * 512:(hf + 1) * 512], in1=pt[:, :], op=ALU.mult), r=[sgb, pt], w=[sgb])
            op("dve", lambda e: e.tensor_tensor(out=mbf[:], in0=mm[:], in1=sgb[:], op=ALU.add), r=[mm, sgb], w=[mbf])
            for c in range(8):
                op("pe", lambda e, c=c: e.transpose(out=PTR[:, c * 128:(c + 1) * 128], in_=mbf[:, c * 128:(c + 1) * 128], identity=ident[:]), r=[mbf, ident], w=[PTR])
            op("act", lambda e: e.activation(out=oT[:], in_=PTR[:, :].rearrange("p (c n) -> p c n", c=8), func=AF.Copy), r=[PTR], w=[oT])
            for hf in range(2):
                pt = P[4 + hf]
                for c in range(8):
                    op("pe", lambda e, c=c, pt=pt, hf=hf: e.matmul(pt[:, :], lhsT=oT[:, c, :], rhs=WA[:, WOOFF + c * 1024 + hf * 512: WOOFF + c * 1024 + (hf + 1) * 512], start=(c == 0), stop=(c == 7)), r=[oT, WARENA_B], w=[pt])
                op("dve", lambda e, pt=pt, hf=hf: e.tensor_tensor(out=xt[:, hf * 512:(hf + 1) * 512], in0=xt[:, hf * 512:(hf + 1) * 512], in1=pt[:, :], op=ALU.add), r=[xt, pt], w=[xt])

            op("sp", lambda e: e.dma_start(out=xmid[rows, :], in_=xt[:]), r=[xt], w=[xmres[t]], dma=True)

        if do_peer:
            kb.barrier()
            load_cast_rows(WQ, lambda c, o0, wdt: pwq[l, c * 128:(c + 1) * 128, o0:o0 + wdt], 8, 1024,
                           lambda c, o0, wdt: WQ[:, c * 1024 + o0: c * 1024 + o0 + wdt], n2s, lambda c: c)
            for which, src in ((0, pk1T), (1, pk2T)):
                s_ = stg[which]
                op("sp", lambda e, s_=s_, src=src: e.dma_start(out=s_[0:64, 0:1024], in_=src[l, :, :, :].rearrange("d h k -> d (h k)")), r=[in_res], w=[s_], dma=True)
                op("dve", lambda e, s_=s_, which=which: e.tensor_copy(out=k12[:, which, :, :], in_=s_[0:64, 0:1024].rearrange("d (h k) -> d h k", h=8)), r=[s_], w=[k12])
            def b2_front(t, TT):
                rows = slice(t * 128, (t + 1) * 128)
                op("sp", lambda e: e.dma_start(out=xt[:], in_=xmid[rows, :]), r=[xmres[t]], w=[xt], dma=True)
                rmsnorm_rows(xt, hb, sq)
                make_hT()
                op("pool", lambda e: e.tensor_copy(out=h2T_all[:, :, t * 128:(t + 1) * 128], in_=hT[:]), r=[hT], w=[h2T_all])
                for hf in range(2):
                    pt = P[hf]
                    for c in range(8):
                        op("pe", lambda e, c=c, pt=pt, hf=hf: e.matmul(pt[:, :], lhsT=hT[:, c, :], rhs=WQ[:, c * 1024 + hf * 512: c * 1024 + (hf + 1) * 512], start=(c == 0), stop=(c == 7)), r=[hT, WQ], w=[pt])
                    op("act", lambda e, pt=pt, hf=hf: e.activation(out=qsb[:, hf * 512:(hf + 1) * 512], in_=pt[:, :], func=AF.Copy), r=[pt], w=[qsb])
                for hf in range(2):
                    for b in range(8):
                        bb = hf * 8 + b
                        op("pe", lambda e, b=b, bb=bb: e.transpose(out=PTR[0:64, b * 128:(b + 1) * 128], in_=qsb[:, bb * 64:(bb + 1) * 64], identity=ident[:]), r=[qsb, ident], w=[PTR])
                    op("act", lambda e, hf=hf: e.activation(out=qT[:, hf * 8:(hf + 1) * 8, :], in_=PTR[0:64, :].rearrange("p (b n) -> p b n", b=8), func=AF.Copy), r=[PTR], w=[qT])
                yield
                for which in range(2):
                    for h in range(8):
                        pt = P[2 + h // 4]
                        op("pe", lambda e, h=h, pt=pt, which=which: e.matmul(pt[:, (h % 4) * 128:(h % 4 + 1) * 128], lhsT=qT[:, h * 2 + which, :], rhs=k12[:, which, h, :], start=True, stop=True), r=[qT, k12], w=[pt])
                    for hf in range(2):
                        pt = P[2 + hf]
                        op("act", lambda e, pt=pt, which=which, hf=hf: e.activation(out=s12[:, which, hf * 4:(hf + 1) * 4, :], in_=pt[:, :].rearrange("p (h k) -> p h k", h=4), func=AF.Copy), r=[pt], w=[s12])
                yield
                for which in range(2):
                    for h in range(8):
                        sv = s12[:, which, h, :]
                        op("dve", lambda e, sv=sv, which=which, h=h: e.max(out=v12[:, which, h, 0:8], in_=sv), r=[s12], w=[v12])
                        op("dve", lambda e, sv=sv, which=which, h=h: e.max_index(out=i12[:, which, h, 0:8], in_max=v12[:, which, h, 0:8], in_values=sv), r=[s12, v12], w=[i12])
                        op("dve", lambda e, sv=sv, which=which, h=h: e.match_replace(out=swk[:, 0:128], in_to_replace=v12[:, which, h, 0:8], in_values=sv, imm_value=NEG), r=[s12, v12], w=[swk])
                        op("dve", lambda e, which=which, h=h: e.max(out=v12[:, which, h, 8:16], in_=swk[:, 0:128]), r=[swk], w=[v12])
                        op("dve", lambda e, which=which, h=h: e.max_index(out=i12[:, which, h, 8:16], in_max=v12[:, which, h, 8:16], in_values=swk[:, 0:128]), r=[swk, v12], w=[i12])
                yield
                op("dve", lambda e: e.tensor_copy(out=i12f[:], in_=i12[:]), r=[i12], w=[i12f])
                c4 = cand[:, :, :].rearrange("p h (r c) -> p h r c", r=16)
                op("dve", lambda e: e.tensor_tensor(out=c4, in0=v12[:, 0, :, :].unsqueeze(3).to_broadcast([128, 8, 16, 16]), in1=v12[:, 1, :, :].unsqueeze(2).to_broadcast([128, 8, 16, 16]), op=ALU.add), r=[v12], w=[cand])
                for h in range(8):
                    op("dve", lambda e, h=h: e.max(out=sc[:, h, 0:8], in_=cand[:, h, :]), r=[cand], w=[sc])
                    op("dve", lambda e, h=h: e.max_index(out=pos[:, h, 0:8], in_max=sc[:, h, 0:8], in_values=cand[:, h, :]), r=[cand, sc], w=[pos])
                    op("dve", lambda e, h=h: e.match_replace(out=swk[:, 0:256], in_to_replace=sc[:, h, 0:8], in_values=cand[:, h, :], imm_value=NEG), r=[cand, sc], w=[swk])
                    op("dve", lambda e, h=h: e.max(out=sc[:, h, 8:16], in_=swk[:, 0:256]), r=[swk], w=[sc])
                    op("dve", lambda e, h=h: e.max_index(out=pos[:, h, 8:16], in_max=sc[:, h, 8:16], in_values=swk[:, 0:256]), r=[swk, sc], w=[pos])
                op("dve", lambda e: e.tensor_copy(out=posf[:], in_=pos[:]), r=[pos], w=[posf])
                op("dve", lambda e: e.tensor_tensor(out=oh[:], in0=posf[:, :, :].unsqueeze(3).to_broadcast([128, 8, 16, 16]),
                                                    in1=thr16[:, :].unsqueeze(1).unsqueeze(1).to_broadcast([128, 8, 16, 16]), op=ALU.is_ge), r=[posf, thr16], w=[oh])
                op("dve", lambda e: e.tensor_reduce(out=posrf[:], in_=oh[:], axis=AX.X, op=ALU.add), r=[oh], w=[posrf])
                op("dve", lambda e: e.scalar_tensor_tensor(out=poscf[:], in0=posrf[:], scalar=-16.0, in1=posf[:], op0=ALU.mult, op1=ALU.add), r=[posrf, posf], w=[poscf])
                io4 = iota16[:, :].unsqueeze(1).unsqueeze(1).to_broadcast([128, 8, 16, 16])
                for pf_, which, dsel in ((posrf, 0, sel1), (poscf, 1, sel2)):
                    op("dve", lambda e, pf_=pf_: e.tensor_tensor(out=oh[:], in0=io4, in1=pf_[:, :, :].unsqueeze(3).to_broadcast([128, 8, 16, 16]), op=ALU.is_equal), r=[iota16, pf_], w=[oh])
                    op("dve", lambda e, which=which: e.tensor_tensor(out=oh[:], in0=oh[:], in1=i12f[:, which, :, :].unsqueeze(2).to_broadcast([128, 8, 16, 16]), op=ALU.mult), r=[oh, i12f], w=[oh])
                    op("dve", lambda e, dsel=dsel: e.tensor_reduce(out=dsel[:], in_=oh[:], axis=AX.X, op=ALU.add), r=[oh], w=[dsel])
                op("dve", lambda e: e.tensor_tensor(out=gsm[:], in0=sc[:], in1=sc[:, :, 0:1].to_broadcast([128, 8, 16]), op=ALU.subtract), r=[sc], w=[gsm])
                op("act", lambda e: e.activation(out=gsm[:], in_=gsm[:], func=AF.Exp), r=[gsm], w=[gsm])
                op("dve", lambda e: e.tensor_reduce(out=st8[:, 48:56], in_=gsm[:], axis=AX.X, op=ALU.add), r=[gsm], w=[st8])
                op("dve", lambda e: e.reciprocal(out=st8[:, 56:64], in_=st8[:, 48:56]), r=[st8], w=[st8])
                op("dve", lambda e: e.tensor_tensor(out=gsm[:], in0=gsm[:], in1=st8[:, 56:64].unsqueeze(2).to_broadcast([128, 8, 16]), op=ALU.mult), r=[gsm, st8], w=[gsm])
                for i_, src in enumerate((sel1, sel2, gsm)):
                    op("pe", lambda e, i_=i_, src=src: e.transpose(out=P[0][:, i_ * 128:(i_ + 1) * 128], in_=src[:, :, :].rearrange("p h k -> p (h k)"), identity=identf[:]), r=[src, identf], w=[P[0]])
                op("act", lambda e: e.activation(out=TT[:], in_=P[0][:, 0:384].rearrange("p (a n) -> p a n", a=3), func=AF.Copy), r=[P[0]], w=[TT])

            def b2_back(t, TT):
                iob = iota128[:, :].unsqueeze(1).to_broadcast([128, 32, 128])
                for qq in range(4):
                    n0 = qq * 32
                    p1 = P1q[qq % 2]
                    p2 = P2q[qq % 2]
                    op("dve", lambda e, n0=n0, p2=p2: e.tensor_tensor(out=p2[:], in0=iob, in1=TT[:, 1, n0:n0 + 32].unsqueeze(2).to_broadcast([128, 32, 128]), op=ALU.is_equal), r=[iota128, TT], w=[p2])
                    op("dve", lambda e, n0=n0, p1=p1: e.tensor_tensor(out=p1[:], in0=iob, in1=TT[:, 0, n0:n0 + 32].unsqueeze(2).to_broadcast([128, 32, 128]), op=ALU.is_equal), r=[iota128, TT], w=[p1])
                    op("pool", lambda e, n0=n0, p1=p1: e.tensor_tensor(out=p1[:], in0=p1[:], in1=TT[:, 2, n0:n0 + 32].unsqueeze(2).to_broadcast([128, 32, 128]), op=ALU.mult), r=[p1, TT], w=[p1])
                    for q4 in range(8):
                        bank = P[4 + (q4 % 3)]
                        for k in range(4):
                            n = q4 * 4 + k
                            op("pe", lambda e, bank=bank, k=k, n=n, p1=p1, p2=p2: e.matmul(bank[:, :].rearrange("p (i n) -> p n i", n=4)[:, k, :], lhsT=p1[:, n, :], rhs=p2[:, n, :], start=True, stop=True), r=[p1, p2], w=[bank])
                        nn = n0 + q4 * 4
                        op("act", lambda e, bank=bank, nn=nn: e.activation(out=Gs[:, :, nn:nn + 4], in_=bank[:, :].rearrange("p (i n) -> p i n", n=4), func=AF.Copy), r=[bank], w=[Gs])
                    yield

            TTs = [TT, TT2]
            if tiles:
                for _ in b2_front(tiles[0], TTs[0]):
                    pass
            for i_, t in enumerate(tiles):
                fg = b2_front(tiles[i_ + 1], TTs[(i_ + 1) % 2]) if i_ + 1 < len(tiles) else iter(())
                bg = b2_back(t, TTs[i_ % 2])
                for _q in range(4):
                    next(bg, None)
                    next(fg, None)
                for _ in bg:
                    pass
                for _ in fg:
                    pass
                op("sp", lambda e: e.dma_start(out=Gd[t, :, :], in_=Gs[:, :, :].rearrange("p i n -> p (i n)")), r=[Gs], w=[gdres[t]], dma=True)

            kb.barrier()
            pvv = pv[l][:, :].rearrange("(i1 i2) d -> i1 i2 d", i2=128)
            tgroups = [tiles[i:i + 2] for i in range(0, len(tiles), 2)]
            NSB = 128 // NBLK
            cring = [0]
            pend_casts = {}

            def emit_loads(sbk):
                ub_ = u16[sbk % 2]
                vb_ = v16[sbk % 2]
                casts = []
                for c in range(8):
                    s_ = cstg[cring[0] % NCS]
                    cring[0] += 1
                    op("pool", lambda e, s_=s_, c=c: e.dma_start(out=s_[:, 0:512], in_=puT[l][c * 128:(c + 1) * 128, sbk * NBLK * 128:(sbk + 1) * NBLK * 128]), r=[in_res], w=[s_], dma=True)
                    casts.append(lambda s_=s_, c=c, ub_=ub_: op("act", lambda e: e.activation(out=ub_[:, c, :], in_=s_[:, 0:512], func=AF.Copy, scale=n2s[:, c:c + 1]), r=[s_, n2s], w=[ub_]))
                for blk in range(NBLK):
                    for hf in range(2):
                        s_ = cstg[cring[0] % NCS]
                        cring[0] += 1
                        op("pool", lambda e, s_=s_, blk=blk, hf=hf: e.dma_start(out=s_[:, 0:512], in_=pvv[:, sbk * NBLK + blk, hf * 512:(hf + 1) * 512]), r=[in_res], w=[s_], dma=True)
                        casts.append(lambda s_=s_, blk=blk, hf=hf, vb_=vb_: op("act", lambda e: e.activation(out=vb_[:, blk, hf * 512:(hf + 1) * 512], in_=s_[:, 0:512], func=AF.Copy), r=[s_], w=[vb_]))
                return casts

            items = [(sbk, gi_, blk) for sbk in range(NSB) for gi_ in range(len(tgroups)) for blk in range(NBLK)]

            def emit_a(it_):
                sbk, gi_, blk = it_
                ub_ = u16[sbk % 2]
                tl = tgroups[gi_]
                nt_ = len(tl)
                ntk = 128 * nt_
                A = P[blk % 2]
                if blk == 0:
                    gc = Gc[(sbk * len(tgroups) + gi_) % 2]
                    for ti, t in enumerate(tl):
                        op("sp", lambda e, ti=ti, t=t: e.dma_start(out=gc[:, ti, :, :], in_=Gd[t, :, sbk * NBLK * 128:(sbk + 1) * NBLK * 128].rearrange("p (i n) -> p i n", i=NBLK)), r=[gdres[t]], w=[gc], dma=True)
                contiguous = (nt_ == 2 and tl[1] == tl[0] + 1) or nt_ == 1
                if contiguous:
                    for c in range(8):
                        op("pe", lambda e, c=c: e.matmul(A[:, 0:ntk], lhsT=ub_[:, c, blk * 128:(blk + 1) * 128], rhs=h2T_all_c[:, c, tl[0] * 128: tl[0] * 128 + ntk], start=(c == 0), stop=(c == 7)), r=[ub_, h2T_all_c], w=[A])
                else:
                    for ti, t in enumerate(tl):
                        for c in range(8):
                            op("pe", lambda e, c=c, ti=ti, t=t: e.matmul(A[:, ti * 128:(ti + 1) * 128], lhsT=ub_[:, c, blk * 128:(blk + 1) * 128], rhs=h2T_all_c[:, c, t * 128:(t + 1) * 128], start=(c == 0 and ti == 0), stop=(c == 7)), r=[ub_, h2T_all_c], w=[A])

            def emit_rest(it_):
                sbk, gi_, blk = it_
                vb_ = v16[sbk % 2]
                tl = tgroups[gi_]
                nt_ = len(tl)
                ntk = 128 * nt_
                A = P[blk % 2]
                ge = gel[blk % 2]
                cf = cfT[blk % 2]
                gc = Gc[(sbk * len(tgroups) + gi_) % 2]
                op("act", lambda e: e.activation(out=ge[:, 0:ntk], in_=A[:, 0:ntk], func=AF.Gelu), r=[A], w=[ge])
                op("dve", lambda e: e.tensor_tensor(out=cf[:, 0:ntk].rearrange("p (t n) -> p t n", t=nt_), in0=ge[:, 0:ntk].rearrange("p (t n) -> p t n", t=nt_), in1=gc[:, 0:nt_, blk, :], op=ALU.mult), r=[ge, gc], w=[cf])
                for ti, t in enumerate(tl):
                    for hf in range(2):
                        ab = P[2 + ti * 2 + hf]
                        op("pe", lambda e, ab=ab, ti=ti, hf=hf: e.matmul(ab[:, :], lhsT=cf[:, ti * 128:(ti + 1) * 128], rhs=vb_[:, blk, hf * 512:(hf + 1) * 512], start=(blk == 0), stop=(blk == NBLK - 1)), r=[cf, vb_], w=[ab])
                if blk == NBLK - 1:
                    for ti, t in enumerate(tl):
                        for hf in range(2):
                            ab = P[2 + ti * 2 + hf]
                            if sbk == 0:
                                op("dve", lambda e, ab=ab, t=t, hf=hf: e.tensor_copy(out=accs[:, t, hf * 512:(hf + 1) * 512], in_=ab[:, :]), r=[ab], w=[accres[t]])
                            else:
                                op("dve", lambda e, ab=ab, t=t, hf=hf: e.tensor_tensor(out=accs[:, t, hf * 512:(hf + 1) * 512], in0=accs[:, t, hf * 512:(hf + 1) * 512], in1=ab[:, :], op=ALU.add), r=[ab, accres[t]], w=[accres[t]])

            for cst_ in emit_loads(0):
                cst_()
            ng = len(tgroups)
            cast_at = {max(0, ng // 3): (0, 8), max(1, (2 * ng) // 3): (8, 16)} if ng >= 3 else {0: (0, 16)}
            emit_a(items[0])
            for i_, it_ in enumerate(items):
                sbk, gi_, blk = it_
                if gi_ == 0 and blk == 0 and sbk + 1 < NSB:
                    pend_casts[sbk + 1] = emit_loads(sbk + 1)
                if blk == 0 and gi_ in cast_at and (sbk + 1) in pend_casts:
                    a_, b_ = cast_at[gi_]
                    for cst_ in pend_casts[sbk + 1][a_:b_]:
                        cst_()
                if i_ + 1 < len(items):
                    emit_a(items[i_ + 1])
                emit_rest(it_)

        for t in tiles:
            samp = (t == NT - 1)
            rows = slice(t * 128, (t + 1) * 128)
            op("sp", lambda e: e.dma_start(out=xt[:], in_=xmid[rows, :]), r=[xmres[t]], w=[xt], dma=True)
            if do_peer:
                op("dve", lambda e: e.tensor_tensor(out=xt[:], in0=xt[:], in1=accs[:, t, :], op=ALU.add), r=[xt, accres[t]], w=[xt])
            if last:
                if samp:
                    op("sp", lambda e: e.dma_start(out=y_s[:, :], in_=xt[0:DEC, :]), r=[xt], w=[out_res], dma=True)
                else:
                    op("sp", lambda e: e.dma_start(out=y_p[rows, :], in_=xt[:]), r=[xt], w=[out_res], dma=True)
            else:
                op("sp", lambda e: e.dma_start(out=xout_d[rows, :], in_=xt[:]), r=[xt], w=[xout_r[t]], dma=True)

    kb.finish()
    es.close()
    return nc, kb.ninst


def _consts():
    j = np.arange(128)
    ident = np.eye(128, dtype=np.float32)
    negtri = -(j[:, None] >= j[None, :]).astype(np.float32)
    mlt = (j[:, None] < j[None, :]).astype(np.float32)
    ident4 = np.tile(ident, (1, 4))
    negm = np.where(j[:, None] >= j[None, :], -30000.0, 0.0).astype(np.float32)
    negm4 = np.tile(negm, (1, 4))
    pow2 = np.tile((0.5 ** np.arange(1, NBIS + 1)).astype(np.float32)[None, :], (128, 1))
    iota = np.tile(np.arange(16, dtype=np.float32)[None, :], (128, 1))
    iota128 = np.tile(np.arange(128, dtype=np.float32)[None, :], (128, 1))
    return np.concatenate([ident, negtri, mlt, ident4, negm4, pow2, iota, iota128], axis=1).astype(np.float32)


def _rope_table():
    pos = np.concatenate([np.arange(SEQ), SEQ + np.arange(128)]).astype(np.float32)
    half = 32
    inv = (10000.0 ** (-np.arange(half, dtype=np.float32) / half)).astype(np.float32)
    ang = pos[:, None] * inv[None, :]
    return np.concatenate([np.cos(ang), np.sin(ang)], axis=1).astype(np.float32)


_PROG = {}


def _prep(x_prompt, x_sample, cache_a_k, cache_a_v, cache_idx_k, cache_b_k, cache_b_v,
          norm1, w_in, q_norm_a, k_norm_a, idx_k_norm, w_pa, w_pb, w_o, norm2,
          peer_wq, peer_k1, peer_k2, peer_u, peer_v, cores=range(8)):
    f = lambda a: np.ascontiguousarray(np.asarray(a, dtype=np.float32))
    x_prompt = f(x_prompt); x_sample = f(x_sample)
    shared = {
        "rope": _rope_table(),
        "cst": _consts(),
        "n1T": f(np.asarray(norm1).reshape(DEPTH, 8, 128).transpose(0, 2, 1)),
        "n2T": f(np.asarray(norm2).reshape(DEPTH, 8, 128).transpose(0, 2, 1)),
        "n2r": f(norm2),
        "w_in": f(w_in), "qn": f(q_norm_a), "kn": f(k_norm_a), "ikn": f(idx_k_norm),
        "w_pa": f(w_pa), "w_pb": f(w_pb), "w_o": f(w_o), "pwq": f(peer_wq),
        "pk1T": f(np.asarray(peer_k1).transpose(0, 3, 1, 2)),
        "pk2T": f(np.asarray(peer_k2).transpose(0, 3, 1, 2)),
    }
    pu = np.asarray(peer_u); pvv = np.asarray(peer_v)
    for l in range(DEPTH):
        shared["puT%d" % l] = f(pu[l].reshape(128, 128, D).transpose(2, 1, 0).reshape(D, 16384))
        shared["pv%d" % l] = f(pvv[l])
    cak = np.asarray(cache_a_k); cav = np.asarray(cache_a_v); cik = np.asarray(cache_idx_k)
    cbk = np.asarray(cache_b_k); cbv = np.asarray(cache_b_v)
    in_maps = []
    for b in cores:
        xa = np.zeros((NT * 128, D), np.float32)
        xa[:SEQ] = x_prompt[b]
        xa[SEQ:SEQ + DEC] = x_sample[b]
        m = dict(shared)
        m["x_in"] = xa
        m["cak"] = f(cak[:, b].reshape(DEPTH, SEQ, 128))
        m["cav"] = f(cav[:, b].reshape(DEPTH, SEQ, 128))
        m["cik"] = f(cik[:, b].reshape(DEPTH, SEQ, 64))
        m["cbk"] = f(cbk[:, b].reshape(DEPTH, SEQ, 512))
        m["cbv"] = f(cbv[:, b].reshape(DEPTH, SEQ, 512))
        in_maps.append(m)
    return in_maps


def kernel(**inputs):
    if "full" not in _PROG:
        _PROG["full"] = build_program()[0]
    nc = _PROG["full"]
    in_maps = _prep(**inputs)
    res = run_bass_kernel_spmd(nc, in_maps, core_ids=list(range(8)))
    R = res.results
    st = lambda k, shp: np.stack([np.asarray(R[b][k], dtype=np.float32) for b in range(8)], axis=1).reshape(shp)
    y_p = np.stack([np.asarray(R[b]["y_p"], dtype=np.float32) for b in range(8)], axis=0)
    y_s = np.stack([np.asarray(R[b]["y_s"], dtype=np.float32) for b in range(8)], axis=0)
    return (y_p, y_s,
            st("ak_p", (DEPTH, 8, SEQ, 2, 64)), st("av_p", (DEPTH, 8, SEQ, 2, 64)), st("ik_p", (DEPTH, 8, SEQ, 64)),
            st("bk_p", (DEPTH, 8, SEQ, 8, 64)), st("bv_p", (DEPTH, 8, SEQ, 8, 64)),
            st("ak_s", (DEPTH, 8, DEC, 2, 64)), st("av_s", (DEPTH, 8, DEC, 2, 64)), st("ik_s", (DEPTH, 8, DEC, 64)),
            st("bk_s", (DEPTH, 8, DEC, 8, 64)), st("bv_s", (DEPTH, 8, DEC, 8, 64)))
```

```python
import contextlib
import numpy as np
import concourse.bass as bass
import concourse.mybir as mybir
from concourse.bass_utils import run_bass_kernel_spmd

F32 = mybir.dt.float32
BF16 = mybir.dt.bfloat16
I32 = mybir.dt.int32
U32 = mybir.dt.uint32
ALU = mybir.AluOpType
AF = mybir.ActivationFunctionType
AX = mybir.AxisListType

D = 1024
DEPTH = 4
NT = 17
NKS = 17
SEQ = 2048
DEC = 16
NIN = 4936
EPS = 1e-6
NEG = -1e30
TOPK = 256
NBIS = 22


class Res:
    __slots__ = ("t", "w", "r")

    def __init__(self, t=None):
        self.t = t
        self.w = None
        self.r = {}

    def __getitem__(self, k):
        return self.t[k]


class KB:
    def __init__(self, nc, es):
        self.nc = nc
        self.es = es
        self.E = {"pe": nc.tensor, "act": nc.scalar, "dve": nc.vector, "pool": nc.gpsimd, "sp": nc.sync}
        self.sems = {}
        self.cnt = {}
        self.seen = {e: {} for e in self.E}
        self.ninst = 0
        self.dpool = {"sp": ["dsp%d" % i for i in range(24)], "pool": ["dpl%d" % i for i in range(6)]}
        self.dnext = {"sp": 0, "pool": 0}

    def sem(self, key):
        if key not in self.sems:
            self.sems[key] = self.es.enter_context(self.nc.semaphore("s_" + key))
            self.cnt[key] = 0
        return self.sems[key]

    def op(self, eng, fn, r=(), w=(), dma=False):
        deps = {}
        for x in r:
            if x.w is not None:
                k, v = x.w
                if deps.get(k, 0) < v:
                    deps[k] = v
        inorder = (not dma) and eng in ("act", "dve")
        for x in w:
            if x.w is not None:
                k, v = x.w
                if not (inorder and k == eng) and deps.get(k, 0) < v:
                    deps[k] = v
            for k, v in x.r.items():
                if not (inorder and k == eng) and deps.get(k, 0) < v:
                    deps[k] = v
        E = self.E[eng]
        seen = self.seen[eng]
        for k, v in deps.items():
            if k == "pe" and eng == "pe" and not dma:
                continue
            if seen.get(k, 0) < v:
                E.wait_ge(self.sem(k), v)
                seen[k] = v
        if dma:
            pl = self.dpool[eng]
            key = pl[self.dnext[eng]]
            self.dnext[eng] = (self.dnext[eng] + 1) % len(pl)
            s = self.sem(key)
            prev = self.cnt[key]
            if prev > 0 and seen.get(key, 0) < prev:
                E.wait_ge(s, prev)
                seen[key] = prev
        else:
            key = eng
            s = self.sem(key)
        ins = fn(E)
        inc = 16 if dma else 1
        self.cnt[key] += inc
        c = self.cnt[key]
        ins.then_inc(s, inc)
        for x in r:
            x.r[key] = c
        for x in w:
            x.w = (key, c)
            x.r = {}
        self.ninst += 1
        return ins

    def barrier(self):
        for en, E in self.E.items():
            seen = self.seen[en]
            for k, sm in self.sems.items():
                v = self.cnt[k]
                if v > 0 and seen.get(k, 0) < v:
                    E.wait_ge(sm, v)
                    seen[k] = v

    def finish(self):
        E = self.E["sp"]
        for k, s in self.sems.items():
            if self.cnt[k] > 0:
                E.wait_ge(s, self.cnt[k])


def build_program(depth=DEPTH, tiles=None, do_peer=True):
    if tiles is None:
        tiles = list(range(NT))
    nc = bass.Bass("TRN2", target_bir_lowering=False)
    es = contextlib.ExitStack()
    kb = KB(nc, es)

    def dram(name, shape, dt, kind):
        return nc.dram_tensor(name, shape, dt, kind=kind)

    x_in = dram("x_in", [NT * 128, D], F32, "ExternalInput")
    rope_in = dram("rope", [NT * 128, 64], F32, "ExternalInput")
    cak = dram("cak", [DEPTH, SEQ, 128], F32, "ExternalInput")
    cav = dram("cav", [DEPTH, SEQ, 128], F32, "ExternalInput")
    cik = dram("cik", [DEPTH, SEQ, 64], F32, "ExternalInput")
    cbk = dram("cbk", [DEPTH, SEQ, 512], F32, "ExternalInput")
    cbv = dram("cbv", [DEPTH, SEQ, 512], F32, "ExternalInput")
    n1T = dram("n1T", [DEPTH, 128, 8], F32, "ExternalInput")
    n2T = dram("n2T", [DEPTH, 128, 8], F32, "ExternalInput")
    n2r = dram("n2r", [DEPTH, D], F32, "ExternalInput")
    w_in = dram("w_in", [DEPTH, D, NIN], F32, "ExternalInput")
    qn = dram("qn", [DEPTH, 64], F32, "ExternalInput")
    kn = dram("kn", [DEPTH, 64], F32, "ExternalInput")
    ikn = dram("ikn", [DEPTH, 64], F32, "ExternalInput")
    w_pa = dram("w_pa", [DEPTH, 512, D], F32, "ExternalInput")
    w_pb = dram("w_pb", [DEPTH, 512, D], F32, "ExternalInput")
    w_o = dram("w_o", [DEPTH, D, D], F32, "ExternalInput")
    pwq = dram("pwq", [DEPTH, D, D], F32, "ExternalInput")
    pk1T = dram("pk1T", [DEPTH, 64, 8, 128], F32, "ExternalInput")
    pk2T = dram("pk2T", [DEPTH, 64, 8, 128], F32, "ExternalInput")
    puT = [dram("puT%d" % l, [D, 16384], F32, "ExternalInput") for l in range(DEPTH)]
    pv = [dram("pv%d" % l, [16384, D], F32, "ExternalInput") for l in range(DEPTH)]

    y_p = dram("y_p", [SEQ, D], F32, "ExternalOutput")
    y_s = dram("y_s", [DEC, D], F32, "ExternalOutput")
    ak_p = dram("ak_p", [DEPTH, SEQ, 128], F32, "ExternalOutput")
    av_p = dram("av_p", [DEPTH, SEQ, 128], F32, "ExternalOutput")
    ik_p = dram("ik_p", [DEPTH, SEQ, 64], F32, "ExternalOutput")
    bk_p = dram("bk_p", [DEPTH, SEQ, 512], F32, "ExternalOutput")
    bv_p = dram("bv_p", [DEPTH, SEQ, 512], F32, "ExternalOutput")
    ak_s = dram("ak_s", [DEPTH, DEC, 128], F32, "ExternalOutput")
    av_s = dram("av_s", [DEPTH, DEC, 128], F32, "ExternalOutput")
    ik_s = dram("ik_s", [DEPTH, DEC, 64], F32, "ExternalOutput")
    bk_s = dram("bk_s", [DEPTH, DEC, 512], F32, "ExternalOutput")
    bv_s = dram("bv_s", [DEPTH, DEC, 512], F32, "ExternalOutput")
    xs_dram = [dram("xscr%d" % i, [NT * 128, D], F32, "Internal") for i in range(2)]
    xs_res = [[Res() for _ in range(NT)] for _ in range(2)]
    out_res = Res()
    in_res = Res()

    ARENA_F32 = 52000
    arena = es.enter_context(nc.sbuf_tensor("arena", [128, ARENA_F32], F32))
    aoff = [0]
    DTB = {F32: 4, BF16: 2, I32: 4, U32: 4}

    def sb(name, shape, dt):
        nb = DTB[dt]
        n = 1
        for d_ in shape[1:]:
            n *= d_
        nbytes = (n * nb + 31) // 32 * 32
        o = aoff[0]
        assert o % 4 == 0
        aoff[0] = o + nbytes
        assert aoff[0] <= ARENA_F32 * 4, (name, aoff[0])
        v = arena[0:shape[0], o // 4:(o + nbytes) // 4]
        if dt != F32:
            v = v.bitcast(dt)
        v = v[:, 0:n]
        if len(shape) == 3:
            v = v.rearrange("p (a b) -> p a b", a=shape[1])
        elif len(shape) == 4:
            v = v.rearrange("p (a b c) -> p a b c", a=shape[1], b=shape[2])
        return Res(v)

    def pst(name, shape, dt):
        return Res(es.enter_context(nc.psum_tensor(name, shape, dt)))

    ident = sb("ident", [128, 128], BF16)
    ident4 = sb("ident4", [128, 512], BF16)
    negtri = sb("negtri", [128, 128], BF16)
    mlt = sb("mlt", [128, 128], BF16)
    negm4 = sb("negm4", [128, 512], BF16)
    ones1 = sb("ones1", [128, 1], BF16)
    pow2 = sb("pow2", [128, NBIS], F32)
    iota16 = sb("iota16", [128, 16], F32)
    thr16 = sb("thr16", [128, 16], F32)
    identf = sb("identf", [128, 128], F32)
    iota128 = sb("iota128", [128, 128], F32)
    NCST = 128 * 3 + 512 * 2 + NBIS + 16 + 128
    cst_in = dram("cst", [128, NCST], F32, "ExternalInput")
    gq = sb("gq", [128, 64], F32)
    gk = sb("gk", [128, 64], F32)
    gik = sb("gik", [128, 64], F32)
    n1s = sb("n1s", [128, 8], F32)
    n2s = sb("n2s", [128, 8], F32)
    n2b = sb("n2b", [128, D], F32)
    xt = sb("xt", [128, D], F32)
    hb = sb("hb", [128, D], BF16)
    hT = sb("hT", [128, 8, 128], BF16)
    sq = sb("sq", [128, D], F32)
    st8 = sb("st8", [128, 64], F32)
    STW = 1536
    stg = [sb("stg%d" % i, [128, STW], F32) for i in range(2)]
    mark0 = aoff[0]

    WARENA_A = sb("warenaA", [128, 8 * 2888], BF16)
    kaT = sb("kaT", [64, 2, NKS * 128], BF16)
    kiT = sb("kiT", [64, NKS * 128], BF16)
    kbT = sb("kbT", [64, 8, NKS * 128], BF16)
    vaA = sb("vaA", [128, NKS, 2, 65], BF16)
    vbB = sb("vbB", [128, NKS, 8, 64], BF16)
    kvres = [Res() for _ in range(NKS)]
    ropet = sb("ropet", [128, 64], F32)
    pf = sb("pf", [128, 512], F32)
    pf2 = sb("pf2", [128, 512], F32)
    pbf = sb("pbf", [128, 512], BF16)
    r1 = sb("r1", [128, 256], F32)
    r2 = sb("r2", [128, 256], F32)
    qaT = sb("qaT", [64, 8, 128], BF16)
    qiT = sb("qiT", [64, 8, 128], BF16)
    qbT = sb("qbT", [64, 8, 128], BF16)
    wi = sb("wi", [128, 8], F32)
    isc = sb("isc", [128, 2560], F32)
    cstage = Res(isc.t[:, 0:NCST])
    mbias = sb("mbias", [128, 2560], BF16)
    rl = [sb("rl%d" % i, [128, 512], F32) for i in range(2)]
    PT = [sb("PT%d" % i, [128, 512], BF16) for i in range(2)]
    oa = sb("oa", [128, 512], F32)
    ob = sb("ob", [128, 512], F32)
    ebuf = sb("ebuf", [128, 1024], F32)
    spT = sb("spT", [128, 1024], BF16)
    ET = sb("ET", [128, 1024], BF16)
    ebufs = [ebuf, sb("ebuf2", [128, 1024], F32)]
    spTs = [spT, sb("spT2", [128, 1024], BF16)]
    ETs = [ET, sb("ET2", [128, 1024], BF16)]
    dd = sb("dd", [128, 8], F32)
    pvs = sb("pvs", [128, 512], F32)
    bis = sb("bis", [128, 8], F32)
    dtab = sb("dtab", [128, NBIS], F32)
    endA = aoff[0]

    aoff[0] = mark0
    WARENA_B = sb("warenaB", [128, 32768], BF16)
    oabb = sb("oabb", [128, D], BF16)
    sga = sb("sga", [128, D], F32)
    sgb = sb("sgb", [128, D], F32)
    oT = sb("oT", [128, 8, 128], BF16)
    mm = sb("mm", [128, D], F32)
    mbf = sb("mbf", [128, D], BF16)
    oabf = sb("oabf", [128, D], F32)
    endB1 = aoff[0]

    aoff[0] = mark0
    h2T_all = sb("h2T_all", [128, 8, NT * 128], BF16)
    WQ = sb("WQ", [128, 8 * 1024], BF16)
    qsb = sb("qsb", [128, D], BF16)
    qT = sb("qT", [64, 16, 128], BF16)
    k12 = sb("k12", [64, 2, 8, 128], BF16)
    s12 = sb("s12", [128, 2, 8, 128], F32)
    swk = sb("swk", [128, 256], F32)
    v12 = sb("v12", [128, 2, 8, 16], F32)
    i12 = sb("i12", [128, 2, 8, 16], U32)
    i12f = sb("i12f", [128, 2, 8, 16], F32)
    cand = sb("cand", [128, 8, 256], F32)
    sc = sb("sc", [128, 8, 16], F32)
    pos = sb("pos", [128, 8, 16], U32)
    posf = sb("posf", [128, 8, 16], F32)
    posrf = sb("posrf", [128, 8, 16], F32)
    poscf = sb("poscf", [128, 8, 16], F32)
    oh = sb("oh", [128, 8, 16, 16], F32)
    sel1 = sb("sel1", [128, 8, 16], F32)
    sel2 = sb("sel2", [128, 8, 16], F32)
    gsm = sb("gsm", [128, 8, 16], F32)
    TT = sb("TT", [128, 3, 128], F32)
    TT2 = sb("TT2", [128, 3, 128], F32)
    P1q = [sb("P1q%d" % i, [128, 32, 128], BF16) for i in range(2)]
    P2q = [sb("P2q%d" % i, [128, 32, 128], BF16) for i in range(2)]
    Gs = sb("Gs", [128, 128, 128], BF16)
    endB2 = aoff[0]

    aoff[0] = mark0
    h2T_all_c = sb("h2T_all_c", [128, 8, NT * 128], BF16)
    accs = sb("accs", [128, NT, D], F32)
    accres = [Res() for _ in range(NT)]
    NBLK = 4
    u16 = [sb("u16_%d" % i, [128, 8, NBLK * 128], BF16) for i in range(2)]
    v16 = [sb("v16_%d" % i, [128, NBLK, D], BF16) for i in range(2)]
    Gc = [sb("Gc%d" % i, [128, 2, NBLK, 128], BF16) for i in range(2)]
    gel = [sb("gel%d" % i, [128, 256], BF16) for i in range(2)]
    cfT = [sb("cfT%d" % i, [128, 256], BF16) for i in range(2)]
    cstg = [sb("cstg%d" % i, [128, 512], F32) for i in range(12)]
    cstg += [Res(stg[i][:, k * 512:(k + 1) * 512]) for i in range(2) for k in range(2)]
    NCS = len(cstg)
    assert NCS == 16
    endC = aoff[0]
    print("arena bytes: persistent", mark0, "A", endA, "B1", endB1, "B2", endB2, "C", endC, "cap", ARENA_F32 * 4)

    Gd = dram("Gd", [NT, 128, 16384], BF16, "Internal")
    gdres = [Res() for _ in range(NT)]
    xmid = dram("xmid", [NT * 128, D], F32, "Internal")
    xmres = [Res() for _ in range(NT)]
    oab_dram = dram("oabscr", [NT * 128, D], F32, "Internal")
    oab_res = [Res() for _ in range(NT)]

    P = [pst("ps%d" % i, [128, 512], F32) for i in range(7)]
    PTR = pst("ptr", [128, 1024], BF16)

    op = kb.op

    op("sp", lambda e: e.dma_start(out=cstage[:], in_=cst_in[:, :]), r=[in_res], w=[cstage], dma=True)
    o = 0
    for dst, wdt in ((ident, 128), (negtri, 128), (mlt, 128), (ident4, 512), (negm4, 512)):
        op("dve", lambda e, dst=dst, o=o, wdt=wdt: e.tensor_copy(out=dst[:], in_=cstage[:, o:o + wdt]), r=[cstage], w=[dst])
        o += wdt
    op("dve", lambda e, o=o: e.tensor_copy(out=pow2[:], in_=cstage[:, o:o + NBIS]), r=[cstage], w=[pow2])
    o += NBIS
    op("dve", lambda e, o=o: e.tensor_copy(out=iota16[:], in_=cstage[:, o:o + 16]), r=[cstage], w=[iota16])
    o += 16
    op("dve", lambda e, o=o: e.tensor_copy(out=iota128[:], in_=cstage[:, o:o + 128]), r=[cstage], w=[iota128])
    op("dve", lambda e: e.tensor_copy(out=identf[:], in_=cstage[:, 0:128]), r=[cstage], w=[identf])
    op("dve", lambda e: e.memset(ones1[:], 1.0), w=[ones1])
    op("dve", lambda e: e.tensor_scalar(out=thr16[:], in0=iota16[:], scalar1=16.0, scalar2=16.0, op0=ALU.mult, op1=ALU.add), r=[iota16], w=[thr16])
    op("dve", lambda e: e.memset(thr16[:, 15:16], 1e9), w=[thr16])
    op("dve", lambda e: e.memset(vaA[:], 1.0), w=[vaA] + kvres)
    kb.barrier()

    def transpose_blocks(src, nblk, dstT, dst_res, ptile=PTR):
        for b in range(nblk):
            op("pe", lambda e, b=b: e.transpose(out=ptile[0:64, b * 128:(b + 1) * 128], in_=src[:, b * 64:(b + 1) * 64], identity=ident[:]),
               r=[src, ident], w=[ptile])
        op("act", lambda e: e.activation(out=dstT, in_=ptile[0:64, 0:nblk * 128].rearrange("p (b n) -> p b n", b=nblk), func=AF.Copy),
           r=[ptile], w=dst_res)

    def rmsnorm_rows(xres, outbf, scratch):
        op("act", lambda e: e.activation(out=scratch[:], in_=xres[:], func=AF.Square, accum_out=st8[:, 0:1]), r=[xres], w=[scratch, st8])
        op("dve", lambda e: e.tensor_scalar(out=st8[:, 1:2], in0=st8[:, 0:1], scalar1=1.0 / D, scalar2=EPS, op0=ALU.mult, op1=ALU.add), r=[st8], w=[st8])
        op("act", lambda e: e.activation(out=st8[:, 2:3], in_=st8[:, 1:2], func=AF.Sqrt), r=[st8], w=[st8])
        op("dve", lambda e: e.reciprocal(out=st8[:, 3:4], in_=st8[:, 2:3]), r=[st8], w=[st8])
        op("dve", lambda e: e.tensor_scalar(out=outbf[:], in0=xres[:], scalar1=st8[:, 3:4], scalar2=None, op0=ALU.mult), r=[xres, st8], w=[outbf])

    def make_hT():
        for c in range(8):
            op("pe", lambda e, c=c: e.transpose(out=PTR[:, c * 128:(c + 1) * 128], in_=hb[:, c * 128:(c + 1) * 128], identity=ident[:]),
               r=[hb, ident], w=[PTR])
        op("act", lambda e: e.activation(out=hT[:], in_=PTR[:, :].rearrange("p (c n) -> p c n", c=8), func=AF.Copy), r=[PTR], w=[hT])

    def headnorm(src_res, src_ap, H, gain, dst):
        W = H * 64
        op("act", lambda e: e.activation(out=sq[:, 0:W], in_=src_ap, func=AF.Square), r=[src_res], w=[sq])
        op("dve", lambda e: e.tensor_reduce(out=st8[:, 8:8 + H], in_=sq[:, 0:W].rearrange("p (h d) -> p h d", h=H), axis=AX.X, op=ALU.add), r=[sq], w=[st8])
        op("dve", lambda e: e.tensor_scalar(out=st8[:, 16:16 + H], in0=st8[:, 8:8 + H], scalar1=1.0 / 64, scalar2=EPS, op0=ALU.mult, op1=ALU.add), r=[st8], w=[st8])
        op("act", lambda e: e.activation(out=st8[:, 24:24 + H], in_=st8[:, 16:16 + H], func=AF.Sqrt), r=[st8], w=[st8])
        op("dve", lambda e: e.reciprocal(out=st8[:, 32:32 + H], in_=st8[:, 24:24 + H]), r=[st8], w=[st8])
        d3 = dst[:, 0:W].rearrange("p (h d) -> p h d", h=H)
        op("dve", lambda e: e.tensor_tensor(out=d3, in0=src_ap.rearrange("p (h d) -> p h d", h=H),
                                            in1=st8[:, 32:32 + H].unsqueeze(2).to_broadcast([128, H, 64]), op=ALU.mult), r=[st8, src_res], w=[dst])
        op("dve", lambda e: e.tensor_tensor(out=d3, in0=d3, in1=gain[:, :].unsqueeze(1).to_broadcast([128, H, 64]), op=ALU.mult), r=[gain, dst], w=[dst])

    def rope(src, H, dst, scale=None):
        W = H * 64
        s3 = src[:, 0:W].rearrange("p (h d) -> p h d", h=H)
        d3 = dst[:, 0:W].rearrange("p (h d) -> p h d", h=H)
        cosb = ropet[:, 0:32].unsqueeze(1).to_broadcast([128, H, 32])
        sinb = ropet[:, 32:64].unsqueeze(1).to_broadcast([128, H, 32])
        a3 = r1[:, 0:H * 32].rearrange("p (h d) -> p h d", h=H)
        b3 = r2[:, 0:H * 32].rearrange("p (h d) -> p h d", h=H)
        op("dve", lambda e: e.tensor_tensor(out=a3, in0=s3[:, :, 0:32], in1=cosb, op=ALU.mult), r=[src, ropet], w=[r1])
        op("dve", lambda e: e.tensor_tensor(out=b3, in0=s3[:, :, 32:64], in1=sinb, op=ALU.mult), r=[src, ropet], w=[r2])
        op("dve", lambda e: e.tensor_tensor(out=d3[:, :, 0:32], in0=a3, in1=b3, op=ALU.subtract), r=[r1, r2], w=[dst])
        op("dve", lambda e: e.tensor_tensor(out=a3, in0=s3[:, :, 32:64], in1=cosb, op=ALU.mult), r=[src, ropet], w=[r1])
        op("dve", lambda e: e.tensor_tensor(out=b3, in0=s3[:, :, 0:32], in1=sinb, op=ALU.mult), r=[src, ropet], w=[r2])
        op("dve", lambda e: e.tensor_tensor(out=d3[:, :, 32:64], in0=a3, in1=b3, op=ALU.add), r=[r1, r2], w=[dst])

    def load_cast_rows(wres, dram_ap_fn, nchunks, width, dst_fn, scale_res=None, scale_col=None):
        i = 0
        for c in range(nchunks):
            for o0 in range(0, width, STW):
                wdt = min(STW, width - o0)
                s = stg[i % 2]
                i += 1
                op("sp", lambda e, c=c, o0=o0, wdt=wdt, s=s: e.dma_start(out=s[:, 0:wdt], in_=dram_ap_fn(c, o0, wdt)), r=[in_res], w=[s], dma=True)
                if scale_res is not None:
                    op("dve", lambda e, c=c, o0=o0, wdt=wdt, s=s: e.tensor_scalar(out=dst_fn(c, o0, wdt), in0=s[:, 0:wdt], scalar1=scale_res[:, scale_col(c):scale_col(c) + 1], scalar2=None, op0=ALU.mult),
                       r=[s, scale_res], w=[wres])
                else:
                    op("pool", lambda e, c=c, o0=o0, wdt=wdt, s=s: e.tensor_copy(out=dst_fn(c, o0, wdt), in_=s[:, 0:wdt]), r=[s], w=[wres])

    WAA = WARENA_A.t
    WA = WARENA_B.t
    NQKV = 2888
    wa_qkv = lambda c, o0, wdt: WAA[:, c * NQKV + o0: c * NQKV + o0 + wdt]
    GOFF = 0
    PAOFF = 8 * 2048
    PBOFF = PAOFF + 4 * 1024
    WOOFF = PBOFF + 4 * 1024
    PQOFF = WOOFF + 8 * 1024

    for l in range(depth):
        xin_d = x_in if l == 0 else xs_dram[(l - 1) % 2]
        xin_r = [in_res] * NT if l == 0 else xs_res[(l - 1) % 2]
        xout_d = xs_dram[l % 2]
        xout_r = xs_res[l % 2]
        last = (l == depth - 1)

        op("sp", lambda e: e.dma_start(out=n1s[:], in_=n1T[l, :, :]), r=[in_res], w=[n1s], dma=True)
        op("sp", lambda e: e.dma_start(out=n2s[:], in_=n2T[l, :, :]), r=[in_res], w=[n2s], dma=True)
        op("sp", lambda e: e.dma_start(out=gq[:], in_=qn[l, :].partition_broadcast(128)), r=[in_res], w=[gq], dma=True)
        op("sp", lambda e: e.dma_start(out=gk[:], in_=kn[l, :].partition_broadcast(128)), r=[in_res], w=[gk], dma=True)
        op("sp", lambda e: e.dma_start(out=gik[:], in_=ikn[l, :].partition_broadcast(128)), r=[in_res], w=[gik], dma=True)
        op("sp", lambda e: e.dma_start(out=n2b[:], in_=n2r[l, :].partition_broadcast(128)), r=[in_res], w=[n2b], dma=True)

        kb.barrier()
        if l > 0:
            op("pool", lambda e: e.memset(vaA[:], 1.0), w=[vaA] + kvres)
        load_cast_rows(WARENA_A, lambda c, o0, wdt: w_in[l, c * 128:(c + 1) * 128, o0:o0 + wdt], 8, NQKV, wa_qkv, n1s, lambda c: c)

        for t in tiles:
            samp = (t == NT - 1)
            nk = t + 1
            if samp:
                for k0 in range(0, 16, 4):
                    for kk in range(k0, k0 + 4):
                        rows = slice(kk * 128, (kk + 1) * 128)
                        s = stg[kk % 2]
                        op("sp", lambda e, s=s, rows=rows: e.dma_start(out=s[:, 0:128], in_=cak[l, rows, :]), r=[in_res], w=[s], dma=True)
                        op("sp", lambda e, s=s, rows=rows: e.dma_start(out=s[:, 128:192], in_=cik[l, rows, :]), r=[in_res], w=[s], dma=True)
                        op("sp", lambda e, s=s, rows=rows: e.dma_start(out=s[:, 192:320], in_=cav[l, rows, :]), r=[in_res], w=[s], dma=True)
                        op("sp", lambda e, s=s, rows=rows: e.dma_start(out=s[:, 512:1024], in_=cbk[l, rows, :]), r=[in_res], w=[s], dma=True)
                        op("sp", lambda e, s=s, rows=rows: e.dma_start(out=s[:, 1024:1536], in_=cbv[l, rows, :]), r=[in_res], w=[s], dma=True)
                        op("dve", lambda e, s=s: e.tensor_copy(out=pbf[:, 0:192], in_=s[:, 0:192]), r=[s], w=[pbf])
                        for b in range(3):
                            op("pe", lambda e, b=b: e.transpose(out=PTR[0:64, b * 128:(b + 1) * 128], in_=pbf[:, b * 64:(b + 1) * 64], identity=ident[:]), r=[pbf, ident], w=[PTR])
                        ks = slice(kk * 128, (kk + 1) * 128)
                        op("act", lambda e, ks=ks: e.activation(out=kaT[:, :, ks], in_=PTR[0:64, 0:256].rearrange("p (b n) -> p b n", b=2), func=AF.Copy), r=[PTR], w=[kvres[kk]])
                        op("act", lambda e, ks=ks: e.activation(out=kiT[:, ks], in_=PTR[0:64, 256:384], func=AF.Copy), r=[PTR], w=[kvres[kk]])
                        op("dve", lambda e, s=s, kk=kk: e.tensor_copy(out=vaA[:, kk, :, 0:64], in_=s[:, 192:320].rearrange("p (g d) -> p g d", g=2)), r=[s], w=[kvres[kk]])
                        op("dve", lambda e, s=s: e.tensor_copy(out=pbf[:, 0:512], in_=s[:, 512:1024]), r=[s], w=[pbf])
                        for b in range(8):
                            op("pe", lambda e, b=b: e.transpose(out=PTR[0:64, b * 128:(b + 1) * 128], in_=pbf[:, b * 64:(b + 1) * 64], identity=ident[:]), r=[pbf, ident], w=[PTR])
                        op("act", lambda e, ks=ks: e.activation(out=kbT[:, :, ks], in_=PTR[0:64, :].rearrange("p (b n) -> p b n", b=8), func=AF.Copy), r=[PTR], w=[kvres[kk]])
                        op("pool", lambda e, s=s, kk=kk: e.tensor_copy(out=vbB[:, kk, :, :], in_=s[:, 1024:1536].rearrange("p (h d) -> p h d", h=8)), r=[s], w=[kvres[kk]])

            rows = slice(t * 128, (t + 1) * 128)
            op("sp", lambda e: e.dma_start(out=xt[:], in_=xin_d[rows, :]), r=[xin_r[t]], w=[xt], dma=True)
            op("sp", lambda e: e.dma_start(out=ropet[:], in_=rope_in[rows, :]), r=[in_res], w=[ropet], dma=True)
            rmsnorm_rows(xt, hb, sq)
            make_hT()

            def proj(pt, c0, wdt):
                for c in range(8):
                    op("pe", lambda e, c=c: e.matmul(pt[:, 0:wdt], lhsT=hT[:, c, :], rhs=WAA[:, c * NQKV + c0: c * NQKV + c0 + wdt], start=(c == 0), stop=(c == 7)),
                       r=[hT, WARENA_A], w=[pt])

            def out_rows(dst_p, dst_s, src, wdt):
                if samp:
                    op("sp", lambda e: e.dma_start(out=dst_s[l, :, :], in_=src[0:DEC, 0:wdt]), r=[src], w=[out_res], dma=True)
                else:
                    op("sp", lambda e: e.dma_start(out=dst_p[l, rows, :], in_=src[:, 0:wdt]), r=[src], w=[out_res], dma=True)

            ks = slice(t * 128, (t + 1) * 128)
            proj(P[0], 0, 512)
            headnorm(P[0], P[0][:, 0:512], 8, gq, pf)
            rope(pf, 8, pf2)
            op("dve", lambda e: e.tensor_scalar(out=pbf[:], in0=pf2[:], scalar1=0.125, scalar2=None, op0=ALU.mult), r=[pf2], w=[pbf])
            transpose_blocks(pbf, 8, qaT[:], [qaT])
            proj(P[1], 512, 256)
            headnorm(P[1], P[1][:, 0:128], 2, gk, pf)
            rope(pf, 2, pf2)
            out_rows(ak_p, ak_s, pf2, 128)
            op("dve", lambda e: e.tensor_copy(out=pbf[:, 0:128], in_=pf2[:, 0:128]), r=[pf2], w=[pbf])
            op("act", lambda e: e.activation(out=pf[:, 0:128], in_=P[1][:, 128:256], func=AF.Copy), r=[P[1]], w=[pf])
            out_rows(av_p, av_s, pf, 128)
            op("dve", lambda e: e.tensor_copy(out=vaA[:, t, :, 0:64], in_=pf[:, 0:128].rearrange("p (g d) -> p g d", g=2)), r=[pf], w=[kvres[t]])
            transpose_blocks(pbf, 2, kaT[:, :, ks], [kvres[t]])
            proj(P[0], 768, 512)
            rope(P[0], 8, pf2)
            op("dve", lambda e: e.tensor_copy(out=pbf[:], in_=pf2[:]), r=[pf2], w=[pbf])
            transpose_blocks(pbf, 8, qiT[:], [qiT])
            proj(P[1], 1280, 72)
            headnorm(P[1], P[1][:, 0:64], 1, gik, pf)
            rope(pf, 1, pf2)
            out_rows(ik_p, ik_s, pf2, 64)
            op("dve", lambda e: e.tensor_copy(out=pbf[:, 0:64], in_=pf2[:, 0:64]), r=[pf2], w=[pbf])
            op("act", lambda e: e.activation(out=wi[:], in_=P[1][:, 64:72], func=AF.Copy), r=[P[1]], w=[wi])
            for b in range(1):
                op("pe", lambda e: e.transpose(out=PTR[0:64, 0:128], in_=pbf[:, 0:64], identity=ident[:]), r=[pbf, ident], w=[PTR])
            op("act", lambda e: e.activation(out=kiT[:, ks], in_=PTR[0:64, 0:128], func=AF.Copy), r=[PTR], w=[kvres[t]])
            proj(P[0], 1352, 512)
            op("act", lambda e: e.activation(out=pbf[:], in_=P[0][:, :], func=AF.Copy, scale=0.125), r=[P[0]], w=[pbf])
            transpose_blocks(pbf, 8, qbT[:], [qbT])
            proj(P[1], 1864, 512)
            op("act", lambda e: e.activation(out=pf[:], in_=P[1][:, :], func=AF.Copy), r=[P[1]], w=[pf])
            out_rows(bk_p, bk_s, pf, 512)
            op("dve", lambda e: e.tensor_copy(out=pbf[:], in_=pf[:]), r=[pf], w=[pbf])
            transpose_blocks(pbf, 8, kbT[:, :, ks], [kvres[t]])
            proj(P[0], 2376, 512)
            op("act", lambda e: e.activation(out=pf2[:], in_=P[0][:, :], func=AF.Copy), r=[P[0]], w=[pf2])
            out_rows(bv_p, bv_s, pf2, 512)
            op("dve", lambda e: e.tensor_copy(out=vbB[:, t, :, :], in_=pf2[:, :].rearrange("p (h d) -> p h d", h=8)), r=[pf2], w=[kvres[t]])

            S = nk * 128
            nblk = (S + 511) // 512
            kvr = [kvres[i] for i in range(nk)]
            for bi in range(nblk):
                c0 = bi * 512
                wdt = min(512, S - c0)
                for h in range(8):
                    pt = P[2 + (h % 2)]
                    rb = rl[h % 2]
                    op("pe", lambda e, h=h, pt=pt: e.matmul(pt[:, 0:wdt], lhsT=qiT[:, h, :], rhs=kiT[:, c0:c0 + wdt], start=True, stop=True), r=[qiT] + kvr, w=[pt])
                    op("act", lambda e, pt=pt, rb=rb: e.activation(out=rb[:, 0:wdt], in_=pt[:, 0:wdt], func=AF.Relu, scale=0.125 * (8 ** -0.5)), r=[pt], w=[rb])
                    if h == 0:
                        op("dve", lambda e, rb=rb: e.tensor_scalar(out=isc[:, c0:c0 + wdt], in0=rb[:, 0:wdt], scalar1=wi[:, 0:1], scalar2=None, op0=ALU.mult), r=[rb, wi], w=[isc])
                    else:
                        op("dve", lambda e, rb=rb, h=h: e.scalar_tensor_tensor(out=isc[:, c0:c0 + wdt], in0=rb[:, 0:wdt], scalar=wi[:, h:h + 1], in1=isc[:, c0:c0 + wdt], op0=ALU.mult, op1=ALU.add), r=[rb, wi, isc], w=[isc])
            Zs = [[P[2], P[3]], [P[4], P[5]]]
            PVs = [P[6], P[1]]
            TSs = [(P[0], P[0][:, 0:8]), (PTR, PTR[:, 0:16].bitcast(F32))]

            def sb_stage1(kt, b):
                ksl = slice(kt * 128, (kt + 1) * 128)
                Z = Zs[b]
                eb, sp_ = ebufs[b], spTs[b]
                for h in range(8):
                    op("pe", lambda e, h=h: e.matmul(Z[h // 4][:, (h % 4) * 128:(h % 4 + 1) * 128], lhsT=kbT[:, h, ksl], rhs=qbT[:, h, :], start=(h % 4 == 0), stop=False), r=[kvres[kt], qbT], w=[Z[h // 4]])
                for hf in range(2):
                    op("act", lambda e, hf=hf: e.activation(out=eb[:, hf * 512:(hf + 1) * 512], in_=Z[hf][:, :], func=AF.Exp), r=[Z[hf]], w=[eb])
                    op("act", lambda e, hf=hf: e.activation(out=sp_[:, hf * 512:(hf + 1) * 512], in_=eb[:, hf * 512:(hf + 1) * 512], func=AF.Ln, bias=1.0), r=[eb], w=[sp_])
                if kt == nk - 1:
                    s3 = sp_[:, :].rearrange("p (h q) -> p h q", h=8)
                    op("pool", lambda e, s3=s3: e.tensor_tensor(out=s3, in0=s3, in1=mlt[:, :].unsqueeze(1).to_broadcast([128, 8, 128]), op=ALU.mult), r=[sp_, mlt], w=[sp_])

            def sb_stage2(kt, b):
                diag = (kt == nk - 1)
                Z = Zs[b]
                sp_, et_ = spTs[b], ETs[b]
                PVb = PVs[b]
                TSr, TSv = TSs[b]
                for hf in range(2):
                    op("pe", lambda e, hf=hf: e.matmul(Z[hf][:, :], lhsT=negtri[:], rhs=sp_[:, hf * 512:(hf + 1) * 512], start=False, stop=(not diag)), r=[negtri, sp_], w=[Z[hf]])
                    if diag:
                        op("pe", lambda e, hf=hf: e.matmul(Z[hf][:, :], lhsT=ident[:], rhs=negm4[:], start=False, stop=True), r=[ident, negm4], w=[Z[hf]])
                    op("act", lambda e, hf=hf: e.activation(out=et_[:, hf * 512:(hf + 1) * 512], in_=Z[hf][:, :], func=AF.Exp), r=[Z[hf]], w=[et_])
                for h in range(8):
                    op("pe", lambda e, h=h: e.matmul(TSv[:, h:h + 1], lhsT=sp_[:, h * 128:(h + 1) * 128], rhs=ones1[:], start=True, stop=True), r=[sp_, ones1], w=[TSr])
                for h in range(8):
                    op("pe", lambda e, h=h: e.matmul(PVb[:, h * 64:(h + 1) * 64], lhsT=et_[:, h * 128:(h + 1) * 128], rhs=vbB[:, kt, h, :], start=True, stop=True), r=[et_, kvres[kt]], w=[PVb])
                if kt == 0:
                    op("act", lambda e: e.activation(out=ob[:], in_=PVb[:, :], func=AF.Copy), r=[PVb], w=[ob])
                else:
                    op("act", lambda e: e.activation(out=dd[:], in_=TSv, func=AF.Exp, scale=-1.0), r=[TSr], w=[dd])
                    o3 = ob[:, :].rearrange("p (h d) -> p h d", h=8)
                    op("act", lambda e: e.activation(out=pvs[:], in_=PVb[:, :], func=AF.Copy), r=[PVb], w=[pvs])
                    op("pool", lambda e, o3=o3: e.tensor_tensor(out=o3, in0=o3, in1=dd[:, :].unsqueeze(2).to_broadcast([128, 8, 64]), op=ALU.mult), r=[ob, dd], w=[ob])
                    op("pool", lambda e: e.tensor_tensor(out=ob[:], in0=ob[:], in1=pvs[:], op=ALU.add), r=[ob, pvs], w=[ob])

            sb_stage1(0, 0)
            for kt in range(nk):
                if kt + 1 < nk:
                    sb_stage1(kt + 1, (kt + 1) % 2)
                sb_stage2(kt, kt % 2)
            op("sp", lambda e: e.dma_start(out=oab_dram[rows, 512:1024], in_=ob[:]), r=[ob], w=[oab_res[t]], dma=True)

            need_thr = nk > 2
            if need_thr:
                op("dve", lambda e: e.tensor_reduce(out=bis[:, 5:6], in_=isc[:, 0:S], axis=AX.X, op=ALU.max), r=[isc], w=[bis])
                op("dve", lambda e: e.tensor_reduce(out=bis[:, 6:7], in_=isc[:, 0:S], axis=AX.X, op=ALU.min), r=[isc], w=[bis])
                op("dve", lambda e: e.tensor_scalar(out=bis[:, 6:7], in0=bis[:, 6:7], scalar1=-1.0, scalar2=None, op0=ALU.mult), r=[bis], w=[bis])
                op("dve", lambda e: e.tensor_tensor(out=bis[:, 4:5], in0=bis[:, 5:6], in1=bis[:, 6:7], op=ALU.max), r=[bis], w=[bis])
                op("dve", lambda e: e.tensor_scalar(out=bis[:, 0:1], in0=bis[:, 4:5], scalar1=-1.0, scalar2=None, op0=ALU.mult), r=[bis], w=[bis])
                op("dve", lambda e: e.tensor_scalar(out=dtab[:], in0=pow2[:], scalar1=bis[:, 4:5], scalar2=2.002, op0=ALU.mult, op1=ALU.mult), r=[bis, pow2], w=[dtab])
            if samp:
                op("dve", lambda e: e.memset(isc[:, 16 * 128 + DEC:17 * 128], NEG), r=[], w=[isc])
            else:
                op("dve", lambda e: e.memset(isc[0:64, t * 128 + 64:(t + 1) * 128], NEG), r=[], w=[isc])
            if need_thr:
                for k in range(NBIS):
                    op("dve", lambda e, k=k: e.tensor_tensor(out=bis[:, 1:2], in0=bis[:, 0:1], in1=dtab[:, k:k + 1], op=ALU.add), r=[bis, dtab], w=[bis])
                    op("dve", lambda e: e.tensor_scalar(out=mbias[:, 0:S], in0=isc[:, 0:S], scalar1=bis[:, 1:2], scalar2=None, op0=ALU.is_ge, op1=ALU.add, accum_out=bis[:, 2:3]), r=[isc, bis], w=[mbias, bis])
                    op("dve", lambda e, k=k: e.scalar_tensor_tensor(out=bis[:, 3:4], in0=bis[:, 2:3], scalar=float(TOPK), in1=dtab[:, k:k + 1], op0=ALU.is_ge, op1=ALU.mult), r=[bis, dtab], w=[bis])
                    op("dve", lambda e: e.tensor_tensor(out=bis[:, 0:1], in0=bis[:, 0:1], in1=bis[:, 3:4], op=ALU.add), r=[bis], w=[bis])
            else:
                op("dve", lambda e: e.memset(bis[:, 0:1], -1e29), r=[], w=[bis])
            op("dve", lambda e: e.tensor_scalar(out=mbias[:, 0:S], in0=isc[:, 0:S], scalar1=bis[:, 0:1], scalar2=-30000.0, op0=ALU.is_lt, op1=ALU.mult), r=[isc, bis], w=[mbias])
            OAp = [P[4], P[5]]
            for kt in range(nk):
                ksl = slice(kt * 128, (kt + 1) * 128)
                for g in range(2):
                    pt = P[2 + g]
                    pb_ = PT[g]
                    op("pe", lambda e, g=g, pt=pt, ksl=ksl: e.matmul(pt[:, :], lhsT=kaT[:, g, ksl], rhs=qaT[:, 4 * g:4 * g + 4, :], start=True, stop=False), r=[kvres[kt], qaT], w=[pt])
                    op("pe", lambda e, pt=pt, ksl=ksl: e.matmul(pt[:, :], lhsT=mbias[:, ksl], rhs=ident4[:], start=False, stop=True), r=[mbias, ident4], w=[pt])
                    op("act", lambda e, pt=pt, pb_=pb_: e.activation(out=pb_[:], in_=pt[:, :], func=AF.Exp), r=[pt], w=[pb_])
                    for hh in range(4):
                        op("pe", lambda e, g=g, hh=hh, pb_=pb_, kt=kt: e.matmul(OAp[g][:, hh * 65:(hh + 1) * 65], lhsT=pb_[:, hh * 128:(hh + 1) * 128], rhs=vaA[:, kt, g, :], start=(kt == 0 and hh == 0), stop=(kt == nk - 1)),
                           r=[pb_, kvres[kt]], w=[OAp[g]])
            for g in range(2):
                o3 = OAp[g][:, 0:260].rearrange("p (h d) -> p h d", h=4)
                op("dve", lambda e, g=g, o3=o3: e.reciprocal(out=st8[:, 40 + 4 * g:44 + 4 * g], in_=o3[:, :, 64]), r=[OAp[g]], w=[st8])
                op("dve", lambda e, g=g, o3=o3: e.tensor_tensor(out=oa[:, g * 256:(g + 1) * 256].rearrange("p (h d) -> p h d", h=4), in0=o3[:, :, 0:64],
                                                                 in1=st8[:, 40 + 4 * g:44 + 4 * g].unsqueeze(2).to_broadcast([128, 4, 64]), op=ALU.mult), r=[OAp[g], st8], w=[oa])
            op("sp", lambda e: e.dma_start(out=oab_dram[rows, 0:512], in_=oa[:]), r=[oa], w=[oab_res[t]], dma=True)

        kb.barrier()
        load_cast_rows(WARENA_B, lambda c, o0, wdt: w_in[l, c * 128:(c + 1) * 128, 2888 + o0:2888 + o0 + wdt], 8, 2048,
                       lambda c, o0, wdt: WA[:, GOFF + c * 2048 + o0: GOFF + c * 2048 + o0 + wdt], n1s, lambda c: c)
        load_cast_rows(WARENA_B, lambda c, o0, wdt: w_pa[l, c * 128:(c + 1) * 128, o0:o0 + wdt], 4, 1024,
                       lambda c, o0, wdt: WA[:, PAOFF + c * 1024 + o0: PAOFF + c * 1024 + o0 + wdt])
        load_cast_rows(WARENA_B, lambda c, o0, wdt: w_pb[l, c * 128:(c + 1) * 128, o0:o0 + wdt], 4, 1024,
                       lambda c, o0, wdt: WA[:, PBOFF + c * 1024 + o0: PBOFF + c * 1024 + o0 + wdt])
        load_cast_rows(WARENA_B, lambda c, o0, wdt: w_o[l, c * 128:(c + 1) * 128, o0:o0 + wdt], 8, 1024,
                       lambda c, o0, wdt: WA[:, WOOFF + c * 1024 + o0: WOOFF + c * 1024 + o0 + wdt])
        for t in tiles:
            samp = (t == NT - 1)
            rows = slice(t * 128, (t + 1) * 128)
            op("sp", lambda e: e.dma_start(out=xt[:], in_=xin_d[rows, :]), r=[xin_r[t]], w=[xt], dma=True)
            rmsnorm_rows(xt, hb, sq)
            make_hT()
            for gi, gdst in ((0, sga), (1, sgb)):
                for hf in range(2):
                    pt = P[hf]
                    c0 = GOFF + gi * 1024 + hf * 512
                    for c in range(8):
                        op("pe", lambda e, c=c, pt=pt, c0=c0: e.matmul(pt[:, :], lhsT=hT[:, c, :], rhs=WA[:, c * 2048 + c0: c * 2048 + c0 + 512], start=(c == 0), stop=(c == 7)), r=[hT, WARENA_B], w=[pt])
                    op("act", lambda e, pt=pt, gdst=gdst, hf=hf: e.activation(out=gdst[:, hf * 512:(hf + 1) * 512], in_=pt[:, :], func=AF.Sigmoid), r=[pt], w=[gdst])
            op("sp", lambda e: e.dma_start(out=oabf[:], in_=oab_dram[rows, :]), r=[oab_res[t]], w=[oabf], dma=True)
            op("pool", lambda e: e.tensor_copy(out=oabb[:], in_=oabf[:]), r=[oabf], w=[oabb])
            for bi, (woff, gdst) in enumerate(((PAOFF, sga), (PBOFF, sgb))):
                for c in range(4):
                    op("pe", lambda e, c=c, bi=bi: e.transpose(out=PTR[:, c * 128:(c + 1) * 128], in_=oabb[:, bi * 512 + c * 128: bi * 512 + (c + 1) * 128], identity=ident[:]), r=[oabb, ident], w=[PTR])
                op("act", lambda e: e.activation(out=oT[:, 0:4, :], in_=PTR[:, 0:512].rearrange("p (c n) -> p c n", c=4), func=AF.Copy), r=[PTR], w=[oT])
                for hf in range(2):
                    pt = P[2 + hf]
                    for c in range(4):
                        op("pe", lambda e, c=c, pt=pt, hf=hf, woff=woff: e.matmul(pt[:, :], lhsT=oT[:, c, :], rhs=WA[:, woff + c * 1024 + hf * 512: woff + c * 1024 + (hf + 1) * 512], start=(c == 0), stop=(c == 3)), r=[oT, WARENA_B], w=[pt])
                    if bi == 0:
                        op("dve", lambda e, pt=pt, hf=hf: e.tensor_tensor(out=mm[:, hf * 512:(hf + 1) * 512], in0=sga[:, hf * 512:(hf + 1) * 512], in1=pt[:, :], op=ALU.mult), r=[sga, pt], w=[mm])
                    else:
                        op("dve", lambda e, pt=pt, hf=hf: e.tensor_tensor(out=sgb[:, hf * 512:(hf + 1) * 512], in0=sgb[:, hf * 512:(hf + 1) * 512], in1=pt[:, :], op=ALU.mult), r=[sgb, pt], w=[sgb])
            op("dve", lambda e: e.tensor_tensor(out=mbf[:], in0=mm[:], in1=sgb[:], op=ALU.add), r=[mm, sgb], w=[mbf])
            for c in range(8):
                op("pe", lambda e, c=c: e.transpose(out=PTR[:, c * 128:(c + 1) * 128], in_=mbf[:, c * 128:(c + 1) * 128], identity=ident[:]), r=[mbf, ident], w=[PTR])
            op("act", lambda e: e.activation(out=oT[:], in_=PTR[:, :].rearrange("p (c n) -> p c n", c=8), func=AF.Copy), r=[PTR], w=[oT])
            for hf in range(2):
                pt = P[4 + hf]
                for c in range(8):
                    op("pe", lambda e, c=c, pt=pt, hf=hf: e.matmul(pt[:, :], lhsT=oT[:, c, :], rhs=WA[:, WOOFF + c * 1024 + hf * 512: WOOFF + c * 1024 + (hf + 1) * 512], start=(c == 0), stop=(c == 7)), r=[oT, WARENA_B], w=[pt])
                op("dve", lambda e, pt=pt, hf=hf: e.tensor_tensor(out=xt[:, hf * 512:(hf + 1) * 512], in0=xt[:, hf * 512:(hf + 1) * 512], in1=pt[:, :], op=ALU.add), r=[xt, pt], w=[xt])

            op("sp", lambda e: e.dma_start(out=xmid[rows, :], in_=xt[:]), r=[xt], w=[xmres[t]], dma=True)

        if do_peer:
            kb.barrier()
            load_cast_rows(WQ, lambda c, o0, wdt: pwq[l, c * 128:(c + 1) * 128, o0:o0 + wdt], 8, 1024,
                           lambda c, o0, wdt: WQ[:, c * 1024 + o0: c * 1024 + o0 + wdt], n2s, lambda c: c)
            for which, src in ((0, pk1T), (1, pk2T)):
                s_ = stg[which]
                op("sp", lambda e, s_=s_, src=src: e.dma_start(out=s_[0:64, 0:1024], in_=src[l, :, :, :].rearrange("d h k -> d (h k)")), r=[in_res], w=[s_], dma=True)
                op("dve", lambda e, s_=s_, which=which: e.tensor_copy(out=k12[:, which, :, :], in_=s_[0:64, 0:1024].rearrange("d (h k) -> d h k", h=8)), r=[s_], w=[k12])
            def b2_front(t, TT):
                rows = slice(t * 128, (t + 1) * 128)
                op("sp", lambda e: e.dma_start(out=xt[:], in_=xmid[rows, :]), r=[xmres[t]], w=[xt], dma=True)
                rmsnorm_rows(xt, hb, sq)
                make_hT()
                op("pool", lambda e: e.tensor_copy(out=h2T_all[:, :, t * 128:(t + 1) * 128], in_=hT[:]), r=[hT], w=[h2T_all])
                for hf in range(2):
                    pt = P[hf]
                    for c in range(8):
                        op("pe", lambda e, c=c, pt=pt, hf=hf: e.matmul(pt[:, :], lhsT=hT[:, c, :], rhs=WQ[:, c * 1024 + hf * 512: c * 1024 + (hf + 1) * 512], start=(c == 0), stop=(c == 7)), r=[hT, WQ], w=[pt])
                    op("act", lambda e, pt=pt, hf=hf: e.activation(out=qsb[:, hf * 512:(hf + 1) * 512], in_=pt[:, :], func=AF.Copy), r=[pt], w=[qsb])
                for hf in range(2):
                    for b in range(8):
                        bb = hf * 8 + b
                        op("pe", lambda e, b=b, bb=bb: e.transpose(out=PTR[0:64, b * 128:(b + 1) * 128], in_=qsb[:, bb * 64:(bb + 1) * 64], identity=ident[:]), r=[qsb, ident], w=[PTR])
                    op("act", lambda e, hf=hf: e.activation(out=qT[:, hf * 8:(hf + 1) * 8, :], in_=PTR[0:64, :].rearrange("p (b n) -> p b n", b=8), func=AF.Copy), r=[PTR], w=[qT])
                yield
                for which in range(2):
                    for h in range(8):
                        pt = P[2 + h // 4]
                        op("pe", lambda e, h=h, pt=pt, which=which: e.matmul(pt[:, (h % 4) * 128:(h % 4 + 1) * 128], lhsT=qT[:, h * 2 + which, :], rhs=k12[:, which, h, :], start=True, stop=True), r=[qT, k12], w=[pt])
                    for hf in range(2):
                        pt = P[2 + hf]
                        op("act", lambda e, pt=pt, which=which, hf=hf: e.activation(out=s12[:, which, hf * 4:(hf + 1) * 4, :], in_=pt[:, :].rearrange("p (h k) -> p h k", h=4), func=AF.Copy), r=[pt], w=[s12])
                yield
                for which in range(2):
                    for h in range(8):
                        sv = s12[:, which, h, :]
                        op("dve", lambda e, sv=sv, which=which, h=h: e.max(out=v12[:, which, h, 0:8], in_=sv), r=[s12], w=[v12])
                        op("dve", lambda e, sv=sv, which=which, h=h: e.max_index(out=i12[:, which, h, 0:8], in_max=v12[:, which, h, 0:8], in_values=sv), r=[s12, v12], w=[i12])
                        op("dve", lambda e, sv=sv, which=which, h=h: e.match_replace(out=swk[:, 0:128], in_to_replace=v12[:, which, h, 0:8], in_values=sv, imm_value=NEG), r=[s12, v12], w=[swk])
                        op("dve", lambda e, which=which, h=h: e.max(out=v12[:, which, h, 8:16], in_=swk[:, 0:128]), r=[swk], w=[v12])
                        op("dve", lambda e, which=which, h=h: e.max_index(out=i12[:, which, h, 8:16], in_max=v12[:, which, h, 8:16], in_values=swk[:, 0:128]), r=[swk, v12], w=[i12])
                yield
                op("dve", lambda e: e.tensor_copy(out=i12f[:], in_=i12[:]), r=[i12], w=[i12f])
                c4 = cand[:, :, :].rearrange("p h (r c) -> p h r c", r=16)
                op("dve", lambda e: e.tensor_tensor(out=c4, in0=v12[:, 0, :, :].unsqueeze(3).to_broadcast([128, 8, 16, 16]), in1=v12[:, 1, :, :].unsqueeze(2).to_broadcast([128, 8, 16, 16]), op=ALU.add), r=[v12], w=[cand])
                for h in range(8):
                    op("dve", lambda e, h=h: e.max(out=sc[:, h, 0:8], in_=cand[:, h, :]), r=[cand], w=[sc])
                    op("dve", lambda e, h=h: e.max_index(out=pos[:, h, 0:8], in_max=sc[:, h, 0:8], in_values=cand[:, h, :]), r=[cand, sc], w=[pos])
                    op("dve", lambda e, h=h: e.match_replace(out=swk[:, 0:256], in_to_replace=sc[:, h, 0:8], in_values=cand[:, h, :], imm_value=NEG), r=[cand, sc], w=[swk])
                    op("dve", lambda e, h=h: e.max(out=sc[:, h, 8:16], in_=swk[:, 0:256]), r=[swk], w=[sc])
                    op("dve", lambda e, h=h: e.max_index(out=pos[:, h, 8:16], in_max=sc[:, h, 8:16], in_values=swk[:, 0:256]), r=[swk, sc], w=[pos])
                op("dve", lambda e: e.tensor_copy(out=posf[:], in_=pos[:]), r=[pos], w=[posf])
                op("dve", lambda e: e.tensor_tensor(out=oh[:], in0=posf[:, :, :].unsqueeze(3).to_broadcast([128, 8, 16, 16]),
                                                    in1=thr16[:, :].unsqueeze(1).unsqueeze(1).to_broadcast([128, 8, 16, 16]), op=ALU.is_ge), r=[posf, thr16], w=[oh])
                op("dve", lambda e: e.tensor_reduce(out=posrf[:], in_=oh[:], axis=AX.X, op=ALU.add), r=[oh], w=[posrf])
                op("dve", lambda e: e.scalar_tensor_tensor(out=poscf[:], in0=posrf[:], scalar=-16.0, in1=posf[:], op0=ALU.mult, op1=ALU.add), r=[posrf, posf], w=[poscf])
                io4 = iota16[:, :].unsqueeze(1).unsqueeze(1).to_broadcast([128, 8, 16, 16])
                for pf_, which, dsel in ((posrf, 0, sel1), (poscf, 1, sel2)):
                    op("dve", lambda e, pf_=pf_: e.tensor_tensor(out=oh[:], in0=io4, in1=pf_[:, :, :].unsqueeze(3).to_broadcast([128, 8, 16, 16]), op=ALU.is_equal), r=[iota16, pf_], w=[oh])
                    op("dve", lambda e, which=which: e.tensor_tensor(out=oh[:], in0=oh[:], in1=i12f[:, which, :, :].unsqueeze(2).to_broadcast([128, 8, 16, 16]), op=ALU.mult), r=[oh, i12f], w=[oh])
                    op("dve", lambda e, dsel=dsel: e.tensor_reduce(out=dsel[:], in_=oh[:], axis=AX.X, op=ALU.add), r=[oh], w=[dsel])
                op("dve", lambda e: e.tensor_tensor(out=gsm[:], in0=sc[:], in1=sc[:, :, 0:1].to_broadcast([128, 8, 16]), op=ALU.subtract), r=[sc], w=[gsm])
                op("act", lambda e: e.activation(out=gsm[:], in_=gsm[:], func=AF.Exp), r=[gsm], w=[gsm])
                op("dve", lambda e: e.tensor_reduce(out=st8[:, 48:56], in_=gsm[:], axis=AX.X, op=ALU.add), r=[gsm], w=[st8])
                op("dve", lambda e: e.reciprocal(out=st8[:, 56:64], in_=st8[:, 48:56]), r=[st8], w=[st8])
                op("dve", lambda e: e.tensor_tensor(out=gsm[:], in0=gsm[:], in1=st8[:, 56:64].unsqueeze(2).to_broadcast([128, 8, 16]), op=ALU.mult), r=[gsm, st8], w=[gsm])
                for i_, src in enumerate((sel1, sel2, gsm)):
                    op("pe", lambda e, i_=i_, src=src: e.transpose(out=P[0][:, i_ * 128:(i_ + 1) * 128], in_=src[:, :, :].rearrange("p h k -> p (h k)"), identity=identf[:]), r=[src, identf], w=[P[0]])
                op("act", lambda e: e.activation(out=TT[:], in_=P[0][:, 0:384].rearrange("p (a n) -> p a n", a=3), func=AF.Copy), r=[P[0]], w=[TT])

            def b2_back(t, TT):
                iob = iota128[:, :].unsqueeze(1).to_broadcast([128, 32, 128])
                for qq in range(4):
                    n0 = qq * 32
                    p1 = P1q[qq % 2]
                    p2 = P2q[qq % 2]
                    op("dve", lambda e, n0=n0, p2=p2: e.tensor_tensor(out=p2[:], in0=iob, in1=TT[:, 1, n0:n0 + 32].unsqueeze(2).to_broadcast([128, 32, 128]), op=ALU.is_equal), r=[iota128, TT], w=[p2])
                    op("dve", lambda e, n0=n0, p1=p1: e.tensor_tensor(out=p1[:], in0=iob, in1=TT[:, 0, n0:n0 + 32].unsqueeze(2).to_broadcast([128, 32, 128]), op=ALU.is_equal), r=[iota128, TT], w=[p1])
                    op("pool", lambda e, n0=n0, p1=p1: e.tensor_tensor(out=p1[:], in0=p1[:], in1=TT[:, 2, n0:n0 + 32].unsqueeze(2).to_broadcast([128, 32, 128]), op=ALU.mult), r=[p1, TT], w=[p1])
                    for q4 in range(8):
                        bank = P[4 + (q4 % 3)]
                        for k in range(4):
                            n = q4 * 4 + k
                            op("pe", lambda e, bank=bank, k=k, n=n, p1=p1, p2=p2: e.matmul(bank[:, :].rearrange("p (i n) -> p n i", n=4)[:, k, :], lhsT=p1[:, n, :], rhs=p2[:, n, :], start=True, stop=True), r=[p1, p2], w=[bank])
                        nn = n0 + q4 * 4
                        op("act", lambda e, bank=bank, nn=nn: e.activation(out=Gs[:, :, nn:nn + 4], in_=bank[:, :].rearrange("p (i n) -> p i n", n=4), func=AF.Copy), r=[bank], w=[Gs])
                    yield

            TTs = [TT, TT2]
            if tiles:
                for _ in b2_front(tiles[0], TTs[0]):
                    pass
            for i_, t in enumerate(tiles):
                fg = b2_front(tiles[i_ + 1], TTs[(i_ + 1) % 2]) if i_ + 1 < len(tiles) else iter(())
                bg = b2_back(t, TTs[i_ % 2])
                for _q in range(4):
                    next(bg, None)
                    next(fg, None)
                for _ in bg:
                    pass
                for _ in fg:
                    pass
                op("sp", lambda e: e.dma_start(out=Gd[t, :, :], in_=Gs[:, :, :].rearrange("p i n -> p (i n)")), r=[Gs], w=[gdres[t]], dma=True)

            kb.barrier()
            pvv = pv[l][:, :].rearrange("(i1 i2) d -> i1 i2 d", i2=128)
            tgroups = [tiles[i:i + 2] for i in range(0, len(tiles), 2)]
            NSB = 128 // NBLK
            cring = [0]
            pend_casts = {}

            def emit_loads(sbk):
                ub_ = u16[sbk % 2]
                vb_ = v16[sbk % 2]
                casts = []
                for c in range(8):
                    s_ = cstg[cring[0] % NCS]
                    cring[0] += 1
                    op("pool", lambda e, s_=s_, c=c: e.dma_start(out=s_[:, 0:512], in_=puT[l][c * 128:(c + 1) * 128, sbk * NBLK * 128:(sbk + 1) * NBLK * 128]), r=[in_res], w=[s_], dma=True)
                    casts.append(lambda s_=s_, c=c, ub_=ub_: op("act", lambda e: e.activation(out=ub_[:, c, :], in_=s_[:, 0:512], func=AF.Copy, scale=n2s[:, c:c + 1]), r=[s_, n2s], w=[ub_]))
                for blk in range(NBLK):
                    for hf in range(2):
                        s_ = cstg[cring[0] % NCS]
                        cring[0] += 1
                        op("pool", lambda e, s_=s_, blk=blk, hf=hf: e.dma_start(out=s_[:, 0:512], in_=pvv[:, sbk * NBLK + blk, hf * 512:(hf + 1) * 512]), r=[in_res], w=[s_], dma=True)
                        casts.append(lambda s_=s_, blk=blk, hf=hf, vb_=vb_: op("act", lambda e: e.activation(out=vb_[:, blk, hf * 512:(hf + 1) * 512], in_=s_[:, 0:512], func=AF.Copy), r=[s_], w=[vb_]))
                return casts

            items = [(sbk, gi_, blk) for sbk in range(NSB) for gi_ in range(len(tgroups)) for blk in range(NBLK)]

            def emit_a(it_):
                sbk, gi_, blk = it_
                ub_ = u16[sbk % 2]
                tl = tgroups[gi_]
                nt_ = len(tl)
                ntk = 128 * nt_
                A = P[blk % 2]
                if blk == 0:
                    gc = Gc[(sbk * len(tgroups) + gi_) % 2]
                    for ti, t in enumerate(tl):
                        op("sp", lambda e, ti=ti, t=t: e.dma_start(out=gc[:, ti, :, :], in_=Gd[t, :, sbk * NBLK * 128:(sbk + 1) * NBLK * 128].rearrange("p (i n) -> p i n", i=NBLK)), r=[gdres[t]], w=[gc], dma=True)
                contiguous = (nt_ == 2 and tl[1] == tl[0] + 1) or nt_ == 1
                if contiguous:
                    for c in range(8):
                        op("pe", lambda e, c=c: e.matmul(A[:, 0:ntk], lhsT=ub_[:, c, blk * 128:(blk + 1) * 128], rhs=h2T_all_c[:, c, tl[0] * 128: tl[0] * 128 + ntk], start=(c == 0), stop=(c == 7)), r=[ub_, h2T_all_c], w=[A])
                else:
                    for ti, t in enumerate(tl):
                        for c in range(8):
                            op("pe", lambda e, c=c, ti=ti, t=t: e.matmul(A[:, ti * 128:(ti + 1) * 128], lhsT=ub_[:, c, blk * 128:(blk + 1) * 128], rhs=h2T_all_c[:, c, t * 128:(t + 1) * 128], start=(c == 0 and ti == 0), stop=(c == 7)), r=[ub_, h2T_all_c], w=[A])

            def emit_rest(it_):
                sbk, gi_, blk = it_
                vb_ = v16[sbk % 2]
                tl = tgroups[gi_]
                nt_ = len(tl)
                ntk = 128 * nt_
                A = P[blk % 2]
                ge = gel[blk % 2]
                cf = cfT[blk % 2]
                gc = Gc[(sbk * len(tgroups) + gi_) % 2]
                op("act", lambda e: e.activation(out=ge[:, 0:ntk], in_=A[:, 0:ntk], func=AF.Gelu), r=[A], w=[ge])
                op("dve", lambda e: e.tensor_tensor(out=cf[:, 0:ntk].rearrange("p (t n) -> p t n", t=nt_), in0=ge[:, 0:ntk].rearrange("p (t n) -> p t n", t=nt_), in1=gc[:, 0:nt_, blk, :], op=ALU.mult), r=[ge, gc], w=[cf])
                for ti, t in enumerate(tl):
                    for hf in range(2):
                        ab = P[2 + ti * 2 + hf]
                        op("pe", lambda e, ab=ab, ti=ti, hf=hf: e.matmul(ab[:, :], lhsT=cf[:, ti * 128:(ti + 1) * 128], rhs=vb_[:, blk, hf * 512:(hf + 1) * 512], start=(blk == 0), stop=(blk == NBLK - 1)), r=[cf, vb_], w=[ab])
                if blk == NBLK - 1:
                    for ti, t in enumerate(tl):
                        for hf in range(2):
                            ab = P[2 + ti * 2 + hf]
                            if sbk == 0:
                                op("dve", lambda e, ab=ab, t=t, hf=hf: e.tensor_copy(out=accs[:, t, hf * 512:(hf + 1) * 512], in_=ab[:, :]), r=[ab], w=[accres[t]])
                            else:
                                op("dve", lambda e, ab=ab, t=t, hf=hf: e.tensor_tensor(out=accs[:, t, hf * 512:(hf + 1) * 512], in0=accs[:, t, hf * 512:(hf + 1) * 512], in1=ab[:, :], op=ALU.add), r=[ab, accres[t]], w=[accres[t]])

            for cst_ in emit_loads(0):
                cst_()
            ng = len(tgroups)
            cast_at = {max(0, ng // 3): (0, 8), max(1, (2 * ng) // 3): (8, 16)} if ng >= 3 else {0: (0, 16)}
            emit_a(items[0])
            for i_, it_ in enumerate(items):
                sbk, gi_, blk = it_
                if gi_ == 0 and blk == 0 and sbk + 1 < NSB:
                    pend_casts[sbk + 1] = emit_loads(sbk + 1)
                if blk == 0 and gi_ in cast_at and (sbk + 1) in pend_casts:
                    a_, b_ = cast_at[gi_]
                    for cst_ in pend_casts[sbk + 1][a_:b_]:
                        cst_()
                if i_ + 1 < len(items):
                    emit_a(items[i_ + 1])
                emit_rest(it_)

        for t in tiles:
            samp = (t == NT - 1)
            rows = slice(t * 128, (t + 1) * 128)
            op("sp", lambda e: e.dma_start(out=xt[:], in_=xmid[rows, :]), r=[xmres[t]], w=[xt], dma=True)
            if do_peer:
                op("dve", lambda e: e.tensor_tensor(out=xt[:], in0=xt[:], in1=accs[:, t, :], op=ALU.add), r=[xt, accres[t]], w=[xt])
            if last:
                if samp:
                    op("sp", lambda e: e.dma_start(out=y_s[:, :], in_=xt[0:DEC, :]), r=[xt], w=[out_res], dma=True)
                else:
                    op("sp", lambda e: e.dma_start(out=y_p[rows, :], in_=xt[:]), r=[xt], w=[out_res], dma=True)
            else:
                op("sp", lambda e: e.dma_start(out=xout_d[rows, :], in_=xt[:]), r=[xt], w=[xout_r[t]], dma=True)

    kb.finish()
    es.close()
    return nc, kb.ninst


def _consts():
    j = np.arange(128)
    ident = np.eye(128, dtype=np.float32)
    negtri = -(j[:, None] >= j[None, :]).astype(np.float32)
    mlt = (j[:, None] < j[None, :]).astype(np.float32)
    ident4 = np.tile(ident, (1, 4))
    negm = np.where(j[:, None] >= j[None, :], -30000.0, 0.0).astype(np.float32)
    negm4 = np.tile(negm, (1, 4))
    pow2 = np.tile((0.5 ** np.arange(1, NBIS + 1)).astype(np.float32)[None, :], (128, 1))
    iota = np.tile(np.arange(16, dtype=np.float32)[None, :], (128, 1))
    iota128 = np.tile(np.arange(128, dtype=np.float32)[None, :], (128, 1))
    return np.concatenate([ident, negtri, mlt, ident4, negm4, pow2, iota, iota128], axis=1).astype(np.float32)


def _rope_table():
    pos = np.concatenate([np.arange(SEQ), SEQ + np.arange(128)]).astype(np.float32)
    half = 32
    inv = (10000.0 ** (-np.arange(half, dtype=np.float32) / half)).astype(np.float32)
    ang = pos[:, None] * inv[None, :]
    return np.concatenate([np.cos(ang), np.sin(ang)], axis=1).astype(np.float32)


_PROG = {}


def _prep(x_prompt, x_sample, cache_a_k, cache_a_v, cache_idx_k, cache_b_k, cache_b_v,
          norm1, w_in, q_norm_a, k_norm_a, idx_k_norm, w_pa, w_pb, w_o, norm2,
          peer_wq, peer_k1, peer_k2, peer_u, peer_v, cores=range(8)):
    f = lambda a: np.ascontiguousarray(np.asarray(a, dtype=np.float32))
    x_prompt = f(x_prompt); x_sample = f(x_sample)
    shared = {
        "rope": _rope_table(),
        "cst": _consts(),
        "n1T": f(np.asarray(norm1).reshape(DEPTH, 8, 128).transpose(0, 2, 1)),
        "n2T": f(np.asarray(norm2).reshape(DEPTH, 8, 128).transpose(0, 2, 1)),
        "n2r": f(norm2),
        "w_in": f(w_in), "qn": f(q_norm_a), "kn": f(k_norm_a), "ikn": f(idx_k_norm),
        "w_pa": f(w_pa), "w_pb": f(w_pb), "w_o": f(w_o), "pwq": f(peer_wq),
        "pk1T": f(np.asarray(peer_k1).transpose(0, 3, 1, 2)),
        "pk2T": f(np.asarray(peer_k2).transpose(0, 3, 1, 2)),
    }
    pu = np.asarray(peer_u); pvv = np.asarray(peer_v)
    for l in range(DEPTH):
        shared["puT%d" % l] = f(pu[l].reshape(128, 128, D).transpose(2, 1, 0).reshape(D, 16384))
        shared["pv%d" % l] = f(pvv[l])
    cak = np.asarray(cache_a_k); cav = np.asarray(cache_a_v); cik = np.asarray(cache_idx_k)
    cbk = np.asarray(cache_b_k); cbv = np.asarray(cache_b_v)
    in_maps = []
    for b in cores:
        xa = np.zeros((NT * 128, D), np.float32)
        xa[:SEQ] = x_prompt[b]
        xa[SEQ:SEQ + DEC] = x_sample[b]
        m = dict(shared)
        m["x_in"] = xa
        m["cak"] = f(cak[:, b].reshape(DEPTH, SEQ, 128))
        m["cav"] = f(cav[:, b].reshape(DEPTH, SEQ, 128))
        m["cik"] = f(cik[:, b].reshape(DEPTH, SEQ, 64))
        m["cbk"] = f(cbk[:, b].reshape(DEPTH, SEQ, 512))
        m["cbv"] = f(cbv[:, b].reshape(DEPTH, SEQ, 512))
        in_maps.append(m)
    return in_maps


def kernel(**inputs):
    if "full" not in _PROG:
        _PROG["full"] = build_program()[0]
    nc = _PROG["full"]
    in_maps = _prep(**inputs)
    res = run_bass_kernel_spmd(nc, in_maps, core_ids=list(range(8)))
    R = res.results
    st = lambda k, shp: np.stack([np.asarray(R[b][k], dtype=np.float32) for b in range(8)], axis=1).reshape(shp)
    y_p = np.stack([np.asarray(R[b]["y_p"], dtype=np.float32) for b in range(8)], axis=0)
    y_s = np.stack([np.asarray(R[b]["y_s"], dtype=np.float32) for b in range(8)], axis=0)
    return (y_p, y_s,
            st("ak_p", (DEPTH, 8, SEQ, 2, 64)), st("av_p", (DEPTH, 8, SEQ, 2, 64)), st("ik_p", (DEPTH, 8, SEQ, 64)),
            st("bk_p", (DEPTH, 8, SEQ, 8, 64)), st("bv_p", (DEPTH, 8, SEQ, 8, 64)),
            st("ak_s", (DEPTH, 8, DEC, 2, 64)), st("av_s", (DEPTH, 8, DEC, 2, 64)), st("ik_s", (DEPTH, 8, DEC, 64)),
            st("bk_s", (DEPTH, 8, DEC, 8, 64)), st("bv_s", (DEPTH, 8, DEC, 8, 64)))
```

```python
import contextlib
import numpy as np
import concourse.bass as bass
import concourse.mybir as mybir
from concourse.bass_utils import run_bass_kernel_spmd

F32 = mybir.dt.float32
BF16 = mybir.dt.bfloat16
I32 = mybir.dt.int32
U32 = mybir.dt.uint32
ALU = mybir.AluOpType
AF = mybir.ActivationFunctionType
AX = mybir.AxisListType

D = 1024
DEPTH = 4
NT = 17
NKS = 17
SEQ = 2048
DEC = 16
NIN = 4936
EPS = 1e-6
NEG = -1e30
TOPK = 256
NBIS = 22


class Res:
    __slots__ = ("t", "w", "r")

    def __init__(self, t=None):
        self.t = t
        self.w = None
        self.r = {}

    def __getitem__(self, k):
        return self.t[k]


class KB:
    def __init__(self, nc, es):
        self.nc = nc
        self.es = es
        self.E = {"pe": nc.tensor, "act": nc.scalar, "dve": nc.vector, "pool": nc.gpsimd, "sp": nc.sync}
        self.sems = {}
        self.cnt = {}
        self.seen = {e: {} for e in self.E}
        self.ninst = 0
        self.dpool = {"sp": ["dsp%d" % i for i in range(24)], "pool": ["dpl%d" % i for i in range(6)]}
        self.dnext = {"sp": 0, "pool": 0}

    def sem(self, key):
        if key not in self.sems:
            self.sems[key] = self.es.enter_context(self.nc.semaphore("s_" + key))
            self.cnt[key] = 0
        return self.sems[key]

    def op(self, eng, fn, r=(), w=(), dma=False):
        deps = {}
        for x in r:
            if x.w is not None:
                k, v = x.w
                if deps.get(k, 0) < v:
                    deps[k] = v
        inorder = (not dma) and eng in ("act", "dve")
        for x in w:
            if x.w is not None:
                k, v = x.w
                if not (inorder and k == eng) and deps.get(k, 0) < v:
                    deps[k] = v
            for k, v in x.r.items():
                if not (inorder and k == eng) and deps.get(k, 0) < v:
                    deps[k] = v
        E = self.E[eng]
        seen = self.seen[eng]
        for k, v in deps.items():
            if k == "pe" and eng == "pe" and not dma:
                continue
            if seen.get(k, 0) < v:
                E.wait_ge(self.sem(k), v)
                seen[k] = v
        if dma:
            pl = self.dpool[eng]
            key = pl[self.dnext[eng]]
            self.dnext[eng] = (self.dnext[eng] + 1) % len(pl)
            s = self.sem(key)
            prev = self.cnt[key]
            if prev > 0 and seen.get(key, 0) < prev:
                E.wait_ge(s, prev)
                seen[key] = prev
        else:
            key = eng
            s = self.sem(key)
        ins = fn(E)
        inc = 16 if dma else 1
        self.cnt[key] += inc
        c = self.cnt[key]
        ins.then_inc(s, inc)
        for x in r:
            x.r[key] = c
        for x in w:
            x.w = (key, c)
            x.r = {}
        self.ninst += 1
        return ins

    def barrier(self):
        for en, E in self.E.items():
            seen = self.seen[en]
            for k, sm in self.sems.items():
                v = self.cnt[k]
                if v > 0 and seen.get(k, 0) < v:
                    E.wait_ge(sm, v)
                    seen[k] = v

    def finish(self):
        E = self.E["sp"]
        for k, s in self.sems.items():
            if self.cnt[k] > 0:
                E.wait_ge(s, self.cnt[k])


def build_program(depth=DEPTH, tiles=None, do_peer=True):
    if tiles is None:
        tiles = list(range(NT))
    nc = bass.Bass("TRN2", target_bir_lowering=False)
    es = contextlib.ExitStack()
    kb = KB(nc, es)

    def dram(name, shape, dt, kind):
        return nc.dram_tensor(name, shape, dt, kind=kind)

    x_in = dram("x_in", [NT * 128, D], F32, "ExternalInput")
    rope_in = dram("rope", [NT * 128, 64], F32, "ExternalInput")
    cak = dram("cak", [DEPTH, SEQ, 128], F32, "ExternalInput")
    cav = dram("cav", [DEPTH, SEQ, 128], F32, "ExternalInput")
    cik = dram("cik", [DEPTH, SEQ, 64], F32, "ExternalInput")
    cbk = dram("cbk", [DEPTH, SEQ, 512], F32, "ExternalInput")
    cbv = dram("cbv", [DEPTH, SEQ, 512], F32, "ExternalInput")
    n1T = dram("n1T", [DEPTH, 128, 8], F32, "ExternalInput")
    n2T = dram("n2T", [DEPTH, 128, 8], F32, "ExternalInput")
    n2r = dram("n2r", [DEPTH, D], F32, "ExternalInput")
    w_in = dram("w_in", [DEPTH, D, NIN], F32, "ExternalInput")
    qn = dram("qn", [DEPTH, 64], F32, "ExternalInput")
    kn = dram("kn", [DEPTH, 64], F32, "ExternalInput")
    ikn = dram("ikn", [DEPTH, 64], F32, "ExternalInput")
    w_pa = dram("w_pa", [DEPTH, 512, D], F32, "ExternalInput")
    w_pb = dram("w_pb", [DEPTH, 512, D], F32, "ExternalInput")
    w_o = dram("w_o", [DEPTH, D, D], F32, "ExternalInput")
    pwq = dram("pwq", [DEPTH, D, D], F32, "ExternalInput")
    pk1T = dram("pk1T", [DEPTH, 64, 8, 128], F32, "ExternalInput")
    pk2T = dram("pk2T", [DEPTH, 64, 8, 128], F32, "ExternalInput")
    puT = [dram("puT%d" % l, [D, 16384], F32, "ExternalInput") for l in range(DEPTH)]
    pv = [dram("pv%d" % l, [16384, D], F32, "ExternalInput") for l in range(DEPTH)]

    y_p = dram("y_p", [SEQ, D], F32, "ExternalOutput")
    y_s = dram("y_s", [DEC, D], F32, "ExternalOutput")
    ak_p = dram("ak_p", [DEPTH, SEQ, 128], F32, "ExternalOutput")
    av_p = dram("av_p", [DEPTH, SEQ, 128], F32, "ExternalOutput")
    ik_p = dram("ik_p", [DEPTH, SEQ, 64], F32, "ExternalOutput")
    bk_p = dram("bk_p", [DEPTH, SEQ, 512], F32, "ExternalOutput")
    bv_p = dram("bv_p", [DEPTH, SEQ, 512], F32, "ExternalOutput")
    ak_s = dram("ak_s", [DEPTH, DEC, 128], F32, "ExternalOutput")
    av_s = dram("av_s", [DEPTH, DEC, 128], F32, "ExternalOutput")
    ik_s = dram("ik_s", [DEPTH, DEC, 64], F32, "ExternalOutput")
    bk_s = dram("bk_s", [DEPTH, DEC, 512], F32, "ExternalOutput")
    bv_s = dram("bv_s", [DEPTH, DEC, 512], F32, "ExternalOutput")
    xs_dram = [dram("xscr%d" % i, [NT * 128, D], F32, "Internal") for i in range(2)]
    xs_res = [[Res() for _ in range(NT)] for _ in range(2)]
    out_res = Res()
    in_res = Res()

    ARENA_F32 = 52000
    arena = es.enter_context(nc.sbuf_tensor("arena", [128, ARENA_F32], F32))
    aoff = [0]
    DTB = {F32: 4, BF16: 2, I32: 4, U32: 4}

    def sb(name, shape, dt):
        nb = DTB[dt]
        n = 1
        for d_ in shape[1:]:
            n *= d_
        nbytes = (n * nb + 31) // 32 * 32
        o = aoff[0]
        assert o % 4 == 0
        aoff[0] = o + nbytes
        assert aoff[0] <= ARENA_F32 * 4, (name, aoff[0])
        v = arena[0:shape[0], o // 4:(o + nbytes) // 4]
        if dt != F32:
            v = v.bitcast(dt)
        v = v[:, 0:n]
        if len(shape) == 3:
            v = v.rearrange("p (a b) -> p a b", a=shape[1])
        elif len(shape) == 4:
            v = v.rearrange("p (a b c) -> p a b c", a=shape[1], b=shape[2])
        return Res(v)

    def pst(name, shape, dt):
        return Res(es.enter_context(nc.psum_tensor(name, shape, dt)))

    ident = sb("ident", [128, 128], BF16)
    ident4 = sb("ident4", [128, 512], BF16)
    negtri = sb("negtri", [128, 128], BF16)
    mlt = sb("mlt", [128, 128], BF16)
    negm4 = sb("negm4", [128, 512], BF16)
    ones1 = sb("ones1", [128, 1], BF16)
    pow2 = sb("pow2", [128, NBIS], F32)
    iota16 = sb("iota16", [128, 16], F32)
    thr16 = sb("thr16", [128, 16], F32)
    identf = sb("identf", [128, 128], F32)
    iota128 = sb("iota128", [128, 128], F32)
    NCST = 128 * 3 + 512 * 2 + NBIS + 16 + 128
    cst_in = dram("cst", [128, NCST], F32, "ExternalInput")
    gq = sb("gq", [128, 64], F32)
    gk = sb("gk", [128, 64], F32)
    gik = sb("gik", [128, 64], F32)
    n1s = sb("n1s", [128, 8], F32)
    n2s = sb("n2s", [128, 8], F32)
    n2b = sb("n2b", [128, D], F32)
    xt = sb("xt", [128, D], F32)
    hb = sb("hb", [128, D], BF16)
    hT = sb("hT", [128, 8, 128], BF16)
    sq = sb("sq", [128, D], F32)
    st8 = sb("st8", [128, 64], F32)
    STW = 1536
    stg = [sb("stg%d" % i, [128, STW], F32) for i in range(2)]
    mark0 = aoff[0]

    WARENA_A = sb("warenaA", [128, 8 * 2888], BF16)
    kaT = sb("kaT", [64, 2, NKS * 128], BF16)
    kiT = sb("kiT", [64, NKS * 128], BF16)
    kbT = sb("kbT", [64, 8, NKS * 128], BF16)
    vaA = sb("vaA", [128, NKS, 2, 65], BF16)
    vbB = sb("vbB", [128, NKS, 8, 64], BF16)
    kvres = [Res() for _ in range(NKS)]
    ropet = sb("ropet", [128, 64], F32)
    pf = sb("pf", [128, 512], F32)
    pf2 = sb("pf2", [128, 512], F32)
    pbf = sb("pbf", [128, 512], BF16)
    r1 = sb("r1", [128, 256], F32)
    r2 = sb("r2", [128, 256], F32)
    qaT = sb("qaT", [64, 8, 128], BF16)
    qiT = sb("qiT", [64, 8, 128], BF16)
    qbT = sb("qbT", [64, 8, 128], BF16)
    wi = sb("wi", [128, 8], F32)
    isc = sb("isc", [128, 2560], F32)
    cstage = Res(isc.t[:, 0:NCST])
    mbias = sb("mbias", [128, 2560], BF16)
    rl = [sb("rl%d" % i, [128, 512], F32) for i in range(2)]
    PT = [sb("PT%d" % i, [128, 512], BF16) for i in range(2)]
    oa = sb("oa", [128, 512], F32)
    ob = sb("ob", [128, 512], F32)
    ebuf = sb("ebuf", [128, 1024], F32)
    spT = sb("spT", [128, 1024], BF16)
    ET = sb("ET", [128, 1024], BF16)
    ebufs = [ebuf, sb("ebuf2", [128, 1024], F32)]
    spTs = [spT, sb("spT2", [128, 1024], BF16)]
    ETs = [ET, sb("ET2", [128, 1024], BF16)]
    dd = sb("dd", [128, 8], F32)
    pvs = sb("pvs", [128, 512], F32)
    bis = sb("bis", [128, 8], F32)
    dtab = sb("dtab", [128, NBIS], F32)
    endA = aoff[0]

    aoff[0] = mark0
    WARENA_B = sb("warenaB", [128, 32768], BF16)
    oabb = sb("oabb", [128, D], BF16)
    sga = sb("sga", [128, D], F32)
    sgb = sb("sgb", [128, D], F32)
    oT = sb("oT", [128, 8, 128], BF16)
    mm = sb("mm", [128, D], F32)
    mbf = sb("mbf", [128, D], BF16)
    oabf = sb("oabf", [128, D], F32)
    endB1 = aoff[0]

    aoff[0] = mark0
    h2T_all = sb("h2T_all", [128, 8, NT * 128], BF16)
    WQ = sb("WQ", [128, 8 * 1024], BF16)
    qsb = sb("qsb", [128, D], BF16)
    qT = sb("qT", [64, 16, 128], BF16)
    k12 = sb("k12", [64, 2, 8, 128], BF16)
    s12 = sb("s12", [128, 2, 8, 128], F32)
    swk = sb("swk", [128, 256], F32)
    v12 = sb("v12", [128, 2, 8, 16], F32)
    i12 = sb("i12", [128, 2, 8, 16], U32)
    i12f = sb("i12f", [128, 2, 8, 16], F32)
    cand = sb("cand", [128, 8, 256], F32)
    sc = sb("sc", [128, 8, 16], F32)
    pos = sb("pos", [128, 8, 16], U32)
    posf = sb("posf", [128, 8, 16], F32)
    posrf = sb("posrf", [128, 8, 16], F32)
    poscf = sb("poscf", [128, 8, 16], F32)
    oh = sb("oh", [128, 8, 16, 16], F32)
    sel1 = sb("sel1", [128, 8, 16], F32)
    sel2 = sb("sel2", [128, 8, 16], F32)
    gsm = sb("gsm", [128, 8, 16], F32)
    TT = sb("TT", [128, 3, 128], F32)
    TT2 = sb("TT2", [128, 3, 128], F32)
    P1q = [sb("P1q%d" % i, [128, 32, 128], BF16) for i in range(2)]
    P2q = [sb("P2q%d" % i, [128, 32, 128], BF16) for i in range(2)]
    Gs = sb("Gs", [128, 128, 128], BF16)
    endB2 = aoff[0]

    aoff[0] = mark0
    h2T_all_c = sb("h2T_all_c", [128, 8, NT * 128], BF16)
    accs = sb("accs", [128, NT, D], F32)
    accres = [Res() for _ in range(NT)]
    NBLK = 4
    u16 = [sb("u16_%d" % i, [128, 8, NBLK * 128], BF16) for i in range(2)]
    v16 = [sb("v16_%d" % i, [128, NBLK, D], BF16) for i in range(2)]
    Gc = [sb("Gc%d" % i, [128, 2, NBLK, 128], BF16) for i in range(2)]
    gel = [sb("gel%d" % i, [128, 256], BF16) for i in range(2)]
    cfT = [sb("cfT%d" % i, [128, 256], BF16) for i in range(2)]
    cstg = [sb("cstg%d" % i, [128, 512], F32) for i in range(12)]
    cstg += [Res(stg[i][:, k * 512:(k + 1) * 512]) for i in range(2) for k in range(2)]
    NCS = len(cstg)
    assert NCS == 16
    endC = aoff[0]
    print("arena bytes: persistent", mark0, "A", endA, "B1", endB1, "B2", endB2, "C", endC, "cap", ARENA_F32 * 4)

    Gd = dram("Gd", [NT, 128, 16384], BF16, "Internal")
    gdres = [Res() for _ in range(NT)]
    xmid = dram("xmid", [NT * 128, D], F32, "Internal")
    xmres = [Res() for _ in range(NT)]
    oab_dram = dram("oabscr", [NT * 128, D], F32, "Internal")
    oab_res = [Res() for _ in range(NT)]

    P = [pst("ps%d" % i, [128, 512], F32) for i in range(7)]
    PTR = pst("ptr", [128, 1024], BF16)

    op = kb.op

    op("sp", lambda e: e.dma_start(out=cstage[:], in_=cst_in[:, :]), r=[in_res], w=[cstage], dma=True)
    o = 0
    for dst, wdt in ((ident, 128), (negtri, 128), (mlt, 128), (ident4, 512), (negm4, 512)):
        op("dve", lambda e, dst=dst, o=o, wdt=wdt: e.tensor_copy(out=dst[:], in_=cstage[:, o:o + wdt]), r=[cstage], w=[dst])
        o += wdt
    op("dve", lambda e, o=o: e.tensor_copy(out=pow2[:], in_=cstage[:, o:o + NBIS]), r=[cstage], w=[pow2])
    o += NBIS
    op("dve", lambda e, o=o: e.tensor_copy(out=iota16[:], in_=cstage[:, o:o + 16]), r=[cstage], w=[iota16])
    o += 16
    op("dve", lambda e, o=o: e.tensor_copy(out=iota128[:], in_=cstage[:, o:o + 128]), r=[cstage], w=[iota128])
    op("dve", lambda e: e.tensor_copy(out=identf[:], in_=cstage[:, 0:128]), r=[cstage], w=[identf])
    op("dve", lambda e: e.memset(ones1[:], 1.0), w=[ones1])
    op("dve", lambda e: e.tensor_scalar(out=thr16[:], in0=iota16[:], scalar1=16.0, scalar2=16.0, op0=ALU.mult, op1=ALU.add), r=[iota16], w=[thr16])
    op("dve", lambda e: e.memset(thr16[:, 15:16], 1e9), w=[thr16])
    op("dve", lambda e: e.memset(vaA[:], 1.0), w=[vaA] + kvres)
    kb.barrier()

    def transpose_blocks(src, nblk, dstT, dst_res, ptile=PTR):
        for b in range(nblk):
            op("pe", lambda e, b=b: e.transpose(out=ptile[0:64, b * 128:(b + 1) * 128], in_=src[:, b * 64:(b + 1) * 64], identity=ident[:]),
               r=[src, ident], w=[ptile])
        op("act", lambda e: e.activation(out=dstT, in_=ptile[0:64, 0:nblk * 128].rearrange("p (b n) -> p b n", b=nblk), func=AF.Copy),
           r=[ptile], w=dst_res)

    def rmsnorm_rows(xres, outbf, scratch):
        op("act", lambda e: e.activation(out=scratch[:], in_=xres[:], func=AF.Square, accum_out=st8[:, 0:1]), r=[xres], w=[scratch, st8])
        op("dve", lambda e: e.tensor_scalar(out=st8[:, 1:2], in0=st8[:, 0:1], scalar1=1.0 / D, scalar2=EPS, op0=ALU.mult, op1=ALU.add), r=[st8], w=[st8])
        op("act", lambda e: e.activation(out=st8[:, 2:3], in_=st8[:, 1:2], func=AF.Sqrt), r=[st8], w=[st8])
        op("dve", lambda e: e.reciprocal(out=st8[:, 3:4], in_=st8[:, 2:3]), r=[st8], w=[st8])
        op("dve", lambda e: e.tensor_scalar(out=outbf[:], in0=xres[:], scalar1=st8[:, 3:4], scalar2=None, op0=ALU.mult), r=[xres, st8], w=[outbf])

    def make_hT():
        for c in range(8):
            op("pe", lambda e, c=c: e.transpose(out=PTR[:, c * 128:(c + 1) * 128], in_=hb[:, c * 128:(c + 1) * 128], identity=ident[:]),
               r=[hb, ident], w=[PTR])
        op("act", lambda e: e.activation(out=hT[:], in_=PTR[:, :].rearrange("p (c n) -> p c n", c=8), func=AF.Copy), r=[PTR], w=[hT])

    def headnorm(src_res, src_ap, H, gain, dst):
        W = H * 64
        op("act", lambda e: e.activation(out=sq[:, 0:W], in_=src_ap, func=AF.Square), r=[src_res], w=[sq])
        op("dve", lambda e: e.tensor_reduce(out=st8[:, 8:8 + H], in_=sq[:, 0:W].rearrange("p (h d) -> p h d", h=H), axis=AX.X, op=ALU.add), r=[sq], w=[st8])
        op("dve", lambda e: e.tensor_scalar(out=st8[:, 16:16 + H], in0=st8[:, 8:8 + H], scalar1=1.0 / 64, scalar2=EPS, op0=ALU.mult, op1=ALU.add), r=[st8], w=[st8])
        op("act", lambda e: e.activation(out=st8[:, 24:24 + H], in_=st8[:, 16:16 + H], func=AF.Sqrt), r=[st8], w=[st8])
        op("dve", lambda e: e.reciprocal(out=st8[:, 32:32 + H], in_=st8[:, 24:24 + H]), r=[st8], w=[st8])
        d3 = dst[:, 0:W].rearrange("p (h d) -> p h d", h=H)
        op("dve", lambda e: e.tensor_tensor(out=d3, in0=src_ap.rearrange("p (h d) -> p h d", h=H),
                                            in1=st8[:, 32:32 + H].unsqueeze(2).to_broadcast([128, H, 64]), op=ALU.mult), r=[st8, src_res], w=[dst])
        op("dve", lambda e: e.tensor_tensor(out=d3, in0=d3, in1=gain[:, :].unsqueeze(1).to_broadcast([128, H, 64]), op=ALU.mult), r=[gain, dst], w=[dst])

    def rope(src, H, dst, scale=None):
        W = H * 64
        s3 = src[:, 0:W].rearrange("p (h d) -> p h d", h=H)
        d3 = dst[:, 0:W].rearrange("p (h d) -> p h d", h=H)
        cosb = ropet[:, 0:32].unsqueeze(1).to_broadcast([128, H, 32])
        sinb = ropet[:, 32:64].unsqueeze(1).to_broadcast([128, H, 32])
        a3 = r1[:, 0:H * 32].rearrange("p (h d) -> p h d", h=H)
        b3 = r2[:, 0:H * 32].rearrange("p (h d) -> p h d", h=H)
        op("dve", lambda e: e.tensor_tensor(out=a3, in0=s3[:, :, 0:32], in1=cosb, op=ALU.mult), r=[src, ropet], w=[r1])
        op("dve", lambda e: e.tensor_tensor(out=b3, in0=s3[:, :, 32:64], in1=sinb, op=ALU.mult), r=[src, ropet], w=[r2])
        op("dve", lambda e: e.tensor_tensor(out=d3[:, :, 0:32], in0=a3, in1=b3, op=ALU.subtract), r=[r1, r2], w=[dst])
        op("dve", lambda e: e.tensor_tensor(out=a3, in0=s3[:, :, 32:64], in1=cosb, op=ALU.mult), r=[src, ropet], w=[r1])
        op("dve", lambda e: e.tensor_tensor(out=b3, in0=s3[:, :, 0:32], in1=sinb, op=ALU.mult), r=[src, ropet], w=[r2])
        op("dve", lambda e: e.tensor_tensor(out=d3[:, :, 32:64], in0=a3, in1=b3, op=ALU.add), r=[r1, r2], w=[dst])

    def load_cast_rows(wres, dram_ap_fn, nchunks, width, dst_fn, scale_res=None, scale_col=None):
        i = 0
        for c in range(nchunks):
            for o0 in range(0, width, STW):
                wdt = min(STW, width - o0)
                s = stg[i % 2]
                i += 1
                op("sp", lambda e, c=c, o0=o0, wdt=wdt, s=s: e.dma_start(out=s[:, 0:wdt], in_=dram_ap_fn(c, o0, wdt)), r=[in_res], w=[s], dma=True)
                if scale_res is not None:
                    op("dve", lambda e, c=c, o0=o0, wdt=wdt, s=s: e.tensor_scalar(out=dst_fn(c, o0, wdt), in0=s[:, 0:wdt], scalar1=scale_res[:, scale_col(c):scale_col(c) + 1], scalar2=None, op0=ALU.mult),
                       r=[s, scale_res], w=[wres])
                else:
                    op("pool", lambda e, c=c, o0=o0, wdt=wdt, s=s: e.tensor_copy(out=dst_fn(c, o0, wdt), in_=s[:, 0:wdt]), r=[s], w=[wres])

    WAA = WARENA_A.t
    WA = WARENA_B.t
    NQKV = 2888
    wa_qkv = lambda c, o0, wdt: WAA[:, c * NQKV + o0: c * NQKV + o0 + wdt]
    GOFF = 0
    PAOFF = 8 * 2048
    PBOFF = PAOFF + 4 * 1024
    WOOFF = PBOFF + 4 * 1024
    PQOFF = WOOFF + 8 * 1024

    for l in range(depth):
        xin_d = x_in if l == 0 else xs_dram[(l - 1) % 2]
        xin_r = [in_res] * NT if l == 0 else xs_res[(l - 1) % 2]
        xout_d = xs_dram[l % 2]
        xout_r = xs_res[l % 2]
        last = (l == depth - 1)

        op("sp", lambda e: e.dma_start(out=n1s[:], in_=n1T[l, :, :]), r=[in_res], w=[n1s], dma=True)
        op("sp", lambda e: e.dma_start(out=n2s[:], in_=n2T[l, :, :]), r=[in_res], w=[n2s], dma=True)
        op("sp", lambda e: e.dma_start(out=gq[:], in_=qn[l, :].partition_broadcast(128)), r=[in_res], w=[gq], dma=True)
        op("sp", lambda e: e.dma_start(out=gk[:], in_=kn[l, :].partition_broadcast(128)), r=[in_res], w=[gk], dma=True)
        op("sp", lambda e: e.dma_start(out=gik[:], in_=ikn[l, :].partition_broadcast(128)), r=[in_res], w=[gik], dma=True)
        op("sp", lambda e: e.dma_start(out=n2b[:], in_=n2r[l, :].partition_broadcast(128)), r=[in_res], w=[n2b], dma=True)

        kb.barrier()
        if l > 0:
            op("pool", lambda e: e.memset(vaA[:], 1.0), w=[vaA] + kvres)
        load_cast_rows(WARENA_A, lambda c, o0, wdt: w_in[l, c * 128:(c + 1) * 128, o0:o0 + wdt], 8, NQKV, wa_qkv, n1s, lambda c: c)

        for t in tiles:
            samp = (t == NT - 1)
            nk = t + 1
            if samp:
                for k0 in range(0, 16, 4):
                    for kk in range(k0, k0 + 4):
                        rows = slice(kk * 128, (kk + 1) * 128)
                        s = stg[kk % 2]
                        op("sp", lambda e, s=s, rows=rows: e.dma_start(out=s[:, 0:128], in_=cak[l, rows, :]), r=[in_res], w=[s], dma=True)
                        op("sp", lambda e, s=s, rows=rows: e.dma_start(out=s[:, 128:192], in_=cik[l, rows, :]), r=[in_res], w=[s], dma=True)
                        op("sp", lambda e, s=s, rows=rows: e.dma_start(out=s[:, 192:320], in_=cav[l, rows, :]), r=[in_res], w=[s], dma=True)
                        op("sp", lambda e, s=s, rows=rows: e.dma_start(out=s[:, 512:1024], in_=cbk[l, rows, :]), r=[in_res], w=[s], dma=True)
                        op("sp", lambda e, s=s, rows=rows: e.dma_start(out=s[:, 1024:1536], in_=cbv[l, rows, :]), r=[in_res], w=[s], dma=True)
                        op("dve", lambda e, s=s: e.tensor_copy(out=pbf[:, 0:192], in_=s[:, 0:192]), r=[s], w=[pbf])
                        for b in range(3):
                            op("pe", lambda e, b=b: e.transpose(out=PTR[0:64, b * 128:(b + 1) * 128], in_=pbf[:, b * 64:(b + 1) * 64], identity=ident[:]), r=[pbf, ident], w=[PTR])
                        ks = slice(kk * 128, (kk + 1) * 128)
                        op("act", lambda e, ks=ks: e.activation(out=kaT[:, :, ks], in_=PTR[0:64, 0:256].rearrange("p (b n) -> p b n", b=2), func=AF.Copy), r=[PTR], w=[kvres[kk]])
                        op("act", lambda e, ks=ks: e.activation(out=kiT[:, ks], in_=PTR[0:64, 256:384], func=AF.Copy), r=[PTR], w=[kvres[kk]])
                        op("dve", lambda e, s=s, kk=kk: e.tensor_copy(out=vaA[:, kk, :, 0:64], in_=s[:, 192:320].rearrange("p (g d) -> p g d", g=2)), r=[s], w=[kvres[kk]])
                        op("dve", lambda e, s=s: e.tensor_copy(out=pbf[:, 0:512], in_=s[:, 512:1024]), r=[s], w=[pbf])
                        for b in range(8):
                            op("pe", lambda e, b=b: e.transpose(out=PTR[0:64, b * 128:(b + 1) * 128], in_=pbf[:, b * 64:(b + 1) * 64], identity=ident[:]), r=[pbf, ident], w=[PTR])
                        op("act", lambda e, ks=ks: e.activation(out=kbT[:, :, ks], in_=PTR[0:64, :].rearrange("p (b n) -> p b n", b=8), func=AF.Copy), r=[PTR], w=[kvres[kk]])
                        op("pool", lambda e, s=s, kk=kk: e.tensor_copy(out=vbB[:, kk, :, :], in_=s[:, 1024:1536].rearrange("p (h d) -> p h d", h=8)), r=[s], w=[kvres[kk]])

            rows = slice(t * 128, (t + 1) * 128)
            op("sp", lambda e: e.dma_start(out=xt[:], in_=xin_d[rows, :]), r=[xin_r[t]], w=[xt], dma=True)
            op("sp", lambda e: e.dma_start(out=ropet[:], in_=rope_in[rows, :]), r=[in_res], w=[ropet], dma=True)
            rmsnorm_rows(xt, hb, sq)
            make_hT()

            def proj(pt, c0, wdt):
                for c in range(8):
                    op("pe", lambda e, c=c: e.matmul(pt[:, 0:wdt], lhsT=hT[:, c, :], rhs=WAA[:, c * NQKV + c0: c * NQKV + c0 + wdt], start=(c == 0), stop=(c == 7)),
                       r=[hT, WARENA_A], w=[pt])

            def out_rows(dst_p, dst_s, src, wdt):
                if samp:
                    op("pool", lambda e: e.dma_start(out=dst_s[l, :, :], in_=src[0:DEC, 0:wdt]), r=[src], w=[out_res], dma=True)
                else:
                    op("pool", lambda e: e.dma_start(out=dst_p[l, rows, :], in_=src[:, 0:wdt]), r=[src], w=[out_res], dma=True)

            ks = slice(t * 128, (t + 1) * 128)
            proj(P[0], 0, 512)
            headnorm(P[0], P[0][:, 0:512], 8, gq, pf)
            rope(pf, 8, pf2)
            op("dve", lambda e: e.tensor_scalar(out=pbf[:], in0=pf2[:], scalar1=0.125, scalar2=None, op0=ALU.mult), r=[pf2], w=[pbf])
            transpose_blocks(pbf, 8, qaT[:], [qaT])
            proj(P[1], 512, 256)
            headnorm(P[1], P[1][:, 0:128], 2, gk, pf)
            rope(pf, 2, pf2)
            out_rows(ak_p, ak_s, pf2, 128)
            op("dve", lambda e: e.tensor_copy(out=pbf[:, 0:128], in_=pf2[:, 0:128]), r=[pf2], w=[pbf])
            op("act", lambda e: e.activation(out=pf[:, 0:128], in_=P[1][:, 128:256], func=AF.Copy), r=[P[1]], w=[pf])
            out_rows(av_p, av_s, pf, 128)
            op("dve", lambda e: e.tensor_copy(out=vaA[:, t, :, 0:64], in_=pf[:, 0:128].rearrange("p (g d) -> p g d", g=2)), r=[pf], w=[kvres[t]])
            transpose_blocks(pbf, 2, kaT[:, :, ks], [kvres[t]])
            proj(P[0], 768, 512)
            rope(P[0], 8, pf2)
            op("dve", lambda e: e.tensor_copy(out=pbf[:], in_=pf2[:]), r=[pf2], w=[pbf])
            transpose_blocks(pbf, 8, qiT[:], [qiT])
            proj(P[1], 1280, 72)
            headnorm(P[1], P[1][:, 0:64], 1, gik, pf)
            rope(pf, 1, pf2)
            out_rows(ik_p, ik_s, pf2, 64)
            op("dve", lambda e: e.tensor_copy(out=pbf[:, 0:64], in_=pf2[:, 0:64]), r=[pf2], w=[pbf])
            op("act", lambda e: e.activation(out=wi[:], in_=P[1][:, 64:72], func=AF.Copy), r=[P[1]], w=[wi])
            for b in range(1):
                op("pe", lambda e: e.transpose(out=PTR[0:64, 0:128], in_=pbf[:, 0:64], identity=ident[:]), r=[pbf, ident], w=[PTR])
            op("act", lambda e: e.activation(out=kiT[:, ks], in_=PTR[0:64, 0:128], func=AF.Copy), r=[PTR], w=[kvres[t]])
            proj(P[0], 1352, 512)
            op("act", lambda e: e.activation(out=pbf[:], in_=P[0][:, :], func=AF.Copy, scale=0.125), r=[P[0]], w=[pbf])
            transpose_blocks(pbf, 8, qbT[:], [qbT])
            proj(P[1], 1864, 512)
            op("act", lambda e: e.activation(out=pf[:], in_=P[1][:, :], func=AF.Copy), r=[P[1]], w=[pf])
            out_rows(bk_p, bk_s, pf, 512)
            op("dve", lambda e: e.tensor_copy(out=pbf[:], in_=pf[:]), r=[pf], w=[pbf])
            transpose_blocks(pbf, 8, kbT[:, :, ks], [kvres[t]])
            proj(P[0], 2376, 512)
            op("act", lambda e: e.activation(out=pf2[:], in_=P[0][:, :], func=AF.Copy), r=[P[0]], w=[pf2])
            out_rows(bv_p, bv_s, pf2, 512)
            op("dve", lambda e: e.tensor_copy(out=vbB[:, t, :, :], in_=pf2[:, :].rearrange("p (h d) -> p h d", h=8)), r=[pf2], w=[kvres[t]])

            S = nk * 128
            nblk = (S + 511) // 512
            kvr = [kvres[i] for i in range(nk)]
            for bi in range(nblk):
                c0 = bi * 512
                wdt = min(512, S - c0)
                for h in range(8):
                    pt = P[2 + (h % 2)]
                    rb = rl[h % 2]
                    op("pe", lambda e, h=h, pt=pt: e.matmul(pt[:, 0:wdt], lhsT=qiT[:, h, :], rhs=kiT[:, c0:c0 + wdt], start=True, stop=True), r=[qiT] + kvr, w=[pt])
                    op("act", lambda e, pt=pt, rb=rb: e.activation(out=rb[:, 0:wdt], in_=pt[:, 0:wdt], func=AF.Relu, scale=0.125 * (8 ** -0.5)), r=[pt], w=[rb])
                    if h == 0:
                        op("dve", lambda e, rb=rb: e.tensor_scalar(out=isc[:, c0:c0 + wdt], in0=rb[:, 0:wdt], scalar1=wi[:, 0:1], scalar2=None, op0=ALU.mult), r=[rb, wi], w=[isc])
                    else:
                        op("dve", lambda e, rb=rb, h=h: e.scalar_tensor_tensor(out=isc[:, c0:c0 + wdt], in0=rb[:, 0:wdt], scalar=wi[:, h:h + 1], in1=isc[:, c0:c0 + wdt], op0=ALU.mult, op1=ALU.add), r=[rb, wi, isc], w=[isc])
            Zs = [[P[2], P[3]], [P[4], P[5]]]
            PVs = [P[6], P[1]]
            TSs = [(P[0], P[0][:, 0:8]), (PTR, PTR[:, 0:16].bitcast(F32))]

            def sb_stage1(kt, b):
                ksl = slice(kt * 128, (kt + 1) * 128)
                Z = Zs[b]
                eb, sp_ = ebufs[b], spTs[b]
                for h in range(8):
                    op("pe", lambda e, h=h: e.matmul(Z[h // 4][:, (h % 4) * 128:(h % 4 + 1) * 128], lhsT=kbT[:, h, ksl], rhs=qbT[:, h, :], start=(h % 4 == 0), stop=False), r=[kvres[kt], qbT], w=[Z[h // 4]])
                for hf in range(2):
                    op("act", lambda e, hf=hf: e.activation(out=eb[:, hf * 512:(hf + 1) * 512], in_=Z[hf][:, :], func=AF.Exp), r=[Z[hf]], w=[eb])
                    op("act", lambda e, hf=hf: e.activation(out=sp_[:, hf * 512:(hf + 1) * 512], in_=eb[:, hf * 512:(hf + 1) * 512], func=AF.Ln, bias=1.0), r=[eb], w=[sp_])
                if kt == nk - 1:
                    s3 = sp_[:, :].rearrange("p (h q) -> p h q", h=8)
                    op("pool", lambda e, s3=s3: e.tensor_tensor(out=s3, in0=s3, in1=mlt[:, :].unsqueeze(1).to_broadcast([128, 8, 128]), op=ALU.mult), r=[sp_, mlt], w=[sp_])

            def sb_stage2(kt, b):
                diag = (kt == nk - 1)
                Z = Zs[b]
                sp_, et_ = spTs[b], ETs[b]
                PVb = PVs[b]
                TSr, TSv = TSs[b]
                for hf in range(2):
                    op("pe", lambda e, hf=hf: e.matmul(Z[hf][:, :], lhsT=negtri[:], rhs=sp_[:, hf * 512:(hf + 1) * 512], start=False, stop=(not diag)), r=[negtri, sp_], w=[Z[hf]])
                    if diag:
                        op("pe", lambda e, hf=hf: e.matmul(Z[hf][:, :], lhsT=ident[:], rhs=negm4[:], start=False, stop=True), r=[ident, negm4], w=[Z[hf]])
                    op("act", lambda e, hf=hf: e.activation(out=et_[:, hf * 512:(hf + 1) * 512], in_=Z[hf][:, :], func=AF.Exp), r=[Z[hf]], w=[et_])
                for h in range(8):
                    op("pe", lambda e, h=h: e.matmul(TSv[:, h:h + 1], lhsT=sp_[:, h * 128:(h + 1) * 128], rhs=ones1[:], start=True, stop=True), r=[sp_, ones1], w=[TSr])
                for h in range(8):
                    op("pe", lambda e, h=h: e.matmul(PVb[:, h * 64:(h + 1) * 64], lhsT=et_[:, h * 128:(h + 1) * 128], rhs=vbB[:, kt, h, :], start=True, stop=True), r=[et_, kvres[kt]], w=[PVb])
                if kt == 0:
                    op("act", lambda e: e.activation(out=ob[:], in_=PVb[:, :], func=AF.Copy), r=[PVb], w=[ob])
                else:
                    op("act", lambda e: e.activation(out=dd[:], in_=TSv, func=AF.Exp, scale=-1.0), r=[TSr], w=[dd])
                    o3 = ob[:, :].rearrange("p (h d) -> p h d", h=8)
                    op("act", lambda e: e.activation(out=pvs[:], in_=PVb[:, :], func=AF.Copy), r=[PVb], w=[pvs])
                    op("pool", lambda e, o3=o3: e.tensor_tensor(out=o3, in0=o3, in1=dd[:, :].unsqueeze(2).to_broadcast([128, 8, 64]), op=ALU.mult), r=[ob, dd], w=[ob])
                    op("pool", lambda e: e.tensor_tensor(out=ob[:], in0=ob[:], in1=pvs[:], op=ALU.add), r=[ob, pvs], w=[ob])

            sb_stage1(0, 0)
            for kt in range(nk):
                if kt + 1 < nk:
                    sb_stage1(kt + 1, (kt + 1) % 2)
                sb_stage2(kt, kt % 2)
            op("pool", lambda e: e.dma_start(out=oab_dram[rows, 512:1024], in_=ob[:]), r=[ob], w=[oab_res[t]], dma=True)

            need_thr = nk > 2
            if need_thr:
                op("dve", lambda e: e.tensor_reduce(out=bis[:, 5:6], in_=isc[:, 0:S], axis=AX.X, op=ALU.max), r=[isc], w=[bis])
                op("dve", lambda e: e.tensor_reduce(out=bis[:, 6:7], in_=isc[:, 0:S], axis=AX.X, op=ALU.min), r=[isc], w=[bis])
                op("dve", lambda e: e.tensor_scalar(out=bis[:, 6:7], in0=bis[:, 6:7], scalar1=-1.0, scalar2=None, op0=ALU.mult), r=[bis], w=[bis])
                op("dve", lambda e: e.tensor_tensor(out=bis[:, 4:5], in0=bis[:, 5:6], in1=bis[:, 6:7], op=ALU.max), r=[bis], w=[bis])
                op("dve", lambda e: e.tensor_scalar(out=bis[:, 0:1], in0=bis[:, 4:5], scalar1=-1.0, scalar2=None, op0=ALU.mult), r=[bis], w=[bis])
                op("dve", lambda e: e.tensor_scalar(out=dtab[:], in0=pow2[:], scalar1=bis[:, 4:5], scalar2=2.002, op0=ALU.mult, op1=ALU.mult), r=[bis, pow2], w=[dtab])
            if samp:
                op("dve", lambda e: e.memset(isc[:, 16 * 128 + DEC:17 * 128], NEG), r=[], w=[isc])
            else:
                op("dve", lambda e: e.memset(isc[0:64, t * 128 + 64:(t + 1) * 128], NEG), r=[], w=[isc])
            if need_thr:
                for k in range(NBIS):
                    op("dve", lambda e, k=k: e.tensor_tensor(out=bis[:, 1:2], in0=bis[:, 0:1], in1=dtab[:, k:k + 1], op=ALU.add), r=[bis, dtab], w=[bis])
                    op("dve", lambda e: e.tensor_scalar(out=mbias[:, 0:S], in0=isc[:, 0:S], scalar1=bis[:, 1:2], scalar2=None, op0=ALU.is_ge, op1=ALU.add, accum_out=bis[:, 2:3]), r=[isc, bis], w=[mbias, bis])
                    op("dve", lambda e, k=k: e.scalar_tensor_tensor(out=bis[:, 3:4], in0=bis[:, 2:3], scalar=float(TOPK), in1=dtab[:, k:k + 1], op0=ALU.is_ge, op1=ALU.mult), r=[bis, dtab], w=[bis])
                    op("dve", lambda e: e.tensor_tensor(out=bis[:, 0:1], in0=bis[:, 0:1], in1=bis[:, 3:4], op=ALU.add), r=[bis], w=[bis])
            else:
                op("dve", lambda e: e.memset(bis[:, 0:1], -1e29), r=[], w=[bis])
            op("dve", lambda e: e.tensor_scalar(out=mbias[:, 0:S], in0=isc[:, 0:S], scalar1=bis[:, 0:1], scalar2=-30000.0, op0=ALU.is_lt, op1=ALU.mult), r=[isc, bis], w=[mbias])
            OAp = [P[4], P[5]]
            for kt in range(nk):
                ksl = slice(kt * 128, (kt + 1) * 128)
                for g in range(2):
                    pt = P[2 + g]
                    pb_ = PT[g]
                    op("pe", lambda e, g=g, pt=pt, ksl=ksl: e.matmul(pt[:, :], lhsT=kaT[:, g, ksl], rhs=qaT[:, 4 * g:4 * g + 4, :], start=True, stop=False), r=[kvres[kt], qaT], w=[pt])
                    op("pe", lambda e, pt=pt, ksl=ksl: e.matmul(pt[:, :], lhsT=mbias[:, ksl], rhs=ident4[:], start=False, stop=True), r=[mbias, ident4], w=[pt])
                    op("act", lambda e, pt=pt, pb_=pb_: e.activation(out=pb_[:], in_=pt[:, :], func=AF.Exp), r=[pt], w=[pb_])
                    for hh in range(4):
                        op("pe", lambda e, g=g, hh=hh, pb_=pb_, kt=kt: e.matmul(OAp[g][:, hh * 65:(hh + 1) * 65], lhsT=pb_[:, hh * 128:(hh + 1) * 128], rhs=vaA[:, kt, g, :], start=(kt == 0 and hh == 0), stop=(kt == nk - 1)),
                           r=[pb_, kvres[kt]], w=[OAp[g]])
            for g in range(2):
                o3 = OAp[g][:, 0:260].rearrange("p (h d) -> p h d", h=4)
                op("dve", lambda e, g=g, o3=o3: e.reciprocal(out=st8[:, 40 + 4 * g:44 + 4 * g], in_=o3[:, :, 64]), r=[OAp[g]], w=[st8])
                op("dve", lambda e, g=g, o3=o3: e.tensor_tensor(out=oa[:, g * 256:(g + 1) * 256].rearrange("p (h d) -> p h d", h=4), in0=o3[:, :, 0:64],
                                                                 in1=st8[:, 40 + 4 * g:44 + 4 * g].unsqueeze(2).to_broadcast([128, 4, 64]), op=ALU.mult), r=[OAp[g], st8], w=[oa])
            op("pool", lambda e: e.dma_start(out=oab_dram[rows, 0:512], in_=oa[:]), r=[oa], w=[oab_res[t]], dma=True)

        kb.barrier()
        load_cast_rows(WARENA_B, lambda c, o0, wdt: w_in[l, c * 128:(c + 1) * 128, 2888 + o0:2888 + o0 + wdt], 8, 2048,
                       lambda c, o0, wdt: WA[:, GOFF + c * 2048 + o0: GOFF + c * 2048 + o0 + wdt], n1s, lambda c: c)
        load_cast_rows(WARENA_B, lambda c, o0, wdt: w_pa[l, c * 128:(c + 1) * 128, o0:o0 + wdt], 4, 1024,
                       lambda c, o0, wdt: WA[:, PAOFF + c * 1024 + o0: PAOFF + c * 1024 + o0 + wdt])
        load_cast_rows(WARENA_B, lambda c, o0, wdt: w_pb[l, c * 128:(c + 1) * 128, o0:o0 + wdt], 4, 1024,
                       lambda c, o0, wdt: WA[:, PBOFF + c * 1024 + o0: PBOFF + c * 1024 + o0 + wdt])
        load_cast_rows(WARENA_B, lambda c, o0, wdt: w_o[l, c * 128:(c + 1) * 128, o0:o0 + wdt], 8, 1024,
                       lambda c, o0, wdt: WA[:, WOOFF + c * 1024 + o0: WOOFF + c * 1024 + o0 + wdt])
        for t in tiles:
            samp = (t == NT - 1)
            rows = slice(t * 128, (t + 1) * 128)
            op("sp", lambda e: e.dma_start(out=xt[:], in_=xin_d[rows, :]), r=[xin_r[t]], w=[xt], dma=True)
            rmsnorm_rows(xt, hb, sq)
            make_hT()
            for gi, gdst in ((0, sga), (1, sgb)):
                for hf in range(2):
                    pt = P[hf]
                    c0 = GOFF + gi * 1024 + hf * 512
                    for c in range(8):
                        op("pe", lambda e, c=c, pt=pt, c0=c0: e.matmul(pt[:, :], lhsT=hT[:, c, :], rhs=WA[:, c * 2048 + c0: c * 2048 + c0 + 512], start=(c == 0), stop=(c == 7)), r=[hT, WARENA_B], w=[pt])
                    op("act", lambda e, pt=pt, gdst=gdst, hf=hf: e.activation(out=gdst[:, hf * 512:(hf + 1) * 512], in_=pt[:, :], func=AF.Sigmoid), r=[pt], w=[gdst])
            op("sp", lambda e: e.dma_start(out=oabf[:], in_=oab_dram[rows, :]), r=[oab_res[t]], w=[oabf], dma=True)
            op("pool", lambda e: e.tensor_copy(out=oabb[:], in_=oabf[:]), r=[oabf], w=[oabb])
            for bi, (woff, gdst) in enumerate(((PAOFF, sga), (PBOFF, sgb))):
                for c in range(4):
                    op("pe", lambda e, c=c, bi=bi: e.transpose(out=PTR[:, c * 128:(c + 1) * 128], in_=oabb[:, bi * 512 + c * 128: bi * 512 + (c + 1) * 128], identity=ident[:]), r=[oabb, ident], w=[PTR])
                op("act", lambda e: e.activation(out=oT[:, 0:4, :], in_=PTR[:, 0:512].rearrange("p (c n) -> p c n", c=4), func=AF.Copy), r=[PTR], w=[oT])
                for hf in range(2):
                    pt = P[2 + hf]
                    for c in range(4):
                        op("pe", lambda e, c=c, pt=pt, hf=hf, woff=woff: e.matmul(pt[:, :], lhsT=oT[:, c, :], rhs=WA[:, woff + c * 1024 + hf * 512: woff + c * 1024 + (hf + 1) * 512], start=(c == 0), stop=(c == 3)), r=[oT, WARENA_B], w=[pt])
                    if bi == 0:
                        op("dve", lambda e, pt=pt, hf=hf: e.tensor_tensor(out=mm[:, hf * 512:(hf + 1) * 512], in0=sga[:, hf * 512:(hf + 1) * 512], in1=pt[:, :], op=ALU.mult), r=[sga, pt], w=[mm])
                    else:
                        op("dve", lambda e, pt=pt, hf=hf: e.tensor_tensor(out=sgb[:, hf * 512:(hf + 1) * 512], in0=sgb[:, hf * 512:(hf + 1) * 512], in1=pt[:, :], op=ALU.mult), r=[sgb, pt], w=[sgb])
            op("dve", lambda e: e.tensor_tensor(out=mbf[:], in0=mm[:], in1=sgb[:], op=ALU.add), r=[mm, sgb], w=[mbf])
            for c in range(8):
                op("pe", lambda e, c=c: e.transpose(out=PTR[:, c * 128:(c + 1) * 128], in_=mbf[:, c * 128:(c + 1) * 128], identity=ident[:]), r=[mbf, ident], w=[PTR])
            op("act", lambda e: e.activation(out=oT[:], in_=PTR[:, :].rearrange("p (c n) -> p c n", c=8), func=AF.Copy), r=[PTR], w=[oT])
            for hf in range(2):
                pt = P[4 + hf]
                for c in range(8):
                    op("pe", lambda e, c=c, pt=pt, hf=hf: e.matmul(pt[:, :], lhsT=oT[:, c, :], rhs=WA[:, WOOFF + c * 1024 + hf * 512: WOOFF + c * 1024 + (hf + 1) * 512], start=(c == 0), stop=(c == 7)), r=[oT, WARENA_B], w=[pt])
                op("dve", lambda e, pt=pt, hf=hf: e.tensor_tensor(out=xt[:, hf * 512:(hf + 1) * 512], in0=xt[:, hf * 512:(hf + 1) * 512], in1=pt[:, :], op=ALU.add), r=[xt, pt], w=[xt])

            op("sp", lambda e: e.dma_start(out=xmid[rows, :], in_=xt[:]), r=[xt], w=[xmres[t]], dma=True)

        if do_peer:
            kb.barrier()
            load_cast_rows(WQ, lambda c, o0, wdt: pwq[l, c * 128:(c + 1) * 128, o0:o0 + wdt], 8, 1024,
                           lambda c, o0, wdt: WQ[:, c * 1024 + o0: c * 1024 + o0 + wdt], n2s, lambda c: c)
            for which, src in ((0, pk1T), (1, pk2T)):
                s_ = stg[which]
                op("sp", lambda e, s_=s_, src=src: e.dma_start(out=s_[0:64, 0:1024], in_=src[l, :, :, :].rearrange("d h k -> d (h k)")), r=[in_res], w=[s_], dma=True)
                op("dve", lambda e, s_=s_, which=which: e.tensor_copy(out=k12[:, which, :, :], in_=s_[0:64, 0:1024].rearrange("d (h k) -> d h k", h=8)), r=[s_], w=[k12])
            def b2_front(t, TT):
                rows = slice(t * 128, (t + 1) * 128)
                op("sp", lambda e: e.dma_start(out=xt[:], in_=xmid[rows, :]), r=[xmres[t]], w=[xt], dma=True)
                rmsnorm_rows(xt, hb, sq)
                make_hT()
                op("pool", lambda e: e.tensor_copy(out=h2T_all[:, :, t * 128:(t + 1) * 128], in_=hT[:]), r=[hT], w=[h2T_all])
                for hf in range(2):
                    pt = P[hf]
                    for c in range(8):
                        op("pe", lambda e, c=c, pt=pt, hf=hf: e.matmul(pt[:, :], lhsT=hT[:, c, :], rhs=WQ[:, c * 1024 + hf * 512: c * 1024 + (hf + 1) * 512], start=(c == 0), stop=(c == 7)), r=[hT, WQ], w=[pt])
                    op("act", lambda e, pt=pt, hf=hf: e.activation(out=qsb[:, hf * 512:(hf + 1) * 512], in_=pt[:, :], func=AF.Copy), r=[pt], w=[qsb])
                for hf in range(2):
                    for b in range(8):
                        bb = hf * 8 + b
                        op("pe", lambda e, b=b, bb=bb: e.transpose(out=PTR[0:64, b * 128:(b + 1) * 128], in_=qsb[:, bb * 64:(bb + 1) * 64], identity=ident[:]), r=[qsb, ident], w=[PTR])
                    op("act", lambda e, hf=hf: e.activation(out=qT[:, hf * 8:(hf + 1) * 8, :], in_=PTR[0:64, :].rearrange("p (b n) -> p b n", b=8), func=AF.Copy), r=[PTR], w=[qT])
                yield
                for which in range(2):
                    for h in range(8):
                        pt = P[2 + h // 4]
                        op("pe", lambda e, h=h, pt=pt, which=which: e.matmul(pt[:, (h % 4) * 128:(h % 4 + 1) * 128], lhsT=qT[:, h * 2 + which, :], rhs=k12[:, which, h, :], start=True, stop=True), r=[qT, k12], w=[pt])
                    for hf in range(2):
                        pt = P[2 + hf]
                        op("act", lambda e, pt=pt, which=which, hf=hf: e.activation(out=s12[:, which, hf * 4:(hf + 1) * 4, :], in_=pt[:, :].rearrange("p (h k) -> p h k", h=4), func=AF.Copy), r=[pt], w=[s12])
                yield
                for which in range(2):
                    for h in range(8):
                        sv = s12[:, which, h, :]
                        op("dve", lambda e, sv=sv, which=which, h=h: e.max(out=v12[:, which, h, 0:8], in_=sv), r=[s12], w=[v12])
                        op("dve", lambda e, sv=sv, which=which, h=h: e.max_index(out=i12[:, which, h, 0:8], in_max=v12[:, which, h, 0:8], in_values=sv), r=[s12, v12], w=[i12])
                        op("dve", lambda e, sv=sv, which=which, h=h: e.match_replace(out=swk[:, 0:128], in_to_replace=v12[:, which, h, 0:8], in_values=sv, imm_value=NEG), r=[s12, v12], w=[swk])
                        op("dve", lambda e, which=which, h=h: e.max(out=v12[:, which, h, 8:16], in_=swk[:, 0:128]), r=[swk], w=[v12])
                        op("dve", lambda e, which=which, h=h: e.max_index(out=i12[:, which, h, 8:16], in_max=v12[:, which, h, 8:16], in_values=swk[:, 0:128]), r=[swk, v12], w=[i12])
                yield
                op("dve", lambda e: e.tensor_copy(out=i12f[:], in_=i12[:]), r=[i12], w=[i12f])
                c4 = cand[:, :, :].rearrange("p h (r c) -> p h r c", r=16)
                op("dve", lambda e: e.tensor_tensor(out=c4, in0=v12[:, 0, :, :].unsqueeze(3).to_broadcast([128, 8, 16, 16]), in1=v12[:, 1, :, :].unsqueeze(2).to_broadcast([128, 8, 16, 16]), op=ALU.add), r=[v12], w=[cand])
                for h in range(8):
                    op("dve", lambda e, h=h: e.max(out=sc[:, h, 0:8], in_=cand[:, h, :]), r=[cand], w=[sc])
                    op("dve", lambda e, h=h: e.max_index(out=pos[:, h, 0:8], in_max=sc[:, h, 0:8], in_values=cand[:, h, :]), r=[cand, sc], w=[pos])
                    op("dve", lambda e, h=h: e.match_replace(out=swk[:, 0:256], in_to_replace=sc[:, h, 0:8], in_values=cand[:, h, :], imm_value=NEG), r=[cand, sc], w=[swk])
                    op("dve", lambda e, h=h: e.max(out=sc[:, h, 8:16], in_=swk[:, 0:256]), r=[swk], w=[sc])
                    op("dve", lambda e, h=h: e.max_index(out=pos[:, h, 8:16], in_max=sc[:, h, 8:16], in_values=swk[:, 0:256]), r=[swk, sc], w=[pos])
                op("dve", lambda e: e.tensor_copy(out=posf[:], in_=pos[:]), r=[pos], w=[posf])
                op("dve", lambda e: e.tensor_tensor(out=oh[:], in0=posf[:, :, :].unsqueeze(3).to_broadcast([128, 8, 16, 16]),
                                                    in1=thr16[:, :].unsqueeze(1).unsqueeze(1).to_broadcast([128, 8, 16, 16]), op=ALU.is_ge), r=[posf, thr16], w=[oh])
                op("dve", lambda e: e.tensor_reduce(out=posrf[:], in_=oh[:], axis=AX.X, op=ALU.add), r=[oh], w=[posrf])
                op("dve", lambda e: e.scalar_tensor_tensor(out=poscf[:], in0=posrf[:], scalar=-16.0, in1=posf[:], op0=ALU.mult, op1=ALU.add), r=[posrf, posf], w=[poscf])
                io4 = iota16[:, :].unsqueeze(1).unsqueeze(1).to_broadcast([128, 8, 16, 16])
                for pf_, which, dsel in ((posrf, 0, sel1), (poscf, 1, sel2)):
                    op("dve", lambda e, pf_=pf_: e.tensor_tensor(out=oh[:], in0=io4, in1=pf_[:, :, :].unsqueeze(3).to_broadcast([128, 8, 16, 16]), op=ALU.is_equal), r=[iota16, pf_], w=[oh])
                    op("dve", lambda e, which=which: e.tensor_tensor(out=oh[:], in0=oh[:], in1=i12f[:, which, :, :].unsqueeze(2).to_broadcast([128, 8, 16, 16]), op=ALU.mult), r=[oh, i12f], w=[oh])
                    op("dve", lambda e, dsel=dsel: e.tensor_reduce(out=dsel[:], in_=oh[:], axis=AX.X, op=ALU.add), r=[oh], w=[dsel])
                op("dve", lambda e: e.tensor_tensor(out=gsm[:], in0=sc[:], in1=sc[:, :, 0:1].to_broadcast([128, 8, 16]), op=ALU.subtract), r=[sc], w=[gsm])
                op("act", lambda e: e.activation(out=gsm[:], in_=gsm[:], func=AF.Exp), r=[gsm], w=[gsm])
                op("dve", lambda e: e.tensor_reduce(out=st8[:, 48:56], in_=gsm[:], axis=AX.X, op=ALU.add), r=[gsm], w=[st8])
                op("dve", lambda e: e.reciprocal(out=st8[:, 56:64], in_=st8[:, 48:56]), r=[st8], w=[st8])
                op("dve", lambda e: e.tensor_tensor(out=gsm[:], in0=gsm[:], in1=st8[:, 56:64].unsqueeze(2).to_broadcast([128, 8, 16]), op=ALU.mult), r=[gsm, st8], w=[gsm])
                for i_, src in enumerate((sel1, sel2, gsm)):
                    op("pe", lambda e, i_=i_, src=src: e.transpose(out=P[0][:, i_ * 128:(i_ + 1) * 128], in_=src[:, :, :].rearrange("p h k -> p (h k)"), identity=identf[:]), r=[src, identf], w=[P[0]])
                op("act", lambda e: e.activation(out=TT[:], in_=P[0][:, 0:384].rearrange("p (a n) -> p a n", a=3), func=AF.Copy), r=[P[0]], w=[TT])

            def b2_back(t, TT):
                iob = iota128[:, :].unsqueeze(1).to_broadcast([128, 32, 128])
                for qq in range(4):
                    n0 = qq * 32
                    p1 = P1q[qq % 2]
                    p2 = P2q[qq % 2]
                    op("dve", lambda e, n0=n0, p2=p2: e.tensor_tensor(out=p2[:], in0=iob, in1=TT[:, 1, n0:n0 + 32].unsqueeze(2).to_broadcast([128, 32, 128]), op=ALU.is_equal), r=[iota128, TT], w=[p2])
                    op("dve", lambda e, n0=n0, p1=p1: e.tensor_tensor(out=p1[:], in0=iob, in1=TT[:, 0, n0:n0 + 32].unsqueeze(2).to_broadcast([128, 32, 128]), op=ALU.is_equal), r=[iota128, TT], w=[p1])
                    op("pool", lambda e, n0=n0, p1=p1: e.tensor_tensor(out=p1[:], in0=p1[:], in1=TT[:, 2, n0:n0 + 32].unsqueeze(2).to_broadcast([128, 32, 128]), op=ALU.mult), r=[p1, TT], w=[p1])
                    for q4 in range(8):
                        bank = P[4 + (q4 % 3)]
                        for k in range(4):
                            n = q4 * 4 + k
                            op("pe", lambda e, bank=bank, k=k, n=n, p1=p1, p2=p2: e.matmul(bank[:, :].rearrange("p (i n) -> p n i", n=4)[:, k, :], lhsT=p1[:, n, :], rhs=p2[:, n, :], start=True, stop=True), r=[p1, p2], w=[bank])
                        nn = n0 + q4 * 4
                        op("act", lambda e, bank=bank, nn=nn: e.activation(out=Gs[:, :, nn:nn + 4], in_=bank[:, :].rearrange("p (i n) -> p i n", n=4), func=AF.Copy), r=[bank], w=[Gs])
                    yield

            TTs = [TT, TT2]
            if tiles:
                for _ in b2_front(tiles[0], TTs[0]):
                    pass
            for i_, t in enumerate(tiles):
                fg = b2_front(tiles[i_ + 1], TTs[(i_ + 1) % 2]) if i_ + 1 < len(tiles) else iter(())
                bg = b2_back(t, TTs[i_ % 2])
                for _q in range(4):
                    next(bg, None)
                    next(fg, None)
                for _ in bg:
                    pass
                for _ in fg:
                    pass
                op("sp", lambda e: e.dma_start(out=Gd[t, :, :], in_=Gs[:, :, :].rearrange("p i n -> p (i n)")), r=[Gs], w=[gdres[t]], dma=True)

            kb.barrier()
            pvv = pv[l][:, :].rearrange("(i1 i2) d -> i1 i2 d", i2=128)
            tgroups = [tiles[i:i + 2] for i in range(0, len(tiles), 2)]
            NSB = 128 // NBLK
            cring = [0]
            pend_casts = {}

            def emit_loads(sbk):
                ub_ = u16[sbk % 2]
                vb_ = v16[sbk % 2]
                casts = []
                for c in range(8):
                    s_ = cstg[cring[0] % NCS]
                    cring[0] += 1
                    op("pool", lambda e, s_=s_, c=c: e.dma_start(out=s_[:, 0:512], in_=puT[l][c * 128:(c + 1) * 128, sbk * NBLK * 128:(sbk + 1) * NBLK * 128]), r=[in_res], w=[s_], dma=True)
                    casts.append(lambda s_=s_, c=c, ub_=ub_: op("act", lambda e: e.activation(out=ub_[:, c, :], in_=s_[:, 0:512], func=AF.Copy, scale=n2s[:, c:c + 1]), r=[s_, n2s], w=[ub_]))
                for blk in range(NBLK):
                    for hf in range(2):
                        s_ = cstg[cring[0] % NCS]
                        cring[0] += 1
                        op("pool", lambda e, s_=s_, blk=blk, hf=hf: e.dma_start(out=s_[:, 0:512], in_=pvv[:, sbk * NBLK + blk, hf * 512:(hf + 1) * 512]), r=[in_res], w=[s_], dma=True)
                        casts.append(lambda s_=s_, blk=blk, hf=hf, vb_=vb_: op("act", lambda e: e.activation(out=vb_[:, blk, hf * 512:(hf + 1) * 512], in_=s_[:, 0:512], func=AF.Copy), r=[s_], w=[vb_]))
                return casts

            items = [(sbk, gi_, blk) for sbk in range(NSB) for gi_ in range(len(tgroups)) for blk in range(NBLK)]

            def emit_a(it_):
                sbk, gi_, blk = it_
                ub_ = u16[sbk % 2]
                tl = tgroups[gi_]
                nt_ = len(tl)
                ntk = 128 * nt_
                A = P[blk % 2]
                if blk == 0:
                    gc = Gc[(sbk * len(tgroups) + gi_) % 2]
                    for ti, t in enumerate(tl):
                        op("sp", lambda e, ti=ti, t=t: e.dma_start(out=gc[:, ti, :, :], in_=Gd[t, :, sbk * NBLK * 128:(sbk + 1) * NBLK * 128].rearrange("p (i n) -> p i n", i=NBLK)), r=[gdres[t]], w=[gc], dma=True)
                contiguous = (nt_ == 2 and tl[1] == tl[0] + 1) or nt_ == 1
                if contiguous:
                    for c in range(8):
                        op("pe", lambda e, c=c: e.matmul(A[:, 0:ntk], lhsT=ub_[:, c, blk * 128:(blk + 1) * 128], rhs=h2T_all_c[:, c, tl[0] * 128: tl[0] * 128 + ntk], start=(c == 0), stop=(c == 7)), r=[ub_, h2T_all_c], w=[A])
                else:
                    for ti, t in enumerate(tl):
                        for c in range(8):
                            op("pe", lambda e, c=c, ti=ti, t=t: e.matmul(A[:, ti * 128:(ti + 1) * 128], lhsT=ub_[:, c, blk * 128:(blk + 1) * 128], rhs=h2T_all_c[:, c, t * 128:(t + 1) * 128], start=(c == 0 and ti == 0), stop=(c == 7)), r=[ub_, h2T_all_c], w=[A])

            def emit_rest(it_):
                sbk, gi_, blk = it_
                vb_ = v16[sbk % 2]
                tl = tgroups[gi_]
                nt_ = len(tl)
                ntk = 128 * nt_
                A = P[blk % 2]
                ge = gel[blk % 2]
                cf = cfT[blk % 2]
                gc = Gc[(sbk * len(tgroups) + gi_) % 2]
                op("act", lambda e: e.activation(out=ge[:, 0:ntk], in_=A[:, 0:ntk], func=AF.Gelu), r=[A], w=[ge])
                op("dve", lambda e: e.tensor_tensor(out=cf[:, 0:ntk].rearrange("p (t n) -> p t n", t=nt_), in0=ge[:, 0:ntk].rearrange("p (t n) -> p t n", t=nt_), in1=gc[:, 0:nt_, blk, :], op=ALU.mult), r=[ge, gc], w=[cf])
                for ti, t in enumerate(tl):
                    for hf in range(2):
                        ab = P[2 + ti * 2 + hf]
                        op("pe", lambda e, ab=ab, ti=ti, hf=hf: e.matmul(ab[:, :], lhsT=cf[:, ti * 128:(ti + 1) * 128], rhs=vb_[:, blk, hf * 512:(hf + 1) * 512], start=(blk == 0), stop=(blk == NBLK - 1)), r=[cf, vb_], w=[ab])
                if blk == NBLK - 1:
                    for ti, t in enumerate(tl):
                        for hf in range(2):
                            ab = P[2 + ti * 2 + hf]
                            if sbk == 0:
                                op("dve", lambda e, ab=ab, t=t, hf=hf: e.tensor_copy(out=accs[:, t, hf * 512:(hf + 1) * 512], in_=ab[:, :]), r=[ab], w=[accres[t]])
                            else:
                                op("dve", lambda e, ab=ab, t=t, hf=hf: e.tensor_tensor(out=accs[:, t, hf * 512:(hf + 1) * 512], in0=accs[:, t, hf * 512:(hf + 1) * 512], in1=ab[:, :], op=ALU.add), r=[ab, accres[t]], w=[accres[t]])

            for cst_ in emit_loads(0):
                cst_()
            ng = len(tgroups)
            cast_at = {max(0, ng // 3): (0, 8), max(1, (2 * ng) // 3): (8, 16)} if ng >= 3 else {0: (0, 16)}
            emit_a(items[0])
            for i_, it_ in enumerate(items):
                sbk, gi_, blk = it_
                if gi_ == 0 and blk == 0 and sbk + 1 < NSB:
                    pend_casts[sbk + 1] = emit_loads(sbk + 1)
                if blk == 0 and gi_ in cast_at and (sbk + 1) in pend_casts:
                    a_, b_ = cast_at[gi_]
                    for cst_ in pend_casts[sbk + 1][a_:b_]:
                        cst_()
                if i_ + 1 < len(items):
                    emit_a(items[i_ + 1])
                emit_rest(it_)

        for t in tiles:
            samp = (t == NT - 1)
            rows = slice(t * 128, (t + 1) * 128)
            op("sp", lambda e: e.dma_start(out=xt[:], in_=xmid[rows, :]), r=[xmres[t]], w=[xt], dma=True)
            if do_peer:
                op("dve", lambda e: e.tensor_tensor(out=xt[:], in0=xt[:], in1=accs[:, t, :], op=ALU.add), r=[xt, accres[t]], w=[xt])
            if last:
                if samp:
                    op("sp", lambda e: e.dma_start(out=y_s[:, :], in_=xt[0:DEC, :]), r=[xt], w=[out_res], dma=True)
                else:
                    op("sp", lambda e: e.dma_start(out=y_p[rows, :], in_=xt[:]), r=[xt], w=[out_res], dma=True)
            else:
                op("sp", lambda e: e.dma_start(out=xout_d[rows, :], in_=xt[:]), r=[xt], w=[xout_r[t]], dma=True)

    kb.finish()
    es.close()
    return nc, kb.ninst


def _consts():
    j = np.arange(128)
    ident = np.eye(128, dtype=np.float32)
    negtri = -(j[:, None] >= j[None, :]).astype(np.float32)
    mlt = (j[:, None] < j[None, :]).astype(np.float32)
    ident4 = np.tile(ident, (1, 4))
    negm = np.where(j[:, None] >= j[None, :], -30000.0, 0.0).astype(np.float32)
    negm4 = np.tile(negm, (1, 4))
    pow2 = np.tile((0.5 ** np.arange(1, NBIS + 1)).astype(np.float32)[None, :], (128, 1))
    iota = np.tile(np.arange(16, dtype=np.float32)[None, :], (128, 1))
    iota128 = np.tile(np.arange(128, dtype=np.float32)[None, :], (128, 1))
    return np.concatenate([ident, negtri, mlt, ident4, negm4, pow2, iota, iota128], axis=1).astype(np.float32)


def _rope_table():
    pos = np.concatenate([np.arange(SEQ), SEQ + np.arange(128)]).astype(np.float32)
    half = 32
    inv = (10000.0 ** (-np.arange(half, dtype=np.float32) / half)).astype(np.float32)
    ang = pos[:, None] * inv[None, :]
    return np.concatenate([np.cos(ang), np.sin(ang)], axis=1).astype(np.float32)


_PROG = {}


def _prep(x_prompt, x_sample, cache_a_k, cache_a_v, cache_idx_k, cache_b_k, cache_b_v,
          norm1, w_in, q_norm_a, k_norm_a, idx_k_norm, w_pa, w_pb, w_o, norm2,
          peer_wq, peer_k1, peer_k2, peer_u, peer_v, cores=range(8)):
    f = lambda a: np.ascontiguousarray(np.asarray(a, dtype=np.float32))
    x_prompt = f(x_prompt); x_sample = f(x_sample)
    shared = {
        "rope": _rope_table(),
        "cst": _consts(),
        "n1T": f(np.asarray(norm1).reshape(DEPTH, 8, 128).transpose(0, 2, 1)),
        "n2T": f(np.asarray(norm2).reshape(DEPTH, 8, 128).transpose(0, 2, 1)),
        "n2r": f(norm2),
        "w_in": f(w_in), "qn": f(q_norm_a), "kn": f(k_norm_a), "ikn": f(idx_k_norm),
        "w_pa": f(w_pa), "w_pb": f(w_pb), "w_o": f(w_o), "pwq": f(peer_wq),
        "pk1T": f(np.asarray(peer_k1).transpose(0, 3, 1, 2)),
        "pk2T": f(np.asarray(peer_k2).transpose(0, 3, 1, 2)),
    }
    pu = np.asarray(peer_u); pvv = np.asarray(peer_v)
    for l in range(DEPTH):
        shared["puT%d" % l] = f(pu[l].reshape(128, 128, D).transpose(2, 1, 0).reshape(D, 16384))
        shared["pv%d" % l] = f(pvv[l])
    cak = np.asarray(cache_a_k); cav = np.asarray(cache_a_v); cik = np.asarray(cache_idx_k)
    cbk = np.asarray(cache_b_k); cbv = np.asarray(cache_b_v)
    in_maps = []
    for b in cores:
        xa = np.zeros((NT * 128, D), np.float32)
        xa[:SEQ] = x_prompt[b]
        xa[SEQ:SEQ + DEC] = x_sample[b]
        m = dict(shared)
        m["x_in"] = xa
        m["cak"] = f(cak[:, b].reshape(DEPTH, SEQ, 128))
        m["cav"] = f(cav[:, b].reshape(DEPTH, SEQ, 128))
        m["cik"] = f(cik[:, b].reshape(DEPTH, SEQ, 64))
        m["cbk"] = f(cbk[:, b].reshape(DEPTH, SEQ, 512))
        m["cbv"] = f(cbv[:, b].reshape(DEPTH, SEQ, 512))
        in_maps.append(m)
    return in_maps


def kernel(**inputs):
    if "full" not in _PROG:
        _PROG["full"] = build_program()[0]
    nc = _PROG["full"]
    in_maps = _prep(**inputs)
    res = run_bass_kernel_spmd(nc, in_maps, core_ids=list(range(8)))
    R = res.results
    st = lambda k, shp: np.stack([np.asarray(R[b][k], dtype=np.float32) for b in range(8)], axis=1).reshape(shp)
    y_p = np.stack([np.asarray(R[b]["y_p"], dtype=np.float32) for b in range(8)], axis=0)
    y_s = np.stack([np.asarray(R[b]["y_s"], dtype=np.float32) for b in range(8)], axis=0)
    return (y_p, y_s,
            st("ak_p", (DEPTH, 8, SEQ, 2, 64)), st("av_p", (DEPTH, 8, SEQ, 2, 64)), st("ik_p", (DEPTH, 8, SEQ, 64)),
            st("bk_p", (DEPTH, 8, SEQ, 8, 64)), st("bv_p", (DEPTH, 8, SEQ, 8, 64)),
            st("ak_s", (DEPTH, 8, DEC, 2, 64)), st("av_s", (DEPTH, 8, DEC, 2, 64)), st("ik_s", (DEPTH, 8, DEC, 64)),
            st("bk_s", (DEPTH, 8, DEC, 8, 64)), st("bv_s", (DEPTH, 8, DEC, 8, 64)))
```
